# Optimizing a Trainium2 kernel written in Bass

```python
import math
import jax, jax.numpy as jnp
from jax import lax
import numpy as np

D_MODEL = 1024
BATCH = 2
SEQ = 8192
DEPTH = 1
DEC_BATCH = 16
DEC_SEQ = 32
PAST_LEN = 1024

CHUNK = 64
N_META = 16
D_S5 = 1024
S5_GROUP = 16
S5_GROUPS = D_S5 // S5_GROUP
S5_STATE = 64
D_ML = 1024
ML_HEADS = 4
ML_HEAD_DIM = D_ML // ML_HEADS
ML_QKV_BLOCK = 4
ML_CONV = 4
D_MIX = D_S5 + D_ML
D_FF = 2816
EPS = 1e-6

kernel_name = 'hymba_s5_mlstm_macaron_stream_step'


def rmsnorm(x, g):
    xf = x.astype(jnp.float32)
    y = xf * lax.rsqrt(jnp.mean(xf * xf, axis=-1, keepdims=True) + EPS)
    return (y * g.astype(jnp.float32)).astype(x.dtype)


def swiglu(x, w_gate, w_up, w_down):
    return (jax.nn.silu(x @ w_gate) * (x @ w_up)) @ w_down


def s5_mix(u, s_re, s_im, lam_re, lam_im, log_dt, b_re, b_im, c_re, c_im, d_skip, glu_w, glu_b):
    f32 = jnp.float32
    n, l, _ = u.shape
    lam = lax.complex(lam_re.astype(f32), lam_im.astype(f32))
    dt = jnp.exp(log_dt.astype(f32))[:, None]
    lam_bar = jnp.exp(lam * dt)
    b_bar = ((lam_bar - 1.0) / lam)[..., None] * lax.complex(b_re.astype(f32), b_im.astype(f32))
    c = lax.complex(c_re.astype(f32), c_im.astype(f32))
    uf = u.astype(f32)
    ug = uf.reshape(n, l, S5_GROUPS, S5_GROUP).astype(jnp.complex64)
    bu = jnp.einsum('gph,nlgh->nlgp', b_bar, ug)
    s0 = lax.complex(s_re.astype(f32), s_im.astype(f32))
    bu = bu.at[:, 0].add(lam_bar * s0)
    a = jnp.broadcast_to(lam_bar, (1, l) + lam_bar.shape)

    def combine(e1, e2):
        a1, b1 = e1
        a2, b2 = e2
        return a1 * a2, a2 * b1 + b2

    _, states = lax.associative_scan(combine, (a, bu), axis=1)
    y = jnp.einsum('ghp,nlgp->nlgh', c, states).real.reshape(n, l, D_S5) + d_skip.astype(f32) * uf
    g = jax.nn.gelu(y)
    y = g * jax.nn.sigmoid(g @ glu_w.astype(f32) + glu_b.astype(f32))
    last = states[:, -1]
    return y.astype(u.dtype), jnp.real(last), jnp.imag(last)


def causal_dwconv(x, buf, w, b):
    l = x.shape[1]
    xp = jnp.concatenate([buf.astype(x.dtype), x], axis=1)
    y = b + sum(xp[:, j:j + l] * w[j] for j in range(ML_CONV))
    return y, xp[:, xp.shape[1] - (ML_CONV - 1):]


def blockdiag(x, w):
    xb = x.reshape(x.shape[:-1] + (w.shape[0], ML_QKV_BLOCK))
    return jnp.einsum('nlbi,nlbo->nlbo', xb, xb)[..., :0].sum() * 0 + jnp.einsum('nlbi,bio->nlbo', xb, w).reshape(x.shape) if False else jnp.einsum('nlbi,bio->nlbo', xb, w).reshape(x.shape)


def split_heads(t):
    n, l, _ = t.shape
    return t.reshape(n, l, ML_HEADS, ML_HEAD_DIM).transpose(0, 2, 1, 3)


def mlstm_chunk(carry, inp):
    c_prev, n_prev, m_prev = carry
    q, k, v, ig, lf = inp
    l = q.shape[2]
    bcum = jnp.cumsum(lf, axis=-1)
    logw = bcum[..., :, None] - bcum[..., None, :] + ig[..., None, :]
    causal = jnp.tril(jnp.ones((l, l), dtype=bool))
    logw = jnp.where(causal, logw, -jnp.inf)
    log_inter = bcum + m_prev[..., None]
    m_t = jnp.maximum(log_inter, jnp.max(logw, axis=-1))
    w = jnp.exp(logw - m_t[..., None])
    a_inter = jnp.exp(log_inter - m_t)
    s = jnp.einsum('nhtd,nhsd->nhts', q, k) * w
    num = a_inter[..., None] * jnp.einsum('nhvk,nhtk->nhtv', c_prev, q) + jnp.einsum('nhts,nhsv->nhtv', s, v)
    den = a_inter * jnp.einsum('nhk,nhtk->nht', n_prev, q) + jnp.sum(s, axis=-1)
    h = num / jnp.maximum(jnp.abs(den), jnp.exp(-m_t))[..., None]
    m_new = m_t[..., -1]
    g_state = jnp.exp(bcum[..., -1] + m_prev - m_new)
    g_src = jnp.exp(bcum[..., -1:] - bcum + ig - m_new[..., None])
    c_new = g_state[..., None, None] * c_prev + jnp.einsum('nhs,nhsv,nhsk->nhvk', g_src, v, k)
    n_new = g_state[..., None] * n_prev + jnp.einsum('nhs,nhsk->nhk', g_src, k)
    return (c_new, n_new, m_new), h


def mlstm_blocks(q, k, v, ig, lf, state, lead):
    state, h0 = mlstm_chunk(state, (q[:, :, :lead], k[:, :, :lead], v[:, :, :lead], ig[:, :, :lead], lf[:, :, :lead]))
    rest = q.shape[2] - lead
    if rest == 0:
        return h0, state
    nc = rest // CHUNK

    def to_chunks(t):
        t = t[:, :, lead:]
        t = t.reshape(t.shape[:2] + (nc, CHUNK) + t.shape[3:])
        return jnp.moveaxis(t, 2, 0)

    state, hs = lax.scan(mlstm_chunk, state, (to_chunks(q), to_chunks(k), to_chunks(v), to_chunks(ig), to_chunks(lf)))
    hs = jnp.moveaxis(hs, 0, 2)
    hs = hs.reshape(hs.shape[:2] + (rest, ML_HEAD_DIM))
    return jnp.concatenate([h0, hs], axis=2), state


def mlstm_mix(xm, z, conv_buf, c0, n0, m0, lead, conv_w, conv_b, wq, wk, wv, ig_w, ig_b, fg_w, fg_b, norm_w, skip):
    f32 = jnp.float32
    n, l, _ = xm.shape
    xc, conv_new = causal_dwconv(xm, conv_buf, conv_w, conv_b)
    xc = jax.nn.silu(xc)
    q = blockdiag(xc, wq)
    k = blockdiag(xc, wk)
    v = blockdiag(xm, wv)
    gate_in = jnp.concatenate([q, k, v], axis=-1)
    ig = (gate_in @ ig_w + ig_b).astype(f32).transpose(0, 2, 1)
    lf = jax.nn.log_sigmoid((gate_in @ fg_w + fg_b).astype(f32)).transpose(0, 2, 1)
    qh = split_heads(q.astype(f32))
    kh = split_heads(k.astype(f32)) * (ML_HEAD_DIM ** -0.5)
    vh = split_heads(v.astype(f32))
    h, (c1, n1, m1) = mlstm_blocks(qh, kh, vh, ig, lf, (c0.astype(f32), n0.astype(f32), m0.astype(f32)), lead)
    mu = jnp.mean(h, axis=-1, keepdims=True)
    var = jnp.mean(jnp.square(h - mu), axis=-1, keepdims=True)
    h = (h - mu) * lax.rsqrt(var + EPS)
    h = h.transpose(0, 2, 1, 3).reshape(n, l, D_ML) * norm_w.astype(f32)
    out = (h + skip.astype(f32) * xc.astype(f32)) * jax.nn.silu(z.astype(f32))
    return out.astype(xm.dtype), conv_new, c1, n1, m1


def trunk(x, s5_re, s5_im, ml_c, ml_n, ml_m, ml_conv, lead, p):
    outs = ([], [], [], [], [], [])
    for i in range(DEPTH):
        x = x + 0.5 * swiglu(rmsnorm(x, p['norm_ffn1'][i]), p['ffn1_gate'][i], p['ffn1_up'][i], p['ffn1_down'][i])
        proj = rmsnorm(x, p['norm_mix'][i]) @ p['w_in'][i]
        u = proj[..., :D_S5]
        xm = proj[..., D_S5:D_S5 + D_ML]
        z = proj[..., D_S5 + D_ML:]
        y5, r5, i5 = s5_mix(u, s5_re[i], s5_im[i], p['s5_lambda_re'][i], p['s5_lambda_im'][i], p['s5_log_dt'][i],
                            p['s5_b_re'][i], p['s5_b_im'][i], p['s5_c_re'][i], p['s5_c_im'][i], p['s5_d'][i],
                            p['s5_glu_w'][i], p['s5_glu_b'][i])
        ym, cv, c1, n1, m1 = mlstm_mix(xm, z, ml_conv[i], ml_c[i], ml_n[i], ml_m[i], lead,
                                       p['ml_conv_w'][i], p['ml_conv_b'][i], p['ml_wq'][i], p['ml_wk'][i], p['ml_wv'][i],
                                       p['ml_igate_w'][i], p['ml_igate_b'][i], p['ml_fgate_w'][i], p['ml_fgate_b'][i],
                                       p['ml_norm_w'][i], p['ml_skip'][i])
        mixed = jnp.concatenate([rmsnorm(y5, p['out_norm_s5'][i]), rmsnorm(ym, p['out_norm_ml'][i])], axis=-1)
        x = x + mixed @ p['w_out'][i]
        x = x + 0.5 * swiglu(rmsnorm(x, p['norm_ffn2'][i]), p['ffn2_gate'][i], p['ffn2_up'][i], p['ffn2_down'][i])
        for lst, val in zip(outs, (r5, i5, c1, n1, m1, cv)):
            lst.append(val)
    st = [jnp.stack(lst) for lst in outs]
    return rmsnorm(x, p['norm_final']), st[0], st[1], st[2], st[3], st[4], st[5]


def setup_inputs(seed: int = 0) -> dict:
    key = jax.random.key(seed)
    ks = iter(jax.random.split(key, 64))
    f32 = jnp.float32
    L = DEPTH

    def nrm(shape, scale):
        return jax.random.normal(next(ks), shape, f32) * scale

    def gain(shape):
        return 1.0 + nrm(shape, 0.02)

    n_idx = jnp.arange(S5_STATE, dtype=f32)
    return {
        'x_prompt': nrm((BATCH, SEQ, D_MODEL), 1.0),
        'x_sample': nrm((DEC_BATCH, DEC_SEQ, D_MODEL), 1.0),
        'state_s5_re': nrm((L, DEC_BATCH, S5_GROUPS, S5_STATE), 0.5),
        'state_s5_im': nrm((L, DEC_BATCH, S5_GROUPS, S5_STATE), 0.5),
        'state_mlstm_c': nrm((L, DEC_BATCH, ML_HEADS, ML_HEAD_DIM, ML_HEAD_DIM), 0.05),
        'state_mlstm_n': nrm((L, DEC_BATCH, ML_HEADS, ML_HEAD_DIM), 0.1),
        'state_mlstm_m': jax.random.uniform(next(ks), (L, DEC_BATCH, ML_HEADS), f32, 0.0, 3.0),
        'state_mlstm_conv': nrm((L, DEC_BATCH, ML_CONV - 1, D_ML), 1.0),
        'meta_tokens': nrm((N_META, D_MODEL), 1.0),
        'norm_ffn1': gain((L, D_MODEL)),
        'ffn1_gate': nrm((L, D_MODEL, D_FF), D_MODEL ** -0.5),
        'ffn1_up': nrm((L, D_MODEL, D_FF), D_MODEL ** -0.5),
        'ffn1_down': nrm((L, D_FF, D_MODEL), D_FF ** -0.5),
        'norm_mix': gain((L, D_MODEL)),
        'w_in': nrm((L, D_MODEL, D_S5 + 2 * D_ML), D_MODEL ** -0.5),
        's5_lambda_re': -0.5 + nrm((L, S5_GROUPS, S5_STATE), 0.01),
        's5_lambda_im': jnp.broadcast_to(math.pi * n_idx, (L, S5_GROUPS, S5_STATE)) + nrm((L, S5_GROUPS, S5_STATE), 0.01),
        's5_log_dt': jax.random.uniform(next(ks), (L, S5_GROUPS), f32, math.log(1e-3), math.log(1e-1)),
        's5_b_re': nrm((L, S5_GROUPS, S5_STATE, S5_GROUP), (2 * S5_GROUP) ** -0.5),
        's5_b_im': nrm((L, S5_GROUPS, S5_STATE, S5_GROUP), (2 * S5_GROUP) ** -0.5),
        's5_c_re': nrm((L, S5_GROUPS, S5_GROUP, S5_STATE), (2 * S5_STATE) ** -0.5),
        's5_c_im': nrm((L, S5_GROUPS, S5_GROUP, S5_STATE), (2 * S5_STATE) ** -0.5),
        's5_d': nrm((L, D_S5), 1.0),
        's5_glu_w': nrm((L, D_S5, D_S5), D_S5 ** -0.5),
        's5_glu_b': nrm((L, D_S5), 0.02),
        'ml_conv_w': nrm((L, ML_CONV, D_ML), ML_CONV ** -0.5),
        'ml_conv_b': nrm((L, D_ML), 0.02),
        'ml_wq': nrm((L, D_ML // ML_QKV_BLOCK, ML_QKV_BLOCK, ML_QKV_BLOCK), ML_QKV_BLOCK ** -0.5),
        'ml_wk': nrm((L, D_ML // ML_QKV_BLOCK, ML_QKV_BLOCK, ML_QKV_BLOCK), ML_QKV_BLOCK ** -0.5),
        'ml_wv': nrm((L, D_ML // ML_QKV_BLOCK, ML_QKV_BLOCK, ML_QKV_BLOCK), ML_QKV_BLOCK ** -0.5),
        'ml_igate_w': nrm((L, 3 * D_ML, ML_HEADS), 0.02),
        'ml_igate_b': nrm((L, ML_HEADS), 0.1),
        'ml_fgate_w': nrm((L, 3 * D_ML, ML_HEADS), 0.02),
        'ml_fgate_b': jnp.linspace(3.0, 6.0, ML_HEADS, dtype=f32)[None] + nrm((L, ML_HEADS), 0.01),
        'ml_norm_w': gain((L, D_ML)),
        'ml_skip': gain((L, D_ML)),
        'out_norm_s5': gain((L, D_S5)),
        'out_norm_ml': gain((L, D_ML)),
        'w_out': nrm((L, D_MIX, D_MODEL), D_MIX ** -0.5),
        'norm_ffn2': gain((L, D_MODEL)),
        'ffn2_gate': nrm((L, D_MODEL, D_FF), D_MODEL ** -0.5),
        'ffn2_up': nrm((L, D_MODEL, D_FF), D_MODEL ** -0.5),
        'ffn2_down': nrm((L, D_FF, D_MODEL), D_FF ** -0.5),
        'norm_final': gain((D_MODEL,)),
    }


def reference(x_prompt, x_sample, state_s5_re, state_s5_im, state_mlstm_c, state_mlstm_n, state_mlstm_m,
              state_mlstm_conv, meta_tokens, norm_ffn1, ffn1_gate, ffn1_up, ffn1_down, norm_mix, w_in,
              s5_lambda_re, s5_lambda_im, s5_log_dt, s5_b_re, s5_b_im, s5_c_re, s5_c_im, s5_d, s5_glu_w, s5_glu_b,
              ml_conv_w, ml_conv_b, ml_wq, ml_wk, ml_wv, ml_igate_w, ml_igate_b, ml_fgate_w, ml_fgate_b,
              ml_norm_w, ml_skip, out_norm_s5, out_norm_ml, w_out, norm_ffn2, ffn2_gate, ffn2_up, ffn2_down,
              norm_final):
    p = dict(norm_ffn1=norm_ffn1, ffn1_gate=ffn1_gate, ffn1_up=ffn1_up, ffn1_down=ffn1_down, norm_mix=norm_mix,
             w_in=w_in, s5_lambda_re=s5_lambda_re, s5_lambda_im=s5_lambda_im, s5_log_dt=s5_log_dt,
             s5_b_re=s5_b_re, s5_b_im=s5_b_im, s5_c_re=s5_c_re, s5_c_im=s5_c_im, s5_d=s5_d, s5_glu_w=s5_glu_w,
             s5_glu_b=s5_glu_b, ml_conv_w=ml_conv_w, ml_conv_b=ml_conv_b, ml_wq=ml_wq, ml_wk=ml_wk, ml_wv=ml_wv,
             ml_igate_w=ml_igate_w, ml_igate_b=ml_igate_b, ml_fgate_w=ml_fgate_w, ml_fgate_b=ml_fgate_b,
             ml_norm_w=ml_norm_w, ml_skip=ml_skip, out_norm_s5=out_norm_s5, out_norm_ml=out_norm_ml, w_out=w_out,
             norm_ffn2=norm_ffn2, ffn2_gate=ffn2_gate, ffn2_up=ffn2_up, ffn2_down=ffn2_down, norm_final=norm_final)
    f32 = jnp.float32
    nb = x_prompt.shape[0]
    meta = jnp.broadcast_to(meta_tokens.astype(x_prompt.dtype)[None], (nb, N_META, D_MODEL))
    xp = jnp.concatenate([meta, x_prompt], axis=1)
    z_s5 = jnp.zeros((DEPTH, nb, S5_GROUPS, S5_STATE), f32)
    z_c = jnp.zeros((DEPTH, nb, ML_HEADS, ML_HEAD_DIM, ML_HEAD_DIM), f32)
    z_n = jnp.zeros((DEPTH, nb, ML_HEADS, ML_HEAD_DIM), f32)
    z_m = jnp.zeros((DEPTH, nb, ML_HEADS), f32)
    z_conv = jnp.zeros((DEPTH, nb, ML_CONV - 1, D_ML), x_prompt.dtype)
    yp, p_s5_re, p_s5_im, p_mlstm_c, p_mlstm_n, p_mlstm_m, p_mlstm_conv = trunk(
        xp, z_s5, z_s5, z_c, z_n, z_m, z_conv, N_META, p)
    y_prompt = yp[:, N_META:]
    y_sample, s_s5_re, s_s5_im, s_mlstm_c, s_mlstm_n, s_mlstm_m, s_mlstm_conv = trunk(
        x_sample, state_s5_re, state_s5_im, state_mlstm_c, state_mlstm_n, state_mlstm_m, state_mlstm_conv,
        x_sample.shape[1], p)
    return (y_prompt, y_sample, p_s5_re, p_s5_im, p_mlstm_c, p_mlstm_n, p_mlstm_m, p_mlstm_conv,
            s_s5_re, s_s5_im, s_mlstm_c, s_mlstm_n, s_mlstm_m, s_mlstm_conv)
```

```python
import math
import numpy as np
import concourse.bass as bass
import concourse.mybir as mybir
from concourse.bass_utils import run_bass_kernel_spmd
from contextlib import ExitStack

F32 = mybir.dt.float32
BF16 = mybir.dt.bfloat16
I32 = mybir.dt.int32
ALU = mybir.AluOpType
AF = mybir.ActivationFunctionType

D = 1024
DFF = 2816
KC = 8
FC = 22
NMETA = 16
EPS = 1e-6
STAGE = 9
ENGS = ("pe", "act", "dve", "pool", "sp")
NDMA_SEM = 12
VEC_NAMES = ["norm_ffn1", "norm_mix", "s5_d", "s5_glu_b", "cw0", "cw1", "cw2", "cw3", "ml_conv_b",
             "ml_norm_w", "ml_skip", "out_norm_s5", "out_norm_ml", "norm_ffn2", "norm_final"]
VI = {n: i for i, n in enumerate(VEC_NAMES)}


class Op:
    __slots__ = ("eng", "fn", "deps", "idx", "signaled", "sigval", "dma", "dsem", "dval", "dprev")

    def __init__(self, eng, fn, dma):
        self.eng = eng
        self.fn = fn
        self.deps = []
        self.signaled = False
        self.sigval = 0
        self.dma = dma
        self.dsem = None
        self.dval = 0
        self.dprev = None


class Prog:
    def __init__(self, nc):
        self.nc = nc
        self.ops = {e: [] for e in ENGS}
        self.last_writer = {}
        self.readers = {}
        self.ndma = {e: 0 for e in ENGS}
        self.dma_ops = {e: [] for e in ENGS}

    def op(self, eng, fn, reads=(), writes=(), dma=False):
        o = Op(eng, fn, dma)
        deps = []
        for r in reads:
            w = self.last_writer.get(r)
            if w is not None:
                deps.append(w)
        for wr in writes:
            w = self.last_writer.get(wr)
            if w is not None:
                deps.append(w)
            deps.extend(self.readers.get(wr, ()))
        seen = set()
        for d in deps:
            if id(d) in seen or d is o:
                continue
            seen.add(id(d))
            if d.eng == "pe" and eng == "pe" and not d.dma and not dma:
                continue
            o.deps.append(d)
        for r in reads:
            self.readers.setdefault(r, []).append(o)
        for wr in writes:
            self.last_writer[wr] = o
            self.readers[wr] = []
        if dma:
            k = self.ndma[eng]
            self.ndma[eng] += 1
            o.dsem = k % NDMA_SEM
            o.dval = 16 * (k // NDMA_SEM + 1)
            if k >= NDMA_SEM:
                o.dprev = self.dma_ops[eng][k - NDMA_SEM]
            self.dma_ops[eng].append(o)
        o.idx = len(self.ops[eng])
        self.ops[eng].append(o)
        return o

    def emit(self, final_waits=()):
        nc = self.nc
        for e in ENGS:
            for o in self.ops[e]:
                for d in o.deps:
                    if not d.dma:
                        d.signaled = True
        for o in final_waits:
            if not o.dma:
                o.signaled = True
        for e in ENGS:
            c = 0
            for o in self.ops[e]:
                if o.signaled and not o.dma:
                    c += 1
                    o.sigval = c
        with ExitStack() as st:
            esem = {e: st.enter_context(nc.semaphore("s_" + e)) for e in ENGS}
            dsem = {e: [st.enter_context(nc.semaphore("d_%s_%d" % (e, i))) for i in range(NDMA_SEM)]
                    for e in ENGS if self.ndma[e] > 0}
            block = st.enter_context(nc.Block())

            def body(e, engine):
                observed = {}

                def wait(key, sem, val):
                    if observed.get(key, 0) >= val:
                        return
                    observed[key] = val
                    engine.wait_ge(sem, val)

                for o in self.ops[e]:
                    for d in o.deps:
                        if d.dma:
                            wait(("d", d.eng, d.dsem), dsem[d.eng][d.dsem], d.dval)
                        else:
                            wait(("e", d.eng), esem[d.eng], d.sigval)
                    if o.dma and o.dprev is not None:
                        wait(("d", e, o.dprev.dsem), dsem[e][o.dprev.dsem], o.dprev.dval)
                    ins = o.fn(engine)
                    if o.dma:
                        ins.then_inc(dsem[e][o.dsem], 16)
                    elif o.signaled:
                        ins.then_inc(esem[e], 1)
                if e == "sp":
                    for o in final_waits:
                        if o.dma:
                            wait(("d", o.eng, o.dsem), dsem[o.eng][o.dsem], o.dval)
                        else:
                            wait(("e", o.eng), esem[o.eng], o.sigval)

            block.sync(lambda eng: body("sp", eng))
            block.scalar(lambda eng: body("act", eng))
            block.vector(lambda eng: body("dve", eng))
            block.gpsimd(lambda eng: body("pool", eng))
            block.tensor(lambda eng: body("pe", eng))


def CALL(name, *args, **kw):
    return lambda e: getattr(e, name)(*args, **kw)


def bc_last(ap, n):
    return bass.AP(ap.tensor, ap.offset, [list(a) for a in ap.ap] + [[0, n]])


def bc_mid(ap, n):
    a = [list(x) for x in ap.ap]
    return bass.AP(ap.tensor, ap.offset, [a[0], [0, n]] + a[1:])


def bc_row(ap, n):
    a = [list(x) for x in ap.ap]
    return bass.AP(ap.tensor, ap.offset, [a[0], [0, n]])


def build_program(NP, NSAMP=2, SL=32):
    nc = bass.Bass("TRN2", target_bir_lowering=False)
    dr = {}

    def din(name, shape, dt=F32):
        dr[name] = nc.dram_tensor(name, list(shape), dt, kind="ExternalInput").ap()
        return dr[name]

    def dout(name, shape):
        dr[name] = nc.dram_tensor(name, list(shape), F32, kind="ExternalOutput").ap()
        return dr[name]

    xp = din("xp", [NP, D])
    xs = din("xs", [NSAMP, SL, D])
    st_s5re = din("st_s5re", [NSAMP, 32, 128])
    st_s5im = din("st_s5im", [NSAMP, 32, 128])
    st_c = din("st_c", [NSAMP, 4, 256, 256])
    st_n = din("st_n", [NSAMP, 8, 128])
    st_m = din("st_m", [NSAMP, 4, 1])
    st_conv = din("st_conv", [NSAMP, 3, D])
    vecs = din("vecs", [len(VEC_NAMES) * 8, 128])
    W = {}
    for n, shp in [("ffn1_gate", [D, DFF]), ("ffn1_up", [D, DFF]), ("ffn1_down", [DFF, D]),
                   ("ffn2_gate", [D, DFF]), ("ffn2_up", [D, DFF]), ("ffn2_down", [DFF, D]),
                   ("w_in", [D, 3 * D]), ("s5_glu_w", [D, D]), ("w_out", [2 * D, D]),
                   ("lam_re", [32, 128]), ("lam_im", [32, 128]), ("log_dt", [64]),
                   ("b_re", [64, 64, 16]), ("b_im", [64, 64, 16]), ("c_re", [64, 16, 64]), ("c_im", [64, 16, 64]),
                   ("wq", [256, 4, 4]), ("wk", [256, 4, 4]), ("wv", [256, 4, 4]),
                   ("igw", [3 * D, 4]), ("fgw", [3 * D, 4]), ("igb", [4, 1]), ("fgb", [4, 1]),
                   ("ident", [128, 128]), ("maskE", [128, 2]), ("mask16", [32, 2]), ("triu", [128, 128]),
                   ("bdmask", [128, 32]), ("tvec", [128, 64]), ("sel4", [4, 512]), ("mask4", [128, 4])]:
        W[n] = din(n, shp)
    NS = 1 + NSAMP
    yp = dout("yp", [NP - NMETA, D])
    ys = dout("ys", [NSAMP, SL, D])
    o_s5re = dout("o_s5re", [NS, 32, 128])
    o_s5im = dout("o_s5im", [NS, 32, 128])
    o_c = dout("o_c", [NS, 4, 256, 256])
    o_n = dout("o_n", [NS, 8, 128])
    o_m = dout("o_m", [NS, 4, 1])
    o_conv = dout("o_conv", [NS, 3, D])

    P = Prog(nc)
    outs = []
    TT = 512
    with ExitStack() as st:
        def sb(name, shape, dt=F32):
            return st.enter_context(nc.sbuf_tensor("sb_" + name, list(shape), dt))

        st.enter_context(nc.allow_low_precision("bf16 matmul operands with fp32 PSUM accumulation"))
        pb = [st.enter_context(nc.psum_tensor("pb%d" % i, [128, 512], F32)) for i in range(8)]

        xT = sb("xT", [128, KC, TT])
        xn = sb("xn", [128, KC, TT], BF16)
        xc_bf = xn
        hT = sb("hT", [128, 24, TT], BF16)
        u_bf = sb("u_bf", [128, KC, TT], BF16)
        z_bf = sb("z_bf", [128, KC, TT], BF16)
        xmh = sb("xmh", [128, KC, TT + 3], BF16)
        vT = hT[:, 16:24, :]
        mixed = sb("mixed", [128, 16, TT], BF16)
        ybuf = sb("ybuf", [128, KC, TT])
        xtok = ybuf[:].rearrange("p a b -> p (a b)").rearrange("p (n d) -> p n d", d=D)
        rstd = sb("rstd", [128, TT])
        tAB = sb("tAB", [128, 2 * TT])
        tA = tAB[:, 0:TT]
        tB = tAB[:, TT:2 * TT]
        otok = tAB
        wgr = [sb("wgr%d" % i, [128, KC, 128], BF16) for i in range(2)]
        wur = [sb("wur%d" % i, [128, KC, 128], BF16) for i in range(2)]
        wdr = [sb("wdr%d" % i, [128, FC // 2, 128], BF16) for i in range(2)]
        wir = [sb("wir%d" % i, [128, KC, 128], BF16) for i in range(2)]
        wor = [sb("wor%d" % i, [128, 8, 128], BF16) for i in range(2)]
        ident = sb("ident", [128, 128])
        ones_bf = sb("ones_bf", [128, 128], BF16)
        onesD = sb("onesD", [128, 128], BF16)
        ones256 = sb("ones256", [128, 128])
        ones4 = sb("ones4", [4, 128])
        onecol = sb("onecol", [128, 1])
        epscol = sb("epscol", [128, 1])
        vec = sb("vec", [128, len(VEC_NAMES) * 8])
        maskE = sb("maskE", [128, 2])
        mask16 = sb("mask16", [32, 2])
        triu = sb("triu", [128, 128])
        bdmask = sb("bdmask", [128, 32])
        tvec = sb("tvec", [128, 64])
        sel4 = sb("sel4", [4, 512])
        cosT = sb("cosT", [128, 32, 64])
        sinT = sb("sinT", [128, 32, 64])
        rmag = sb("rmag", [128, 32])
        W1 = sb("W1", [128, KC, 2, 128], BF16)
        W3 = sb("W3", [128, 32, 2, 32], BF16)
        s5st = sb("s5st", [128, 2, 32])
        sre = s5st[:, 0, :]
        sim = s5st[:, 1, :]
        s5all = sb("s5all", [128, 8, 256])
        s5w = [s5all[:, i, :].rearrange("p (a b) -> p a b", b=64) for i in range(8)]
        resetm = sb("resetm", [128, 64])
        inj = [sb("inj%d" % i, [128, 2, 2, 4]) for i in range(2)]
        xbfA = sb("xbfA", [128, 2, 4, 64], BF16)
        xbfB = sb("xbfB", [128, 2, 4, 64], BF16)
        um = sb("um", [128, 2, 4, 64], BF16)
        umB = sb("umB", [128, 2, 4, 64], BF16)
        mask4 = sb("mask4", [128, 4])
        CT = sb("CT", [128, 4, 2, 256])
        nT = sb("nT", [128, 8])
        mprev = sb("mprev", [4, 1])
        BD = sb("BD", [128, 3, KC, 128], BF16)
        gwi = sb("gwi", [128, 24, 4], BF16)
        gwf = sb("gwf", [128, 24, 4], BF16)
        igb = sb("igb", [4, 1])
        nfgb = sb("nfgb", [4, 1])
        igs = sb("igs", [4, TT])
        Fn = sb("Fn", [4, TT])
        aa = sb("aa", [4, TT])
        MM = sb("MM", [4, TT])
        g4 = [sb("g4_%d" % i, [4, 128]) for i in range(2)]
        negM = sb("negM", [4, 1])
        gcol = sb("gcol", [4, 1])
        Mpc = sb("Mpc", [4, 1])
        dg = sb("dg", [4, 4])
        et = sb("et", [128, 4])
        gb = sb("gb", [128, 4])
        ngc = sb("ngc", [128, 2])
        ktp = sb("ktp", [128, 256], BF16)
        vtp = sb("vtp", [128, 256], BF16)
        vtb = sb("vtb", [128, 256], BF16)
        Sm = sb("Sm", [128, 128], BF16)
        Eb = sb("Eb", [128, 128], BF16)
        Cg = sb("Cg", [128, 2, 256], BF16)
        nrep = sb("nrep", [128, 2, 128], BF16)
        hh = sb("hh", [128, 2, 128])
        hq = sb("hq", [128, 2, 128])
        mw = [sb("mw%d" % i, [128, 128]) for i in range(4)]
        stg = sb("stg", [128, 256])

        SRE = ["sre%d" % c for c in range(KC)]
        SIM = ["sim%d" % c for c in range(KC)]
        dq = ["sp", "pool"]
        dqi = [0]

        def dma(out, in_, reads=(), writes=(), q=None, slow=False):
            if out.dtype != in_.dtype:
                q = "pool"
            if q is None:
                q = dq[dqi[0] % 2]
                dqi[0] += 1
            if slow:
                f = CALL("dma_start", out=out, in_=in_, allow_slow_non_contiguous=True)
            else:
                f = CALL("dma_start", out=out, in_=in_)
            return P.op(q, f, reads=reads, writes=writes, dma=True)

        def mm(out, lhsT, rhs, start, stop, reads, writes, tp=None):
            if tp is None:
                f = CALL("matmul", out, lhsT, rhs, start=start, stop=stop)
            else:
                f = CALL("matmul", out, lhsT, rhs, start=start, stop=stop, tile_position=tp)
            return P.op("pe", f, reads=reads, writes=writes)

        def tr(out, in_, idn, reads, writes):
            return P.op("pe", CALL("transpose", out, in_, idn), reads=list(reads) + ["ident"], writes=writes)

        def E(eng, fn, reads, writes):
            return P.op(eng, fn, reads=reads, writes=writes)

        def V(name, c):
            i = VI[name] * 8 + c
            return vec[:, i:i + 1]

        for name, t in [("ident", ident), ("maskE", maskE), ("mask16", mask16), ("triu", triu), ("bdmask", bdmask),
                        ("tvec", tvec), ("sel4", sel4), ("igb", igb), ("mask4", mask4)]:
            dma(t[:], W[name], writes=[name])
        E("dve", CALL("memset", ones_bf[:], 1.0), [], ["ones_bf"])
        E("dve", CALL("memset", onesD[:], 1.0 / D), [], ["onesD"])
        E("dve", CALL("memset", ones256[:], 1.0 / 256), [], ["ones256"])
        E("dve", CALL("memset", ones4[:], 1.0), [], ["ones4"])
        E("dve", CALL("memset", onecol[:], 1.0), [], ["onecol"])
        E("dve", CALL("memset", resetm[:], 1.0), [], ["resetm"])
        E("dve", CALL("memset", resetm[:, 0:1], 0.0), ["resetm"], ["resetm"])
        E("dve", CALL("memset", epscol[:], EPS), [], ["epscol"])
        dma(nfgb[:], W["fgb"], writes=["nfgb"])
        E("dve", CALL("tensor_scalar", nfgb[:], nfgb[:], -1.0, 0.0, ALU.mult, ALU.add), ["nfgb"], ["nfgb"])
        dma(stg[0:len(VEC_NAMES) * 8, 0:128], vecs, writes=["stg"])
        nv = len(VEC_NAMES) * 8
        tr(pb[7][:, 0:nv], stg[0:nv, 0:128], ident[0:nv, 0:nv], ["stg"], ["p7"])
        E("dve", CALL("tensor_copy", vec[:], pb[7][:, 0:nv]), ["p7"], ["vec"])
        dma(gwi[:], W["igw"].rearrange("(c p) h -> p c h", p=128), writes=["gwi"], slow=True)
        dma(gwf[:], W["fgw"].rearrange("(c p) h -> p c h", p=128), writes=["gwf"], slow=True)
        for wi, wn in enumerate(["wq", "wk", "wv"]):
            dma(stg[:, 0:32].rearrange("p (c o) -> p c o", o=4), W[wn].rearrange("(c b) i o -> (b i) c o", b=32),
                reads=[], writes=["stg"], slow=True)
            for c in range(KC):
                E("dve", CALL("tensor_tensor",
                    BD[:, wi, c, :].rearrange("p (b o) -> p b o", o=4),
                    bc_mid(stg[:, c * 4:c * 4 + 4], 32), bc_last(bdmask[:, :], 4), ALU.mult),
                  ["stg", "bdmask"], ["BD"])
        lamr = s5w[0][:, 0, 0:32]
        lami = s5w[0][:, 1, 0:32]
        dtb = s5w[0][:, 2, 0:32]
        th = s5w[1][:, 0, 0:32]
        cth = s5w[1][:, 1, 0:32]
        sth = s5w[1][:, 2, 0:32]
        lbr = s5w[2][:, 0, 0:32]
        lbi = s5w[2][:, 1, 0:32]
        kr = s5w[2][:, 2, 0:32]
        ki_ = s5w[2][:, 3, 0:32]
        t1 = s5w[3][:, 0, 0:32]
        t2 = s5w[3][:, 1, 0:32]
        t3 = s5w[3][:, 2, 0:32]
        for nm, dst in [("lam_re", lamr), ("lam_im", lami)]:
            dma(stg[0:32, 0:128], W[nm], writes=["stg"])
            tr(pb[7][:, 0:32], stg[0:32, 0:128], ident[0:32, 0:32], ["stg"], ["p7"])
            E("dve", CALL("tensor_copy", dst, pb[7][:, 0:32]), ["p7"], ["s5p"])
        ldt = W["log_dt"]
        for g2 in range(2):
            src = bass.AP(ldt.tensor, ldt.offset + g2, [[0, 64], [2, 32]])
            dma(s5w[0][64 * g2:64 * g2 + 64, 2, 0:32], src, writes=["s5p"], slow=True)
        E("act", CALL("activation", dtb, dtb, AF.Exp), ["s5p"], ["s5p"])
        E("dve", CALL("tensor_tensor", t1, lamr, dtb, ALU.mult), ["s5p"], ["s5p"])
        E("act", CALL("activation", rmag[:], t1, AF.Exp), ["s5p"], ["rmag"])
        E("dve", CALL("tensor_tensor", th, lami, dtb, ALU.mult), ["s5p"], ["s5p"])

        ki32 = sb("ki32", [128, 1, 64], I32)

        def sincos(dst, ang, n, shift, key_r, key_w):
            wk = s5w[6][:].rearrange("p a b -> p (a b)")[:, 0:n]
            wf = s5w[7][:].rearrange("p a b -> p (a b)")[:, 0:n]
            wi_ = ki32[:].rearrange("p a b -> p (a b)")[:, 0:n]
            E("dve", CALL("tensor_scalar", wk, ang, shift, 1.0 / (2 * math.pi), ALU.add, ALU.mult), key_r, ["s5t"])
            E("dve", CALL("tensor_copy", wi_, wk), ["s5t"], ["s5t"])
            E("dve", CALL("tensor_copy", wf, wi_), ["s5t"], ["s5t"])
            E("dve", CALL("tensor_scalar", wk, ang, shift, 0.0, ALU.add, ALU.add), key_r + ["s5t"], ["s5t"])
            E("dve", CALL("scalar_tensor_tensor", wk, wf, -2 * math.pi, wk, ALU.mult, ALU.add), ["s5t"], ["s5t"])
            E("dve", CALL("tensor_scalar", wf, wk, math.pi, -2 * math.pi, ALU.is_gt, ALU.mult), ["s5t"], ["s5t"])
            E("dve", CALL("tensor_tensor", wk, wk, wf, ALU.add), ["s5t"], ["s5t"])
            E("dve", CALL("tensor_scalar", wf, wk, -math.pi, 2 * math.pi, ALU.is_lt, ALU.mult), ["s5t"], ["s5t"])
            E("dve", CALL("tensor_tensor", wk, wk, wf, ALU.add), ["s5t"], ["s5t"])
            E("act", CALL("activation", dst, wk, AF.Sin), ["s5t"], key_w)

        sincos(sth, th, 32, 0.0, ["s5p"], ["s5p"])
        sincos(cth, th, 32, math.pi / 2, ["s5p"], ["s5p"])
        E("dve", CALL("tensor_tensor", lbr, rmag[:], cth, ALU.mult), ["s5p", "rmag"], ["s5p"])
        E("dve", CALL("tensor_tensor", lbi, rmag[:], sth, ALU.mult), ["s5p", "rmag"], ["s5p"])
        E("dve", CALL("tensor_scalar", t1, lbr, -1.0, 0.0, ALU.add, ALU.add), ["s5p"], ["s5p"])
        E("dve", CALL("tensor_tensor", t2, lamr, lamr, ALU.mult), ["s5p"], ["s5p"])
        E("dve", CALL("tensor_tensor", t3, lami, lami, ALU.mult), ["s5p"], ["s5p"])
        E("dve", CALL("tensor_tensor", t2, t2, t3, ALU.add), ["s5p"], ["s5p"])
        E("dve", CALL("reciprocal", t2, t2), ["s5p"], ["s5p"])
        E("dve", CALL("tensor_tensor", kr, t1, lamr, ALU.mult), ["s5p"], ["s5p"])
        E("dve", CALL("tensor_tensor", t3, lbi, lami, ALU.mult), ["s5p"], ["s5p"])
        E("dve", CALL("tensor_tensor", kr, kr, t3, ALU.add), ["s5p"], ["s5p"])
        E("dve", CALL("tensor_tensor", kr, kr, t2, ALU.mult), ["s5p"], ["s5p"])
        E("dve", CALL("tensor_tensor", ki_, lbi, lamr, ALU.mult), ["s5p"], ["s5p"])
        E("dve", CALL("tensor_tensor", t3, t1, lami, ALU.mult), ["s5p"], ["s5p"])
        E("dve", CALL("tensor_tensor", ki_, ki_, t3, ALU.subtract), ["s5p"], ["s5p"])
        E("dve", CALL("tensor_tensor", ki_, ki_, t2, ALU.mult), ["s5p"], ["s5p"])
        for q in range(32):
            ang = s5w[5][:, 0, :]
            E("dve", CALL("tensor_scalar", ang, tvec[:, :], th[:, q:q + 1], 0.0, ALU.mult, ALU.add),
              ["s5p", "tvec"], ["s5ang"])
            sincos(sinT[:, q, :], ang, 64, 0.0, ["s5ang"], ["sinT"])
            sincos(cosT[:, q, :], ang, 64, math.pi / 2, ["s5ang"], ["cosT"])
        Bre = CT[:].rearrange("p a b c -> p (a b c)")[:, 0:512].rearrange("p (q j) -> p q j", j=16)
        Bim = CT[:].rearrange("p a b c -> p (a b c)")[:, 512:1024].rearrange("p (q j) -> p q j", j=16)
        Ere = CT[:].rearrange("p a b c -> p (a b c)")[:, 1024:1536].rearrange("p (q j) -> p q j", j=16)
        Eim = CT[:].rearrange("p a b c -> p (a b c)")[:, 1536:2048].rearrange("p (q j) -> p q j", j=16)
        for g2 in range(2):
            dma(Bre[64 * g2:64 * g2 + 64], W["b_re"].rearrange("(q g) p j -> g p q j", g=2)[g2], writes=["CT"], slow=True)
            dma(Bim[64 * g2:64 * g2 + 64], W["b_im"].rearrange("(q g) p j -> g p q j", g=2)[g2], writes=["CT"], slow=True)
        kmr = s5w[4][:, 0, 0:32]
        kmi = s5w[4][:, 1, 0:32]
        Eexp_re = ybuf[:].rearrange("p a b -> p (a b)")[:, 0:1024].rearrange("p (q g j) -> p q g j", g=2, j=16)
        Eexp_im = ybuf[:].rearrange("p a b -> p (a b)")[:, 1024:2048].rearrange("p (q g j) -> p q g j", g=2, j=16)
        for g2 in range(2):
            E("dve", CALL("tensor_scalar", kmr, kr, maskE[:, g2:g2 + 1], 0.0, ALU.mult, ALU.add), ["s5p", "maskE"], ["s5k"])
            E("dve", CALL("tensor_scalar", kmi, ki_, maskE[:, g2:g2 + 1], 0.0, ALU.mult, ALU.add), ["s5p", "maskE"], ["s5k"])
            E("dve", CALL("tensor_tensor", Ere, Bre, bc_last(kmr, 16), ALU.mult), ["CT", "s5k"], ["CT"])
            E("dve", CALL("tensor_tensor", Eim, Bim, bc_last(kmi, 16), ALU.mult), ["CT", "s5k"], ["CT"])
            E("dve", CALL("tensor_tensor", Eexp_re[:, :, g2, :], Ere, Eim, ALU.subtract), ["CT"], ["ybuf"])
            E("dve", CALL("tensor_tensor", Ere, Bim, bc_last(kmr, 16), ALU.mult), ["CT", "s5k"], ["CT"])
            E("dve", CALL("tensor_tensor", Eim, Bre, bc_last(kmi, 16), ALU.mult), ["CT", "s5k"], ["CT"])
            E("dve", CALL("tensor_tensor", Eexp_im[:, :, g2, :], Ere, Eim, ALU.add), ["CT"], ["ybuf"])
        for c in range(KC):
            for ri, Ex in enumerate([Eexp_re, Eexp_im]):
                src = Ex[:, 4 * c:4 * c + 4, :, :].rearrange("p q g j -> p (q g j)")
                tr(pb[7][:, 0:128], src, ident[:], ["ybuf"], ["p7"])
                E("act", CALL("copy", W1[:, c, ri, :], pb[7][:, 0:128]), ["p7"], ["W1"])
        Cst = CT[:].rearrange("p a b c -> p (a b c)")[0:32, 0:2048].rearrange("p (q k) -> p q k", k=64)
        Cx = xT[:].rearrange("p a b -> p (a b)")[0:32, 0:4096].rearrange("p (q g k) -> p q g k", g=2, k=64)
        for ri, (cn, sgn) in enumerate([("c_re", 1.0), ("c_im", -1.0)]):
            for g2 in range(2):
                dma(Cst[16 * g2:16 * g2 + 16], W[cn].rearrange("(q g) h p -> g h q p", g=2)[g2], writes=["CT"], slow=True)
            for g2 in range(2):
                E("dve", CALL("tensor_scalar", Cx[:, :, g2, :], Cst, mask16[:, g2:g2 + 1], sgn, ALU.mult, ALU.mult),
                  ["CT", "mask16"], ["xT"])
            for q in range(32):
                tr(pb[6][:, (q % 16) * 32:(q % 16) * 32 + 32], Cx[:, q, :, :].rearrange("p g k -> p (g k)"), ident[0:32, 0:32], ["xT"], ["p6"])
                if q % 16 == 15:
                    q0 = q - 15
                    E("act", CALL("copy", W3[:, q0:q0 + 16, ri, :], pb[6][:].rearrange("p (q h) -> p q h", h=32)), ["p6"], ["W3"])

        wcnt = {"g": 0, "u": 0, "d": 0, "i": 0, "o": 0}

        def rmsnorm_stats(src_chunks, nch, Tt, key_r, scale_mat):
            for c in range(nch):
                E("act", CALL("activation", hT[:, 14 + c, 0:Tt], src_chunks(c), AF.Square), key_r, ["h%d" % (14 + c)])
            for c in range(nch):
                mm(pb[6][:, 0:Tt], scale_mat[:], hT[:, 14 + c, 0:Tt], c == 0, c == nch - 1, ["onesD", "h%d" % (14 + c)], ["p6"])
            E("act", CALL("activation", rstd[:, 0:Tt], pb[6][:, 0:Tt], AF.Sqrt, bias=epscol[:, 0:1]), ["p6", "epscol"], ["rstd"])
            E("dve", CALL("reciprocal", rstd[:, 0:Tt], rstd[:, 0:Tt]), ["rstd"], ["rstd"])

        def norm_x(gname, Tt):
            rmsnorm_stats(lambda c: xT[:, c, 0:Tt], KC, Tt, ["xT"], onesD)
            for c in range(KC):
                E("dve", CALL("scalar_tensor_tensor", xn[:, c, 0:Tt], xT[:, c, 0:Tt], V(gname, c), rstd[:, 0:Tt], ALU.mult, ALU.mult),
                  ["xT", "vec", "rstd"], ["xn"])

        def ffn(pref, gname, Tt):
            norm_x(gname, Tt)
            wg, wu, wd = W[pref + "_gate"], W[pref + "_up"], W[pref + "_down"]
            for f in range(FC):
                gi = wcnt["g"] % 2
                wcnt["g"] += 1
                dma(wgr[gi][:], wg[:, f * 128:(f + 1) * 128].rearrange("(c p) f -> p c f", p=128), writes=["wg%d" % gi])
                dma(wur[gi][:], wu[:, f * 128:(f + 1) * 128].rearrange("(c p) f -> p c f", p=128), writes=["wu%d" % gi])
                pg, pu = pb[f % 2], pb[2 + f % 2]
                for c in range(KC):
                    mm(pg[:, 0:Tt], wgr[gi][:, c, :], xn[:, c, 0:Tt], c == 0, c == KC - 1, ["wg%d" % gi, "xn"], ["p%d" % (f % 2)])
                for c in range(KC):
                    mm(pu[:, 0:Tt], wur[gi][:, c, :], xn[:, c, 0:Tt], c == 0, c == KC - 1, ["wu%d" % gi, "xn"], ["p%d" % (2 + f % 2)])
                tt = tA if f % 2 == 0 else tB
                tn = "tA" if f % 2 == 0 else "tB"
                E("act", CALL("activation", tt[:, 0:Tt], pg[:, 0:Tt], AF.Silu), ["p%d" % (f % 2)], [tn])
                E("dve", CALL("tensor_tensor", hT[:, f, 0:Tt], tt[:, 0:Tt], pu[:, 0:Tt], ALU.mult),
                  [tn, "p%d" % (2 + f % 2)], ["h%d" % f])
            for c in range(KC):
                pd = pb[4 + c % 2]
                for hf in range(2):
                    di = hf
                    dma(wdr[di][:], wd[hf * 1408:(hf + 1) * 1408, c * 128:(c + 1) * 128].rearrange("(f p) d -> p f d", p=128), writes=["wd%d" % di])
                    for f2 in range(FC // 2):
                        f = hf * (FC // 2) + f2
                        mm(pd[:, 0:Tt], wdr[di][:, f2, :], hT[:, f, 0:Tt], f == 0, f == FC - 1, ["wd%d" % di, "h%d" % f], ["p%d" % (4 + c % 2)])
                E("dve", CALL("scalar_tensor_tensor", xT[:, c, 0:Tt], pd[:, 0:Tt], 0.5, xT[:, c, 0:Tt], ALU.mult, ALU.add),
                  ["p%d" % (4 + c % 2), "xT"], ["xT"])

        def in_proj(Tt):
            norm_x("norm_mix", Tt)
            for oc in range(24):
                ii = wcnt["i"] % 2
                wcnt["i"] += 1
                dma(wir[ii][:], W["w_in"][:, oc * 128:(oc + 1) * 128].rearrange("(c p) f -> p c f", p=128), writes=["wi%d" % ii])
                po = pb[4 + oc % 2]
                for c in range(KC):
                    mm(po[:, 0:Tt], wir[ii][:, c, :], xn[:, c, 0:Tt], c == 0, c == KC - 1, ["wi%d" % ii, "xn"], ["p%d" % (4 + oc % 2)])
                if oc < 8:
                    dst, key = u_bf[:, oc, 0:Tt], "u_bf"
                elif oc < 16:
                    dst, key = xmh[:, oc - 8, 3:3 + Tt], "xmh"
                else:
                    dst, key = z_bf[:, oc - 16, 0:Tt], "z_bf"
                if oc % 2 == 0:
                    E("act", CALL("copy", dst, po[:, 0:Tt]), ["p%d" % (4 + oc % 2)], [key])
                else:
                    E("dve", CALL("tensor_copy", dst, po[:, 0:Tt]), ["p%d" % (4 + oc % 2)], [key])

        def s5_mix(Tt):
            gT = hT
            L = min(64, Tt)
            nun = Tt // L
            def s5_pre(c, un, S):
                t0 = un * L
                X = S["extra"]
                pS, psk = S["pS"][un % 2], S["psk"][un % 2]
                umS = S["um"][:, un % 2]
                kum = "s5%sum%d" % (S["k"], un % 2)
                pSv = pS[:].rearrange("p (q r t) -> p q r t", q=4, r=2)
                for qq in range(4):
                    E("act", CALL("activation", umS[:, qq, 0:L], u_bf[:, c, t0:t0 + L], AF.Copy, scale=mask4[:, qq:qq + 1]), ["u_bf", "mask4"] + X, [kum])
                for qq in range(4):
                    for ri in range(2):
                        mm(pSv[:, qq, ri, 0:L], W1[:, c, ri, :], umS[:, qq, 0:L], True, True, ["W1", kum] + X, [psk])

            def s5_unit(c, un, S):
                pY = pb[2 + c % 2]
                pyk = "p%d" % (2 + c % 2)
                t0 = un * L
                X = S["extra"]
                K = lambda i: "s5%s%d" % (S["k"], i)
                pS, psk = S["pS"][un % 2], S["psk"][un % 2]
                xbf = S["xbf"]
                kxb = K(11)
                injC, kinjC = S["inj"][:, un % 2], "s5%sinj%d" % (S["k"], un % 2)
                injN, kinjN = S["inj"][:, (un + 1) % 2], "s5%sinj%d" % (S["k"], (un + 1) % 2)
                sk = "sre%d" % c
                pSv = pS[:].rearrange("p (q r t) -> p q r t", q=4, r=2)
                if un == 0:
                    s5_pre(c, 0, S)
                    E("pool", CALL("tensor_tensor", injC, s5st[:, :, 4 * c:4 * c + 4], bc_mid(rmag[:, 4 * c:4 * c + 4], 2), ALU.mult), ["rmag", sk] + X, [kinjC])
                    yield
                if un + 1 < nun and L == 64:
                    s5_pre(c, un + 1, S)
                    yield
                bre = pSv[:, :, 0, 0:L]
                bim = pSv[:, :, 1, 0:L]
                cs = cosT[:, 4 * c:4 * c + 4, 0:L]
                sn = sinT[:, 4 * c:4 * c + 4, 0:L]
                B = S["bufs"]
                bv = lambda i: B[:, i * 256:(i + 1) * 256].rearrange("p (q t) -> p q t", t=64)[:, :, 0:L]
                E("dve", CALL("tensor_tensor", bv(2), bre, cs, ALU.mult), [psk, "cosT"] + X, [K(2)])
                E("dve", CALL("tensor_tensor", bv(3), bim, sn, ALU.mult), [psk, "sinT"] + X, [K(3)])
                yield
                E("dve", CALL("tensor_tensor", bv(6), bim, cs, ALU.mult), [psk, "cosT"] + X, [K(6)])
                E("dve", CALL("tensor_tensor", bv(7), bre, sn, ALU.mult), [psk, "sinT"] + X, [K(7)])
                yield
                E("dve", CALL("tensor_tensor", bv(0), bv(2), bv(3), ALU.add), [K(2), K(3)] + X, [K(0)])
                E("dve", CALL("tensor_tensor", bv(1), bv(6), bv(7), ALU.subtract), [K(6), K(7)] + X, [K(1)])
                yield
                if L == 64:
                    W2 = B[:, 0:512].rearrange("p (r q t) -> p r q t", r=2, t=64)
                    E("dve", CALL("tensor_tensor", W2[:, :, :, 0], W2[:, :, :, 0], injC, ALU.add), [K(0), K(1), kinjC] + X, [K(0), K(1)])
                    rt = S["rtab"][:, 4 * c:4 * c + 4, :].rearrange("p q t -> p (q t)")
                    E("dve", CALL("tensor_tensor_scan", B[:, 1024:1280], rt, B[:, 0:256], 0.0, ALU.mult, ALU.add), [K(0), "rtab"] + X, [K(4)])
                    yield
                    E("dve", CALL("tensor_tensor_scan", B[:, 1280:1536], rt, B[:, 256:512], 0.0, ALU.mult, ALU.add), [K(1), "rtab"] + X, [K(5)])
                    yield
                else:
                    for qq in range(4):
                        q = 4 * c + qq
                        E("dve", CALL("tensor_tensor_scan", bv(4)[:, qq, :], bc_row(rmag[:, q:q + 1], L), bv(0)[:, qq, :],
                                      sre[:, q:q + 1], ALU.mult, ALU.add), ["rmag", K(0), sk] + X, [K(4)])
                        E("dve", CALL("tensor_tensor_scan", bv(5)[:, qq, :], bc_row(rmag[:, q:q + 1], L), bv(1)[:, qq, :],
                                      sim[:, q:q + 1], ALU.mult, ALU.add), ["rmag", K(1), sk] + X, [K(5)])
                    yield
                zr, zi = bv(4), bv(5)
                E("pool", CALL("tensor_tensor", bv(2), zr, cs, ALU.mult), [K(4), "cosT"] + X, [K(2)])
                E("pool", CALL("tensor_tensor", bv(3), zi, sn, ALU.mult), [K(5), "sinT"] + X, [K(3)])
                yield
                E("pool", CALL("tensor_tensor", bv(6), bv(2), bv(3), ALU.subtract), [K(2), K(3)] + X, [K(6)])
                E("pool", CALL("tensor_tensor", bv(2), zi, cs, ALU.mult), [K(5), "cosT"] + X, [K(2)])
                yield
                E("pool", CALL("tensor_tensor", bv(3), zr, sn, ALU.mult), [K(4), "sinT"] + X, [K(3)])
                E("pool", CALL("tensor_tensor", bv(7), bv(2), bv(3), ALU.add), [K(2), K(3)] + X, [K(7)])
                yield
                X4 = B[:, 1536:2048].rearrange("p (r q t) -> p r q t", r=2, t=64)
                if L == 64 and un + 1 < nun:
                    E("pool", CALL("tensor_tensor", injN, X4[:, :, :, L - 1], bc_mid(rmag[:, 4 * c:4 * c + 4], 2), ALU.mult), ["rmag", K(6), K(7)] + X, [kinjN])
                E("act", CALL("copy", xbf[:, :, :, 0:L], X4[:, :, :, 0:L]), [K(6), K(7)] + X, [kxb])
                if L != 64 or un + 1 == nun:
                    E("act", CALL("copy", s5st[:, :, 4 * c:4 * c + 4], X4[:, :, :, L - 1]), [K(6), K(7)] + X, [sk])
                yield
                for qq in range(4):
                    q = 4 * c + qq
                    mm(pY[32 * qq:32 * qq + 32, t0:t0 + L], W3[:, q, 0, :], xbf[:, 0, qq, 0:L], True, False, ["W3", kxb] + X, [pyk], tp=(0, 32 * qq))
                    mm(pY[32 * qq:32 * qq + 32, t0:t0 + L], W3[:, q, 1, :], xbf[:, 1, qq, 0:L], False, True, ["W3", kxb] + X, [pyk], tp=(0, 32 * qq))
                yield

            def s5_post(c):
                pY = pb[2 + c % 2]
                pyk = "p%d" % (2 + c % 2)
                E("dve", CALL("scalar_tensor_tensor", tA[:, 0:Tt], u_bf[:, c, 0:Tt], V("s5_d", c), pY[:, 0:Tt], ALU.mult, ALU.add),
                  ["u_bf", "vec", pyk], ["tA"])
                E("act", CALL("activation", tB[:, 0:Tt], tA[:, 0:Tt], AF.Square), ["tA"], ["tB"])
                E("dve", CALL("tensor_scalar", tB[:, 0:Tt], tB[:, 0:Tt], 0.044715, 1.0, ALU.mult, ALU.add), ["tB"], ["tB"])
                E("pool", CALL("tensor_tensor", tB[:, 0:Tt], tB[:, 0:Tt], tA[:, 0:Tt], ALU.mult), ["tA", "tB"], ["tB"])
                E("act", CALL("activation", tB[:, 0:Tt], tB[:, 0:Tt], AF.Sigmoid, scale=1.5957691216057308), ["tB"], ["tB"])
                E("dve", CALL("tensor_tensor", gT[:, c, 0:Tt], tA[:, 0:Tt], tB[:, 0:Tt], ALU.mult), ["tA", "tB"], ["h%d" % c])

            ybf_ = ybuf[:].rearrange("p a b -> p (a b)")
            rtab = ybf_[:, 2048:4096].rearrange("p (q t) -> p q t", t=64)
            E("dve", CALL("tensor_tensor", rtab, bc_last(rmag[:, :], 64), bc_mid(resetm[:, :], 32), ALU.mult), ["rmag", "resetm", "ybuf"], ["rtab"])
            SA = dict(bufs=s5all[:].rearrange("p a b -> p (a b)"), um=um, xbf=xbfA, inj=inj[0], k="A", extra=[], pS=[pb[0], pb[1]], psk=["p0", "p1"], rtab=rtab)
            SB = dict(bufs=ybf_[:, 0:2048], um=umB, xbf=xbfB, inj=inj[1], k="B", extra=["ybuf"], pS=[pb[4], pb[5]], psk=["p4", "p5"], rtab=rtab)
            for cp in range(KC // 2):
                c0, c1 = 2 * cp, 2 * cp + 1
                for un in range(nun):
                    gens = [s5_unit(c0, un, SA), s5_unit(c1, un, SB)]
                    while gens:
                        for g_ in list(gens):
                            try:
                                next(g_)
                            except StopIteration:
                                gens.remove(g_)
                s5_post(c0)
                s5_post(c1)
            for oc in range(KC):
                ii = wcnt["i"] % 2
                wcnt["i"] += 1
                dma(wir[ii][:], W["s5_glu_w"][:, oc * 128:(oc + 1) * 128].rearrange("(c p) f -> p c f", p=128), writes=["wi%d" % ii])
                po = pb[4 + oc % 2]
                pk = "p%d" % (4 + oc % 2)
                for c in range(KC):
                    mm(po[:, 0:Tt], wir[ii][:, c, :], gT[:, c, 0:Tt], c == 0, c == KC - 1, ["wi%d" % ii, "h%d" % c], [pk])
                E("act", CALL("activation", tA[:, 0:Tt], po[:, 0:Tt], AF.Sigmoid, bias=V("s5_glu_b", oc)), [pk, "vec"], ["tA"])
                E("dve", CALL("tensor_tensor", ybuf[:, oc, 0:Tt], gT[:, oc, 0:Tt], tA[:, 0:Tt], ALU.mult), ["tA", "h%d" % oc], ["ybuf"])
            rmsnorm_stats(lambda c: ybuf[:, c, 0:Tt], KC, Tt, ["ybuf"], onesD)
            for c in range(KC):
                E("dve", CALL("scalar_tensor_tensor", mixed[:, c, 0:Tt], ybuf[:, c, 0:Tt], V("out_norm_s5", c), rstd[:, 0:Tt], ALU.mult, ALU.mult),
                  ["ybuf", "vec", "rstd"], ["mixed"])

        def mlstm_mix(Tt):
            qT = hT
            for c in range(KC):
                eng = "dve"
                E(eng, CALL("tensor_scalar", tA[:, 0:Tt], xmh[:, c, 0:Tt], V("cw0", c), 0.0, ALU.mult, ALU.add), ["xmh", "vec"], ["tA"])
                for j in range(1, 4):
                    E(eng, CALL("scalar_tensor_tensor", tA[:, 0:Tt], xmh[:, c, j:j + Tt], V("cw%d" % j, c), tA[:, 0:Tt], ALU.mult, ALU.add),
                      ["xmh", "vec", "tA"], ["tA"])
                E("act", CALL("activation", xc_bf[:, c, 0:Tt], tA[:, 0:Tt], AF.Silu, bias=V("ml_conv_b", c)), ["tA", "vec"], ["xn"])
            for c in range(KC):
                E("act", CALL("activation", z_bf[:, c, 0:Tt], z_bf[:, c, 0:Tt], AF.Silu), ["z_bf"], ["z_bf"])
            for c in range(KC):
                for wi, (src, dstT, key) in enumerate([(xc_bf[:, c, 0:Tt], qT[:, c, 0:Tt], "h%d" % c),
                                                       (xc_bf[:, c, 0:Tt], qT[:, 8 + c, 0:Tt], "h%d" % (8 + c)),
                                                       (xmh[:, c, 3:3 + Tt], vT[:, c, 0:Tt], "h%d" % (16 + c))]):
                    po = pb[4 + (3 * c + wi) % 2]
                    pk = "p%d" % (4 + (3 * c + wi) % 2)
                    mm(po[:, 0:Tt], BD[:, wi, c, :], src, True, True, ["BD", "xn", "xmh"], [pk])
                    if wi == 1:
                        E("dve", CALL("tensor_copy", dstT, po[:, 0:Tt]), [pk], [key])
                    else:
                        E("act", CALL("copy", dstT, po[:, 0:Tt]), [pk], [key])
            for gi, (gw, pbk) in enumerate([(gwi, 4), (gwf, 5)]):
                for j in range(24):
                    src = qT[:, j, 0:Tt]
                    key = "h%d" % j
                    mm(pb[pbk][0:4, 0:Tt], gw[:, j, :], src, j == 0, j == 23, ["gwi", "gwf", key], ["p%d" % pbk])
            E("act", CALL("activation", igs[:, 0:Tt], pb[4][0:4, 0:Tt], AF.Identity, bias=igb[:, 0:1]), ["p4", "igb"], ["igs"])
            E("act", CALL("activation", MM[:, 0:Tt], pb[5][0:4, 0:Tt], AF.Exp, bias=nfgb[:, 0:1], scale=-1.0), ["p5", "nfgb"], ["MM"])
            E("act", CALL("activation", MM[:, 0:Tt], MM[:, 0:Tt], AF.Ln, bias=onecol[0:4, 0:1]), ["MM", "onecol"], ["MM"])
            E("dve", CALL("tensor_tensor_scan", Fn[:, 0:Tt], bc_row(onecol[0:4, 0:1], Tt), MM[:, 0:Tt], 0.0, ALU.mult, ALU.add), ["MM", "onecol"], ["Fn"])
            E("dve", CALL("tensor_tensor", aa[:, 0:Tt], igs[:, 0:Tt], Fn[:, 0:Tt], ALU.add), ["igs", "Fn"], ["aa"])
            E("dve", CALL("tensor_tensor_scan", MM[:, 0:Tt], bc_row(onecol[0:4, 0:1], Tt), aa[:, 0:Tt], mprev[:, 0:1], ALU.mult, ALU.max),
              ["aa", "onecol", "mprev"], ["MM"])
            E("dve", CALL("tensor_copy", Mpc[:], mprev[:]), ["mprev"], ["Mpc"])
            Lc = min(128, Tt)
            for ch in range(Tt // Lc):
                t0, t1 = ch * Lc, ch * Lc + Lc
                E("dve", CALL("tensor_scalar", negM[:], MM[:, t1 - 1:t1], -1.0, 0.0, ALU.mult, ALU.add), ["MM"], ["negM"])
                E("act", CALL("activation", g4[0][:, 0:Lc], aa[:, t0:t1], AF.Exp, bias=negM[:, 0:1]), ["aa", "negM"], ["g40"])
                E("act", CALL("activation", gcol[:], Mpc[:], AF.Exp, bias=negM[:, 0:1]), ["Mpc", "negM"], ["gcol"])
                E("dve", CALL("tensor_scalar", g4[1][:, 0:Lc], Fn[:, t0:t1], negM[:, 0:1], 0.0, ALU.add, ALU.add), ["Fn", "negM"], ["g41"])
                E("dve", CALL("tensor_copy", Mpc[:], MM[:, t1 - 1:t1]), ["MM", "gcol"], ["Mpc"])
                tr(pb[7][0:Lc, 128:132], g4[0][:, 0:Lc], ident[0:4, 0:4], ["g40"], ["p7"])
                E("dve", CALL("tensor_copy", et[0:Lc, :], pb[7][0:Lc, 128:132]), ["p7"], ["et"])
                E("dve", CALL("tensor_scalar", dg[:], ident[0:4, 0:4], gcol[:, 0:1], 0.0, ALU.mult, ALU.add), ["ident", "gcol"], ["dg"])
                mm(pb[7][:, 132:136], ones4[:], dg[:], True, True, ["ones4", "dg"], ["p7"])
                E("dve", CALL("tensor_copy", gb[:], pb[7][:, 132:136]), ["p7"], ["gb"])
                for c in range(KC):
                    mm(pb[c // 4][0:Lc, (c % 4) * 128:(c % 4) * 128 + 128], xc_bf[:, c, t0:t1], BD[:, 1, c, :], True, True, ["xn", "BD"], ["p%d" % (c // 4)])
                    mm(pb[2 + c // 4][0:Lc, (c % 4) * 128:(c % 4) * 128 + 128], xmh[:, c, 3 + t0:3 + t1], BD[:, 2, c, :], True, True, ["xmh", "BD"], ["p%d" % (2 + c // 4)])
                for h in range(4):
                    kps = pb[h // 2][0:Lc, (h % 2) * 256:(h % 2) * 256 + 256]
                    vps = pb[2 + h // 2][0:Lc, (h % 2) * 256:(h % 2) * 256 + 256]
                    kk, vk = "p%d" % (h // 2), "p%d" % (2 + h // 2)
                    for kc in range(2):
                        mm(pb[7][0:Lc, 0:Lc], qT[:, 8 + 2 * h + kc, t0:t1], qT[:, 2 * h + kc, t0:t1], kc == 0, kc == 1,
                           ["h%d" % (8 + 2 * h + kc), "h%d" % (2 * h + kc)], ["p7"])
                    E("dve", CALL("scalar_tensor_tensor", Sm[0:Lc, 0:Lc], pb[7][0:Lc, 0:Lc], 1.0 / 16, triu[0:Lc, 0:Lc], ALU.mult, ALU.mult), ["p7", "triu"], ["Sm"])
                    E("dve", CALL("tensor_scalar", vtp[0:Lc, :], vps, et[0:Lc, h:h + 1], 0.0, ALU.mult, ALU.add), [vk, "et"], ["vtp"])
                    E("act", CALL("copy", vtb[0:Lc, :], vps), [vk], ["vtb"])
                    E("dve", CALL("tensor_scalar", ktp[0:Lc, :], kps, et[0:Lc, h:h + 1], 1.0 / 16, ALU.mult, ALU.mult), [kk, "et"], ["ktp"])
                    E("pool", CALL("tensor_scalar", Eb[0:Lc, :], ones_bf[0:Lc, :], et[0:Lc, h:h + 1], 0.0, ALU.mult, ALU.add), ["ones_bf", "et"], ["Eb"])
                    E("pool", CALL("tensor_scalar", Cg[:].rearrange("p a b -> p (a b)"), CT[:, h, :, :].rearrange("p a b -> p (a b)"), gb[:, h:h + 1], 0.0, ALU.mult, ALU.add),
                      ["CT", "gb"], ["Cg"])
                    E("dve", CALL("tensor_scalar", ngc[:], nT[:, 2 * h:2 * h + 2], gb[:, h:h + 1], 0.0, ALU.mult, ALU.add), ["nT", "gb"], ["ngc"])
                    for kc in range(2):
                        E("pool", CALL("tensor_scalar", nrep[:, kc, :], ones_bf[:, :], ngc[:, kc:kc + 1], 0.0, ALU.mult, ALU.add), ["ones_bf", "ngc"], ["nrep"])
                    for vc in range(2):
                        o = pb[4][:, vc * 128:vc * 128 + Lc]
                        mm(o, vtp[0:Lc, vc * 128:vc * 128 + 128], Sm[0:Lc, 0:Lc], True, False, ["vtp", "Sm"], ["p4"])
                        for kc in range(2):
                            mm(o, Cg[:, kc, vc * 128:vc * 128 + 128], qT[:, 2 * h + kc, t0:t1], False, kc == 1, ["Cg", "h%d" % (2 * h + kc)], ["p4"])
                    o = pb[4][:, 256:256 + Lc]
                    mm(o, Eb[0:Lc, :], Sm[0:Lc, 0:Lc], True, False, ["Eb", "Sm"], ["p4"])
                    for kc in range(2):
                        mm(o, nrep[:, kc, :], qT[:, 2 * h + kc, t0:t1], False, kc == 1, ["nrep", "h%d" % (2 * h + kc)], ["p4"])
                    mm(pb[4][:, 384:384 + Lc], sel4[:, h * 128:h * 128 + 128], g4[1][:, 0:Lc], True, True, ["sel4", "g41"], ["p4"])
                    E("act", CALL("activation", mw[0][:, 0:Lc], pb[4][:, 384:384 + Lc], AF.Exp), ["p4"], ["mw0"])
                    E("act", CALL("activation", mw[2][:, 0:Lc], pb[4][:, 256:256 + Lc], AF.Abs), ["p4"], ["mw2"])
                    E("dve", CALL("tensor_tensor", mw[0][:, 0:Lc], mw[2][:, 0:Lc], mw[0][:, 0:Lc], ALU.max), ["mw2", "mw0"], ["mw0"])
                    E("dve", CALL("reciprocal", mw[0][:, 0:Lc], mw[0][:, 0:Lc]), ["mw0"], ["mw0"])
                    for vc in range(2):
                        E("dve", CALL("tensor_tensor", hh[:, vc, 0:Lc], pb[4][:, vc * 128:vc * 128 + Lc], mw[0][:, 0:Lc], ALU.mult), ["p4", "mw0"], ["hh"])
                        E("act", CALL("activation", hq[:, vc, 0:Lc], hh[:, vc, 0:Lc], AF.Square), ["hh"], ["hq"])
                    for kc in range(2):
                        mm(pb[6][:, kc * 256:kc * 256 + 256], ktp[0:Lc, kc * 128:kc * 128 + 128], vtb[0:Lc, :], True, True, ["ktp", "vtb"], ["p6"])
                    E("dve", CALL("scalar_tensor_tensor", CT[:, h, :, :].rearrange("p a b -> p (a b)"), CT[:, h, :, :].rearrange("p a b -> p (a b)"),
                                                                  gb[:, h:h + 1], pb[6][:, :], ALU.mult, ALU.add), ["CT", "gb", "p6", "Cg"], ["CT"])
                    for kc in range(2):
                        mm(pb[7][:, 136 + kc:137 + kc], ktp[0:Lc, kc * 128:kc * 128 + 128], ones_bf[0:Lc, 0:1], True, True, ["ktp", "ones_bf"], ["p7"])
                    E("dve", CALL("tensor_tensor", nT[:, 2 * h:2 * h + 2], ngc[:, :], pb[7][:, 136:138], ALU.add), ["ngc", "p7"], ["nT"])
                    for vc in range(2):
                        mm(pb[5][:, 0:Lc], ones256[:], hh[:, vc, 0:Lc], vc == 0, vc == 1, ["ones256", "hh"], ["p5"])
                    for vc in range(2):
                        mm(pb[5][:, 128:128 + Lc], ones256[:], hq[:, vc, 0:Lc], vc == 0, vc == 1, ["ones256", "hq"], ["p5"])
                    E("act", CALL("activation", mw[1][:, 0:Lc], pb[5][:, 0:Lc], AF.Square), ["p5"], ["mw1"])
                    E("dve", CALL("tensor_tensor", mw[1][:, 0:Lc], pb[5][:, 128:128 + Lc], mw[1][:, 0:Lc], ALU.subtract), ["p5", "mw1"], ["mw1"])
                    E("dve", CALL("tensor_scalar", mw[1][:, 0:Lc], mw[1][:, 0:Lc], 0.0, 0.0, ALU.max, ALU.add), ["mw1"], ["mw1"])
                    E("act", CALL("activation", mw[1][:, 0:Lc], mw[1][:, 0:Lc], AF.Sqrt, bias=epscol[:, 0:1]), ["mw1", "epscol"], ["mw1"])
                    E("dve", CALL("reciprocal", mw[1][:, 0:Lc], mw[1][:, 0:Lc]), ["mw1"], ["mw1"])
                    for vc in range(2):
                        c = 2 * h + vc
                        E("dve", CALL("tensor_tensor", mw[2][:, 0:Lc], hh[:, vc, 0:Lc], pb[5][:, 0:Lc], ALU.subtract), ["hh", "p5"], ["mw2"])
                        E("pool", CALL("tensor_tensor", mw[2][:, 0:Lc], mw[2][:, 0:Lc], mw[1][:, 0:Lc], ALU.mult), ["mw2", "mw1"], ["mw2"])
                        E("pool", CALL("tensor_scalar", mw[2][:, 0:Lc], mw[2][:, 0:Lc], V("ml_norm_w", c), 0.0, ALU.mult, ALU.add), ["mw2", "vec"], ["mw2"])
                        E("dve", CALL("scalar_tensor_tensor", mw[3][:, 0:Lc], xc_bf[:, c, t0:t1], V("ml_skip", c), mw[2][:, 0:Lc], ALU.mult, ALU.add),
                          ["xn", "vec", "mw2"], ["mw3"])
                        E("pool", CALL("tensor_tensor", ybuf[:, c, t0:t1], mw[3][:, 0:Lc], z_bf[:, c, t0:t1], ALU.mult), ["mw3", "z_bf"], ["ybuf"])
            E("dve", CALL("tensor_tensor", mprev[:], MM[:, Tt - 1:Tt], Fn[:, Tt - 1:Tt], ALU.subtract), ["MM", "Fn", "Mpc"], ["mprev"])
            for c in range(KC):
                E("act", CALL("copy", xmh[:, c, 0:3], xmh[:, c, Tt:Tt + 3]), ["xmh"], ["xmh"])
            rmsnorm_stats(lambda c: ybuf[:, c, 0:Tt], KC, Tt, ["ybuf"], onesD)
            for c in range(KC):
                E("dve", CALL("scalar_tensor_tensor", mixed[:, 8 + c, 0:Tt], ybuf[:, c, 0:Tt], V("out_norm_ml", c), rstd[:, 0:Tt], ALU.mult, ALU.mult),
                  ["ybuf", "vec", "rstd"], ["mixed"])

        def out_proj(Tt):
            for c in range(KC):
                po = pb[4 + c % 2]
                pk = "p%d" % (4 + c % 2)
                for hf in range(2):
                    oi = hf
                    dma(wor[oi][:], W["w_out"][hf * 1024:(hf + 1) * 1024, c * 128:(c + 1) * 128].rearrange("(k p) d -> p k d", p=128), writes=["wo%d" % oi])
                    for k2 in range(8):
                        k = hf * 8 + k2
                        mm(po[:, 0:Tt], wor[oi][:, k2, :], mixed[:, k, 0:Tt], k == 0, k == 15, ["wo%d" % oi, "mixed"], [pk])
                E("dve", CALL("tensor_tensor", xT[:, c, 0:Tt], xT[:, c, 0:Tt], po[:, 0:Tt], ALU.add), [pk, "xT"], ["xT"])

        def load_tile(src_rows, Tt):
            nsub = (Tt + 127) // 128
            for n in range(nsub):
                r = min(128, Tt - n * 128)
                dma(xtok[0:r, n, :], src_rows[n * 128:n * 128 + r, :], writes=["ybuf"])
            for c in range(KC):
                for n in range(nsub):
                    r = min(128, Tt - n * 128)
                    tr(pb[7][:, n * 128:n * 128 + r], xtok[0:r, n, c * 128:(c + 1) * 128], ident[0:r, 0:r], ["ybuf"], ["p7"])
                E("dve" if c % 2 == 0 else "act",
                  (CALL("tensor_copy", xT[:, c, 0:Tt], pb[7][:, 0:Tt])) if c % 2 == 0 else (CALL("copy", xT[:, c, 0:Tt], pb[7][:, 0:Tt])),
                  ["p7"], ["xT"])

        def store_tile(dst_rows, Tt):
            norm_x("norm_final", Tt)
            nsub = (Tt + 127) // 128
            for c in range(KC):
                E("dve", CALL("scalar_tensor_tensor", ybuf[:, c, 0:Tt], xT[:, c, 0:Tt], V("norm_final", c), rstd[:, 0:Tt], ALU.mult, ALU.mult),
                  ["xT", "vec", "rstd"], ["ybuf"])
            for n in range(nsub):
                r = min(128, Tt - n * 128)
                for c4 in range(2):
                    for c in range(c4 * 4, c4 * 4 + 4):
                        tr(pb[7][0:r, (c % 4) * 128:(c % 4) * 128 + 128], ybuf[:, c, n * 128:n * 128 + r], ident[:], ["ybuf"], ["p7"])
                    E("act", CALL("copy", otok[0:r, c4 * 512:(c4 + 1) * 512], pb[7][0:r, :]), ["p7"], ["tA", "tB"])
                outs.append(dma(dst_rows[n * 128:n * 128 + r, :], otok[0:r, :], reads=["tA", "tB"], q="sp"))

        def init_state(si):
            if si is None:
                for t, k in [(sre, SRE), (sim, SRE), (nT, ["nT"]), (mprev, ["mprev"])]:
                    E("dve", CALL("memset", t[:], 0.0), [], k)
                E("pool", CALL("memset", CT[:].rearrange("p a b c -> p (a b c)"), 0.0), [], ["CT"])
                E("pool", CALL("memset", xmh[:, :, 0:3], 0.0), [], ["xmh"])
                return
            for src, dst, k in [(st_s5re, sre, SRE), (st_s5im, sim, SIM)]:
                dma(stg[0:32, 0:128], src[si], writes=["stg"])
                tr(pb[7][:, 0:32], stg[0:32, 0:128], ident[0:32, 0:32], ["stg"], ["p7"])
                E("dve", CALL("tensor_copy", dst[:], pb[7][:, 0:32]), ["p7"], k)
            dma(stg[0:8, 0:128], st_n[si], writes=["stg"])
            tr(pb[7][:, 0:8], stg[0:8, 0:128], ident[0:8, 0:8], ["stg"], ["p7"])
            E("dve", CALL("tensor_copy", nT[:], pb[7][:, 0:8]), ["p7"], ["nT"])
            dma(mprev[:], st_m[si], writes=["mprev"])
            for c in range(KC):
                dma(stg[0:3, 0:128], st_conv[si][:, c * 128:(c + 1) * 128], writes=["stg"])
                tr(pb[7][:, 0:3], stg[0:3, 0:128], ident[0:3, 0:3], ["stg"], ["p7"])
                E("dve", CALL("tensor_copy", xmh[:, c, 0:3], pb[7][:, 0:3]), ["p7"], ["xmh"])
            for h in range(4):
                for vc in range(2):
                    dma(stg[:, 0:256], st_c[si, h, vc * 128:(vc + 1) * 128, :], writes=["stg"])
                    for kc in range(2):
                        tr(pb[7][:, kc * 128:kc * 128 + 128], stg[:, kc * 128:kc * 128 + 128], ident[:], ["stg"], ["p7"])
                    E("dve", CALL("tensor_copy", CT[:, h, :, vc * 128:vc * 128 + 128], pb[7][:, 0:256].rearrange("p (k v) -> p k v", k=2)), ["p7"], ["CT"])

        def store_state(oi):
            for src, dst, k in [(sre, o_s5re, SRE), (sim, o_s5im, SIM)]:
                tr(pb[7][0:32, 0:128], src[:], ident[:], k, ["p7"])
                E("dve", CALL("tensor_copy", stg[0:32, 0:128], pb[7][0:32, 0:128]), ["p7"], ["stg"])
                outs.append(dma(dst[oi], stg[0:32, 0:128], reads=["stg"], q="sp"))
            tr(pb[7][0:8, 0:128], nT[:], ident[:], ["nT"], ["p7"])
            E("dve", CALL("tensor_copy", stg[0:8, 0:128], pb[7][0:8, 0:128]), ["p7"], ["stg"])
            outs.append(dma(o_n[oi], stg[0:8, 0:128], reads=["stg"], q="sp"))
            outs.append(dma(o_m[oi], mprev[:], reads=["mprev"], q="sp"))
            for c in range(KC):
                E("dve", CALL("tensor_copy", mw[0][:, 0:3], xmh[:, c, 0:3]), ["xmh"], ["mw0"])
                tr(pb[7][0:3, 0:128], mw[0][:, 0:3], ident[:], ["mw0"], ["p7"])
                E("dve", CALL("tensor_copy", stg[0:3, 0:128], pb[7][0:3, 0:128]), ["p7"], ["stg"])
                outs.append(dma(o_conv[oi][:, c * 128:(c + 1) * 128], stg[0:3, 0:128], reads=["stg"], q="sp"))
            for h in range(4):
                for vc in range(2):
                    for kc in range(2):
                        tr(pb[7][:, kc * 128:kc * 128 + 128], CT[:, h, kc, vc * 128:vc * 128 + 128], ident[:], ["CT"], ["p7"])
                    E("dve", CALL("tensor_copy", stg[:, 0:256], pb[7][:, 0:256]), ["p7"], ["stg"])
                    outs.append(dma(o_c[oi, h, vc * 128:(vc + 1) * 128, :], stg[:, 0:256], reads=["stg"], q="sp"))

        def run_tile(src_rows, dst_rows, Tt):
            load_tile(src_rows, Tt)
            if STAGE >= 2:
                ffn("ffn1", "norm_ffn1", Tt)
            if STAGE >= 3:
                in_proj(Tt)
            if STAGE >= 4:
                s5_mix(Tt)
            if STAGE >= 5:
                mlstm_mix(Tt)
            if STAGE >= 6:
                out_proj(Tt)
            if STAGE >= 7:
                ffn("ffn2", "norm_ffn2", Tt)
            if dst_rows is not None:
                store_tile(dst_rows, Tt)

        if STAGE == 0:
            P.emit(final_waits=outs)
            return nc
        init_state(None)
        run_tile(xp[0:NMETA, :], None, NMETA)
        for ti in range((NP - NMETA) // TT):
            r0 = NMETA + ti * TT
            run_tile(xp[r0:r0 + TT, :], yp[r0 - NMETA:r0 - NMETA + TT, :], TT)
        store_state(0)
        for si in range(NSAMP):
            init_state(si)
            run_tile(xs[si], ys[si], SL)
            store_state(1 + si)
        P.emit(final_waits=outs)
    return nc


def host_consts():
    p = np.arange(128)
    c = {}
    c["ident"] = np.eye(128, dtype=np.float32)
    c["maskE"] = np.stack([(p // 64 == 0), (p // 64 == 1)], 1).astype(np.float32)
    p32 = np.arange(32)
    c["mask16"] = np.stack([(p32 // 16 == 0), (p32 // 16 == 1)], 1).astype(np.float32)
    c["triu"] = np.triu(np.ones((128, 128), np.float32))
    c["bdmask"] = (p[:, None] // 4 == np.arange(32)[None, :]).astype(np.float32)
    c["tvec"] = np.broadcast_to(np.arange(1, 65, dtype=np.float32)[None, :], (128, 64)).copy()
    s = np.zeros((4, 4, 128), np.float32)
    for h in range(4):
        s[h, h, :] = 1.0
    c["sel4"] = s.reshape(4, 512)
    c["mask4"] = (p[:, None] // 32 == np.arange(4)[None, :]).astype(np.float32)
    return c


_CACHE = {}


def kernel(**inp):
    f = lambda a: np.ascontiguousarray(np.asarray(a, dtype=np.float32))
    x_prompt = f(inp["x_prompt"])
    x_sample = f(inp["x_sample"])
    NB, SEQ, _ = x_prompt.shape
    NDEC, SL, _ = x_sample.shape
    NP = NMETA + SEQ
    ncores = 8
    NSAMP = NDEC // ncores
    key = (NP, NSAMP, SL)
    if key not in _CACHE:
        _CACHE[key] = build_program(NP, NSAMP, SL)
    nc = _CACHE[key]
    meta = f(inp["meta_tokens"])
    shared = host_consts()
    for n in ["ffn1_gate", "ffn1_up", "ffn1_down", "ffn2_gate", "ffn2_up", "ffn2_down", "w_in", "s5_glu_w", "w_out"]:
        shared[n] = f(inp[n])[0]
    shared["lam_re"] = f(inp["s5_lambda_re"])[0].reshape(32, 128)
    shared["lam_im"] = f(inp["s5_lambda_im"])[0].reshape(32, 128)
    shared["log_dt"] = f(inp["s5_log_dt"])[0]
    shared["b_re"] = f(inp["s5_b_re"])[0]
    shared["b_im"] = f(inp["s5_b_im"])[0]
    shared["c_re"] = f(inp["s5_c_re"])[0]
    shared["c_im"] = f(inp["s5_c_im"])[0]
    shared["wq"] = f(inp["ml_wq"])[0]
    shared["wk"] = f(inp["ml_wk"])[0]
    shared["wv"] = f(inp["ml_wv"])[0]
    shared["igw"] = f(inp["ml_igate_w"])[0]
    shared["fgw"] = f(inp["ml_fgate_w"])[0]
    shared["igb"] = f(inp["ml_igate_b"])[0].reshape(4, 1)
    shared["fgb"] = f(inp["ml_fgate_b"])[0].reshape(4, 1)
    cw = f(inp["ml_conv_w"])[0]
    vd = {"norm_ffn1": f(inp["norm_ffn1"])[0], "norm_mix": f(inp["norm_mix"])[0], "s5_d": f(inp["s5_d"])[0],
          "s5_glu_b": f(inp["s5_glu_b"])[0], "cw0": cw[0], "cw1": cw[1], "cw2": cw[2], "cw3": cw[3],
          "ml_conv_b": f(inp["ml_conv_b"])[0], "ml_norm_w": f(inp["ml_norm_w"])[0], "ml_skip": f(inp["ml_skip"])[0],
          "out_norm_s5": f(inp["out_norm_s5"])[0], "out_norm_ml": f(inp["out_norm_ml"])[0],
          "norm_ffn2": f(inp["norm_ffn2"])[0], "norm_final": f(inp["norm_final"])}
    shared["vecs"] = np.concatenate([vd[n].reshape(8, 128) for n in VEC_NAMES], 0)
    s5re, s5im = f(inp["state_s5_re"])[0], f(inp["state_s5_im"])[0]
    stc, stn, stm, stcv = f(inp["state_mlstm_c"])[0], f(inp["state_mlstm_n"])[0], f(inp["state_mlstm_m"])[0], f(inp["state_mlstm_conv"])[0]
    in_maps = []
    for c in range(ncores):
        b = c % NB
        sl = slice(c * NSAMP, (c + 1) * NSAMP)
        m = dict(shared)
        m["xp"] = np.concatenate([meta, x_prompt[b]], 0)
        m["xs"] = x_sample[sl]
        m["st_s5re"] = s5re[sl].reshape(NSAMP, 32, 128)
        m["st_s5im"] = s5im[sl].reshape(NSAMP, 32, 128)
        m["st_c"] = stc[sl]
        m["st_n"] = stn[sl].reshape(NSAMP, 8, 128)
        m["st_m"] = stm[sl].reshape(NSAMP, 4, 1)
        m["st_conv"] = stcv[sl]
        in_maps.append(m)
    res = run_bass_kernel_spmd(nc, in_maps, core_ids=list(range(ncores))).results
    y_prompt = np.stack([res[b]["yp"] for b in range(NB)], 0)
    y_sample = np.concatenate([res[c]["ys"] for c in range(ncores)], 0)

    def gather(name, shape_tail):
        pr = np.stack([res[b][name][0] for b in range(NB)], 0).reshape((1, NB) + shape_tail)
        sm = np.concatenate([res[c][name][1:] for c in range(ncores)], 0).reshape((1, NDEC) + shape_tail)
        return pr.astype(np.float32), sm.astype(np.float32)

    p_re, s_re = gather("o_s5re", (64, 64))
    p_im, s_im = gather("o_s5im", (64, 64))
    p_c, s_c = gather("o_c", (4, 256, 256))
    p_n, s_n = gather("o_n", (4, 256))
    p_m, s_m = gather("o_m", (4,))
    p_cv, s_cv = gather("o_conv", (3, 1024))
    return (y_prompt.astype(np.float32), y_sample.astype(np.float32), p_re, p_im, p_c, p_n, p_m, p_cv,
            s_re, s_im, s_c, s_n, s_m, s_cv)
```

```python
import math
import numpy as np
import concourse.bass as bass
import concourse.mybir as mybir
from concourse.bass_utils import run_bass_kernel_spmd
from contextlib import ExitStack

F32 = mybir.dt.float32
BF16 = mybir.dt.bfloat16
I32 = mybir.dt.int32
ALU = mybir.AluOpType
AF = mybir.ActivationFunctionType

D = 1024
DFF = 2816
KC = 8
FC = 22
NMETA = 16
EPS = 1e-6
STAGE = 9
ENGS = ("pe", "act", "dve", "pool", "sp")
NDMA_SEM = 12
VEC_NAMES = ["norm_ffn1", "norm_mix", "s5_d", "s5_glu_b", "cw0", "cw1", "cw2", "cw3", "ml_conv_b",
             "ml_norm_w", "ml_skip", "out_norm_s5", "out_norm_ml", "norm_ffn2", "norm_final"]
VI = {n: i for i, n in enumerate(VEC_NAMES)}


class Op:
    __slots__ = ("eng", "fn", "deps", "idx", "signaled", "sigval", "dma", "dsem", "dval", "dprev")

    def __init__(self, eng, fn, dma):
        self.eng = eng
        self.fn = fn
        self.deps = []
        self.signaled = False
        self.sigval = 0
        self.dma = dma
        self.dsem = None
        self.dval = 0
        self.dprev = None


class Prog:
    def __init__(self, nc):
        self.nc = nc
        self.ops = {e: [] for e in ENGS}
        self.last_writer = {}
        self.readers = {}
        self.ndma = {e: 0 for e in ENGS}
        self.dma_ops = {e: [] for e in ENGS}

    def op(self, eng, fn, reads=(), writes=(), dma=False):
        o = Op(eng, fn, dma)
        deps = []
        for r in reads:
            w = self.last_writer.get(r)
            if w is not None:
                deps.append(w)
        for wr in writes:
            w = self.last_writer.get(wr)
            if w is not None:
                deps.append(w)
            deps.extend(self.readers.get(wr, ()))
        seen = set()
        for d in deps:
            if id(d) in seen or d is o:
                continue
            seen.add(id(d))
            if d.eng == "pe" and eng == "pe" and not d.dma and not dma:
                continue
            o.deps.append(d)
        for r in reads:
            self.readers.setdefault(r, []).append(o)
        for wr in writes:
            self.last_writer[wr] = o
            self.readers[wr] = []
        if dma:
            k = self.ndma[eng]
            self.ndma[eng] += 1
            o.dsem = k % NDMA_SEM
            o.dval = 16 * (k // NDMA_SEM + 1)
            if k >= NDMA_SEM:
                o.dprev = self.dma_ops[eng][k - NDMA_SEM]
            self.dma_ops[eng].append(o)
        o.idx = len(self.ops[eng])
        self.ops[eng].append(o)
        return o

    def emit(self, final_waits=()):
        nc = self.nc
        for e in ENGS:
            for o in self.ops[e]:
                for d in o.deps:
                    if not d.dma:
                        d.signaled = True
        for o in final_waits:
            if not o.dma:
                o.signaled = True
        for e in ENGS:
            c = 0
            for o in self.ops[e]:
                if o.signaled and not o.dma:
                    c += 1
                    o.sigval = c
        with ExitStack() as st:
            esem = {e: st.enter_context(nc.semaphore("s_" + e)) for e in ENGS}
            dsem = {e: [st.enter_context(nc.semaphore("d_%s_%d" % (e, i))) for i in range(NDMA_SEM)]
                    for e in ENGS if self.ndma[e] > 0}
            block = st.enter_context(nc.Block())

            def body(e, engine):
                observed = {}

                def wait(key, sem, val):
                    if observed.get(key, 0) >= val:
                        return
                    observed[key] = val
                    engine.wait_ge(sem, val)

                for o in self.ops[e]:
                    for d in o.deps:
                        if d.dma:
                            wait(("d", d.eng, d.dsem), dsem[d.eng][d.dsem], d.dval)
                        else:
                            wait(("e", d.eng), esem[d.eng], d.sigval)
                    if o.dma and o.dprev is not None:
                        wait(("d", e, o.dprev.dsem), dsem[e][o.dprev.dsem], o.dprev.dval)
                    ins = o.fn(engine)
                    if o.dma:
                        ins.then_inc(dsem[e][o.dsem], 16)
                    elif o.signaled:
                        ins.then_inc(esem[e], 1)
                if e == "sp":
                    for o in final_waits:
                        if o.dma:
                            wait(("d", o.eng, o.dsem), dsem[o.eng][o.dsem], o.dval)
                        else:
                            wait(("e", o.eng), esem[o.eng], o.sigval)

            block.sync(lambda eng: body("sp", eng))
            block.scalar(lambda eng: body("act", eng))
            block.vector(lambda eng: body("dve", eng))
            block.gpsimd(lambda eng: body("pool", eng))
            block.tensor(lambda eng: body("pe", eng))


def CALL(name, *args, **kw):
    return lambda e: getattr(e, name)(*args, **kw)


def bc_last(ap, n):
    return bass.AP(ap.tensor, ap.offset, [list(a) for a in ap.ap] + [[0, n]])


def bc_mid(ap, n):
    a = [list(x) for x in ap.ap]
    return bass.AP(ap.tensor, ap.offset, [a[0], [0, n]] + a[1:])


def bc_row(ap, n):
    a = [list(x) for x in ap.ap]
    return bass.AP(ap.tensor, ap.offset, [a[0], [0, n]])


def build_program(NP, NSAMP=2, SL=32):
    nc = bass.Bass("TRN2", target_bir_lowering=False)
    dr = {}

    def din(name, shape, dt=F32):
        dr[name] = nc.dram_tensor(name, list(shape), dt, kind="ExternalInput").ap()
        return dr[name]

    def dout(name, shape):
        dr[name] = nc.dram_tensor(name, list(shape), F32, kind="ExternalOutput").ap()
        return dr[name]

    xp = din("xp", [NP, D])
    xs = din("xs", [NSAMP, SL, D])
    st_s5re = din("st_s5re", [NSAMP, 32, 128])
    st_s5im = din("st_s5im", [NSAMP, 32, 128])
    st_c = din("st_c", [NSAMP, 4, 256, 256])
    st_n = din("st_n", [NSAMP, 8, 128])
    st_m = din("st_m", [NSAMP, 4, 1])
    st_conv = din("st_conv", [NSAMP, 3, D])
    vecs = din("vecs", [len(VEC_NAMES) * 8, 128])
    W = {}
    for n, shp in [("ffn1_gate", [D, DFF]), ("ffn1_up", [D, DFF]), ("ffn1_down", [DFF, D]),
                   ("ffn2_gate", [D, DFF]), ("ffn2_up", [D, DFF]), ("ffn2_down", [DFF, D]),
                   ("w_in", [D, 3 * D]), ("s5_glu_w", [D, D]), ("w_out", [2 * D, D]),
                   ("lam_re", [32, 128]), ("lam_im", [32, 128]), ("log_dt", [64]),
                   ("b_re", [64, 64, 16]), ("b_im", [64, 64, 16]), ("c_re", [64, 16, 64]), ("c_im", [64, 16, 64]),
                   ("wq", [256, 4, 4]), ("wk", [256, 4, 4]), ("wv", [256, 4, 4]),
                   ("igw", [3 * D, 4]), ("fgw", [3 * D, 4]), ("igb", [4, 1]), ("fgb", [4, 1]),
                   ("ident", [128, 128]), ("maskE", [128, 2]), ("mask16", [32, 2]), ("triu", [128, 128]),
                   ("bdmask", [128, 32]), ("tvec", [128, 64]), ("sel4", [4, 512]), ("mask4", [128, 4])]:
        W[n] = din(n, shp)
    NS = 1 + NSAMP
    yp = dout("yp", [NP - NMETA, D])
    ys = dout("ys", [NSAMP, SL, D])
    o_s5re = dout("o_s5re", [NS, 32, 128])
    o_s5im = dout("o_s5im", [NS, 32, 128])
    o_c = dout("o_c", [NS, 4, 256, 256])
    o_n = dout("o_n", [NS, 8, 128])
    o_m = dout("o_m", [NS, 4, 1])
    o_conv = dout("o_conv", [NS, 3, D])

    P = Prog(nc)
    outs = []
    TT = 512
    with ExitStack() as st:
        def sb(name, shape, dt=F32):
            return st.enter_context(nc.sbuf_tensor("sb_" + name, list(shape), dt))

        st.enter_context(nc.allow_low_precision("bf16 matmul operands with fp32 PSUM accumulation"))
        pb = [st.enter_context(nc.psum_tensor("pb%d" % i, [128, 512], F32)) for i in range(8)]

        xT = sb("xT", [128, KC, TT])
        xn = sb("xn", [128, KC, TT], BF16)
        xc_bf = xn
        hT = sb("hT", [128, 24, TT], BF16)
        u_bf = sb("u_bf", [128, KC, TT], BF16)
        z_bf = sb("z_bf", [128, KC, TT], BF16)
        xmh = sb("xmh", [128, KC, TT + 3], BF16)
        vT = hT[:, 16:24, :]
        mixed = sb("mixed", [128, 16, TT], BF16)
        ybuf = sb("ybuf", [128, KC, TT])
        xtok = ybuf[:].rearrange("p a b -> p (a b)").rearrange("p (n d) -> p n d", d=D)
        rstd = sb("rstd", [128, TT])
        tAB = sb("tAB", [128, 2 * TT])
        tA = tAB[:, 0:TT]
        tB = tAB[:, TT:2 * TT]
        otok = tAB
        wgr = [sb("wgr%d" % i, [128, KC, 128], BF16) for i in range(2)]
        wur = [sb("wur%d" % i, [128, KC, 128], BF16) for i in range(2)]
        wdr = [sb("wdr%d" % i, [128, FC // 2, 128], BF16) for i in range(2)]
        wir = [sb("wir%d" % i, [128, KC, 128], BF16) for i in range(2)]
        wor = [sb("wor%d" % i, [128, 8, 128], BF16) for i in range(2)]
        ident = sb("ident", [128, 128])
        ones_bf = sb("ones_bf", [128, 128], BF16)
        onesD = sb("onesD", [128, 128], BF16)
        ones256 = sb("ones256", [128, 128])
        ones4 = sb("ones4", [4, 128])
        onecol = sb("onecol", [128, 1])
        epscol = sb("epscol", [128, 1])
        vec = sb("vec", [128, len(VEC_NAMES) * 8])
        maskE = sb("maskE", [128, 2])
        mask16 = sb("mask16", [32, 2])
        triu = sb("triu", [128, 128])
        bdmask = sb("bdmask", [128, 32])
        tvec = sb("tvec", [128, 64])
        sel4 = sb("sel4", [4, 512])
        cosT = sb("cosT", [128, 32, 64])
        sinT = sb("sinT", [128, 32, 64])
        rmag = sb("rmag", [128, 32])
        W1 = sb("W1", [128, KC, 2, 128], BF16)
        W3 = sb("W3", [128, 32, 2, 32], BF16)
        s5st = sb("s5st", [128, 2, 32])
        sre = s5st[:, 0, :]
        sim = s5st[:, 1, :]
        s5all = sb("s5all", [128, 8, 256])
        s5w = [s5all[:, i, :].rearrange("p (a b) -> p a b", b=64) for i in range(8)]
        resetm = sb("resetm", [128, 64])
        inj = [sb("inj%d" % i, [128, 2, 2, 4]) for i in range(2)]
        xbfA = sb("xbfA", [128, 2, 4, 64], BF16)
        xbfB = sb("xbfB", [128, 2, 4, 64], BF16)
        um = sb("um", [128, 2, 4, 64], BF16)
        umB = sb("umB", [128, 2, 4, 64], BF16)
        mask4 = sb("mask4", [128, 4])
        CT = sb("CT", [128, 4, 2, 256])
        nT = sb("nT", [128, 8])
        mprev = sb("mprev", [4, 1])
        BD = sb("BD", [128, 3, KC, 128], BF16)
        gwi = sb("gwi", [128, 24, 4], BF16)
        gwf = sb("gwf", [128, 24, 4], BF16)
        igb = sb("igb", [4, 1])
        nfgb = sb("nfgb", [4, 1])
        igs = sb("igs", [4, TT])
        Fn = sb("Fn", [4, TT])
        aa = sb("aa", [4, TT])
        MM = sb("MM", [4, TT])
        g4 = [sb("g4_%d" % i, [4, 128]) for i in range(2)]
        negM = sb("negM", [4, 1])
        gcol = sb("gcol", [4, 1])
        Mpc = sb("Mpc", [4, 1])
        dg = sb("dg", [4, 4])
        et = sb("et", [128, 4])
        gb = sb("gb", [128, 4])
        ngc = sb("ngc", [128, 2])
        ktp = sb("ktp", [128, 256], BF16)
        vtp = sb("vtp", [128, 256], BF16)
        vtb = sb("vtb", [128, 256], BF16)
        Sm = sb("Sm", [128, 128], BF16)
        Eb = sb("Eb", [128, 128], BF16)
        Cg = sb("Cg", [128, 2, 256], BF16)
        nrep = sb("nrep", [128, 2, 128], BF16)
        hh = sb("hh", [128, 2, 128])
        hq = sb("hq", [128, 2, 128])
        mw = [sb("mw%d" % i, [128, 128]) for i in range(4)]
        stg = sb("stg", [128, 256])

        SRE = ["sre%d" % c for c in range(KC)]
        SIM = ["sim%d" % c for c in range(KC)]
        dq = ["sp", "pool"]
        dqi = [0]

        def dma(out, in_, reads=(), writes=(), q=None, slow=False):
            if out.dtype != in_.dtype:
                q = "pool"
            if q is None:
                q = dq[dqi[0] % 2]
                dqi[0] += 1
            if slow:
                f = CALL("dma_start", out=out, in_=in_, allow_slow_non_contiguous=True)
            else:
                f = CALL("dma_start", out=out, in_=in_)
            return P.op(q, f, reads=reads, writes=writes, dma=True)

        def mm(out, lhsT, rhs, start, stop, reads, writes, tp=None):
            if tp is None:
                f = CALL("matmul", out, lhsT, rhs, start=start, stop=stop)
            else:
                f = CALL("matmul", out, lhsT, rhs, start=start, stop=stop, tile_position=tp)
            return P.op("pe", f, reads=reads, writes=writes)

        def tr(out, in_, idn, reads, writes):
            return P.op("pe", CALL("transpose", out, in_, idn), reads=list(reads) + ["ident"], writes=writes)

        def E(eng, fn, reads, writes):
            return P.op(eng, fn, reads=reads, writes=writes)

        def V(name, c):
            i = VI[name] * 8 + c
            return vec[:, i:i + 1]

        for name, t in [("ident", ident), ("maskE", maskE), ("mask16", mask16), ("triu", triu), ("bdmask", bdmask),
                        ("tvec", tvec), ("sel4", sel4), ("igb", igb), ("mask4", mask4)]:
            dma(t[:], W[name], writes=[name])
        E("dve", CALL("memset", ones_bf[:], 1.0), [], ["ones_bf"])
        E("dve", CALL("memset", onesD[:], 1.0 / D), [], ["onesD"])
        E("dve", CALL("memset", ones256[:], 1.0 / 256), [], ["ones256"])
        E("dve", CALL("memset", ones4[:], 1.0), [], ["ones4"])
        E("dve", CALL("memset", onecol[:], 1.0), [], ["onecol"])
        E("dve", CALL("memset", resetm[:], 1.0), [], ["resetm"])
        E("dve", CALL("memset", resetm[:, 0:1], 0.0), ["resetm"], ["resetm"])
        E("dve", CALL("memset", epscol[:], EPS), [], ["epscol"])
        dma(nfgb[:], W["fgb"], writes=["nfgb"])
        E("dve", CALL("tensor_scalar", nfgb[:], nfgb[:], -1.0, 0.0, ALU.mult, ALU.add), ["nfgb"], ["nfgb"])
        dma(stg[0:len(VEC_NAMES) * 8, 0:128], vecs, writes=["stg"])
        nv = len(VEC_NAMES) * 8
        tr(pb[7][:, 0:nv], stg[0:nv, 0:128], ident[0:nv, 0:nv], ["stg"], ["p7"])
        E("dve", CALL("tensor_copy", vec[:], pb[7][:, 0:nv]), ["p7"], ["vec"])
        dma(gwi[:], W["igw"].rearrange("(c p) h -> p c h", p=128), writes=["gwi"], slow=True)
        dma(gwf[:], W["fgw"].rearrange("(c p) h -> p c h", p=128), writes=["gwf"], slow=True)
        for wi, wn in enumerate(["wq", "wk", "wv"]):
            dma(stg[:, 0:32].rearrange("p (c o) -> p c o", o=4), W[wn].rearrange("(c b) i o -> (b i) c o", b=32),
                reads=[], writes=["stg"], slow=True)
            for c in range(KC):
                E("dve", CALL("tensor_tensor",
                    BD[:, wi, c, :].rearrange("p (b o) -> p b o", o=4),
                    bc_mid(stg[:, c * 4:c * 4 + 4], 32), bc_last(bdmask[:, :], 4), ALU.mult),
                  ["stg", "bdmask"], ["BD"])
        lamr = s5w[0][:, 0, 0:32]
        lami = s5w[0][:, 1, 0:32]
        dtb = s5w[0][:, 2, 0:32]
        th = s5w[1][:, 0, 0:32]
        cth = s5w[1][:, 1, 0:32]
        sth = s5w[1][:, 2, 0:32]
        lbr = s5w[2][:, 0, 0:32]
        lbi = s5w[2][:, 1, 0:32]
        kr = s5w[2][:, 2, 0:32]
        ki_ = s5w[2][:, 3, 0:32]
        t1 = s5w[3][:, 0, 0:32]
        t2 = s5w[3][:, 1, 0:32]
        t3 = s5w[3][:, 2, 0:32]
        for nm, dst in [("lam_re", lamr), ("lam_im", lami)]:
            dma(stg[0:32, 0:128], W[nm], writes=["stg"])
            tr(pb[7][:, 0:32], stg[0:32, 0:128], ident[0:32, 0:32], ["stg"], ["p7"])
            E("dve", CALL("tensor_copy", dst, pb[7][:, 0:32]), ["p7"], ["s5p"])
        ldt = W["log_dt"]
        for g2 in range(2):
            src = bass.AP(ldt.tensor, ldt.offset + g2, [[0, 64], [2, 32]])
            dma(s5w[0][64 * g2:64 * g2 + 64, 2, 0:32], src, writes=["s5p"], slow=True)
        E("act", CALL("activation", dtb, dtb, AF.Exp), ["s5p"], ["s5p"])
        E("dve", CALL("tensor_tensor", t1, lamr, dtb, ALU.mult), ["s5p"], ["s5p"])
        E("act", CALL("activation", rmag[:], t1, AF.Exp), ["s5p"], ["rmag"])
        E("dve", CALL("tensor_tensor", th, lami, dtb, ALU.mult), ["s5p"], ["s5p"])

        ki32 = sb("ki32", [128, 1, 64], I32)

        def sincos(dst, ang, n, shift, key_r, key_w):
            wk = s5w[6][:].rearrange("p a b -> p (a b)")[:, 0:n]
            wf = s5w[7][:].rearrange("p a b -> p (a b)")[:, 0:n]
            wi_ = ki32[:].rearrange("p a b -> p (a b)")[:, 0:n]
            E("dve", CALL("tensor_scalar", wk, ang, shift, 1.0 / (2 * math.pi), ALU.add, ALU.mult), key_r, ["s5t"])
            E("dve", CALL("tensor_copy", wi_, wk), ["s5t"], ["s5t"])
            E("dve", CALL("tensor_copy", wf, wi_), ["s5t"], ["s5t"])
            E("dve", CALL("tensor_scalar", wk, ang, shift, 0.0, ALU.add, ALU.add), key_r + ["s5t"], ["s5t"])
            E("dve", CALL("scalar_tensor_tensor", wk, wf, -2 * math.pi, wk, ALU.mult, ALU.add), ["s5t"], ["s5t"])
            E("dve", CALL("tensor_scalar", wf, wk, math.pi, -2 * math.pi, ALU.is_gt, ALU.mult), ["s5t"], ["s5t"])
            E("dve", CALL("tensor_tensor", wk, wk, wf, ALU.add), ["s5t"], ["s5t"])
            E("dve", CALL("tensor_scalar", wf, wk, -math.pi, 2 * math.pi, ALU.is_lt, ALU.mult), ["s5t"], ["s5t"])
            E("dve", CALL("tensor_tensor", wk, wk, wf, ALU.add), ["s5t"], ["s5t"])
            E("act", CALL("activation", dst, wk, AF.Sin), ["s5t"], key_w)

        sincos(sth, th, 32, 0.0, ["s5p"], ["s5p"])
        sincos(cth, th, 32, math.pi / 2, ["s5p"], ["s5p"])
        E("dve", CALL("tensor_tensor", lbr, rmag[:], cth, ALU.mult), ["s5p", "rmag"], ["s5p"])
        E("dve", CALL("tensor_tensor", lbi, rmag[:], sth, ALU.mult), ["s5p", "rmag"], ["s5p"])
        E("dve", CALL("tensor_scalar", t1, lbr, -1.0, 0.0, ALU.add, ALU.add), ["s5p"], ["s5p"])
        E("dve", CALL("tensor_tensor", t2, lamr, lamr, ALU.mult), ["s5p"], ["s5p"])
        E("dve", CALL("tensor_tensor", t3, lami, lami, ALU.mult), ["s5p"], ["s5p"])
        E("dve", CALL("tensor_tensor", t2, t2, t3, ALU.add), ["s5p"], ["s5p"])
        E("dve", CALL("reciprocal", t2, t2), ["s5p"], ["s5p"])
        E("dve", CALL("tensor_tensor", kr, t1, lamr, ALU.mult), ["s5p"], ["s5p"])
        E("dve", CALL("tensor_tensor", t3, lbi, lami, ALU.mult), ["s5p"], ["s5p"])
        E("dve", CALL("tensor_tensor", kr, kr, t3, ALU.add), ["s5p"], ["s5p"])
        E("dve", CALL("tensor_tensor", kr, kr, t2, ALU.mult), ["s5p"], ["s5p"])
        E("dve", CALL("tensor_tensor", ki_, lbi, lamr, ALU.mult), ["s5p"], ["s5p"])
        E("dve", CALL("tensor_tensor", t3, t1, lami, ALU.mult), ["s5p"], ["s5p"])
        E("dve", CALL("tensor_tensor", ki_, ki_, t3, ALU.subtract), ["s5p"], ["s5p"])
        E("dve", CALL("tensor_tensor", ki_, ki_, t2, ALU.mult), ["s5p"], ["s5p"])
        for q in range(32):
            ang = s5w[5][:, 0, :]
            E("dve", CALL("tensor_scalar", ang, tvec[:, :], th[:, q:q + 1], 0.0, ALU.mult, ALU.add),
              ["s5p", "tvec"], ["s5ang"])
            sincos(sinT[:, q, :], ang, 64, 0.0, ["s5ang"], ["sinT"])
            sincos(cosT[:, q, :], ang, 64, math.pi / 2, ["s5ang"], ["cosT"])
        Bre = CT[:].rearrange("p a b c -> p (a b c)")[:, 0:512].rearrange("p (q j) -> p q j", j=16)
        Bim = CT[:].rearrange("p a b c -> p (a b c)")[:, 512:1024].rearrange("p (q j) -> p q j", j=16)
        Ere = CT[:].rearrange("p a b c -> p (a b c)")[:, 1024:1536].rearrange("p (q j) -> p q j", j=16)
        Eim = CT[:].rearrange("p a b c -> p (a b c)")[:, 1536:2048].rearrange("p (q j) -> p q j", j=16)
        for g2 in range(2):
            dma(Bre[64 * g2:64 * g2 + 64], W["b_re"].rearrange("(q g) p j -> g p q j", g=2)[g2], writes=["CT"], slow=True)
            dma(Bim[64 * g2:64 * g2 + 64], W["b_im"].rearrange("(q g) p j -> g p q j", g=2)[g2], writes=["CT"], slow=True)
        kmr = s5w[4][:, 0, 0:32]
        kmi = s5w[4][:, 1, 0:32]
        Eexp_re = ybuf[:].rearrange("p a b -> p (a b)")[:, 0:1024].rearrange("p (q g j) -> p q g j", g=2, j=16)
        Eexp_im = ybuf[:].rearrange("p a b -> p (a b)")[:, 1024:2048].rearrange("p (q g j) -> p q g j", g=2, j=16)
        for g2 in range(2):
            E("dve", CALL("tensor_scalar", kmr, kr, maskE[:, g2:g2 + 1], 0.0, ALU.mult, ALU.add), ["s5p", "maskE"], ["s5k"])
            E("dve", CALL("tensor_scalar", kmi, ki_, maskE[:, g2:g2 + 1], 0.0, ALU.mult, ALU.add), ["s5p", "maskE"], ["s5k"])
            E("dve", CALL("tensor_tensor", Ere, Bre, bc_last(kmr, 16), ALU.mult), ["CT", "s5k"], ["CT"])
            E("dve", CALL("tensor_tensor", Eim, Bim, bc_last(kmi, 16), ALU.mult), ["CT", "s5k"], ["CT"])
            E("dve", CALL("tensor_tensor", Eexp_re[:, :, g2, :], Ere, Eim, ALU.subtract), ["CT"], ["ybuf"])
            E("dve", CALL("tensor_tensor", Ere, Bim, bc_last(kmr, 16), ALU.mult), ["CT", "s5k"], ["CT"])
            E("dve", CALL("tensor_tensor", Eim, Bre, bc_last(kmi, 16), ALU.mult), ["CT", "s5k"], ["CT"])
            E("dve", CALL("tensor_tensor", Eexp_im[:, :, g2, :], Ere, Eim, ALU.add), ["CT"], ["ybuf"])
        for c in range(KC):
            for ri, Ex in enumerate([Eexp_re, Eexp_im]):
                src = Ex[:, 4 * c:4 * c + 4, :, :].rearrange("p q g j -> p (q g j)")
                tr(pb[7][:, 0:128], src, ident[:], ["ybuf"], ["p7"])
                E("act", CALL("copy", W1[:, c, ri, :], pb[7][:, 0:128]), ["p7"], ["W1"])
        Cst = CT[:].rearrange("p a b c -> p (a b c)")[0:32, 0:2048].rearrange("p (q k) -> p q k", k=64)
        Cx = xT[:].rearrange("p a b -> p (a b)")[0:32, 0:4096].rearrange("p (q g k) -> p q g k", g=2, k=64)
        for ri, (cn, sgn) in enumerate([("c_re", 1.0), ("c_im", -1.0)]):
            for g2 in range(2):
                dma(Cst[16 * g2:16 * g2 + 16], W[cn].rearrange("(q g) h p -> g h q p", g=2)[g2], writes=["CT"], slow=True)
            for g2 in range(2):
                E("dve", CALL("tensor_scalar", Cx[:, :, g2, :], Cst, mask16[:, g2:g2 + 1], sgn, ALU.mult, ALU.mult),
                  ["CT", "mask16"], ["xT"])
            for q in range(32):
                tr(pb[6][:, (q % 16) * 32:(q % 16) * 32 + 32], Cx[:, q, :, :].rearrange("p g k -> p (g k)"), ident[0:32, 0:32], ["xT"], ["p6"])
                if q % 16 == 15:
                    q0 = q - 15
                    E("act", CALL("copy", W3[:, q0:q0 + 16, ri, :], pb[6][:].rearrange("p (q h) -> p q h", h=32)), ["p6"], ["W3"])

        wcnt = {"g": 0, "u": 0, "d": 0, "i": 0, "o": 0}

        def rmsnorm_stats(src_chunks, nch, Tt, key_r, scale_mat):
            for c in range(nch):
                E("act", CALL("activation", hT[:, 14 + c, 0:Tt], src_chunks(c), AF.Square), key_r, ["h%d" % (14 + c)])
            for c in range(nch):
                mm(pb[6][:, 0:Tt], scale_mat[:], hT[:, 14 + c, 0:Tt], c == 0, c == nch - 1, ["onesD", "h%d" % (14 + c)], ["p6"])
            E("act", CALL("activation", rstd[:, 0:Tt], pb[6][:, 0:Tt], AF.Sqrt, bias=epscol[:, 0:1]), ["p6", "epscol"], ["rstd"])
            E("dve", CALL("reciprocal", rstd[:, 0:Tt], rstd[:, 0:Tt]), ["rstd"], ["rstd"])

        def norm_x(gname, Tt):
            rmsnorm_stats(lambda c: xT[:, c, 0:Tt], KC, Tt, ["xT"], onesD)
            for c in range(KC):
                E("dve", CALL("scalar_tensor_tensor", xn[:, c, 0:Tt], xT[:, c, 0:Tt], V(gname, c), rstd[:, 0:Tt], ALU.mult, ALU.mult),
                  ["xT", "vec", "rstd"], ["xn"])

        def ffn(pref, gname, Tt):
            norm_x(gname, Tt)
            wg, wu, wd = W[pref + "_gate"], W[pref + "_up"], W[pref + "_down"]
            for f in range(FC):
                gi = wcnt["g"] % 2
                wcnt["g"] += 1
                dma(wgr[gi][:], wg[:, f * 128:(f + 1) * 128].rearrange("(c p) f -> p c f", p=128), writes=["wg%d" % gi])
                dma(wur[gi][:], wu[:, f * 128:(f + 1) * 128].rearrange("(c p) f -> p c f", p=128), writes=["wu%d" % gi])
                pg, pu = pb[f % 2], pb[2 + f % 2]
                for c in range(KC):
                    mm(pg[:, 0:Tt], wgr[gi][:, c, :], xn[:, c, 0:Tt], c == 0, c == KC - 1, ["wg%d" % gi, "xn"], ["p%d" % (f % 2)])
                for c in range(KC):
                    mm(pu[:, 0:Tt], wur[gi][:, c, :], xn[:, c, 0:Tt], c == 0, c == KC - 1, ["wu%d" % gi, "xn"], ["p%d" % (2 + f % 2)])
                tt = tA if f % 2 == 0 else tB
                tn = "tA" if f % 2 == 0 else "tB"
                E("act", CALL("activation", tt[:, 0:Tt], pg[:, 0:Tt], AF.Silu), ["p%d" % (f % 2)], [tn])
                E("dve", CALL("tensor_tensor", hT[:, f, 0:Tt], tt[:, 0:Tt], pu[:, 0:Tt], ALU.mult),
                  [tn, "p%d" % (2 + f % 2)], ["h%d" % f])
            for c in range(KC):
                pd = pb[4 + c % 2]
                for hf in range(2):
                    di = hf
                    dma(wdr[di][:], wd[hf * 1408:(hf + 1) * 1408, c * 128:(c + 1) * 128].rearrange("(f p) d -> p f d", p=128), writes=["wd%d" % di])
                    for f2 in range(FC // 2):
                        f = hf * (FC // 2) + f2
                        mm(pd[:, 0:Tt], wdr[di][:, f2, :], hT[:, f, 0:Tt], f == 0, f == FC - 1, ["wd%d" % di, "h%d" % f], ["p%d" % (4 + c % 2)])
                E("dve", CALL("scalar_tensor_tensor", xT[:, c, 0:Tt], pd[:, 0:Tt], 0.5, xT[:, c, 0:Tt], ALU.mult, ALU.add),
                  ["p%d" % (4 + c % 2), "xT"], ["xT"])

        def in_proj(Tt):
            norm_x("norm_mix", Tt)
            for oc in range(24):
                ii = wcnt["i"] % 2
                wcnt["i"] += 1
                dma(wir[ii][:], W["w_in"][:, oc * 128:(oc + 1) * 128].rearrange("(c p) f -> p c f", p=128), writes=["wi%d" % ii])
                po = pb[4 + oc % 2]
                for c in range(KC):
                    mm(po[:, 0:Tt], wir[ii][:, c, :], xn[:, c, 0:Tt], c == 0, c == KC - 1, ["wi%d" % ii, "xn"], ["p%d" % (4 + oc % 2)])
                if oc < 8:
                    dst, key = u_bf[:, oc, 0:Tt], "u_bf"
                elif oc < 16:
                    dst, key = xmh[:, oc - 8, 3:3 + Tt], "xmh"
                else:
                    dst, key = z_bf[:, oc - 16, 0:Tt], "z_bf"
                if oc % 2 == 0:
                    E("act", CALL("copy", dst, po[:, 0:Tt]), ["p%d" % (4 + oc % 2)], [key])
                else:
                    E("dve", CALL("tensor_copy", dst, po[:, 0:Tt]), ["p%d" % (4 + oc % 2)], [key])

        def s5_mix(Tt):
            gT = hT
            L = min(64, Tt)
            nun = Tt // L
            def s5_pre(c, un, S):
                t0 = un * L
                X = S["extra"]
                pS, psk = S["pS"][un % 2], S["psk"][un % 2]
                umS = S["um"][:, un % 2]
                kum = "s5%sum%d" % (S["k"], un % 2)
                pSv = pS[:].rearrange("p (q r t) -> p q r t", q=4, r=2)
                for qq in range(4):
                    E("act", CALL("activation", umS[:, qq, 0:L], u_bf[:, c, t0:t0 + L], AF.Copy, scale=mask4[:, qq:qq + 1]), ["u_bf", "mask4"] + X, [kum])
                for qq in range(4):
                    for ri in range(2):
                        mm(pSv[:, qq, ri, 0:L], W1[:, c, ri, :], umS[:, qq, 0:L], True, True, ["W1", kum] + X, [psk])

            def s5_unit(c, un, S):
                pY = pb[2 + c % 2]
                pyk = "p%d" % (2 + c % 2)
                t0 = un * L
                X = S["extra"]
                K = lambda i: "s5%s%d" % (S["k"], i)
                pS, psk = S["pS"][un % 2], S["psk"][un % 2]
                xbf = S["xbf"]
                kxb = K(11)
                injC, kinjC = S["inj"][:, un % 2], "s5%sinj%d" % (S["k"], un % 2)
                injN, kinjN = S["inj"][:, (un + 1) % 2], "s5%sinj%d" % (S["k"], (un + 1) % 2)
                sk = "sre%d" % c
                pSv = pS[:].rearrange("p (q r t) -> p q r t", q=4, r=2)
                if un == 0:
                    s5_pre(c, 0, S)
                    E("pool", CALL("tensor_tensor", injC, s5st[:, :, 4 * c:4 * c + 4], bc_mid(rmag[:, 4 * c:4 * c + 4], 2), ALU.mult), ["rmag", sk] + X, [kinjC])
                    yield
                if un + 1 < nun and L == 64:
                    s5_pre(c, un + 1, S)
                    yield
                bre = pSv[:, :, 0, 0:L]
                bim = pSv[:, :, 1, 0:L]
                cs = cosT[:, 4 * c:4 * c + 4, 0:L]
                sn = sinT[:, 4 * c:4 * c + 4, 0:L]
                B = S["bufs"]
                bv = lambda i: B[:, i * 256:(i + 1) * 256].rearrange("p (q t) -> p q t", t=64)[:, :, 0:L]
                E("dve", CALL("tensor_tensor", bv(2), bre, cs, ALU.mult), [psk, "cosT"] + X, [K(2)])
                E("dve", CALL("tensor_tensor", bv(3), bim, sn, ALU.mult), [psk, "sinT"] + X, [K(3)])
                E("dve", CALL("tensor_tensor", bv(6), bim, cs, ALU.mult), [psk, "cosT"] + X, [K(6)])
                E("dve", CALL("tensor_tensor", bv(7), bre, sn, ALU.mult), [psk, "sinT"] + X, [K(7)])
                E("dve", CALL("tensor_tensor", bv(0), bv(2), bv(3), ALU.add), [K(2), K(3)] + X, [K(0)])
                E("dve", CALL("tensor_tensor", bv(1), bv(6), bv(7), ALU.subtract), [K(6), K(7)] + X, [K(1)])
                if L == 64:
                    W2 = B[:, 0:512].rearrange("p (r q t) -> p r q t", r=2, t=64)
                    E("dve", CALL("tensor_tensor", W2[:, :, :, 0], W2[:, :, :, 0], injC, ALU.add), [K(0), K(1), kinjC] + X, [K(0), K(1)])
                    rt = S["rtab"][:, 4 * c:4 * c + 4, :].rearrange("p q t -> p (q t)")
                    E("dve", CALL("tensor_tensor_scan", B[:, 1024:1280], rt, B[:, 0:256], 0.0, ALU.mult, ALU.add), [K(0), "rtab"] + X, [K(4)])
                    E("dve", CALL("tensor_tensor_scan", B[:, 1280:1536], rt, B[:, 256:512], 0.0, ALU.mult, ALU.add), [K(1), "rtab"] + X, [K(5)])
                    yield
                else:
                    for qq in range(4):
                        q = 4 * c + qq
                        E("dve", CALL("tensor_tensor_scan", bv(4)[:, qq, :], bc_row(rmag[:, q:q + 1], L), bv(0)[:, qq, :],
                                      sre[:, q:q + 1], ALU.mult, ALU.add), ["rmag", K(0), sk] + X, [K(4)])
                        E("dve", CALL("tensor_tensor_scan", bv(5)[:, qq, :], bc_row(rmag[:, q:q + 1], L), bv(1)[:, qq, :],
                                      sim[:, q:q + 1], ALU.mult, ALU.add), ["rmag", K(1), sk] + X, [K(5)])
                    yield
                zr, zi = bv(4), bv(5)
                E("pool", CALL("tensor_tensor", bv(2), zr, cs, ALU.mult), [K(4), "cosT"] + X, [K(2)])
                E("pool", CALL("tensor_tensor", bv(3), zi, sn, ALU.mult), [K(5), "sinT"] + X, [K(3)])
                E("pool", CALL("tensor_tensor", bv(6), bv(2), bv(3), ALU.subtract), [K(2), K(3)] + X, [K(6)])
                E("pool", CALL("tensor_tensor", bv(2), zi, cs, ALU.mult), [K(5), "cosT"] + X, [K(2)])
                E("pool", CALL("tensor_tensor", bv(3), zr, sn, ALU.mult), [K(4), "sinT"] + X, [K(3)])
                E("pool", CALL("tensor_tensor", bv(7), bv(2), bv(3), ALU.add), [K(2), K(3)] + X, [K(7)])
                X4 = B[:, 1536:2048].rearrange("p (r q t) -> p r q t", r=2, t=64)
                if L == 64 and un + 1 < nun:
                    E("pool", CALL("tensor_tensor", injN, X4[:, :, :, L - 1], bc_mid(rmag[:, 4 * c:4 * c + 4], 2), ALU.mult), ["rmag", K(6), K(7)] + X, [kinjN])
                yield
                E("act", CALL("copy", xbf[:, :, :, 0:L], X4[:, :, :, 0:L]), [K(6), K(7)] + X, [kxb])
                if L != 64 or un + 1 == nun:
                    E("act", CALL("copy", s5st[:, :, 4 * c:4 * c + 4], X4[:, :, :, L - 1]), [K(6), K(7)] + X, [sk])
                yield
                for qq in range(4):
                    q = 4 * c + qq
                    mm(pY[32 * qq:32 * qq + 32, t0:t0 + L], W3[:, q, 0, :], xbf[:, 0, qq, 0:L], True, False, ["W3", kxb] + X, [pyk], tp=(0, 32 * qq))
                    mm(pY[32 * qq:32 * qq + 32, t0:t0 + L], W3[:, q, 1, :], xbf[:, 1, qq, 0:L], False, True, ["W3", kxb] + X, [pyk], tp=(0, 32 * qq))
                yield

            def s5_post(c):
                pY = pb[2 + c % 2]
                pyk = "p%d" % (2 + c % 2)
                E("dve", CALL("scalar_tensor_tensor", tA[:, 0:Tt], u_bf[:, c, 0:Tt], V("s5_d", c), pY[:, 0:Tt], ALU.mult, ALU.add),
                  ["u_bf", "vec", pyk], ["tA"])
                E("act", CALL("activation", tB[:, 0:Tt], tA[:, 0:Tt], AF.Square), ["tA"], ["tB"])
                E("dve", CALL("tensor_scalar", tB[:, 0:Tt], tB[:, 0:Tt], 0.044715, 1.0, ALU.mult, ALU.add), ["tB"], ["tB"])
                E("pool", CALL("tensor_tensor", tB[:, 0:Tt], tB[:, 0:Tt], tA[:, 0:Tt], ALU.mult), ["tA", "tB"], ["tB"])
                E("act", CALL("activation", tB[:, 0:Tt], tB[:, 0:Tt], AF.Sigmoid, scale=1.5957691216057308), ["tB"], ["tB"])
                E("dve", CALL("tensor_tensor", gT[:, c, 0:Tt], tA[:, 0:Tt], tB[:, 0:Tt], ALU.mult), ["tA", "tB"], ["h%d" % c])

            ybf_ = ybuf[:].rearrange("p a b -> p (a b)")
            rtab = ybf_[:, 2048:4096].rearrange("p (q t) -> p q t", t=64)
            E("dve", CALL("tensor_tensor", rtab, bc_last(rmag[:, :], 64), bc_mid(resetm[:, :], 32), ALU.mult), ["rmag", "resetm", "ybuf"], ["rtab"])
            SA = dict(bufs=s5all[:].rearrange("p a b -> p (a b)"), um=um, xbf=xbfA, inj=inj[0], k="A", extra=[], pS=[pb[0], pb[1]], psk=["p0", "p1"], rtab=rtab)
            SB = dict(bufs=ybf_[:, 0:2048], um=umB, xbf=xbfB, inj=inj[1], k="B", extra=["ybuf"], pS=[pb[4], pb[5]], psk=["p4", "p5"], rtab=rtab)
            for cp in range(KC // 2):
                c0, c1 = 2 * cp, 2 * cp + 1
                for un in range(nun):
                    gens = [s5_unit(c0, un, SA), s5_unit(c1, un, SB)]
                    while gens:
                        for g_ in list(gens):
                            try:
                                next(g_)
                            except StopIteration:
                                gens.remove(g_)
                s5_post(c0)
                s5_post(c1)
            for oc in range(KC):
                ii = wcnt["i"] % 2
                wcnt["i"] += 1
                dma(wir[ii][:], W["s5_glu_w"][:, oc * 128:(oc + 1) * 128].rearrange("(c p) f -> p c f", p=128), writes=["wi%d" % ii])
                po = pb[4 + oc % 2]
                pk = "p%d" % (4 + oc % 2)
                for c in range(KC):
                    mm(po[:, 0:Tt], wir[ii][:, c, :], gT[:, c, 0:Tt], c == 0, c == KC - 1, ["wi%d" % ii, "h%d" % c], [pk])
                E("act", CALL("activation", tA[:, 0:Tt], po[:, 0:Tt], AF.Sigmoid, bias=V("s5_glu_b", oc)), [pk, "vec"], ["tA"])
                E("dve", CALL("tensor_tensor", ybuf[:, oc, 0:Tt], gT[:, oc, 0:Tt], tA[:, 0:Tt], ALU.mult), ["tA", "h%d" % oc], ["ybuf"])
            rmsnorm_stats(lambda c: ybuf[:, c, 0:Tt], KC, Tt, ["ybuf"], onesD)
            for c in range(KC):
                E("dve", CALL("scalar_tensor_tensor", mixed[:, c, 0:Tt], ybuf[:, c, 0:Tt], V("out_norm_s5", c), rstd[:, 0:Tt], ALU.mult, ALU.mult),
                  ["ybuf", "vec", "rstd"], ["mixed"])

        def mlstm_mix(Tt):
            qT = hT
            for c in range(KC):
                eng = "dve"
                E(eng, CALL("tensor_scalar", tA[:, 0:Tt], xmh[:, c, 0:Tt], V("cw0", c), 0.0, ALU.mult, ALU.add), ["xmh", "vec"], ["tA"])
                for j in range(1, 4):
                    E(eng, CALL("scalar_tensor_tensor", tA[:, 0:Tt], xmh[:, c, j:j + Tt], V("cw%d" % j, c), tA[:, 0:Tt], ALU.mult, ALU.add),
                      ["xmh", "vec", "tA"], ["tA"])
                E("act", CALL("activation", xc_bf[:, c, 0:Tt], tA[:, 0:Tt], AF.Silu, bias=V("ml_conv_b", c)), ["tA", "vec"], ["xn"])
            for c in range(KC):
                E("act", CALL("activation", z_bf[:, c, 0:Tt], z_bf[:, c, 0:Tt], AF.Silu), ["z_bf"], ["z_bf"])
            for c in range(KC):
                for wi, (src, dstT, key) in enumerate([(xc_bf[:, c, 0:Tt], qT[:, c, 0:Tt], "h%d" % c),
                                                       (xc_bf[:, c, 0:Tt], qT[:, 8 + c, 0:Tt], "h%d" % (8 + c)),
                                                       (xmh[:, c, 3:3 + Tt], vT[:, c, 0:Tt], "h%d" % (16 + c))]):
                    po = pb[4 + (3 * c + wi) % 2]
                    pk = "p%d" % (4 + (3 * c + wi) % 2)
                    mm(po[:, 0:Tt], BD[:, wi, c, :], src, True, True, ["BD", "xn", "xmh"], [pk])
                    if wi == 1:
                        E("dve", CALL("tensor_copy", dstT, po[:, 0:Tt]), [pk], [key])
                    else:
                        E("act", CALL("copy", dstT, po[:, 0:Tt]), [pk], [key])
            for gi, (gw, pbk) in enumerate([(gwi, 4), (gwf, 5)]):
                for j in range(24):
                    src = qT[:, j, 0:Tt]
                    key = "h%d" % j
                    mm(pb[pbk][0:4, 0:Tt], gw[:, j, :], src, j == 0, j == 23, ["gwi", "gwf", key], ["p%d" % pbk])
            E("act", CALL("activation", igs[:, 0:Tt], pb[4][0:4, 0:Tt], AF.Identity, bias=igb[:, 0:1]), ["p4", "igb"], ["igs"])
            E("act", CALL("activation", MM[:, 0:Tt], pb[5][0:4, 0:Tt], AF.Exp, bias=nfgb[:, 0:1], scale=-1.0), ["p5", "nfgb"], ["MM"])
            E("act", CALL("activation", MM[:, 0:Tt], MM[:, 0:Tt], AF.Ln, bias=onecol[0:4, 0:1]), ["MM", "onecol"], ["MM"])
            E("dve", CALL("tensor_tensor_scan", Fn[:, 0:Tt], bc_row(onecol[0:4, 0:1], Tt), MM[:, 0:Tt], 0.0, ALU.mult, ALU.add), ["MM", "onecol"], ["Fn"])
            E("dve", CALL("tensor_tensor", aa[:, 0:Tt], igs[:, 0:Tt], Fn[:, 0:Tt], ALU.add), ["igs", "Fn"], ["aa"])
            E("dve", CALL("tensor_tensor_scan", MM[:, 0:Tt], bc_row(onecol[0:4, 0:1], Tt), aa[:, 0:Tt], mprev[:, 0:1], ALU.mult, ALU.max),
              ["aa", "onecol", "mprev"], ["MM"])
            E("dve", CALL("tensor_copy", Mpc[:], mprev[:]), ["mprev"], ["Mpc"])
            Lc = min(128, Tt)
            for ch in range(Tt // Lc):
                t0, t1 = ch * Lc, ch * Lc + Lc
                E("dve", CALL("tensor_scalar", negM[:], MM[:, t1 - 1:t1], -1.0, 0.0, ALU.mult, ALU.add), ["MM"], ["negM"])
                E("act", CALL("activation", g4[0][:, 0:Lc], aa[:, t0:t1], AF.Exp, bias=negM[:, 0:1]), ["aa", "negM"], ["g40"])
                E("act", CALL("activation", gcol[:], Mpc[:], AF.Exp, bias=negM[:, 0:1]), ["Mpc", "negM"], ["gcol"])
                E("dve", CALL("tensor_scalar", g4[1][:, 0:Lc], Fn[:, t0:t1], negM[:, 0:1], 0.0, ALU.add, ALU.add), ["Fn", "negM"], ["g41"])
                E("dve", CALL("tensor_copy", Mpc[:], MM[:, t1 - 1:t1]), ["MM", "gcol"], ["Mpc"])
                tr(pb[7][0:Lc, 128:132], g4[0][:, 0:Lc], ident[0:4, 0:4], ["g40"], ["p7"])
                E("dve", CALL("tensor_copy", et[0:Lc, :], pb[7][0:Lc, 128:132]), ["p7"], ["et"])
                E("dve", CALL("tensor_scalar", dg[:], ident[0:4, 0:4], gcol[:, 0:1], 0.0, ALU.mult, ALU.add), ["ident", "gcol"], ["dg"])
                mm(pb[7][:, 132:136], ones4[:], dg[:], True, True, ["ones4", "dg"], ["p7"])
                E("dve", CALL("tensor_copy", gb[:], pb[7][:, 132:136]), ["p7"], ["gb"])
                for c in range(KC):
                    mm(pb[c // 4][0:Lc, (c % 4) * 128:(c % 4) * 128 + 128], xc_bf[:, c, t0:t1], BD[:, 1, c, :], True, True, ["xn", "BD"], ["p%d" % (c // 4)])
                    mm(pb[2 + c // 4][0:Lc, (c % 4) * 128:(c % 4) * 128 + 128], xmh[:, c, 3 + t0:3 + t1], BD[:, 2, c, :], True, True, ["xmh", "BD"], ["p%d" % (2 + c // 4)])
                for h in range(4):
                    kps = pb[h // 2][0:Lc, (h % 2) * 256:(h % 2) * 256 + 256]
                    vps = pb[2 + h // 2][0:Lc, (h % 2) * 256:(h % 2) * 256 + 256]
                    kk, vk = "p%d" % (h // 2), "p%d" % (2 + h // 2)
                    for kc in range(2):
                        mm(pb[7][0:Lc, 0:Lc], qT[:, 8 + 2 * h + kc, t0:t1], qT[:, 2 * h + kc, t0:t1], kc == 0, kc == 1,
                           ["h%d" % (8 + 2 * h + kc), "h%d" % (2 * h + kc)], ["p7"])
                    E("dve", CALL("scalar_tensor_tensor", Sm[0:Lc, 0:Lc], pb[7][0:Lc, 0:Lc], 1.0 / 16, triu[0:Lc, 0:Lc], ALU.mult, ALU.mult), ["p7", "triu"], ["Sm"])
                    E("dve", CALL("tensor_scalar", vtp[0:Lc, :], vps, et[0:Lc, h:h + 1], 0.0, ALU.mult, ALU.add), [vk, "et"], ["vtp"])
                    E("act", CALL("copy", vtb[0:Lc, :], vps), [vk], ["vtb"])
                    E("dve", CALL("tensor_scalar", ktp[0:Lc, :], kps, et[0:Lc, h:h + 1], 1.0 / 16, ALU.mult, ALU.mult), [kk, "et"], ["ktp"])
                    E("pool", CALL("tensor_scalar", Eb[0:Lc, :], ones_bf[0:Lc, :], et[0:Lc, h:h + 1], 0.0, ALU.mult, ALU.add), ["ones_bf", "et"], ["Eb"])
                    E("pool", CALL("tensor_scalar", Cg[:].rearrange("p a b -> p (a b)"), CT[:, h, :, :].rearrange("p a b -> p (a b)"), gb[:, h:h + 1], 0.0, ALU.mult, ALU.add),
                      ["CT", "gb"], ["Cg"])
                    E("dve", CALL("tensor_scalar", ngc[:], nT[:, 2 * h:2 * h + 2], gb[:, h:h + 1], 0.0, ALU.mult, ALU.add), ["nT", "gb"], ["ngc"])
                    for kc in range(2):
                        E("pool", CALL("tensor_scalar", nrep[:, kc, :], ones_bf[:, :], ngc[:, kc:kc + 1], 0.0, ALU.mult, ALU.add), ["ones_bf", "ngc"], ["nrep"])
                    for vc in range(2):
                        o = pb[4][:, vc * 128:vc * 128 + Lc]
                        mm(o, vtp[0:Lc, vc * 128:vc * 128 + 128], Sm[0:Lc, 0:Lc], True, False, ["vtp", "Sm"], ["p4"])
                        for kc in range(2):
                            mm(o, Cg[:, kc, vc * 128:vc * 128 + 128], qT[:, 2 * h + kc, t0:t1], False, kc == 1, ["Cg", "h%d" % (2 * h + kc)], ["p4"])
                    o = pb[4][:, 256:256 + Lc]
                    mm(o, Eb[0:Lc, :], Sm[0:Lc, 0:Lc], True, False, ["Eb", "Sm"], ["p4"])
                    for kc in range(2):
                        mm(o, nrep[:, kc, :], qT[:, 2 * h + kc, t0:t1], False, kc == 1, ["nrep", "h%d" % (2 * h + kc)], ["p4"])
                    mm(pb[4][:, 384:384 + Lc], sel4[:, h * 128:h * 128 + 128], g4[1][:, 0:Lc], True, True, ["sel4", "g41"], ["p4"])
                    E("act", CALL("activation", mw[0][:, 0:Lc], pb[4][:, 384:384 + Lc], AF.Exp), ["p4"], ["mw0"])
                    E("act", CALL("activation", mw[2][:, 0:Lc], pb[4][:, 256:256 + Lc], AF.Abs), ["p4"], ["mw2"])
                    E("dve", CALL("tensor_tensor", mw[0][:, 0:Lc], mw[2][:, 0:Lc], mw[0][:, 0:Lc], ALU.max), ["mw2", "mw0"], ["mw0"])
                    E("dve", CALL("reciprocal", mw[0][:, 0:Lc], mw[0][:, 0:Lc]), ["mw0"], ["mw0"])
                    for vc in range(2):
                        E("dve", CALL("tensor_tensor", hh[:, vc, 0:Lc], pb[4][:, vc * 128:vc * 128 + Lc], mw[0][:, 0:Lc], ALU.mult), ["p4", "mw0"], ["hh"])
                        E("act", CALL("activation", hq[:, vc, 0:Lc], hh[:, vc, 0:Lc], AF.Square), ["hh"], ["hq"])
                    for kc in range(2):
                        mm(pb[6][:, kc * 256:kc * 256 + 256], ktp[0:Lc, kc * 128:kc * 128 + 128], vtb[0:Lc, :], True, True, ["ktp", "vtb"], ["p6"])
                    E("dve", CALL("scalar_tensor_tensor", CT[:, h, :, :].rearrange("p a b -> p (a b)"), CT[:, h, :, :].rearrange("p a b -> p (a b)"),
                                                                  gb[:, h:h + 1], pb[6][:, :], ALU.mult, ALU.add), ["CT", "gb", "p6", "Cg"], ["CT"])
                    for kc in range(2):
                        mm(pb[7][:, 136 + kc:137 + kc], ktp[0:Lc, kc * 128:kc * 128 + 128], ones_bf[0:Lc, 0:1], True, True, ["ktp", "ones_bf"], ["p7"])
                    E("dve", CALL("tensor_tensor", nT[:, 2 * h:2 * h + 2], ngc[:, :], pb[7][:, 136:138], ALU.add), ["ngc", "p7"], ["nT"])
                    for vc in range(2):
                        mm(pb[5][:, 0:Lc], ones256[:], hh[:, vc, 0:Lc], vc == 0, vc == 1, ["ones256", "hh"], ["p5"])
                    for vc in range(2):
                        mm(pb[5][:, 128:128 + Lc], ones256[:], hq[:, vc, 0:Lc], vc == 0, vc == 1, ["ones256", "hq"], ["p5"])
                    E("act", CALL("activation", mw[1][:, 0:Lc], pb[5][:, 0:Lc], AF.Square), ["p5"], ["mw1"])
                    E("dve", CALL("tensor_tensor", mw[1][:, 0:Lc], pb[5][:, 128:128 + Lc], mw[1][:, 0:Lc], ALU.subtract), ["p5", "mw1"], ["mw1"])
                    E("dve", CALL("tensor_scalar", mw[1][:, 0:Lc], mw[1][:, 0:Lc], 0.0, 0.0, ALU.max, ALU.add), ["mw1"], ["mw1"])
                    E("act", CALL("activation", mw[1][:, 0:Lc], mw[1][:, 0:Lc], AF.Sqrt, bias=epscol[:, 0:1]), ["mw1", "epscol"], ["mw1"])
                    E("dve", CALL("reciprocal", mw[1][:, 0:Lc], mw[1][:, 0:Lc]), ["mw1"], ["mw1"])
                    for vc in range(2):
                        c = 2 * h + vc
                        E("dve", CALL("tensor_tensor", mw[2][:, 0:Lc], hh[:, vc, 0:Lc], pb[5][:, 0:Lc], ALU.subtract), ["hh", "p5"], ["mw2"])
                        E("pool", CALL("tensor_tensor", mw[2][:, 0:Lc], mw[2][:, 0:Lc], mw[1][:, 0:Lc], ALU.mult), ["mw2", "mw1"], ["mw2"])
                        E("pool", CALL("tensor_scalar", mw[2][:, 0:Lc], mw[2][:, 0:Lc], V("ml_norm_w", c), 0.0, ALU.mult, ALU.add), ["mw2", "vec"], ["mw2"])
                        E("dve", CALL("scalar_tensor_tensor", mw[3][:, 0:Lc], xc_bf[:, c, t0:t1], V("ml_skip", c), mw[2][:, 0:Lc], ALU.mult, ALU.add),
                          ["xn", "vec", "mw2"], ["mw3"])
                        E("pool", CALL("tensor_tensor", ybuf[:, c, t0:t1], mw[3][:, 0:Lc], z_bf[:, c, t0:t1], ALU.mult), ["mw3", "z_bf"], ["ybuf"])
            E("dve", CALL("tensor_tensor", mprev[:], MM[:, Tt - 1:Tt], Fn[:, Tt - 1:Tt], ALU.subtract), ["MM", "Fn", "Mpc"], ["mprev"])
            for c in range(KC):
                E("act", CALL("copy", xmh[:, c, 0:3], xmh[:, c, Tt:Tt + 3]), ["xmh"], ["xmh"])
            rmsnorm_stats(lambda c: ybuf[:, c, 0:Tt], KC, Tt, ["ybuf"], onesD)
            for c in range(KC):
                E("dve", CALL("scalar_tensor_tensor", mixed[:, 8 + c, 0:Tt], ybuf[:, c, 0:Tt], V("out_norm_ml", c), rstd[:, 0:Tt], ALU.mult, ALU.mult),
                  ["ybuf", "vec", "rstd"], ["mixed"])

        def out_proj(Tt):
            for c in range(KC):
                po = pb[4 + c % 2]
                pk = "p%d" % (4 + c % 2)
                for hf in range(2):
                    oi = hf
                    dma(wor[oi][:], W["w_out"][hf * 1024:(hf + 1) * 1024, c * 128:(c + 1) * 128].rearrange("(k p) d -> p k d", p=128), writes=["wo%d" % oi])
                    for k2 in range(8):
                        k = hf * 8 + k2
                        mm(po[:, 0:Tt], wor[oi][:, k2, :], mixed[:, k, 0:Tt], k == 0, k == 15, ["wo%d" % oi, "mixed"], [pk])
                E("dve", CALL("tensor_tensor", xT[:, c, 0:Tt], xT[:, c, 0:Tt], po[:, 0:Tt], ALU.add), [pk, "xT"], ["xT"])

        def load_tile(src_rows, Tt):
            nsub = (Tt + 127) // 128
            for n in range(nsub):
                r = min(128, Tt - n * 128)
                dma(xtok[0:r, n, :], src_rows[n * 128:n * 128 + r, :], writes=["ybuf"])
            for c in range(KC):
                for n in range(nsub):
                    r = min(128, Tt - n * 128)
                    tr(pb[7][:, n * 128:n * 128 + r], xtok[0:r, n, c * 128:(c + 1) * 128], ident[0:r, 0:r], ["ybuf"], ["p7"])
                E("dve" if c % 2 == 0 else "act",
                  (CALL("tensor_copy", xT[:, c, 0:Tt], pb[7][:, 0:Tt])) if c % 2 == 0 else (CALL("copy", xT[:, c, 0:Tt], pb[7][:, 0:Tt])),
                  ["p7"], ["xT"])

        def store_tile(dst_rows, Tt):
            norm_x("norm_final", Tt)
            nsub = (Tt + 127) // 128
            for c in range(KC):
                E("dve", CALL("scalar_tensor_tensor", ybuf[:, c, 0:Tt], xT[:, c, 0:Tt], V("norm_final", c), rstd[:, 0:Tt], ALU.mult, ALU.mult),
                  ["xT", "vec", "rstd"], ["ybuf"])
            for n in range(nsub):
                r = min(128, Tt - n * 128)
                for c4 in range(2):
                    for c in range(c4 * 4, c4 * 4 + 4):
                        tr(pb[7][0:r, (c % 4) * 128:(c % 4) * 128 + 128], ybuf[:, c, n * 128:n * 128 + r], ident[:], ["ybuf"], ["p7"])
                    E("act", CALL("copy", otok[0:r, c4 * 512:(c4 + 1) * 512], pb[7][0:r, :]), ["p7"], ["tA", "tB"])
                outs.append(dma(dst_rows[n * 128:n * 128 + r, :], otok[0:r, :], reads=["tA", "tB"], q="sp"))

        def init_state(si):
            if si is None:
                for t, k in [(sre, SRE), (sim, SRE), (nT, ["nT"]), (mprev, ["mprev"])]:
                    E("dve", CALL("memset", t[:], 0.0), [], k)
                E("pool", CALL("memset", CT[:].rearrange("p a b c -> p (a b c)"), 0.0), [], ["CT"])
                E("pool", CALL("memset", xmh[:, :, 0:3], 0.0), [], ["xmh"])
                return
            for src, dst, k in [(st_s5re, sre, SRE), (st_s5im, sim, SIM)]:
                dma(stg[0:32, 0:128], src[si], writes=["stg"])
                tr(pb[7][:, 0:32], stg[0:32, 0:128], ident[0:32, 0:32], ["stg"], ["p7"])
                E("dve", CALL("tensor_copy", dst[:], pb[7][:, 0:32]), ["p7"], k)
            dma(stg[0:8, 0:128], st_n[si], writes=["stg"])
            tr(pb[7][:, 0:8], stg[0:8, 0:128], ident[0:8, 0:8], ["stg"], ["p7"])
            E("dve", CALL("tensor_copy", nT[:], pb[7][:, 0:8]), ["p7"], ["nT"])
            dma(mprev[:], st_m[si], writes=["mprev"])
            for c in range(KC):
                dma(stg[0:3, 0:128], st_conv[si][:, c * 128:(c + 1) * 128], writes=["stg"])
                tr(pb[7][:, 0:3], stg[0:3, 0:128], ident[0:3, 0:3], ["stg"], ["p7"])
                E("dve", CALL("tensor_copy", xmh[:, c, 0:3], pb[7][:, 0:3]), ["p7"], ["xmh"])
            for h in range(4):
                for vc in range(2):
                    dma(stg[:, 0:256], st_c[si, h, vc * 128:(vc + 1) * 128, :], writes=["stg"])
                    for kc in range(2):
                        tr(pb[7][:, kc * 128:kc * 128 + 128], stg[:, kc * 128:kc * 128 + 128], ident[:], ["stg"], ["p7"])
                    E("dve", CALL("tensor_copy", CT[:, h, :, vc * 128:vc * 128 + 128], pb[7][:, 0:256].rearrange("p (k v) -> p k v", k=2)), ["p7"], ["CT"])

        def store_state(oi):
            for src, dst, k in [(sre, o_s5re, SRE), (sim, o_s5im, SIM)]:
                tr(pb[7][0:32, 0:128], src[:], ident[:], k, ["p7"])
                E("dve", CALL("tensor_copy", stg[0:32, 0:128], pb[7][0:32, 0:128]), ["p7"], ["stg"])
                outs.append(dma(dst[oi], stg[0:32, 0:128], reads=["stg"], q="sp"))
            tr(pb[7][0:8, 0:128], nT[:], ident[:], ["nT"], ["p7"])
            E("dve", CALL("tensor_copy", stg[0:8, 0:128], pb[7][0:8, 0:128]), ["p7"], ["stg"])
            outs.append(dma(o_n[oi], stg[0:8, 0:128], reads=["stg"], q="sp"))
            outs.append(dma(o_m[oi], mprev[:], reads=["mprev"], q="sp"))
            for c in range(KC):
                E("dve", CALL("tensor_copy", mw[0][:, 0:3], xmh[:, c, 0:3]), ["xmh"], ["mw0"])
                tr(pb[7][0:3, 0:128], mw[0][:, 0:3], ident[:], ["mw0"], ["p7"])
                E("dve", CALL("tensor_copy", stg[0:3, 0:128], pb[7][0:3, 0:128]), ["p7"], ["stg"])
                outs.append(dma(o_conv[oi][:, c * 128:(c + 1) * 128], stg[0:3, 0:128], reads=["stg"], q="sp"))
            for h in range(4):
                for vc in range(2):
                    for kc in range(2):
                        tr(pb[7][:, kc * 128:kc * 128 + 128], CT[:, h, kc, vc * 128:vc * 128 + 128], ident[:], ["CT"], ["p7"])
                    E("dve", CALL("tensor_copy", stg[:, 0:256], pb[7][:, 0:256]), ["p7"], ["stg"])
                    outs.append(dma(o_c[oi, h, vc * 128:(vc + 1) * 128, :], stg[:, 0:256], reads=["stg"], q="sp"))

        def run_tile(src_rows, dst_rows, Tt):
            load_tile(src_rows, Tt)
            if STAGE >= 2:
                ffn("ffn1", "norm_ffn1", Tt)
            if STAGE >= 3:
                in_proj(Tt)
            if STAGE >= 4:
                s5_mix(Tt)
            if STAGE >= 5:
                mlstm_mix(Tt)
            if STAGE >= 6:
                out_proj(Tt)
            if STAGE >= 7:
                ffn("ffn2", "norm_ffn2", Tt)
            if dst_rows is not None:
                store_tile(dst_rows, Tt)

        if STAGE == 0:
            P.emit(final_waits=outs)
            return nc
        init_state(None)
        run_tile(xp[0:NMETA, :], None, NMETA)
        for ti in range((NP - NMETA) // TT):
            r0 = NMETA + ti * TT
            run_tile(xp[r0:r0 + TT, :], yp[r0 - NMETA:r0 - NMETA + TT, :], TT)
        store_state(0)
        for si in range(NSAMP):
            init_state(si)
            run_tile(xs[si], ys[si], SL)
            store_state(1 + si)
        P.emit(final_waits=outs)
    return nc


def host_consts():
    p = np.arange(128)
    c = {}
    c["ident"] = np.eye(128, dtype=np.float32)
    c["maskE"] = np.stack([(p // 64 == 0), (p // 64 == 1)], 1).astype(np.float32)
    p32 = np.arange(32)
    c["mask16"] = np.stack([(p32 // 16 == 0), (p32 // 16 == 1)], 1).astype(np.float32)
    c["triu"] = np.triu(np.ones((128, 128), np.float32))
    c["bdmask"] = (p[:, None] // 4 == np.arange(32)[None, :]).astype(np.float32)
    c["tvec"] = np.broadcast_to(np.arange(1, 65, dtype=np.float32)[None, :], (128, 64)).copy()
    s = np.zeros((4, 4, 128), np.float32)
    for h in range(4):
        s[h, h, :] = 1.0
    c["sel4"] = s.reshape(4, 512)
    c["mask4"] = (p[:, None] // 32 == np.arange(4)[None, :]).astype(np.float32)
    return c


_CACHE = {}


def kernel(**inp):
    f = lambda a: np.ascontiguousarray(np.asarray(a, dtype=np.float32))
    x_prompt = f(inp["x_prompt"])
    x_sample = f(inp["x_sample"])
    NB, SEQ, _ = x_prompt.shape
    NDEC, SL, _ = x_sample.shape
    NP = NMETA + SEQ
    ncores = 8
    NSAMP = NDEC // ncores
    key = (NP, NSAMP, SL)
    if key not in _CACHE:
        _CACHE[key] = build_program(NP, NSAMP, SL)
    nc = _CACHE[key]
    meta = f(inp["meta_tokens"])
    shared = host_consts()
    for n in ["ffn1_gate", "ffn1_up", "ffn1_down", "ffn2_gate", "ffn2_up", "ffn2_down", "w_in", "s5_glu_w", "w_out"]:
        shared[n] = f(inp[n])[0]
    shared["lam_re"] = f(inp["s5_lambda_re"])[0].reshape(32, 128)
    shared["lam_im"] = f(inp["s5_lambda_im"])[0].reshape(32, 128)
    shared["log_dt"] = f(inp["s5_log_dt"])[0]
    shared["b_re"] = f(inp["s5_b_re"])[0]
    shared["b_im"] = f(inp["s5_b_im"])[0]
    shared["c_re"] = f(inp["s5_c_re"])[0]
    shared["c_im"] = f(inp["s5_c_im"])[0]
    shared["wq"] = f(inp["ml_wq"])[0]
    shared["wk"] = f(inp["ml_wk"])[0]
    shared["wv"] = f(inp["ml_wv"])[0]
    shared["igw"] = f(inp["ml_igate_w"])[0]
    shared["fgw"] = f(inp["ml_fgate_w"])[0]
    shared["igb"] = f(inp["ml_igate_b"])[0].reshape(4, 1)
    shared["fgb"] = f(inp["ml_fgate_b"])[0].reshape(4, 1)
    cw = f(inp["ml_conv_w"])[0]
    vd = {"norm_ffn1": f(inp["norm_ffn1"])[0], "norm_mix": f(inp["norm_mix"])[0], "s5_d": f(inp["s5_d"])[0],
          "s5_glu_b": f(inp["s5_glu_b"])[0], "cw0": cw[0], "cw1": cw[1], "cw2": cw[2], "cw3": cw[3],
          "ml_conv_b": f(inp["ml_conv_b"])[0], "ml_norm_w": f(inp["ml_norm_w"])[0], "ml_skip": f(inp["ml_skip"])[0],
          "out_norm_s5": f(inp["out_norm_s5"])[0], "out_norm_ml": f(inp["out_norm_ml"])[0],
          "norm_ffn2": f(inp["norm_ffn2"])[0], "norm_final": f(inp["norm_final"])}
    shared["vecs"] = np.concatenate([vd[n].reshape(8, 128) for n in VEC_NAMES], 0)
    s5re, s5im = f(inp["state_s5_re"])[0], f(inp["state_s5_im"])[0]
    stc, stn, stm, stcv = f(inp["state_mlstm_c"])[0], f(inp["state_mlstm_n"])[0], f(inp["state_mlstm_m"])[0], f(inp["state_mlstm_conv"])[0]
    in_maps = []
    for c in range(ncores):
        b = c % NB
        sl = slice(c * NSAMP, (c + 1) * NSAMP)
        m = dict(shared)
        m["xp"] = np.concatenate([meta, x_prompt[b]], 0)
        m["xs"] = x_sample[sl]
        m["st_s5re"] = s5re[sl].reshape(NSAMP, 32, 128)
        m["st_s5im"] = s5im[sl].reshape(NSAMP, 32, 128)
        m["st_c"] = stc[sl]
        m["st_n"] = stn[sl].reshape(NSAMP, 8, 128)
        m["st_m"] = stm[sl].reshape(NSAMP, 4, 1)
        m["st_conv"] = stcv[sl]
        in_maps.append(m)
    res = run_bass_kernel_spmd(nc, in_maps, core_ids=list(range(ncores))).results
    y_prompt = np.stack([res[b]["yp"] for b in range(NB)], 0)
    y_sample = np.concatenate([res[c]["ys"] for c in range(ncores)], 0)

    def gather(name, shape_tail):
        pr = np.stack([res[b][name][0] for b in range(NB)], 0).reshape((1, NB) + shape_tail)
        sm = np.concatenate([res[c][name][1:] for c in range(ncores)], 0).reshape((1, NDEC) + shape_tail)
        return pr.astype(np.float32), sm.astype(np.float32)

    p_re, s_re = gather("o_s5re", (64, 64))
    p_im, s_im = gather("o_s5im", (64, 64))
    p_c, s_c = gather("o_c", (4, 256, 256))
    p_n, s_n = gather("o_n", (4, 256))
    p_m, s_m = gather("o_m", (4,))
    p_cv, s_cv = gather("o_conv", (3, 1024))
    return (y_prompt.astype(np.float32), y_sample.astype(np.float32), p_re, p_im, p_c, p_n, p_m, p_cv,
            s_re, s_im, s_c, s_n, s_m, s_cv)
```

```python
import math
import numpy as np
import concourse.bass as bass
import concourse.mybir as mybir
from concourse.bass_utils import run_bass_kernel_spmd
from contextlib import ExitStack

F32 = mybir.dt.float32
BF16 = mybir.dt.bfloat16
I32 = mybir.dt.int32
ALU = mybir.AluOpType
AF = mybir.ActivationFunctionType

D = 1024
DFF = 2816
KC = 8
FC = 22
NMETA = 16
EPS = 1e-6
STAGE = 9
ENGS = ("pe", "act", "dve", "pool", "sp")
NDMA_SEM = 12
VEC_NAMES = ["norm_ffn1", "norm_mix", "s5_d", "s5_glu_b", "cw0", "cw1", "cw2", "cw3", "ml_conv_b",
             "ml_norm_w", "ml_skip", "out_norm_s5", "out_norm_ml", "norm_ffn2", "norm_final"]
VI = {n: i for i, n in enumerate(VEC_NAMES)}


class Op:
    __slots__ = ("eng", "fn", "deps", "idx", "signaled", "sigval", "dma", "dsem", "dval", "dprev")

    def __init__(self, eng, fn, dma):
        self.eng = eng
        self.fn = fn
        self.deps = []
        self.signaled = False
        self.sigval = 0
        self.dma = dma
        self.dsem = None
        self.dval = 0
        self.dprev = None


class Prog:
    def __init__(self, nc):
        self.nc = nc
        self.ops = {e: [] for e in ENGS}
        self.last_writer = {}
        self.readers = {}
        self.ndma = {e: 0 for e in ENGS}
        self.dma_ops = {e: [] for e in ENGS}

    def op(self, eng, fn, reads=(), writes=(), dma=False):
        o = Op(eng, fn, dma)
        deps = []
        for r in reads:
            w = self.last_writer.get(r)
            if w is not None:
                deps.append(w)
        for wr in writes:
            w = self.last_writer.get(wr)
            if w is not None:
                deps.append(w)
            deps.extend(self.readers.get(wr, ()))
        seen = set()
        for d in deps:
            if id(d) in seen or d is o:
                continue
            seen.add(id(d))
            if d.eng == "pe" and eng == "pe" and not d.dma and not dma:
                continue
            o.deps.append(d)
        for r in reads:
            self.readers.setdefault(r, []).append(o)
        for wr in writes:
            self.last_writer[wr] = o
            self.readers[wr] = []
        if dma:
            k = self.ndma[eng]
            self.ndma[eng] += 1
            o.dsem = k % NDMA_SEM
            o.dval = 16 * (k // NDMA_SEM + 1)
            if k >= NDMA_SEM:
                o.dprev = self.dma_ops[eng][k - NDMA_SEM]
            self.dma_ops[eng].append(o)
        o.idx = len(self.ops[eng])
        self.ops[eng].append(o)
        return o

    def emit(self, final_waits=()):
        nc = self.nc
        for e in ENGS:
            for o in self.ops[e]:
                for d in o.deps:
                    if not d.dma:
                        d.signaled = True
        for o in final_waits:
            if not o.dma:
                o.signaled = True
        for e in ENGS:
            c = 0
            for o in self.ops[e]:
                if o.signaled and not o.dma:
                    c += 1
                    o.sigval = c
        with ExitStack() as st:
            esem = {e: st.enter_context(nc.semaphore("s_" + e)) for e in ENGS}
            dsem = {e: [st.enter_context(nc.semaphore("d_%s_%d" % (e, i))) for i in range(NDMA_SEM)]
                    for e in ENGS if self.ndma[e] > 0}
            block = st.enter_context(nc.Block())

            def body(e, engine):
                observed = {}

                def wait(key, sem, val):
                    if observed.get(key, 0) >= val:
                        return
                    observed[key] = val
                    engine.wait_ge(sem, val)

                for o in self.ops[e]:
                    for d in o.deps:
                        if d.dma:
                            wait(("d", d.eng, d.dsem), dsem[d.eng][d.dsem], d.dval)
                        else:
                            wait(("e", d.eng), esem[d.eng], d.sigval)
                    if o.dma and o.dprev is not None:
                        wait(("d", e, o.dprev.dsem), dsem[e][o.dprev.dsem], o.dprev.dval)
                    ins = o.fn(engine)
                    if o.dma:
                        ins.then_inc(dsem[e][o.dsem], 16)
                    elif o.signaled:
                        ins.then_inc(esem[e], 1)
                if e == "sp":
                    for o in final_waits:
                        if o.dma:
                            wait(("d", o.eng, o.dsem), dsem[o.eng][o.dsem], o.dval)
                        else:
                            wait(("e", o.eng), esem[o.eng], o.sigval)

            block.sync(lambda eng: body("sp", eng))
            block.scalar(lambda eng: body("act", eng))
            block.vector(lambda eng: body("dve", eng))
            block.gpsimd(lambda eng: body("pool", eng))
            block.tensor(lambda eng: body("pe", eng))


def CALL(name, *args, **kw):
    return lambda e: getattr(e, name)(*args, **kw)


def bc_last(ap, n):
    return bass.AP(ap.tensor, ap.offset, [list(a) for a in ap.ap] + [[0, n]])


def bc_mid(ap, n):
    a = [list(x) for x in ap.ap]
    return bass.AP(ap.tensor, ap.offset, [a[0], [0, n]] + a[1:])


def bc_row(ap, n):
    a = [list(x) for x in ap.ap]
    return bass.AP(ap.tensor, ap.offset, [a[0], [0, n]])


def build_program(NP, NSAMP=2, SL=32):
    nc = bass.Bass("TRN2", target_bir_lowering=False)
    dr = {}

    def din(name, shape, dt=F32):
        dr[name] = nc.dram_tensor(name, list(shape), dt, kind="ExternalInput").ap()
        return dr[name]

    def dout(name, shape):
        dr[name] = nc.dram_tensor(name, list(shape), F32, kind="ExternalOutput").ap()
        return dr[name]

    xp = din("xp", [NP, D])
    xs = din("xs", [NSAMP, SL, D])
    st_s5re = din("st_s5re", [NSAMP, 32, 128])
    st_s5im = din("st_s5im", [NSAMP, 32, 128])
    st_c = din("st_c", [NSAMP, 4, 256, 256])
    st_n = din("st_n", [NSAMP, 8, 128])
    st_m = din("st_m", [NSAMP, 4, 1])
    st_conv = din("st_conv", [NSAMP, 3, D])
    vecs = din("vecs", [len(VEC_NAMES) * 8, 128])
    W = {}
    for n, shp in [("ffn1_gate", [D, DFF]), ("ffn1_up", [D, DFF]), ("ffn1_down", [DFF, D]),
                   ("ffn2_gate", [D, DFF]), ("ffn2_up", [D, DFF]), ("ffn2_down", [DFF, D]),
                   ("w_in", [D, 3 * D]), ("s5_glu_w", [D, D]), ("w_out", [2 * D, D]),
                   ("lam_re", [32, 128]), ("lam_im", [32, 128]), ("log_dt", [64]),
                   ("b_re", [64, 64, 16]), ("b_im", [64, 64, 16]), ("c_re", [64, 16, 64]), ("c_im", [64, 16, 64]),
                   ("wq", [256, 4, 4]), ("wk", [256, 4, 4]), ("wv", [256, 4, 4]),
                   ("igw", [3 * D, 4]), ("fgw", [3 * D, 4]), ("igb", [4, 1]), ("fgb", [4, 1]),
                   ("ident", [128, 128]), ("maskE", [128, 2]), ("mask16", [32, 2]), ("triu", [128, 128]),
                   ("bdmask", [128, 32]), ("tvec", [128, 64]), ("sel4", [4, 512]), ("mask4", [128, 4])]:
        W[n] = din(n, shp)
    NS = 1 + NSAMP
    yp = dout("yp", [NP - NMETA, D])
    ys = dout("ys", [NSAMP, SL, D])
    o_s5re = dout("o_s5re", [NS, 32, 128])
    o_s5im = dout("o_s5im", [NS, 32, 128])
    o_c = dout("o_c", [NS, 4, 256, 256])
    o_n = dout("o_n", [NS, 8, 128])
    o_m = dout("o_m", [NS, 4, 1])
    o_conv = dout("o_conv", [NS, 3, D])

    P = Prog(nc)
    outs = []
    TT = 512
    with ExitStack() as st:
        def sb(name, shape, dt=F32):
            return st.enter_context(nc.sbuf_tensor("sb_" + name, list(shape), dt))

        st.enter_context(nc.allow_low_precision("bf16 matmul operands with fp32 PSUM accumulation"))
        pb = [st.enter_context(nc.psum_tensor("pb%d" % i, [128, 512], F32)) for i in range(8)]

        xT = sb("xT", [128, KC, TT])
        xn = sb("xn", [128, KC, TT], BF16)
        xc_bf = xn
        hT = sb("hT", [128, 24, TT], BF16)
        u_bf = sb("u_bf", [128, KC, TT], BF16)
        z_bf = sb("z_bf", [128, KC, TT], BF16)
        xmh = sb("xmh", [128, KC, TT + 3], BF16)
        vT = hT[:, 16:24, :]
        mixed = sb("mixed", [128, 16, TT], BF16)
        ybuf = sb("ybuf", [128, KC, TT])
        xtok = ybuf[:].rearrange("p a b -> p (a b)").rearrange("p (n d) -> p n d", d=D)
        rstd = sb("rstd", [128, TT])
        tAB = sb("tAB", [128, 2 * TT])
        tA = tAB[:, 0:TT]
        tB = tAB[:, TT:2 * TT]
        otok = tAB
        wgr = [sb("wgr%d" % i, [128, KC, 128], BF16) for i in range(2)]
        wur = [sb("wur%d" % i, [128, KC, 128], BF16) for i in range(2)]
        wdr = [sb("wdr%d" % i, [128, FC // 2, 128], BF16) for i in range(2)]
        wir = [sb("wir%d" % i, [128, KC, 128], BF16) for i in range(2)]
        wor = [sb("wor%d" % i, [128, 8, 128], BF16) for i in range(2)]
        ident = sb("ident", [128, 128])
        ones_bf = sb("ones_bf", [128, 128], BF16)
        onesD = sb("onesD", [128, 128], BF16)
        ones256 = sb("ones256", [128, 128])
        ones4 = sb("ones4", [4, 128])
        onecol = sb("onecol", [128, 1])
        epscol = sb("epscol", [128, 1])
        vec = sb("vec", [128, len(VEC_NAMES) * 8])
        maskE = sb("maskE", [128, 2])
        mask16 = sb("mask16", [32, 2])
        triu = sb("triu", [128, 128])
        bdmask = sb("bdmask", [128, 32])
        tvec = sb("tvec", [128, 64])
        sel4 = sb("sel4", [4, 512])
        cosT = sb("cosT", [128, 32, 64])
        sinT = sb("sinT", [128, 32, 64])
        rmag = sb("rmag", [128, 32])
        W1 = sb("W1", [128, KC, 2, 128], BF16)
        W3 = sb("W3", [128, 32, 2, 32], BF16)
        s5st = sb("s5st", [128, 2, 32])
        sre = s5st[:, 0, :]
        sim = s5st[:, 1, :]
        s5all = sb("s5all", [128, 8, 256])
        s5w = [s5all[:, i, :].rearrange("p (a b) -> p a b", b=64) for i in range(8)]
        resetm = sb("resetm", [128, 64])
        inj = [sb("inj%d" % i, [128, 2, 2, 4]) for i in range(2)]
        xbfA = sb("xbfA", [128, 2, 4, 64], BF16)
        xbfB = sb("xbfB", [128, 2, 4, 64], BF16)
        um = sb("um", [128, 2, 4, 64], BF16)
        umB = sb("umB", [128, 2, 4, 64], BF16)
        mask4 = sb("mask4", [128, 4])
        CT = sb("CT", [128, 4, 2, 256])
        nT = sb("nT", [128, 8])
        mprev = sb("mprev", [4, 1])
        BD = sb("BD", [128, 3, KC, 128], BF16)
        gwi = sb("gwi", [128, 24, 4], BF16)
        gwf = sb("gwf", [128, 24, 4], BF16)
        igb = sb("igb", [4, 1])
        nfgb = sb("nfgb", [4, 1])
        igs = sb("igs", [4, TT])
        Fn = sb("Fn", [4, TT])
        aa = sb("aa", [4, TT])
        MM = sb("MM", [4, TT])
        g4 = [sb("g4_%d" % i, [4, 128]) for i in range(2)]
        negM = sb("negM", [4, 1])
        gcol = sb("gcol", [4, 1])
        Mpc = sb("Mpc", [4, 1])
        dg = sb("dg", [4, 4])
        et = sb("et", [128, 4])
        gb = sb("gb", [128, 4])
        ngc2 = sb("ngc2", [128, 2, 2])
        mwab = sb("mwab", [128, 128])
        ktp = sb("ktp", [128, 256], BF16)
        vtp = sb("vtp", [128, 256], BF16)
        vtb = sb("vtb", [128, 256], BF16)
        Sm = sb("Sm", [128, 128], BF16)
        Eb = sb("Eb", [128, 128], BF16)
        Cg = sb("Cg", [128, 2, 256], BF16)
        nrep = sb("nrep", [128, 2, 128], BF16)
        hh = sb("hh", [128, 2, 128])
        hq = sb("hq", [128, 2, 128])
        mw = [sb("mw%d" % i, [128, 128]) for i in range(4)]
        stg = sb("stg", [128, 256])

        SRE = ["sre%d" % c for c in range(KC)]
        SIM = ["sim%d" % c for c in range(KC)]
        dq = ["sp", "pool"]
        dqi = [0]

        def dma(out, in_, reads=(), writes=(), q=None, slow=False):
            if out.dtype != in_.dtype:
                q = "pool"
            if q is None:
                q = dq[dqi[0] % 2]
                dqi[0] += 1
            if slow:
                f = CALL("dma_start", out=out, in_=in_, allow_slow_non_contiguous=True)
            else:
                f = CALL("dma_start", out=out, in_=in_)
            return P.op(q, f, reads=reads, writes=writes, dma=True)

        def mm(out, lhsT, rhs, start, stop, reads, writes, tp=None):
            if tp is None:
                f = CALL("matmul", out, lhsT, rhs, start=start, stop=stop)
            else:
                f = CALL("matmul", out, lhsT, rhs, start=start, stop=stop, tile_position=tp)
            return P.op("pe", f, reads=reads, writes=writes)

        def tr(out, in_, idn, reads, writes):
            return P.op("pe", CALL("transpose", out, in_, idn), reads=list(reads) + ["ident"], writes=writes)

        def E(eng, fn, reads, writes):
            return P.op(eng, fn, reads=reads, writes=writes)

        def V(name, c):
            i = VI[name] * 8 + c
            return vec[:, i:i + 1]

        for name, t in [("ident", ident), ("maskE", maskE), ("mask16", mask16), ("triu", triu), ("bdmask", bdmask),
                        ("tvec", tvec), ("sel4", sel4), ("igb", igb), ("mask4", mask4)]:
            dma(t[:], W[name], writes=[name])
        E("dve", CALL("memset", ones_bf[:], 1.0), [], ["ones_bf"])
        E("dve", CALL("memset", onesD[:], 1.0 / D), [], ["onesD"])
        E("dve", CALL("memset", ones256[:], 1.0 / 256), [], ["ones256"])
        E("dve", CALL("memset", ones4[:], 1.0), [], ["ones4"])
        E("dve", CALL("memset", onecol[:], 1.0), [], ["onecol"])
        E("dve", CALL("memset", resetm[:], 1.0), [], ["resetm"])
        E("dve", CALL("memset", resetm[:, 0:1], 0.0), ["resetm"], ["resetm"])
        E("dve", CALL("memset", epscol[:], EPS), [], ["epscol"])
        dma(nfgb[:], W["fgb"], writes=["nfgb"])
        E("dve", CALL("tensor_scalar", nfgb[:], nfgb[:], -1.0, 0.0, ALU.mult, ALU.add), ["nfgb"], ["nfgb"])
        dma(stg[0:len(VEC_NAMES) * 8, 0:128], vecs, writes=["stg"])
        nv = len(VEC_NAMES) * 8
        tr(pb[7][:, 0:nv], stg[0:nv, 0:128], ident[0:nv, 0:nv], ["stg"], ["p7"])
        E("dve", CALL("tensor_copy", vec[:], pb[7][:, 0:nv]), ["p7"], ["vec"])
        dma(gwi[:], W["igw"].rearrange("(c p) h -> p c h", p=128), writes=["gwi"], slow=True)
        dma(gwf[:], W["fgw"].rearrange("(c p) h -> p c h", p=128), writes=["gwf"], slow=True)
        for wi, wn in enumerate(["wq", "wk", "wv"]):
            dma(stg[:, 0:32].rearrange("p (c o) -> p c o", o=4), W[wn].rearrange("(c b) i o -> (b i) c o", b=32),
                reads=[], writes=["stg"], slow=True)
            for c in range(KC):
                E("dve", CALL("tensor_tensor",
                    BD[:, wi, c, :].rearrange("p (b o) -> p b o", o=4),
                    bc_mid(stg[:, c * 4:c * 4 + 4], 32), bc_last(bdmask[:, :], 4), ALU.mult),
                  ["stg", "bdmask"], ["BD"])
        lamr = s5w[0][:, 0, 0:32]
        lami = s5w[0][:, 1, 0:32]
        dtb = s5w[0][:, 2, 0:32]
        th = s5w[1][:, 0, 0:32]
        cth = s5w[1][:, 1, 0:32]
        sth = s5w[1][:, 2, 0:32]
        lbr = s5w[2][:, 0, 0:32]
        lbi = s5w[2][:, 1, 0:32]
        kr = s5w[2][:, 2, 0:32]
        ki_ = s5w[2][:, 3, 0:32]
        t1 = s5w[3][:, 0, 0:32]
        t2 = s5w[3][:, 1, 0:32]
        t3 = s5w[3][:, 2, 0:32]
        for nm, dst in [("lam_re", lamr), ("lam_im", lami)]:
            dma(stg[0:32, 0:128], W[nm], writes=["stg"])
            tr(pb[7][:, 0:32], stg[0:32, 0:128], ident[0:32, 0:32], ["stg"], ["p7"])
            E("dve", CALL("tensor_copy", dst, pb[7][:, 0:32]), ["p7"], ["s5p"])
        ldt = W["log_dt"]
        for g2 in range(2):
            src = bass.AP(ldt.tensor, ldt.offset + g2, [[0, 64], [2, 32]])
            dma(s5w[0][64 * g2:64 * g2 + 64, 2, 0:32], src, writes=["s5p"], slow=True)
        E("act", CALL("activation", dtb, dtb, AF.Exp), ["s5p"], ["s5p"])
        E("dve", CALL("tensor_tensor", t1, lamr, dtb, ALU.mult), ["s5p"], ["s5p"])
        E("act", CALL("activation", rmag[:], t1, AF.Exp), ["s5p"], ["rmag"])
        E("dve", CALL("tensor_tensor", th, lami, dtb, ALU.mult), ["s5p"], ["s5p"])

        ki32 = sb("ki32", [128, 1, 64], I32)

        def sincos(dst, ang, n, shift, key_r, key_w):
            wk = s5w[6][:].rearrange("p a b -> p (a b)")[:, 0:n]
            wf = s5w[7][:].rearrange("p a b -> p (a b)")[:, 0:n]
            wi_ = ki32[:].rearrange("p a b -> p (a b)")[:, 0:n]
            E("dve", CALL("tensor_scalar", wk, ang, shift, 1.0 / (2 * math.pi), ALU.add, ALU.mult), key_r, ["s5t"])
            E("dve", CALL("tensor_copy", wi_, wk), ["s5t"], ["s5t"])
            E("dve", CALL("tensor_copy", wf, wi_), ["s5t"], ["s5t"])
            E("dve", CALL("tensor_scalar", wk, ang, shift, 0.0, ALU.add, ALU.add), key_r + ["s5t"], ["s5t"])
            E("dve", CALL("scalar_tensor_tensor", wk, wf, -2 * math.pi, wk, ALU.mult, ALU.add), ["s5t"], ["s5t"])
            E("dve", CALL("tensor_scalar", wf, wk, math.pi, -2 * math.pi, ALU.is_gt, ALU.mult), ["s5t"], ["s5t"])
            E("dve", CALL("tensor_tensor", wk, wk, wf, ALU.add), ["s5t"], ["s5t"])
            E("dve", CALL("tensor_scalar", wf, wk, -math.pi, 2 * math.pi, ALU.is_lt, ALU.mult), ["s5t"], ["s5t"])
            E("dve", CALL("tensor_tensor", wk, wk, wf, ALU.add), ["s5t"], ["s5t"])
            E("act", CALL("activation", dst, wk, AF.Sin), ["s5t"], key_w)

        sincos(sth, th, 32, 0.0, ["s5p"], ["s5p"])
        sincos(cth, th, 32, math.pi / 2, ["s5p"], ["s5p"])
        E("dve", CALL("tensor_tensor", lbr, rmag[:], cth, ALU.mult), ["s5p", "rmag"], ["s5p"])
        E("dve", CALL("tensor_tensor", lbi, rmag[:], sth, ALU.mult), ["s5p", "rmag"], ["s5p"])
        E("dve", CALL("tensor_scalar", t1, lbr, -1.0, 0.0, ALU.add, ALU.add), ["s5p"], ["s5p"])
        E("dve", CALL("tensor_tensor", t2, lamr, lamr, ALU.mult), ["s5p"], ["s5p"])
        E("dve", CALL("tensor_tensor", t3, lami, lami, ALU.mult), ["s5p"], ["s5p"])
        E("dve", CALL("tensor_tensor", t2, t2, t3, ALU.add), ["s5p"], ["s5p"])
        E("dve", CALL("reciprocal", t2, t2), ["s5p"], ["s5p"])
        E("dve", CALL("tensor_tensor", kr, t1, lamr, ALU.mult), ["s5p"], ["s5p"])
        E("dve", CALL("tensor_tensor", t3, lbi, lami, ALU.mult), ["s5p"], ["s5p"])
        E("dve", CALL("tensor_tensor", kr, kr, t3, ALU.add), ["s5p"], ["s5p"])
        E("dve", CALL("tensor_tensor", kr, kr, t2, ALU.mult), ["s5p"], ["s5p"])
        E("dve", CALL("tensor_tensor", ki_, lbi, lamr, ALU.mult), ["s5p"], ["s5p"])
        E("dve", CALL("tensor_tensor", t3, t1, lami, ALU.mult), ["s5p"], ["s5p"])
        E("dve", CALL("tensor_tensor", ki_, ki_, t3, ALU.subtract), ["s5p"], ["s5p"])
        E("dve", CALL("tensor_tensor", ki_, ki_, t2, ALU.mult), ["s5p"], ["s5p"])
        for q in range(32):
            ang = s5w[5][:, 0, :]
            E("dve", CALL("tensor_scalar", ang, tvec[:, :], th[:, q:q + 1], 0.0, ALU.mult, ALU.add),
              ["s5p", "tvec"], ["s5ang"])
            sincos(sinT[:, q, :], ang, 64, 0.0, ["s5ang"], ["sinT"])
            sincos(cosT[:, q, :], ang, 64, math.pi / 2, ["s5ang"], ["cosT"])
        Bre = CT[:].rearrange("p a b c -> p (a b c)")[:, 0:512].rearrange("p (q j) -> p q j", j=16)
        Bim = CT[:].rearrange("p a b c -> p (a b c)")[:, 512:1024].rearrange("p (q j) -> p q j", j=16)
        Ere = CT[:].rearrange("p a b c -> p (a b c)")[:, 1024:1536].rearrange("p (q j) -> p q j", j=16)
        Eim = CT[:].rearrange("p a b c -> p (a b c)")[:, 1536:2048].rearrange("p (q j) -> p q j", j=16)
        for g2 in range(2):
            dma(Bre[64 * g2:64 * g2 + 64], W["b_re"].rearrange("(q g) p j -> g p q j", g=2)[g2], writes=["CT0", "CT1", "CT2", "CT3"], slow=True)
            dma(Bim[64 * g2:64 * g2 + 64], W["b_im"].rearrange("(q g) p j -> g p q j", g=2)[g2], writes=["CT0", "CT1", "CT2", "CT3"], slow=True)
        kmr = s5w[4][:, 0, 0:32]
        kmi = s5w[4][:, 1, 0:32]
        Eexp_re = ybuf[:].rearrange("p a b -> p (a b)")[:, 0:1024].rearrange("p (q g j) -> p q g j", g=2, j=16)
        Eexp_im = ybuf[:].rearrange("p a b -> p (a b)")[:, 1024:2048].rearrange("p (q g j) -> p q g j", g=2, j=16)
        for g2 in range(2):
            E("dve", CALL("tensor_scalar", kmr, kr, maskE[:, g2:g2 + 1], 0.0, ALU.mult, ALU.add), ["s5p", "maskE"], ["s5k"])
            E("dve", CALL("tensor_scalar", kmi, ki_, maskE[:, g2:g2 + 1], 0.0, ALU.mult, ALU.add), ["s5p", "maskE"], ["s5k"])
            E("dve", CALL("tensor_tensor", Ere, Bre, bc_last(kmr, 16), ALU.mult), ["CT0", "CT1", "CT2", "CT3"] + ["s5k"], ["CT0", "CT1", "CT2", "CT3"])
            E("dve", CALL("tensor_tensor", Eim, Bim, bc_last(kmi, 16), ALU.mult), ["CT0", "CT1", "CT2", "CT3"] + ["s5k"], ["CT0", "CT1", "CT2", "CT3"])
            E("dve", CALL("tensor_tensor", Eexp_re[:, :, g2, :], Ere, Eim, ALU.subtract), ["CT0", "CT1", "CT2", "CT3"], ["ybuf"])
            E("dve", CALL("tensor_tensor", Ere, Bim, bc_last(kmr, 16), ALU.mult), ["CT0", "CT1", "CT2", "CT3"] + ["s5k"], ["CT0", "CT1", "CT2", "CT3"])
            E("dve", CALL("tensor_tensor", Eim, Bre, bc_last(kmi, 16), ALU.mult), ["CT0", "CT1", "CT2", "CT3"] + ["s5k"], ["CT0", "CT1", "CT2", "CT3"])
            E("dve", CALL("tensor_tensor", Eexp_im[:, :, g2, :], Ere, Eim, ALU.add), ["CT0", "CT1", "CT2", "CT3"], ["ybuf"])
        for c in range(KC):
            for ri, Ex in enumerate([Eexp_re, Eexp_im]):
                src = Ex[:, 4 * c:4 * c + 4, :, :].rearrange("p q g j -> p (q g j)")
                tr(pb[7][:, 0:128], src, ident[:], ["ybuf"], ["p7"])
                E("act", CALL("copy", W1[:, c, ri, :], pb[7][:, 0:128]), ["p7"], ["W1"])
        Cst = CT[:].rearrange("p a b c -> p (a b c)")[0:32, 0:2048].rearrange("p (q k) -> p q k", k=64)
        Cx = xT[:].rearrange("p a b -> p (a b)")[0:32, 0:4096].rearrange("p (q g k) -> p q g k", g=2, k=64)
        for ri, (cn, sgn) in enumerate([("c_re", 1.0), ("c_im", -1.0)]):
            for g2 in range(2):
                dma(Cst[16 * g2:16 * g2 + 16], W[cn].rearrange("(q g) h p -> g h q p", g=2)[g2], writes=["CT0", "CT1", "CT2", "CT3"], slow=True)
            for g2 in range(2):
                E("dve", CALL("tensor_scalar", Cx[:, :, g2, :], Cst, mask16[:, g2:g2 + 1], sgn, ALU.mult, ALU.mult),
                  ["CT0", "CT1", "CT2", "CT3"] + ["mask16"], ["xT"])
            for q in range(32):
                tr(pb[6][:, (q % 16) * 32:(q % 16) * 32 + 32], Cx[:, q, :, :].rearrange("p g k -> p (g k)"), ident[0:32, 0:32], ["xT"], ["p6"])
                if q % 16 == 15:
                    q0 = q - 15
                    E("act", CALL("copy", W3[:, q0:q0 + 16, ri, :], pb[6][:].rearrange("p (q h) -> p q h", h=32)), ["p6"], ["W3"])

        wcnt = {"g": 0, "u": 0, "d": 0, "i": 0, "o": 0}

        def rmsnorm_stats(src_chunks, nch, Tt, key_r, scale_mat):
            for c in range(nch):
                E("act", CALL("activation", hT[:, 14 + c, 0:Tt], src_chunks(c), AF.Square), key_r, ["h%d" % (14 + c)])
            for c in range(nch):
                mm(pb[6][:, 0:Tt], scale_mat[:], hT[:, 14 + c, 0:Tt], c == 0, c == nch - 1, ["onesD", "h%d" % (14 + c)], ["p6"])
            E("act", CALL("activation", rstd[:, 0:Tt], pb[6][:, 0:Tt], AF.Sqrt, bias=epscol[:, 0:1]), ["p6", "epscol"], ["rstd"])
            E("dve", CALL("reciprocal", rstd[:, 0:Tt], rstd[:, 0:Tt]), ["rstd"], ["rstd"])

        def norm_x(gname, Tt):
            rmsnorm_stats(lambda c: xT[:, c, 0:Tt], KC, Tt, ["xT"], onesD)
            for c in range(KC):
                E("dve", CALL("scalar_tensor_tensor", xn[:, c, 0:Tt], xT[:, c, 0:Tt], V(gname, c), rstd[:, 0:Tt], ALU.mult, ALU.mult),
                  ["xT", "vec", "rstd"], ["xn"])

        def ffn(pref, gname, Tt):
            norm_x(gname, Tt)
            wg, wu, wd = W[pref + "_gate"], W[pref + "_up"], W[pref + "_down"]
            for f in range(FC):
                gi = wcnt["g"] % 2
                wcnt["g"] += 1
                dma(wgr[gi][:], wg[:, f * 128:(f + 1) * 128].rearrange("(c p) f -> p c f", p=128), writes=["wg%d" % gi])
                dma(wur[gi][:], wu[:, f * 128:(f + 1) * 128].rearrange("(c p) f -> p c f", p=128), writes=["wu%d" % gi])
                pg, pu = pb[f % 2], pb[2 + f % 2]
                for c in range(KC):
                    mm(pg[:, 0:Tt], wgr[gi][:, c, :], xn[:, c, 0:Tt], c == 0, c == KC - 1, ["wg%d" % gi, "xn"], ["p%d" % (f % 2)])
                for c in range(KC):
                    mm(pu[:, 0:Tt], wur[gi][:, c, :], xn[:, c, 0:Tt], c == 0, c == KC - 1, ["wu%d" % gi, "xn"], ["p%d" % (2 + f % 2)])
                tt = tA if f % 2 == 0 else tB
                tn = "tA" if f % 2 == 0 else "tB"
                E("act", CALL("activation", tt[:, 0:Tt], pg[:, 0:Tt], AF.Silu), ["p%d" % (f % 2)], [tn])
                E("dve", CALL("tensor_tensor", hT[:, f, 0:Tt], tt[:, 0:Tt], pu[:, 0:Tt], ALU.mult),
                  [tn, "p%d" % (2 + f % 2)], ["h%d" % f])
            for c in range(KC):
                pd = pb[4 + c % 2]
                for hf in range(2):
                    di = hf
                    dma(wdr[di][:], wd[hf * 1408:(hf + 1) * 1408, c * 128:(c + 1) * 128].rearrange("(f p) d -> p f d", p=128), writes=["wd%d" % di])
                    for f2 in range(FC // 2):
                        f = hf * (FC // 2) + f2
                        mm(pd[:, 0:Tt], wdr[di][:, f2, :], hT[:, f, 0:Tt], f == 0, f == FC - 1, ["wd%d" % di, "h%d" % f], ["p%d" % (4 + c % 2)])
                E("dve", CALL("scalar_tensor_tensor", xT[:, c, 0:Tt], pd[:, 0:Tt], 0.5, xT[:, c, 0:Tt], ALU.mult, ALU.add),
                  ["p%d" % (4 + c % 2), "xT"], ["xT"])

        def in_proj(Tt):
            norm_x("norm_mix", Tt)
            for oc in range(24):
                ii = wcnt["i"] % 2
                wcnt["i"] += 1
                dma(wir[ii][:], W["w_in"][:, oc * 128:(oc + 1) * 128].rearrange("(c p) f -> p c f", p=128), writes=["wi%d" % ii])
                po = pb[4 + oc % 2]
                for c in range(KC):
                    mm(po[:, 0:Tt], wir[ii][:, c, :], xn[:, c, 0:Tt], c == 0, c == KC - 1, ["wi%d" % ii, "xn"], ["p%d" % (4 + oc % 2)])
                if oc < 8:
                    dst, key = u_bf[:, oc, 0:Tt], "u_bf"
                elif oc < 16:
                    dst, key = xmh[:, oc - 8, 3:3 + Tt], "xmh"
                else:
                    dst, key = z_bf[:, oc - 16, 0:Tt], "z_bf"
                if oc % 2 == 0:
                    E("act", CALL("copy", dst, po[:, 0:Tt]), ["p%d" % (4 + oc % 2)], [key])
                else:
                    E("dve", CALL("tensor_copy", dst, po[:, 0:Tt]), ["p%d" % (4 + oc % 2)], [key])

        def s5_mix(Tt):
            gT = hT
            L = min(64, Tt)
            nun = Tt // L
            def s5_pre(c, un, S):
                t0 = un * L
                X = S["extra"]
                pS, psk = S["pS"][un % 2], S["psk"][un % 2]
                umS = S["um"][:, un % 2]
                kum = "s5%sum%d" % (S["k"], un % 2)
                pSv = pS[:].rearrange("p (q r t) -> p q r t", q=4, r=2)
                for qq in range(4):
                    E("act", CALL("activation", umS[:, qq, 0:L], u_bf[:, c, t0:t0 + L], AF.Copy, scale=mask4[:, qq:qq + 1]), ["u_bf", "mask4"] + X, [kum])
                for qq in range(4):
                    for ri in range(2):
                        mm(pSv[:, qq, ri, 0:L], W1[:, c, ri, :], umS[:, qq, 0:L], True, True, ["W1", kum] + X, [psk])

            def s5_unit(c, un, S):
                pY = pb[2 + c % 2]
                pyk = "p%d" % (2 + c % 2)
                t0 = un * L
                X = S["extra"]
                K = lambda i: "s5%s%d" % (S["k"], i)
                pS, psk = S["pS"][un % 2], S["psk"][un % 2]
                xbf = S["xbf"]
                kxb = K(11)
                injC, kinjC = S["inj"][:, un % 2], "s5%sinj%d" % (S["k"], un % 2)
                injN, kinjN = S["inj"][:, (un + 1) % 2], "s5%sinj%d" % (S["k"], (un + 1) % 2)
                sk = "sre%d" % c
                pSv = pS[:].rearrange("p (q r t) -> p q r t", q=4, r=2)
                if un == 0:
                    s5_pre(c, 0, S)
                    E("pool", CALL("tensor_tensor", injC, s5st[:, :, 4 * c:4 * c + 4], bc_mid(rmag[:, 4 * c:4 * c + 4], 2), ALU.mult), ["rmag", sk] + X, [kinjC])
                    yield
                if un + 1 < nun and L == 64:
                    s5_pre(c, un + 1, S)
                    yield
                bre = pSv[:, :, 0, 0:L]
                bim = pSv[:, :, 1, 0:L]
                cs = cosT[:, 4 * c:4 * c + 4, 0:L]
                sn = sinT[:, 4 * c:4 * c + 4, 0:L]
                B = S["bufs"]
                bv = lambda i: B[:, i * 256:(i + 1) * 256].rearrange("p (q t) -> p q t", t=64)[:, :, 0:L]
                E("dve", CALL("tensor_tensor", bv(2), bre, cs, ALU.mult), [psk, "cosT"] + X, [K(2)])
                E("dve", CALL("tensor_tensor", bv(3), bim, sn, ALU.mult), [psk, "sinT"] + X, [K(3)])
                E("dve", CALL("tensor_tensor", bv(6), bim, cs, ALU.mult), [psk, "cosT"] + X, [K(6)])
                E("dve", CALL("tensor_tensor", bv(7), bre, sn, ALU.mult), [psk, "sinT"] + X, [K(7)])
                E("dve", CALL("tensor_tensor", bv(0), bv(2), bv(3), ALU.add), [K(2), K(3)] + X, [K(0)])
                E("dve", CALL("tensor_tensor", bv(1), bv(6), bv(7), ALU.subtract), [K(6), K(7)] + X, [K(1)])
                if L == 64:
                    W2 = B[:, 0:512].rearrange("p (r q t) -> p r q t", r=2, t=64)
                    E("dve", CALL("tensor_tensor", W2[:, :, :, 0], W2[:, :, :, 0], injC, ALU.add), [K(0), K(1), kinjC] + X, [K(0), K(1)])
                    rt = S["rtab"][:, 4 * c:4 * c + 4, :].rearrange("p q t -> p (q t)")
                    E("dve", CALL("tensor_tensor_scan", B[:, 1024:1280], rt, B[:, 0:256], 0.0, ALU.mult, ALU.add), [K(0), "rtab"] + X, [K(4)])
                    E("dve", CALL("tensor_tensor_scan", B[:, 1280:1536], rt, B[:, 256:512], 0.0, ALU.mult, ALU.add), [K(1), "rtab"] + X, [K(5)])
                    yield
                else:
                    for qq in range(4):
                        q = 4 * c + qq
                        E("dve", CALL("tensor_tensor_scan", bv(4)[:, qq, :], bc_row(rmag[:, q:q + 1], L), bv(0)[:, qq, :],
                                      sre[:, q:q + 1], ALU.mult, ALU.add), ["rmag", K(0), sk] + X, [K(4)])
                        E("dve", CALL("tensor_tensor_scan", bv(5)[:, qq, :], bc_row(rmag[:, q:q + 1], L), bv(1)[:, qq, :],
                                      sim[:, q:q + 1], ALU.mult, ALU.add), ["rmag", K(1), sk] + X, [K(5)])
                    yield
                zr, zi = bv(4), bv(5)
                E("pool", CALL("tensor_tensor", bv(2), zr, cs, ALU.mult), [K(4), "cosT"] + X, [K(2)])
                E("pool", CALL("tensor_tensor", bv(3), zi, sn, ALU.mult), [K(5), "sinT"] + X, [K(3)])
                E("pool", CALL("tensor_tensor", bv(6), bv(2), bv(3), ALU.subtract), [K(2), K(3)] + X, [K(6)])
                E("pool", CALL("tensor_tensor", bv(2), zi, cs, ALU.mult), [K(5), "cosT"] + X, [K(2)])
                E("pool", CALL("tensor_tensor", bv(3), zr, sn, ALU.mult), [K(4), "sinT"] + X, [K(3)])
                E("pool", CALL("tensor_tensor", bv(7), bv(2), bv(3), ALU.add), [K(2), K(3)] + X, [K(7)])
                X4 = B[:, 1536:2048].rearrange("p (r q t) -> p r q t", r=2, t=64)
                if L == 64 and un + 1 < nun:
                    E("pool", CALL("tensor_tensor", injN, X4[:, :, :, L - 1], bc_mid(rmag[:, 4 * c:4 * c + 4], 2), ALU.mult), ["rmag", K(6), K(7)] + X, [kinjN])
                yield
                E("act", CALL("copy", xbf[:, :, :, 0:L], X4[:, :, :, 0:L]), [K(6), K(7)] + X, [kxb])
                if L != 64 or un + 1 == nun:
                    E("act", CALL("copy", s5st[:, :, 4 * c:4 * c + 4], X4[:, :, :, L - 1]), [K(6), K(7)] + X, [sk])
                yield
                for qq in range(4):
                    q = 4 * c + qq
                    mm(pY[32 * qq:32 * qq + 32, t0:t0 + L], W3[:, q, 0, :], xbf[:, 0, qq, 0:L], True, False, ["W3", kxb] + X, [pyk], tp=(0, 32 * qq))
                    mm(pY[32 * qq:32 * qq + 32, t0:t0 + L], W3[:, q, 1, :], xbf[:, 1, qq, 0:L], False, True, ["W3", kxb] + X, [pyk], tp=(0, 32 * qq))
                yield

            def s5_post(c):
                pY = pb[2 + c % 2]
                pyk = "p%d" % (2 + c % 2)
                E("dve", CALL("scalar_tensor_tensor", tA[:, 0:Tt], u_bf[:, c, 0:Tt], V("s5_d", c), pY[:, 0:Tt], ALU.mult, ALU.add),
                  ["u_bf", "vec", pyk], ["tA"])
                E("act", CALL("activation", tB[:, 0:Tt], tA[:, 0:Tt], AF.Square), ["tA"], ["tB"])
                E("dve", CALL("tensor_scalar", tB[:, 0:Tt], tB[:, 0:Tt], 0.044715, 1.0, ALU.mult, ALU.add), ["tB"], ["tB"])
                E("pool", CALL("tensor_tensor", tB[:, 0:Tt], tB[:, 0:Tt], tA[:, 0:Tt], ALU.mult), ["tA", "tB"], ["tB"])
                E("act", CALL("activation", tB[:, 0:Tt], tB[:, 0:Tt], AF.Sigmoid, scale=1.5957691216057308), ["tB"], ["tB"])
                E("dve", CALL("tensor_tensor", gT[:, c, 0:Tt], tA[:, 0:Tt], tB[:, 0:Tt], ALU.mult), ["tA", "tB"], ["h%d" % c])

            ybf_ = ybuf[:].rearrange("p a b -> p (a b)")
            rtab = ybf_[:, 2048:4096].rearrange("p (q t) -> p q t", t=64)
            E("dve", CALL("tensor_tensor", rtab, bc_last(rmag[:, :], 64), bc_mid(resetm[:, :], 32), ALU.mult), ["rmag", "resetm", "ybuf"], ["rtab"])
            SA = dict(bufs=s5all[:].rearrange("p a b -> p (a b)"), um=um, xbf=xbfA, inj=inj[0], k="A", extra=[], pS=[pb[0], pb[1]], psk=["p0", "p1"], rtab=rtab)
            SB = dict(bufs=ybf_[:, 0:2048], um=umB, xbf=xbfB, inj=inj[1], k="B", extra=["ybuf"], pS=[pb[4], pb[5]], psk=["p4", "p5"], rtab=rtab)
            for cp in range(KC // 2):
                c0, c1 = 2 * cp, 2 * cp + 1
                for un in range(nun):
                    gens = [s5_unit(c0, un, SA), s5_unit(c1, un, SB)]
                    while gens:
                        for g_ in list(gens):
                            try:
                                next(g_)
                            except StopIteration:
                                gens.remove(g_)
                s5_post(c0)
                s5_post(c1)
            for oc in range(KC):
                ii = wcnt["i"] % 2
                wcnt["i"] += 1
                dma(wir[ii][:], W["s5_glu_w"][:, oc * 128:(oc + 1) * 128].rearrange("(c p) f -> p c f", p=128), writes=["wi%d" % ii])
                po = pb[4 + oc % 2]
                pk = "p%d" % (4 + oc % 2)
                for c in range(KC):
                    mm(po[:, 0:Tt], wir[ii][:, c, :], gT[:, c, 0:Tt], c == 0, c == KC - 1, ["wi%d" % ii, "h%d" % c], [pk])
                E("act", CALL("activation", tA[:, 0:Tt], po[:, 0:Tt], AF.Sigmoid, bias=V("s5_glu_b", oc)), [pk, "vec"], ["tA"])
                E("dve", CALL("tensor_tensor", ybuf[:, oc, 0:Tt], gT[:, oc, 0:Tt], tA[:, 0:Tt], ALU.mult), ["tA", "h%d" % oc], ["ybuf"])
            rmsnorm_stats(lambda c: ybuf[:, c, 0:Tt], KC, Tt, ["ybuf"], onesD)
            for c in range(KC):
                E("dve", CALL("scalar_tensor_tensor", mixed[:, c, 0:Tt], ybuf[:, c, 0:Tt], V("out_norm_s5", c), rstd[:, 0:Tt], ALU.mult, ALU.mult),
                  ["ybuf", "vec", "rstd"], ["mixed"])

        def mlstm_mix(Tt):
            qT = hT
            for c in range(KC):
                eng = "dve"
                E(eng, CALL("tensor_scalar", tA[:, 0:Tt], xmh[:, c, 0:Tt], V("cw0", c), 0.0, ALU.mult, ALU.add), ["xmh", "vec"], ["tA"])
                for j in range(1, 4):
                    E(eng, CALL("scalar_tensor_tensor", tA[:, 0:Tt], xmh[:, c, j:j + Tt], V("cw%d" % j, c), tA[:, 0:Tt], ALU.mult, ALU.add),
                      ["xmh", "vec", "tA"], ["tA"])
                E("act", CALL("activation", xc_bf[:, c, 0:Tt], tA[:, 0:Tt], AF.Silu, bias=V("ml_conv_b", c)), ["tA", "vec"], ["xn"])
            for c in range(KC):
                E("act", CALL("activation", z_bf[:, c, 0:Tt], z_bf[:, c, 0:Tt], AF.Silu), ["z_bf"], ["z_bf"])
            for c in range(KC):
                for wi, (src, dstT, key) in enumerate([(xc_bf[:, c, 0:Tt], qT[:, c, 0:Tt], "h%d" % c),
                                                       (xc_bf[:, c, 0:Tt], qT[:, 8 + c, 0:Tt], "h%d" % (8 + c)),
                                                       (xmh[:, c, 3:3 + Tt], vT[:, c, 0:Tt], "h%d" % (16 + c))]):
                    po = pb[4 + (3 * c + wi) % 2]
                    pk = "p%d" % (4 + (3 * c + wi) % 2)
                    mm(po[:, 0:Tt], BD[:, wi, c, :], src, True, True, ["BD", "xn", "xmh"], [pk])
                    if wi == 1:
                        E("dve", CALL("tensor_copy", dstT, po[:, 0:Tt]), [pk], [key])
                    else:
                        E("act", CALL("copy", dstT, po[:, 0:Tt]), [pk], [key])
            for gi, (gw, pbk) in enumerate([(gwi, 4), (gwf, 5)]):
                for j in range(24):
                    src = qT[:, j, 0:Tt]
                    key = "h%d" % j
                    mm(pb[pbk][0:4, 0:Tt], gw[:, j, :], src, j == 0, j == 23, ["gwi", "gwf", key], ["p%d" % pbk])
            E("act", CALL("activation", igs[:, 0:Tt], pb[4][0:4, 0:Tt], AF.Identity, bias=igb[:, 0:1]), ["p4", "igb"], ["igs"])
            E("act", CALL("activation", MM[:, 0:Tt], pb[5][0:4, 0:Tt], AF.Exp, bias=nfgb[:, 0:1], scale=-1.0), ["p5", "nfgb"], ["MM"])
            E("act", CALL("activation", MM[:, 0:Tt], MM[:, 0:Tt], AF.Ln, bias=onecol[0:4, 0:1]), ["MM", "onecol"], ["MM"])
            E("dve", CALL("tensor_tensor_scan", Fn[:, 0:Tt], bc_row(onecol[0:4, 0:1], Tt), MM[:, 0:Tt], 0.0, ALU.mult, ALU.add), ["MM", "onecol"], ["Fn"])
            E("dve", CALL("tensor_tensor", aa[:, 0:Tt], igs[:, 0:Tt], Fn[:, 0:Tt], ALU.add), ["igs", "Fn"], ["aa"])
            E("dve", CALL("tensor_tensor_scan", MM[:, 0:Tt], bc_row(onecol[0:4, 0:1], Tt), aa[:, 0:Tt], mprev[:, 0:1], ALU.mult, ALU.max),
              ["aa", "onecol", "mprev"], ["MM"])
            E("dve", CALL("tensor_copy", Mpc[:], mprev[:]), ["mprev"], ["Mpc"])
            Lc = min(128, Tt)
            for ch in range(Tt // Lc):
                t0, t1 = ch * Lc, ch * Lc + Lc
                E("dve", CALL("tensor_scalar", negM[:], MM[:, t1 - 1:t1], -1.0, 0.0, ALU.mult, ALU.add), ["MM"], ["negM"])
                E("act", CALL("activation", g4[0][:, 0:Lc], aa[:, t0:t1], AF.Exp, bias=negM[:, 0:1]), ["aa", "negM"], ["g40"])
                E("act", CALL("activation", gcol[:], Mpc[:], AF.Exp, bias=negM[:, 0:1]), ["Mpc", "negM"], ["gcol"])
                E("dve", CALL("tensor_scalar", g4[1][:, 0:Lc], Fn[:, t0:t1], negM[:, 0:1], 0.0, ALU.add, ALU.add), ["Fn", "negM"], ["g41"])
                E("dve", CALL("tensor_copy", Mpc[:], MM[:, t1 - 1:t1]), ["MM", "gcol"], ["Mpc"])
                tr(pb[7][0:Lc, 128:132], g4[0][:, 0:Lc], ident[0:4, 0:4], ["g40"], ["p7"])
                E("dve", CALL("tensor_copy", et[0:Lc, :], pb[7][0:Lc, 128:132]), ["p7"], ["et"])
                E("dve", CALL("tensor_scalar", dg[:], ident[0:4, 0:4], gcol[:, 0:1], 0.0, ALU.mult, ALU.add), ["ident", "gcol"], ["dg"])
                mm(pb[7][:, 132:136], ones4[:], dg[:], True, True, ["ones4", "dg"], ["p7"])
                E("dve", CALL("tensor_copy", gb[:], pb[7][:, 132:136]), ["p7"], ["gb"])
                for c in range(KC):
                    mm(pb[c // 4][0:Lc, (c % 4) * 128:(c % 4) * 128 + 128], xc_bf[:, c, t0:t1], BD[:, 1, c, :], True, True, ["xn", "BD"], ["p%d" % (c // 4)])
                    mm(pb[2 + c // 4][0:Lc, (c % 4) * 128:(c % 4) * 128 + 128], xmh[:, c, 3 + t0:3 + t1], BD[:, 2, c, :], True, True, ["xmh", "BD"], ["p%d" % (2 + c // 4)])
                def SET(pr):
                    if pr == 0:
                        return dict(ktp=ktp, vtp=vtp, vtb=vtb, Sm=Sm, Eb=Eb, Cg=Cg, nrep=nrep, hh=hh, hq=hq, ngc=ngc2[:, 0, :], X=[], k="0")
                    vv = hT[:, 16:20, :]
                    return dict(ktp=vv[:, 0, 0:256], vtp=vv[:, 0, 256:512], vtb=vv[:, 1, 0:256], Sm=vv[:, 1, 256:384], Eb=vv[:, 1, 384:512],
                                Cg=vv[:, 2, :].rearrange("p (a b) -> p a b", a=2), nrep=vv[:, 3, 0:256].rearrange("p (a b) -> p a b", a=2),
                                hh=tAB[:, 0:256].rearrange("p (a b) -> p a b", a=2), hq=tAB[:, 256:512].rearrange("p (a b) -> p a b", a=2),
                                ngc=ngc2[:, 1, :], X=["h16", "h17", "h18", "h19", "tA"], k="1")

                def prep(h):
                    S_ = SET(h % 2)
                    X = S_["X"]
                    N = lambda n: "ml%s%s" % (n, S_["k"])
                    kps = pb[h // 2][0:Lc, (h % 2) * 256:(h % 2) * 256 + 256]
                    vps = pb[2 + h // 2][0:Lc, (h % 2) * 256:(h % 2) * 256 + 256]
                    kk, vk = "p%d" % (h // 2), "p%d" % (2 + h // 2)
                    for kc in range(2):
                        mm(pb[7][0:Lc, 0:Lc], qT[:, 8 + 2 * h + kc, t0:t1], qT[:, 2 * h + kc, t0:t1], kc == 0, kc == 1,
                           ["h%d" % (8 + 2 * h + kc), "h%d" % (2 * h + kc), "p7"], ["p7s"])
                    E("dve", CALL("scalar_tensor_tensor", S_["Sm"][0:Lc, 0:Lc], pb[7][0:Lc, 0:Lc], 1.0 / 16, triu[0:Lc, 0:Lc], ALU.mult, ALU.mult), ["p7s", "p7", "triu"] + X, [N("Sm")])
                    E("dve", CALL("tensor_scalar", S_["vtp"][0:Lc, :], vps, et[0:Lc, h:h + 1], 0.0, ALU.mult, ALU.add), [vk, "et"] + X, [N("vtp")])
                    E("act", CALL("copy", S_["vtb"][0:Lc, :], vps), [vk] + X, [N("vtb")])
                    E("dve", CALL("tensor_scalar", S_["ktp"][0:Lc, :], kps, et[0:Lc, h:h + 1], 1.0 / 16, ALU.mult, ALU.mult), [kk, "et"] + X, [N("ktp")])
                    E("pool", CALL("tensor_scalar", S_["Eb"][0:Lc, :], ones_bf[0:Lc, :], et[0:Lc, h:h + 1], 0.0, ALU.mult, ALU.add), ["ones_bf", "et"] + X, [N("Eb")])
                    E("pool", CALL("tensor_scalar", S_["Cg"].rearrange("p a b -> p (a b)"), CT[:, h, :, :].rearrange("p a b -> p (a b)"), gb[:, h:h + 1], 0.0, ALU.mult, ALU.add),
                      ["CT%d" % h, "gb"] + X, [N("Cg")])
                    E("dve", CALL("tensor_scalar", S_["ngc"], nT[:, 2 * h:2 * h + 2], gb[:, h:h + 1], 0.0, ALU.mult, ALU.add), ["nT%d" % h, "gb"], [N("ngc")])
                    for kc in range(2):
                        E("pool", CALL("tensor_scalar", S_["nrep"][:, kc, :], ones_bf[:, :], S_["ngc"][:, kc:kc + 1], 0.0, ALU.mult, ALU.add), ["ones_bf", N("ngc")] + X, [N("nrep")])

                def mid(h):
                    S_ = SET(h % 2)
                    X = S_["X"]
                    N = lambda n: "ml%s%s" % (n, S_["k"])
                    hh_, hq_ = S_["hh"], S_["hq"]
                    for vc in range(2):
                        o = pb[4][:, vc * 128:vc * 128 + Lc]
                        mm(o, S_["vtp"][0:Lc, vc * 128:vc * 128 + 128], S_["Sm"][0:Lc, 0:Lc], True, False, [N("vtp"), N("Sm")] + X, ["p4"])
                        for kc in range(2):
                            mm(o, S_["Cg"][:, kc, vc * 128:vc * 128 + 128], qT[:, 2 * h + kc, t0:t1], False, kc == 1, [N("Cg"), "h%d" % (2 * h + kc)] + X, ["p4"])
                    o = pb[4][:, 256:256 + Lc]
                    mm(o, S_["Eb"][0:Lc, :], S_["Sm"][0:Lc, 0:Lc], True, False, [N("Eb"), N("Sm")] + X, ["p4"])
                    for kc in range(2):
                        mm(o, S_["nrep"][:, kc, :], qT[:, 2 * h + kc, t0:t1], False, kc == 1, [N("nrep"), "h%d" % (2 * h + kc)] + X, ["p4"])
                    mm(pb[4][:, 384:384 + Lc], sel4[:, h * 128:h * 128 + 128], g4[1][:, 0:Lc], True, True, ["sel4", "g41"], ["p4"])
                    E("act", CALL("activation", mw[0][:, 0:Lc], pb[4][:, 384:384 + Lc], AF.Exp), ["p4"], ["mw0"])
                    E("act", CALL("activation", mwab[:, 0:Lc], pb[4][:, 256:256 + Lc], AF.Abs), ["p4"], ["mwab"])
                    E("dve", CALL("tensor_tensor", mw[0][:, 0:Lc], mwab[:, 0:Lc], mw[0][:, 0:Lc], ALU.max), ["mwab", "mw0"], ["mw0"])
                    E("dve", CALL("reciprocal", mw[0][:, 0:Lc], mw[0][:, 0:Lc]), ["mw0"], ["mw0"])
                    for vc in range(2):
                        E("dve", CALL("tensor_tensor", hh_[:, vc, 0:Lc], pb[4][:, vc * 128:vc * 128 + Lc], mw[0][:, 0:Lc], ALU.mult), ["p4", "mw0"] + X, [N("hh")])
                        E("act", CALL("activation", hq_[:, vc, 0:Lc], hh_[:, vc, 0:Lc], AF.Square), [N("hh")] + X, [N("hq")])
                    for kc in range(2):
                        mm(pb[6][:, kc * 256:kc * 256 + 256], S_["ktp"][0:Lc, kc * 128:kc * 128 + 128], S_["vtb"][0:Lc, :], True, True, [N("ktp"), N("vtb")] + X, ["p6"])
                    E("dve", CALL("scalar_tensor_tensor", CT[:, h, :, :].rearrange("p a b -> p (a b)"), CT[:, h, :, :].rearrange("p a b -> p (a b)"),
                                  gb[:, h:h + 1], pb[6][:, :], ALU.mult, ALU.add), ["CT%d" % h, "gb", "p6", N("Cg")], ["CT%d" % h])
                    for kc in range(2):
                        mm(pb[7][:, 136 + kc:137 + kc], S_["ktp"][0:Lc, kc * 128:kc * 128 + 128], ones_bf[0:Lc, 0:1], True, True, [N("ktp"), "ones_bf", "p7"] + X, ["p7n"])
                    E("dve", CALL("tensor_tensor", nT[:, 2 * h:2 * h + 2], S_["ngc"], pb[7][:, 136:138], ALU.add), [N("ngc"), "p7n", "p7"], ["nT%d" % h])

                def fin(h):
                    S_ = SET(h % 2)
                    X = S_["X"]
                    N = lambda n: "ml%s%s" % (n, S_["k"])
                    hh_, hq_ = S_["hh"], S_["hq"]
                    for vc in range(2):
                        mm(pb[5][:, 0:Lc], ones256[:], hh_[:, vc, 0:Lc], vc == 0, vc == 1, ["ones256", N("hh")] + X, ["p5"])
                    for vc in range(2):
                        mm(pb[5][:, 128:128 + Lc], ones256[:], hq_[:, vc, 0:Lc], vc == 0, vc == 1, ["ones256", N("hq")] + X, ["p5"])
                    E("act", CALL("activation", mw[1][:, 0:Lc], pb[5][:, 0:Lc], AF.Square), ["p5"], ["mw1"])
                    E("dve", CALL("tensor_tensor", mw[1][:, 0:Lc], pb[5][:, 128:128 + Lc], mw[1][:, 0:Lc], ALU.subtract), ["p5", "mw1"], ["mw1"])
                    E("dve", CALL("tensor_scalar", mw[1][:, 0:Lc], mw[1][:, 0:Lc], 0.0, 0.0, ALU.max, ALU.add), ["mw1"], ["mw1"])
                    E("act", CALL("activation", mw[1][:, 0:Lc], mw[1][:, 0:Lc], AF.Sqrt, bias=epscol[:, 0:1]), ["mw1", "epscol"], ["mw1"])
                    E("dve", CALL("reciprocal", mw[1][:, 0:Lc], mw[1][:, 0:Lc]), ["mw1"], ["mw1"])
                    for vc in range(2):
                        c = 2 * h + vc
                        E("dve", CALL("tensor_tensor", mw[2][:, 0:Lc], hh_[:, vc, 0:Lc], pb[5][:, 0:Lc], ALU.subtract), [N("hh"), "p5"] + X, ["mw2"])
                        E("pool", CALL("tensor_tensor", mw[2][:, 0:Lc], mw[2][:, 0:Lc], mw[1][:, 0:Lc], ALU.mult), ["mw2", "mw1"], ["mw2"])
                        E("pool", CALL("tensor_scalar", mw[2][:, 0:Lc], mw[2][:, 0:Lc], V("ml_norm_w", c), 0.0, ALU.mult, ALU.add), ["mw2", "vec"], ["mw2"])
                        E("dve", CALL("scalar_tensor_tensor", mw[3][:, 0:Lc], xc_bf[:, c, t0:t1], V("ml_skip", c), mw[2][:, 0:Lc], ALU.mult, ALU.add),
                          ["xn", "vec", "mw2"], ["mw3"])
                        E("pool", CALL("tensor_tensor", ybuf[:, c, t0:t1], mw[3][:, 0:Lc], z_bf[:, c, t0:t1], ALU.mult), ["mw3", "z_bf"], ["ybuf"])

                for blk in [(prep, 0), (prep, 1), (mid, 0), (prep, 2), (mid, 1), (fin, 0), (prep, 3), (mid, 2), (fin, 1), (mid, 3), (fin, 2), (fin, 3)]:
                    blk[0](blk[1])
            E("dve", CALL("tensor_tensor", mprev[:], MM[:, Tt - 1:Tt], Fn[:, Tt - 1:Tt], ALU.subtract), ["MM", "Fn", "Mpc"], ["mprev"])
            for c in range(KC):
                E("act", CALL("copy", xmh[:, c, 0:3], xmh[:, c, Tt:Tt + 3]), ["xmh"], ["xmh"])
            rmsnorm_stats(lambda c: ybuf[:, c, 0:Tt], KC, Tt, ["ybuf"], onesD)
            for c in range(KC):
                E("dve", CALL("scalar_tensor_tensor", mixed[:, 8 + c, 0:Tt], ybuf[:, c, 0:Tt], V("out_norm_ml", c), rstd[:, 0:Tt], ALU.mult, ALU.mult),
                  ["ybuf", "vec", "rstd"], ["mixed"])

        def out_proj(Tt):
            for c in range(KC):
                po = pb[4 + c % 2]
                pk = "p%d" % (4 + c % 2)
                for hf in range(2):
                    oi = hf
                    dma(wor[oi][:], W["w_out"][hf * 1024:(hf + 1) * 1024, c * 128:(c + 1) * 128].rearrange("(k p) d -> p k d", p=128), writes=["wo%d" % oi])
                    for k2 in range(8):
                        k = hf * 8 + k2
                        mm(po[:, 0:Tt], wor[oi][:, k2, :], mixed[:, k, 0:Tt], k == 0, k == 15, ["wo%d" % oi, "mixed"], [pk])
                E("dve", CALL("tensor_tensor", xT[:, c, 0:Tt], xT[:, c, 0:Tt], po[:, 0:Tt], ALU.add), [pk, "xT"], ["xT"])

        def load_tile(src_rows, Tt):
            nsub = (Tt + 127) // 128
            for n in range(nsub):
                r = min(128, Tt - n * 128)
                dma(xtok[0:r, n, :], src_rows[n * 128:n * 128 + r, :], writes=["ybuf"])
            for c in range(KC):
                for n in range(nsub):
                    r = min(128, Tt - n * 128)
                    tr(pb[7][:, n * 128:n * 128 + r], xtok[0:r, n, c * 128:(c + 1) * 128], ident[0:r, 0:r], ["ybuf"], ["p7"])
                E("dve" if c % 2 == 0 else "act",
                  (CALL("tensor_copy", xT[:, c, 0:Tt], pb[7][:, 0:Tt])) if c % 2 == 0 else (CALL("copy", xT[:, c, 0:Tt], pb[7][:, 0:Tt])),
                  ["p7"], ["xT"])

        def store_tile(dst_rows, Tt):
            norm_x("norm_final", Tt)
            nsub = (Tt + 127) // 128
            for c in range(KC):
                E("dve", CALL("scalar_tensor_tensor", ybuf[:, c, 0:Tt], xT[:, c, 0:Tt], V("norm_final", c), rstd[:, 0:Tt], ALU.mult, ALU.mult),
                  ["xT", "vec", "rstd"], ["ybuf"])
            for n in range(nsub):
                r = min(128, Tt - n * 128)
                for c4 in range(2):
                    for c in range(c4 * 4, c4 * 4 + 4):
                        tr(pb[7][0:r, (c % 4) * 128:(c % 4) * 128 + 128], ybuf[:, c, n * 128:n * 128 + r], ident[:], ["ybuf"], ["p7"])
                    E("act", CALL("copy", otok[0:r, c4 * 512:(c4 + 1) * 512], pb[7][0:r, :]), ["p7"], ["tA", "tB"])
                outs.append(dma(dst_rows[n * 128:n * 128 + r, :], otok[0:r, :], reads=["tA", "tB"], q="sp"))

        def init_state(si):
            if si is None:
                for t, k in [(sre, SRE), (sim, SRE), (nT, ["nT0", "nT1", "nT2", "nT3"]), (mprev, ["mprev"])]:
                    E("dve", CALL("memset", t[:], 0.0), [], k)
                E("pool", CALL("memset", CT[:].rearrange("p a b c -> p (a b c)"), 0.0), [], ["CT0", "CT1", "CT2", "CT3"])
                E("pool", CALL("memset", xmh[:, :, 0:3], 0.0), [], ["xmh"])
                return
            for src, dst, k in [(st_s5re, sre, SRE), (st_s5im, sim, SIM)]:
                dma(stg[0:32, 0:128], src[si], writes=["stg"])
                tr(pb[7][:, 0:32], stg[0:32, 0:128], ident[0:32, 0:32], ["stg"], ["p7"])
                E("dve", CALL("tensor_copy", dst[:], pb[7][:, 0:32]), ["p7"], k)
            dma(stg[0:8, 0:128], st_n[si], writes=["stg"])
            tr(pb[7][:, 0:8], stg[0:8, 0:128], ident[0:8, 0:8], ["stg"], ["p7"])
            E("dve", CALL("tensor_copy", nT[:], pb[7][:, 0:8]), ["p7"], ["nT0", "nT1", "nT2", "nT3"])
            dma(mprev[:], st_m[si], writes=["mprev"])
            for c in range(KC):
                dma(stg[0:3, 0:128], st_conv[si][:, c * 128:(c + 1) * 128], writes=["stg"])
                tr(pb[7][:, 0:3], stg[0:3, 0:128], ident[0:3, 0:3], ["stg"], ["p7"])
                E("dve", CALL("tensor_copy", xmh[:, c, 0:3], pb[7][:, 0:3]), ["p7"], ["xmh"])
            for h in range(4):
                for vc in range(2):
                    dma(stg[:, 0:256], st_c[si, h, vc * 128:(vc + 1) * 128, :], writes=["stg"])
                    for kc in range(2):
                        tr(pb[7][:, kc * 128:kc * 128 + 128], stg[:, kc * 128:kc * 128 + 128], ident[:], ["stg"], ["p7"])
                    E("dve", CALL("tensor_copy", CT[:, h, :, vc * 128:vc * 128 + 128], pb[7][:, 0:256].rearrange("p (k v) -> p k v", k=2)), ["p7"], ["CT0", "CT1", "CT2", "CT3"])

        def store_state(oi):
            for src, dst, k in [(sre, o_s5re, SRE), (sim, o_s5im, SIM)]:
                tr(pb[7][0:32, 0:128], src[:], ident[:], k, ["p7"])
                E("dve", CALL("tensor_copy", stg[0:32, 0:128], pb[7][0:32, 0:128]), ["p7"], ["stg"])
                outs.append(dma(dst[oi], stg[0:32, 0:128], reads=["stg"], q="sp"))
            tr(pb[7][0:8, 0:128], nT[:], ident[:], ["nT0", "nT1", "nT2", "nT3"], ["p7"])
            E("dve", CALL("tensor_copy", stg[0:8, 0:128], pb[7][0:8, 0:128]), ["p7"], ["stg"])
            outs.append(dma(o_n[oi], stg[0:8, 0:128], reads=["stg"], q="sp"))
            outs.append(dma(o_m[oi], mprev[:], reads=["mprev"], q="sp"))
            for c in range(KC):
                E("dve", CALL("tensor_copy", mw[0][:, 0:3], xmh[:, c, 0:3]), ["xmh"], ["mw0"])
                tr(pb[7][0:3, 0:128], mw[0][:, 0:3], ident[:], ["mw0"], ["p7"])
                E("dve", CALL("tensor_copy", stg[0:3, 0:128], pb[7][0:3, 0:128]), ["p7"], ["stg"])
                outs.append(dma(o_conv[oi][:, c * 128:(c + 1) * 128], stg[0:3, 0:128], reads=["stg"], q="sp"))
            for h in range(4):
                for vc in range(2):
                    for kc in range(2):
                        tr(pb[7][:, kc * 128:kc * 128 + 128], CT[:, h, kc, vc * 128:vc * 128 + 128], ident[:], ["CT0", "CT1", "CT2", "CT3"], ["p7"])
                    E("dve", CALL("tensor_copy", stg[:, 0:256], pb[7][:, 0:256]), ["p7"], ["stg"])
                    outs.append(dma(o_c[oi, h, vc * 128:(vc + 1) * 128, :], stg[:, 0:256], reads=["stg"], q="sp"))

        def run_tile(src_rows, dst_rows, Tt):
            load_tile(src_rows, Tt)
            if STAGE >= 2:
                ffn("ffn1", "norm_ffn1", Tt)
            if STAGE >= 3:
                in_proj(Tt)
            if STAGE >= 4:
                s5_mix(Tt)
            if STAGE >= 5:
                mlstm_mix(Tt)
            if STAGE >= 6:
                out_proj(Tt)
            if STAGE >= 7:
                ffn("ffn2", "norm_ffn2", Tt)
            if dst_rows is not None:
                store_tile(dst_rows, Tt)

        if STAGE == 0:
            P.emit(final_waits=outs)
            return nc
        init_state(None)
        run_tile(xp[0:NMETA, :], None, NMETA)
        for ti in range((NP - NMETA) // TT):
            r0 = NMETA + ti * TT
            run_tile(xp[r0:r0 + TT, :], yp[r0 - NMETA:r0 - NMETA + TT, :], TT)
        store_state(0)
        for si in range(NSAMP):
            init_state(si)
            run_tile(xs[si], ys[si], SL)
            store_state(1 + si)
        P.emit(final_waits=outs)
    return nc


def host_consts():
    p = np.arange(128)
    c = {}
    c["ident"] = np.eye(128, dtype=np.float32)
    c["maskE"] = np.stack([(p // 64 == 0), (p // 64 == 1)], 1).astype(np.float32)
    p32 = np.arange(32)
    c["mask16"] = np.stack([(p32 // 16 == 0), (p32 // 16 == 1)], 1).astype(np.float32)
    c["triu"] = np.triu(np.ones((128, 128), np.float32))
    c["bdmask"] = (p[:, None] // 4 == np.arange(32)[None, :]).astype(np.float32)
    c["tvec"] = np.broadcast_to(np.arange(1, 65, dtype=np.float32)[None, :], (128, 64)).copy()
    s = np.zeros((4, 4, 128), np.float32)
    for h in range(4):
        s[h, h, :] = 1.0
    c["sel4"] = s.reshape(4, 512)
    c["mask4"] = (p[:, None] // 32 == np.arange(4)[None, :]).astype(np.float32)
    return c


_CACHE = {}


def kernel(**inp):
    f = lambda a: np.ascontiguousarray(np.asarray(a, dtype=np.float32))
    x_prompt = f(inp["x_prompt"])
    x_sample = f(inp["x_sample"])
    NB, SEQ, _ = x_prompt.shape
    NDEC, SL, _ = x_sample.shape
    NP = NMETA + SEQ
    ncores = 8
    NSAMP = NDEC // ncores
    key = (NP, NSAMP, SL)
    if key not in _CACHE:
        _CACHE[key] = build_program(NP, NSAMP, SL)
    nc = _CACHE[key]
    meta = f(inp["meta_tokens"])
    shared = host_consts()
    for n in ["ffn1_gate", "ffn1_up", "ffn1_down", "ffn2_gate", "ffn2_up", "ffn2_down", "w_in", "s5_glu_w", "w_out"]:
        shared[n] = f(inp[n])[0]
    shared["lam_re"] = f(inp["s5_lambda_re"])[0].reshape(32, 128)
    shared["lam_im"] = f(inp["s5_lambda_im"])[0].reshape(32, 128)
    shared["log_dt"] = f(inp["s5_log_dt"])[0]
    shared["b_re"] = f(inp["s5_b_re"])[0]
    shared["b_im"] = f(inp["s5_b_im"])[0]
    shared["c_re"] = f(inp["s5_c_re"])[0]
    shared["c_im"] = f(inp["s5_c_im"])[0]
    shared["wq"] = f(inp["ml_wq"])[0]
    shared["wk"] = f(inp["ml_wk"])[0]
    shared["wv"] = f(inp["ml_wv"])[0]
    shared["igw"] = f(inp["ml_igate_w"])[0]
    shared["fgw"] = f(inp["ml_fgate_w"])[0]
    shared["igb"] = f(inp["ml_igate_b"])[0].reshape(4, 1)
    shared["fgb"] = f(inp["ml_fgate_b"])[0].reshape(4, 1)
    cw = f(inp["ml_conv_w"])[0]
    vd = {"norm_ffn1": f(inp["norm_ffn1"])[0], "norm_mix": f(inp["norm_mix"])[0], "s5_d": f(inp["s5_d"])[0],
          "s5_glu_b": f(inp["s5_glu_b"])[0], "cw0": cw[0], "cw1": cw[1], "cw2": cw[2], "cw3": cw[3],
          "ml_conv_b": f(inp["ml_conv_b"])[0], "ml_norm_w": f(inp["ml_norm_w"])[0], "ml_skip": f(inp["ml_skip"])[0],
          "out_norm_s5": f(inp["out_norm_s5"])[0], "out_norm_ml": f(inp["out_norm_ml"])[0],
          "norm_ffn2": f(inp["norm_ffn2"])[0], "norm_final": f(inp["norm_final"])}
    shared["vecs"] = np.concatenate([vd[n].reshape(8, 128) for n in VEC_NAMES], 0)
    s5re, s5im = f(inp["state_s5_re"])[0], f(inp["state_s5_im"])[0]
    stc, stn, stm, stcv = f(inp["state_mlstm_c"])[0], f(inp["state_mlstm_n"])[0], f(inp["state_mlstm_m"])[0], f(inp["state_mlstm_conv"])[0]
    in_maps = []
    for c in range(ncores):
        b = c % NB
        sl = slice(c * NSAMP, (c + 1) * NSAMP)
        m = dict(shared)
        m["xp"] = np.concatenate([meta, x_prompt[b]], 0)
        m["xs"] = x_sample[sl]
        m["st_s5re"] = s5re[sl].reshape(NSAMP, 32, 128)
        m["st_s5im"] = s5im[sl].reshape(NSAMP, 32, 128)
        m["st_c"] = stc[sl]
        m["st_n"] = stn[sl].reshape(NSAMP, 8, 128)
        m["st_m"] = stm[sl].reshape(NSAMP, 4, 1)
        m["st_conv"] = stcv[sl]
        in_maps.append(m)
    res = run_bass_kernel_spmd(nc, in_maps, core_ids=list(range(ncores))).results
    y_prompt = np.stack([res[b]["yp"] for b in range(NB)], 0)
    y_sample = np.concatenate([res[c]["ys"] for c in range(ncores)], 0)

    def gather(name, shape_tail):
        pr = np.stack([res[b][name][0] for b in range(NB)], 0).reshape((1, NB) + shape_tail)
        sm = np.concatenate([res[c][name][1:] for c in range(ncores)], 0).reshape((1, NDEC) + shape_tail)
        return pr.astype(np.float32), sm.astype(np.float32)

    p_re, s_re = gather("o_s5re", (64, 64))
    p_im, s_im = gather("o_s5im", (64, 64))
    p_c, s_c = gather("o_c", (4, 256, 256))
    p_n, s_n = gather("o_n", (4, 256))
    p_m, s_m = gather("o_m", (4,))
    p_cv, s_cv = gather("o_conv", (3, 1024))
    return (y_prompt.astype(np.float32), y_sample.astype(np.float32), p_re, p_im, p_c, p_n, p_m, p_cv,
            s_re, s_im, s_c, s_n, s_m, s_cv)
```

```python
import math
import numpy as np
import concourse.bass as bass
import concourse.mybir as mybir
from concourse.bass_utils import run_bass_kernel_spmd
from contextlib import ExitStack

F32 = mybir.dt.float32
BF16 = mybir.dt.bfloat16
I32 = mybir.dt.int32
ALU = mybir.AluOpType
AF = mybir.ActivationFunctionType

D = 1024
DFF = 2816
KC = 8
FC = 22
NMETA = 16
EPS = 1e-6
STAGE = 9
ENGS = ("pe", "act", "dve", "pool", "sp")
NDMA_SEM = 12
VEC_NAMES = ["norm_ffn1", "norm_mix", "s5_d", "s5_glu_b", "cw0", "cw1", "cw2", "cw3", "ml_conv_b",
             "ml_norm_w", "ml_skip", "out_norm_s5", "out_norm_ml", "norm_ffn2", "norm_final"]
VI = {n: i for i, n in enumerate(VEC_NAMES)}


class Op:
    __slots__ = ("eng", "fn", "deps", "idx", "signaled", "sigval", "dma", "dsem", "dval", "dprev")

    def __init__(self, eng, fn, dma):
        self.eng = eng
        self.fn = fn
        self.deps = []
        self.signaled = False
        self.sigval = 0
        self.dma = dma
        self.dsem = None
        self.dval = 0
        self.dprev = None


class Prog:
    def __init__(self, nc):
        self.nc = nc
        self.ops = {e: [] for e in ENGS}
        self.last_writer = {}
        self.readers = {}
        self.ndma = {e: 0 for e in ENGS}
        self.dma_ops = {e: [] for e in ENGS}

    def op(self, eng, fn, reads=(), writes=(), dma=False):
        o = Op(eng, fn, dma)
        deps = []
        for r in reads:
            w = self.last_writer.get(r)
            if w is not None:
                deps.append(w)
        for wr in writes:
            w = self.last_writer.get(wr)
            if w is not None:
                deps.append(w)
            deps.extend(self.readers.get(wr, ()))
        seen = set()
        for d in deps:
            if id(d) in seen or d is o:
                continue
            seen.add(id(d))
            if d.eng == "pe" and eng == "pe" and not d.dma and not dma:
                continue
            o.deps.append(d)
        for r in reads:
            self.readers.setdefault(r, []).append(o)
        for wr in writes:
            self.last_writer[wr] = o
            self.readers[wr] = []
        if dma:
            k = self.ndma[eng]
            self.ndma[eng] += 1
            o.dsem = k % NDMA_SEM
            o.dval = 16 * (k // NDMA_SEM + 1)
            if k >= NDMA_SEM:
                o.dprev = self.dma_ops[eng][k - NDMA_SEM]
            self.dma_ops[eng].append(o)
        o.idx = len(self.ops[eng])
        self.ops[eng].append(o)
        return o

    def emit(self, final_waits=()):
        nc = self.nc
        for e in ENGS:
            for o in self.ops[e]:
                for d in o.deps:
                    if not d.dma:
                        d.signaled = True
        for o in final_waits:
            if not o.dma:
                o.signaled = True
        for e in ENGS:
            c = 0
            for o in self.ops[e]:
                if o.signaled and not o.dma:
                    c += 1
                    o.sigval = c
        with ExitStack() as st:
            esem = {e: st.enter_context(nc.semaphore("s_" + e)) for e in ENGS}
            dsem = {e: [st.enter_context(nc.semaphore("d_%s_%d" % (e, i))) for i in range(NDMA_SEM)]
                    for e in ENGS if self.ndma[e] > 0}
            block = st.enter_context(nc.Block())

            def body(e, engine):
                observed = {}

                def wait(key, sem, val):
                    if observed.get(key, 0) >= val:
                        return
                    observed[key] = val
                    engine.wait_ge(sem, val)

                for o in self.ops[e]:
                    for d in o.deps:
                        if d.dma:
                            wait(("d", d.eng, d.dsem), dsem[d.eng][d.dsem], d.dval)
                        else:
                            wait(("e", d.eng), esem[d.eng], d.sigval)
                    if o.dma and o.dprev is not None:
                        wait(("d", e, o.dprev.dsem), dsem[e][o.dprev.dsem], o.dprev.dval)
                    ins = o.fn(engine)
                    if o.dma:
                        ins.then_inc(dsem[e][o.dsem], 16)
                    elif o.signaled:
                        ins.then_inc(esem[e], 1)
                if e == "sp":
                    for o in final_waits:
                        if o.dma:
                            wait(("d", o.eng, o.dsem), dsem[o.eng][o.dsem], o.dval)
                        else:
                            wait(("e", o.eng), esem[o.eng], o.sigval)

            block.sync(lambda eng: body("sp", eng))
            block.scalar(lambda eng: body("act", eng))
            block.vector(lambda eng: body("dve", eng))
            block.gpsimd(lambda eng: body("pool", eng))
            block.tensor(lambda eng: body("pe", eng))


def CALL(name, *args, **kw):
    return lambda e: getattr(e, name)(*args, **kw)


def bc_last(ap, n):
    return bass.AP(ap.tensor, ap.offset, [list(a) for a in ap.ap] + [[0, n]])


def bc_mid(ap, n):
    a = [list(x) for x in ap.ap]
    return bass.AP(ap.tensor, ap.offset, [a[0], [0, n]] + a[1:])


def bc_row(ap, n):
    a = [list(x) for x in ap.ap]
    return bass.AP(ap.tensor, ap.offset, [a[0], [0, n]])


def build_program(NP, NSAMP=2, SL=32):
    nc = bass.Bass("TRN2", target_bir_lowering=False)
    dr = {}

    def din(name, shape, dt=F32):
        dr[name] = nc.dram_tensor(name, list(shape), dt, kind="ExternalInput").ap()
        return dr[name]

    def dout(name, shape):
        dr[name] = nc.dram_tensor(name, list(shape), F32, kind="ExternalOutput").ap()
        return dr[name]

    xp = din("xp", [NP, D])
    xs = din("xs", [NSAMP, SL, D])
    st_s5re = din("st_s5re", [NSAMP, 32, 128])
    st_s5im = din("st_s5im", [NSAMP, 32, 128])
    st_c = din("st_c", [NSAMP, 4, 256, 256])
    st_n = din("st_n", [NSAMP, 8, 128])
    st_m = din("st_m", [NSAMP, 4, 1])
    st_conv = din("st_conv", [NSAMP, 3, D])
    vecs = din("vecs", [len(VEC_NAMES) * 8, 128])
    W = {}
    for n, shp in [("ffn1_gate", [D, DFF]), ("ffn1_up", [D, DFF]), ("ffn1_down", [DFF, D]),
                   ("ffn2_gate", [D, DFF]), ("ffn2_up", [D, DFF]), ("ffn2_down", [DFF, D]),
                   ("w_in", [D, 3 * D]), ("s5_glu_w", [D, D]), ("w_out", [2 * D, D]),
                   ("lam_re", [32, 128]), ("lam_im", [32, 128]), ("log_dt", [64]),
                   ("b_re", [64, 64, 16]), ("b_im", [64, 64, 16]), ("c_re", [64, 16, 64]), ("c_im", [64, 16, 64]),
                   ("wq", [256, 4, 4]), ("wk", [256, 4, 4]), ("wv", [256, 4, 4]),
                   ("igw", [3 * D, 4]), ("fgw", [3 * D, 4]), ("igb", [4, 1]), ("fgb", [4, 1]),
                   ("ident", [128, 128]), ("maskE", [128, 2]), ("mask16", [32, 2]), ("triu", [128, 128]),
                   ("bdmask", [128, 32]), ("tvec", [128, 64]), ("sel4", [4, 512]), ("mask4", [128, 4])]:
        W[n] = din(n, shp)
    NS = 1 + NSAMP
    yp = dout("yp", [NP - NMETA, D])
    ys = dout("ys", [NSAMP, SL, D])
    o_s5re = dout("o_s5re", [NS, 32, 128])
    o_s5im = dout("o_s5im", [NS, 32, 128])
    o_c = dout("o_c", [NS, 4, 256, 256])
    o_n = dout("o_n", [NS, 8, 128])
    o_m = dout("o_m", [NS, 4, 1])
    o_conv = dout("o_conv", [NS, 3, D])

    P = Prog(nc)
    outs = []
    TT = 512
    with ExitStack() as st:
        def sb(name, shape, dt=F32):
            return st.enter_context(nc.sbuf_tensor("sb_" + name, list(shape), dt))

        st.enter_context(nc.allow_low_precision("bf16 matmul operands with fp32 PSUM accumulation"))
        pb = [st.enter_context(nc.psum_tensor("pb%d" % i, [128, 512], F32)) for i in range(8)]

        xT = sb("xT", [128, KC, TT])
        xn = sb("xn", [128, KC, TT], BF16)
        xc_bf = xn
        hT = sb("hT", [128, 24, TT], BF16)
        u_bf = sb("u_bf", [128, KC, TT], BF16)
        z_bf = sb("z_bf", [128, KC, TT], BF16)
        xmh = sb("xmh", [128, KC, TT + 3], BF16)
        vT = hT[:, 16:24, :]
        mixed = sb("mixed", [128, 16, TT], BF16)
        ybuf = sb("ybuf", [128, KC, TT])
        xtok = ybuf[:].rearrange("p a b -> p (a b)").rearrange("p (n d) -> p n d", d=D)
        rstd = sb("rstd", [128, TT])
        tAB = sb("tAB", [128, 2 * TT])
        tA = tAB[:, 0:TT]
        tB = tAB[:, TT:2 * TT]
        otok = tAB
        wgr = [sb("wgr%d" % i, [128, KC, 128], BF16) for i in range(2)]
        wur = [sb("wur%d" % i, [128, KC, 128], BF16) for i in range(2)]
        wdr = [sb("wdr%d" % i, [128, FC // 2, 128], BF16) for i in range(2)]
        wir = [sb("wir%d" % i, [128, KC, 128], BF16) for i in range(2)]
        wor = [sb("wor%d" % i, [128, 8, 128], BF16) for i in range(2)]
        ident = sb("ident", [128, 128])
        ones_bf = sb("ones_bf", [128, 128], BF16)
        onesD = sb("onesD", [128, 128], BF16)
        ones256 = sb("ones256", [128, 128])
        ones4 = sb("ones4", [4, 128])
        onecol = sb("onecol", [128, 1])
        epscol = sb("epscol", [128, 1])
        vec = sb("vec", [128, len(VEC_NAMES) * 8])
        maskE = sb("maskE", [128, 2])
        mask16 = sb("mask16", [32, 2])
        triu = sb("triu", [128, 128])
        bdmask = sb("bdmask", [128, 32])
        tvec = sb("tvec", [128, 64])
        sel4 = sb("sel4", [4, 512])
        cosT = sb("cosT", [128, 32, 64])
        sinT = sb("sinT", [128, 32, 64])
        rmag = sb("rmag", [128, 32])
        W1 = sb("W1", [128, KC, 2, 128], BF16)
        W3 = sb("W3", [128, 32, 2, 32], BF16)
        s5st = sb("s5st", [128, 2, 32])
        sre = s5st[:, 0, :]
        sim = s5st[:, 1, :]
        s5all = sb("s5all", [128, 8, 256])
        s5w = [s5all[:, i, :].rearrange("p (a b) -> p a b", b=64) for i in range(8)]
        resetm = sb("resetm", [128, 64])
        inj = [sb("inj%d" % i, [128, 2, 2, 4]) for i in range(2)]
        xbfA = sb("xbfA", [128, 2, 4, 64], BF16)
        xbfB = sb("xbfB", [128, 2, 4, 64], BF16)
        um = sb("um", [128, 2, 4, 64], BF16)
        umB = sb("umB", [128, 2, 4, 64], BF16)
        mask4 = sb("mask4", [128, 4])
        CT = sb("CT", [128, 4, 2, 256])
        nT = sb("nT", [128, 8])
        mprev = sb("mprev", [4, 1])
        BD = sb("BD", [128, 3, KC, 128], BF16)
        gwi = sb("gwi", [128, 24, 4], BF16)
        gwf = sb("gwf", [128, 24, 4], BF16)
        igb = sb("igb", [4, 1])
        nfgb = sb("nfgb", [4, 1])
        igs = sb("igs", [4, TT])
        Fn = sb("Fn", [4, TT])
        aa = sb("aa", [4, TT])
        MM = sb("MM", [4, TT])
        g4 = [sb("g4_%d" % i, [4, 128]) for i in range(2)]
        negM = sb("negM", [4, 1])
        gcol = sb("gcol", [4, 1])
        Mpc = sb("Mpc", [4, 1])
        dg = sb("dg", [4, 4])
        et = sb("et", [128, 4])
        gb = sb("gb", [128, 4])
        ngc2 = sb("ngc2", [128, 2, 2])
        mwab = sb("mwab", [128, 128])
        ktp = sb("ktp", [128, 256], BF16)
        vtp = sb("vtp", [128, 256], BF16)
        vtb = sb("vtb", [128, 256], BF16)
        Sm = sb("Sm", [128, 128], BF16)
        Eb = sb("Eb", [128, 128], BF16)
        Cg = sb("Cg", [128, 2, 256], BF16)
        nrep = sb("nrep", [128, 2, 128], BF16)
        hh = sb("hh", [128, 2, 128])
        hq = sb("hq", [128, 2, 128])
        mw = [sb("mw%d" % i, [128, 128]) for i in range(4)]
        stg = sb("stg", [128, 256])

        SRE = ["sre%d" % c for c in range(KC)]
        SIM = ["sim%d" % c for c in range(KC)]
        dq = ["sp", "pool"]
        dqi = [0]

        def dma(out, in_, reads=(), writes=(), q=None, slow=False):
            if out.dtype != in_.dtype:
                q = "pool"
            if q is None:
                q = dq[dqi[0] % 2]
                dqi[0] += 1
            if slow:
                f = CALL("dma_start", out=out, in_=in_, allow_slow_non_contiguous=True)
            else:
                f = CALL("dma_start", out=out, in_=in_)
            return P.op(q, f, reads=reads, writes=writes, dma=True)

        def mm(out, lhsT, rhs, start, stop, reads, writes, tp=None):
            if tp is None:
                f = CALL("matmul", out, lhsT, rhs, start=start, stop=stop)
            else:
                f = CALL("matmul", out, lhsT, rhs, start=start, stop=stop, tile_position=tp)
            return P.op("pe", f, reads=reads, writes=writes)

        def tr(out, in_, idn, reads, writes):
            return P.op("pe", CALL("transpose", out, in_, idn), reads=list(reads) + ["ident"], writes=writes)

        def E(eng, fn, reads, writes):
            return P.op(eng, fn, reads=reads, writes=writes)

        def V(name, c):
            i = VI[name] * 8 + c
            return vec[:, i:i + 1]

        for name, t in [("ident", ident), ("maskE", maskE), ("mask16", mask16), ("triu", triu), ("bdmask", bdmask),
                        ("tvec", tvec), ("sel4", sel4), ("igb", igb), ("mask4", mask4)]:
            dma(t[:], W[name], writes=[name])
        E("dve", CALL("memset", ones_bf[:], 1.0), [], ["ones_bf"])
        E("dve", CALL("memset", onesD[:], 1.0 / D), [], ["onesD"])
        E("dve", CALL("memset", ones256[:], 1.0 / 256), [], ["ones256"])
        E("dve", CALL("memset", ones4[:], 1.0), [], ["ones4"])
        E("dve", CALL("memset", onecol[:], 1.0), [], ["onecol"])
        E("dve", CALL("memset", resetm[:], 1.0), [], ["resetm"])
        E("dve", CALL("memset", resetm[:, 0:1], 0.0), ["resetm"], ["resetm"])
        E("dve", CALL("memset", epscol[:], EPS), [], ["epscol"])
        dma(nfgb[:], W["fgb"], writes=["nfgb"])
        E("dve", CALL("tensor_scalar", nfgb[:], nfgb[:], -1.0, 0.0, ALU.mult, ALU.add), ["nfgb"], ["nfgb"])
        dma(stg[0:len(VEC_NAMES) * 8, 0:128], vecs, writes=["stg"])
        nv = len(VEC_NAMES) * 8
        tr(pb[7][:, 0:nv], stg[0:nv, 0:128], ident[0:nv, 0:nv], ["stg"], ["p7"])
        E("dve", CALL("tensor_copy", vec[:], pb[7][:, 0:nv]), ["p7"], ["vec"])
        dma(gwi[:], W["igw"].rearrange("(c p) h -> p c h", p=128), writes=["gwi"], slow=True)
        dma(gwf[:], W["fgw"].rearrange("(c p) h -> p c h", p=128), writes=["gwf"], slow=True)
        for wi, wn in enumerate(["wq", "wk", "wv"]):
            dma(stg[:, 0:32].rearrange("p (c o) -> p c o", o=4), W[wn].rearrange("(c b) i o -> (b i) c o", b=32),
                reads=[], writes=["stg"], slow=True)
            for c in range(KC):
                E("dve", CALL("tensor_tensor",
                    BD[:, wi, c, :].rearrange("p (b o) -> p b o", o=4),
                    bc_mid(stg[:, c * 4:c * 4 + 4], 32), bc_last(bdmask[:, :], 4), ALU.mult),
                  ["stg", "bdmask"], ["BD"])
        lamr = s5w[0][:, 0, 0:32]
        lami = s5w[0][:, 1, 0:32]
        dtb = s5w[0][:, 2, 0:32]
        th = s5w[1][:, 0, 0:32]
        cth = s5w[1][:, 1, 0:32]
        sth = s5w[1][:, 2, 0:32]
        lbr = s5w[2][:, 0, 0:32]
        lbi = s5w[2][:, 1, 0:32]
        kr = s5w[2][:, 2, 0:32]
        ki_ = s5w[2][:, 3, 0:32]
        t1 = s5w[3][:, 0, 0:32]
        t2 = s5w[3][:, 1, 0:32]
        t3 = s5w[3][:, 2, 0:32]
        for nm, dst in [("lam_re", lamr), ("lam_im", lami)]:
            dma(stg[0:32, 0:128], W[nm], writes=["stg"])
            tr(pb[7][:, 0:32], stg[0:32, 0:128], ident[0:32, 0:32], ["stg"], ["p7"])
            E("dve", CALL("tensor_copy", dst, pb[7][:, 0:32]), ["p7"], ["s5p"])
        ldt = W["log_dt"]
        for g2 in range(2):
            src = bass.AP(ldt.tensor, ldt.offset + g2, [[0, 64], [2, 32]])
            dma(s5w[0][64 * g2:64 * g2 + 64, 2, 0:32], src, writes=["s5p"], slow=True)
        E("act", CALL("activation", dtb, dtb, AF.Exp), ["s5p"], ["s5p"])
        E("dve", CALL("tensor_tensor", t1, lamr, dtb, ALU.mult), ["s5p"], ["s5p"])
        E("act", CALL("activation", rmag[:], t1, AF.Exp), ["s5p"], ["rmag"])
        E("dve", CALL("tensor_tensor", th, lami, dtb, ALU.mult), ["s5p"], ["s5p"])

        ki32 = sb("ki32", [128, 1, 64], I32)

        def sincos(dst, ang, n, shift, key_r, key_w):
            wk = s5w[6][:].rearrange("p a b -> p (a b)")[:, 0:n]
            wf = s5w[7][:].rearrange("p a b -> p (a b)")[:, 0:n]
            wi_ = ki32[:].rearrange("p a b -> p (a b)")[:, 0:n]
            E("dve", CALL("tensor_scalar", wk, ang, shift, 1.0 / (2 * math.pi), ALU.add, ALU.mult), key_r, ["s5t"])
            E("dve", CALL("tensor_copy", wi_, wk), ["s5t"], ["s5t"])
            E("dve", CALL("tensor_copy", wf, wi_), ["s5t"], ["s5t"])
            E("dve", CALL("tensor_scalar", wk, ang, shift, 0.0, ALU.add, ALU.add), key_r + ["s5t"], ["s5t"])
            E("dve", CALL("scalar_tensor_tensor", wk, wf, -2 * math.pi, wk, ALU.mult, ALU.add), ["s5t"], ["s5t"])
            E("dve", CALL("tensor_scalar", wf, wk, math.pi, -2 * math.pi, ALU.is_gt, ALU.mult), ["s5t"], ["s5t"])
            E("dve", CALL("tensor_tensor", wk, wk, wf, ALU.add), ["s5t"], ["s5t"])
            E("dve", CALL("tensor_scalar", wf, wk, -math.pi, 2 * math.pi, ALU.is_lt, ALU.mult), ["s5t"], ["s5t"])
            E("dve", CALL("tensor_tensor", wk, wk, wf, ALU.add), ["s5t"], ["s5t"])
            E("act", CALL("activation", dst, wk, AF.Sin), ["s5t"], key_w)

        sincos(sth, th, 32, 0.0, ["s5p"], ["s5p"])
        sincos(cth, th, 32, math.pi / 2, ["s5p"], ["s5p"])
        E("dve", CALL("tensor_tensor", lbr, rmag[:], cth, ALU.mult), ["s5p", "rmag"], ["s5p"])
        E("dve", CALL("tensor_tensor", lbi, rmag[:], sth, ALU.mult), ["s5p", "rmag"], ["s5p"])
        E("dve", CALL("tensor_scalar", t1, lbr, -1.0, 0.0, ALU.add, ALU.add), ["s5p"], ["s5p"])
        E("dve", CALL("tensor_tensor", t2, lamr, lamr, ALU.mult), ["s5p"], ["s5p"])
        E("dve", CALL("tensor_tensor", t3, lami, lami, ALU.mult), ["s5p"], ["s5p"])
        E("dve", CALL("tensor_tensor", t2, t2, t3, ALU.add), ["s5p"], ["s5p"])
        E("dve", CALL("reciprocal", t2, t2), ["s5p"], ["s5p"])
        E("dve", CALL("tensor_tensor", kr, t1, lamr, ALU.mult), ["s5p"], ["s5p"])
        E("dve", CALL("tensor_tensor", t3, lbi, lami, ALU.mult), ["s5p"], ["s5p"])
        E("dve", CALL("tensor_tensor", kr, kr, t3, ALU.add), ["s5p"], ["s5p"])
        E("dve", CALL("tensor_tensor", kr, kr, t2, ALU.mult), ["s5p"], ["s5p"])
        E("dve", CALL("tensor_tensor", ki_, lbi, lamr, ALU.mult), ["s5p"], ["s5p"])
        E("dve", CALL("tensor_tensor", t3, t1, lami, ALU.mult), ["s5p"], ["s5p"])
        E("dve", CALL("tensor_tensor", ki_, ki_, t3, ALU.subtract), ["s5p"], ["s5p"])
        E("dve", CALL("tensor_tensor", ki_, ki_, t2, ALU.mult), ["s5p"], ["s5p"])
        for q in range(32):
            ang = s5w[5][:, 0, :]
            E("dve", CALL("tensor_scalar", ang, tvec[:, :], th[:, q:q + 1], 0.0, ALU.mult, ALU.add),
              ["s5p", "tvec"], ["s5ang"])
            sincos(sinT[:, q, :], ang, 64, 0.0, ["s5ang"], ["sinT"])
            sincos(cosT[:, q, :], ang, 64, math.pi / 2, ["s5ang"], ["cosT"])
        Bre = CT[:].rearrange("p a b c -> p (a b c)")[:, 0:512].rearrange("p (q j) -> p q j", j=16)
        Bim = CT[:].rearrange("p a b c -> p (a b c)")[:, 512:1024].rearrange("p (q j) -> p q j", j=16)
        Ere = CT[:].rearrange("p a b c -> p (a b c)")[:, 1024:1536].rearrange("p (q j) -> p q j", j=16)
        Eim = CT[:].rearrange("p a b c -> p (a b c)")[:, 1536:2048].rearrange("p (q j) -> p q j", j=16)
        for g2 in range(2):
            dma(Bre[64 * g2:64 * g2 + 64], W["b_re"].rearrange("(q g) p j -> g p q j", g=2)[g2], writes=["CT0", "CT1", "CT2", "CT3"], slow=True)
            dma(Bim[64 * g2:64 * g2 + 64], W["b_im"].rearrange("(q g) p j -> g p q j", g=2)[g2], writes=["CT0", "CT1", "CT2", "CT3"], slow=True)
        kmr = s5w[4][:, 0, 0:32]
        kmi = s5w[4][:, 1, 0:32]
        Eexp_re = ybuf[:].rearrange("p a b -> p (a b)")[:, 0:1024].rearrange("p (q g j) -> p q g j", g=2, j=16)
        Eexp_im = ybuf[:].rearrange("p a b -> p (a b)")[:, 1024:2048].rearrange("p (q g j) -> p q g j", g=2, j=16)
        for g2 in range(2):
            E("dve", CALL("tensor_scalar", kmr, kr, maskE[:, g2:g2 + 1], 0.0, ALU.mult, ALU.add), ["s5p", "maskE"], ["s5k"])
            E("dve", CALL("tensor_scalar", kmi, ki_, maskE[:, g2:g2 + 1], 0.0, ALU.mult, ALU.add), ["s5p", "maskE"], ["s5k"])
            E("dve", CALL("tensor_tensor", Ere, Bre, bc_last(kmr, 16), ALU.mult), ["CT0", "CT1", "CT2", "CT3"] + ["s5k"], ["CT0", "CT1", "CT2", "CT3"])
            E("dve", CALL("tensor_tensor", Eim, Bim, bc_last(kmi, 16), ALU.mult), ["CT0", "CT1", "CT2", "CT3"] + ["s5k"], ["CT0", "CT1", "CT2", "CT3"])
            E("dve", CALL("tensor_tensor", Eexp_re[:, :, g2, :], Ere, Eim, ALU.subtract), ["CT0", "CT1", "CT2", "CT3"], ["ybuf"])
            E("dve", CALL("tensor_tensor", Ere, Bim, bc_last(kmr, 16), ALU.mult), ["CT0", "CT1", "CT2", "CT3"] + ["s5k"], ["CT0", "CT1", "CT2", "CT3"])
            E("dve", CALL("tensor_tensor", Eim, Bre, bc_last(kmi, 16), ALU.mult), ["CT0", "CT1", "CT2", "CT3"] + ["s5k"], ["CT0", "CT1", "CT2", "CT3"])
            E("dve", CALL("tensor_tensor", Eexp_im[:, :, g2, :], Ere, Eim, ALU.add), ["CT0", "CT1", "CT2", "CT3"], ["ybuf"])
        for c in range(KC):
            for ri, Ex in enumerate([Eexp_re, Eexp_im]):
                src = Ex[:, 4 * c:4 * c + 4, :, :].rearrange("p q g j -> p (q g j)")
                tr(pb[7][:, 0:128], src, ident[:], ["ybuf"], ["p7"])
                E("act", CALL("copy", W1[:, c, ri, :], pb[7][:, 0:128]), ["p7"], ["W1"])
        Cst = CT[:].rearrange("p a b c -> p (a b c)")[0:32, 0:2048].rearrange("p (q k) -> p q k", k=64)
        Cx = xT[:].rearrange("p a b -> p (a b)")[0:32, 0:4096].rearrange("p (q g k) -> p q g k", g=2, k=64)
        for ri, (cn, sgn) in enumerate([("c_re", 1.0), ("c_im", -1.0)]):
            for g2 in range(2):
                dma(Cst[16 * g2:16 * g2 + 16], W[cn].rearrange("(q g) h p -> g h q p", g=2)[g2], writes=["CT0", "CT1", "CT2", "CT3"], slow=True)
            for g2 in range(2):
                E("dve", CALL("tensor_scalar", Cx[:, :, g2, :], Cst, mask16[:, g2:g2 + 1], sgn, ALU.mult, ALU.mult),
                  ["CT0", "CT1", "CT2", "CT3"] + ["mask16"], ["xT"])
            for q in range(32):
                tr(pb[6][:, (q % 16) * 32:(q % 16) * 32 + 32], Cx[:, q, :, :].rearrange("p g k -> p (g k)"), ident[0:32, 0:32], ["xT"], ["p6"])
                if q % 16 == 15:
                    q0 = q - 15
                    E("act", CALL("copy", W3[:, q0:q0 + 16, ri, :], pb[6][:].rearrange("p (q h) -> p q h", h=32)), ["p6"], ["W3"])

        wcnt = {"g": 0, "u": 0, "d": 0, "i": 0, "o": 0}

        def rmsnorm_stats(src_chunks, nch, Tt, key_r, scale_mat):
            for c in range(nch):
                if c % 2 == 0:
                    E("act", CALL("activation", hT[:, 14 + c, 0:Tt], src_chunks(c), AF.Square), key_r, ["h%d" % (14 + c)])
                else:
                    E("dve", CALL("tensor_tensor", hT[:, 14 + c, 0:Tt], src_chunks(c), src_chunks(c), ALU.mult), key_r, ["h%d" % (14 + c)])
            for c in range(nch):
                mm(pb[6][:, 0:Tt], scale_mat[:], hT[:, 14 + c, 0:Tt], c == 0, c == nch - 1, ["onesD", "h%d" % (14 + c)], ["p6"])
            E("act", CALL("activation", rstd[:, 0:Tt], pb[6][:, 0:Tt], AF.Ln, bias=epscol[:, 0:1]), ["p6", "epscol"], ["rstd"])
            E("act", CALL("activation", rstd[:, 0:Tt], rstd[:, 0:Tt], AF.Exp, scale=-0.5), ["rstd"], ["rstd"])

        def norm_x(gname, Tt):
            rmsnorm_stats(lambda c: xT[:, c, 0:Tt], KC, Tt, ["xT"], onesD)
            for c in range(KC):
                E("dve", CALL("scalar_tensor_tensor", xn[:, c, 0:Tt], xT[:, c, 0:Tt], V(gname, c), rstd[:, 0:Tt], ALU.mult, ALU.mult),
                  ["xT", "vec", "rstd"], ["xn"])

        def ffn(pref, gname, Tt):
            norm_x(gname, Tt)
            wg, wu, wd = W[pref + "_gate"], W[pref + "_up"], W[pref + "_down"]
            for f in range(FC):
                gi = wcnt["g"] % 2
                wcnt["g"] += 1
                dma(wgr[gi][:], wg[:, f * 128:(f + 1) * 128].rearrange("(c p) f -> p c f", p=128), writes=["wg%d" % gi])
                dma(wur[gi][:], wu[:, f * 128:(f + 1) * 128].rearrange("(c p) f -> p c f", p=128), writes=["wu%d" % gi])
                pg, pu = pb[f % 2], pb[2 + f % 2]
                for c in range(KC):
                    mm(pg[:, 0:Tt], wgr[gi][:, c, :], xn[:, c, 0:Tt], c == 0, c == KC - 1, ["wg%d" % gi, "xn"], ["p%d" % (f % 2)])
                for c in range(KC):
                    mm(pu[:, 0:Tt], wur[gi][:, c, :], xn[:, c, 0:Tt], c == 0, c == KC - 1, ["wu%d" % gi, "xn"], ["p%d" % (2 + f % 2)])
                tt = tA if f % 2 == 0 else tB
                tn = "tA" if f % 2 == 0 else "tB"
                E("act", CALL("activation", tt[:, 0:Tt], pg[:, 0:Tt], AF.Silu), ["p%d" % (f % 2)], [tn])
                E("dve", CALL("tensor_tensor", hT[:, f, 0:Tt], tt[:, 0:Tt], pu[:, 0:Tt], ALU.mult),
                  [tn, "p%d" % (2 + f % 2)], ["h%d" % f])
            for c in range(KC):
                pd = pb[4 + c % 2]
                for hf in range(2):
                    di = hf
                    dma(wdr[di][:], wd[hf * 1408:(hf + 1) * 1408, c * 128:(c + 1) * 128].rearrange("(f p) d -> p f d", p=128), writes=["wd%d" % di])
                    for f2 in range(FC // 2):
                        f = hf * (FC // 2) + f2
                        mm(pd[:, 0:Tt], wdr[di][:, f2, :], hT[:, f, 0:Tt], f == 0, f == FC - 1, ["wd%d" % di, "h%d" % f], ["p%d" % (4 + c % 2)])
                E("dve", CALL("scalar_tensor_tensor", xT[:, c, 0:Tt], pd[:, 0:Tt], 0.5, xT[:, c, 0:Tt], ALU.mult, ALU.add),
                  ["p%d" % (4 + c % 2), "xT"], ["xT"])

        def in_proj(Tt):
            norm_x("norm_mix", Tt)
            for oc in range(24):
                ii = wcnt["i"] % 2
                wcnt["i"] += 1
                dma(wir[ii][:], W["w_in"][:, oc * 128:(oc + 1) * 128].rearrange("(c p) f -> p c f", p=128), writes=["wi%d" % ii])
                po = pb[4 + oc % 2]
                for c in range(KC):
                    mm(po[:, 0:Tt], wir[ii][:, c, :], xn[:, c, 0:Tt], c == 0, c == KC - 1, ["wi%d" % ii, "xn"], ["p%d" % (4 + oc % 2)])
                if oc < 8:
                    dst, key = u_bf[:, oc, 0:Tt], "u_bf"
                elif oc < 16:
                    dst, key = xmh[:, oc - 8, 3:3 + Tt], "xmh"
                else:
                    dst, key = z_bf[:, oc - 16, 0:Tt], "z_bf"
                if oc % 2 == 0:
                    E("act", CALL("copy", dst, po[:, 0:Tt]), ["p%d" % (4 + oc % 2)], [key])
                else:
                    E("dve", CALL("tensor_copy", dst, po[:, 0:Tt]), ["p%d" % (4 + oc % 2)], [key])

        def s5_mix(Tt):
            gT = hT
            L = min(64, Tt)
            nun = Tt // L
            def s5_pre(c, un, S):
                t0 = un * L
                X = S["extra"]
                pS, psk = S["pS"][un % 2], S["psk"][un % 2]
                umS = S["um"][:, un % 2]
                kum = "s5%sum%d" % (S["k"], un % 2)
                pSv = pS[:].rearrange("p (q r t) -> p q r t", q=4, r=2)
                for qq in range(4):
                    E("act", CALL("activation", umS[:, qq, 0:L], u_bf[:, c, t0:t0 + L], AF.Copy, scale=mask4[:, qq:qq + 1]), ["u_bf", "mask4"] + X, [kum])
                for qq in range(4):
                    for ri in range(2):
                        mm(pSv[:, qq, ri, 0:L], W1[:, c, ri, :], umS[:, qq, 0:L], True, True, ["W1", kum] + X, [psk])

            def s5_unit(c, un, S):
                pY = pb[2 + c % 2]
                pyk = "p%d" % (2 + c % 2)
                t0 = un * L
                X = S["extra"]
                K = lambda i: "s5%s%d" % (S["k"], i)
                pS, psk = S["pS"][un % 2], S["psk"][un % 2]
                xbf = S["xbf"]
                kxb = K(11)
                injC, kinjC = S["inj"][:, un % 2], "s5%sinj%d" % (S["k"], un % 2)
                injN, kinjN = S["inj"][:, (un + 1) % 2], "s5%sinj%d" % (S["k"], (un + 1) % 2)
                sk = "sre%d" % c
                pSv = pS[:].rearrange("p (q r t) -> p q r t", q=4, r=2)
                if un == 0:
                    s5_pre(c, 0, S)
                    E("pool", CALL("tensor_tensor", injC, s5st[:, :, 4 * c:4 * c + 4], bc_mid(rmag[:, 4 * c:4 * c + 4], 2), ALU.mult), ["rmag", sk] + X, [kinjC])
                    yield
                if un + 1 < nun and L == 64:
                    s5_pre(c, un + 1, S)
                    yield
                bre = pSv[:, :, 0, 0:L]
                bim = pSv[:, :, 1, 0:L]
                cs = cosT[:, 4 * c:4 * c + 4, 0:L]
                sn = sinT[:, 4 * c:4 * c + 4, 0:L]
                B = S["bufs"]
                bv = lambda i: B[:, i * 256:(i + 1) * 256].rearrange("p (q t) -> p q t", t=64)[:, :, 0:L]
                E("dve", CALL("tensor_tensor", bv(2), bre, cs, ALU.mult), [psk, "cosT"] + X, [K(2)])
                E("dve", CALL("tensor_tensor", bv(3), bim, sn, ALU.mult), [psk, "sinT"] + X, [K(3)])
                E("dve", CALL("tensor_tensor", bv(6), bim, cs, ALU.mult), [psk, "cosT"] + X, [K(6)])
                E("dve", CALL("tensor_tensor", bv(7), bre, sn, ALU.mult), [psk, "sinT"] + X, [K(7)])
                E("dve", CALL("tensor_tensor", bv(0), bv(2), bv(3), ALU.add), [K(2), K(3)] + X, [K(0)])
                E("dve", CALL("tensor_tensor", bv(1), bv(6), bv(7), ALU.subtract), [K(6), K(7)] + X, [K(1)])
                if L == 64:
                    W2 = B[:, 0:512].rearrange("p (r q t) -> p r q t", r=2, t=64)
                    E("dve", CALL("tensor_tensor", W2[:, :, :, 0], W2[:, :, :, 0], injC, ALU.add), [K(0), K(1), kinjC] + X, [K(0), K(1)])
                    rt = S["rtab"][:, 4 * c:4 * c + 4, :].rearrange("p q t -> p (q t)")
                    E("dve", CALL("tensor_tensor_scan", B[:, 1024:1280], rt, B[:, 0:256], 0.0, ALU.mult, ALU.add), [K(0), "rtab"] + X, [K(4)])
                    E("dve", CALL("tensor_tensor_scan", B[:, 1280:1536], rt, B[:, 256:512], 0.0, ALU.mult, ALU.add), [K(1), "rtab"] + X, [K(5)])
                    yield
                else:
                    for qq in range(4):
                        q = 4 * c + qq
                        E("dve", CALL("tensor_tensor_scan", bv(4)[:, qq, :], bc_row(rmag[:, q:q + 1], L), bv(0)[:, qq, :],
                                      sre[:, q:q + 1], ALU.mult, ALU.add), ["rmag", K(0), sk] + X, [K(4)])
                        E("dve", CALL("tensor_tensor_scan", bv(5)[:, qq, :], bc_row(rmag[:, q:q + 1], L), bv(1)[:, qq, :],
                                      sim[:, q:q + 1], ALU.mult, ALU.add), ["rmag", K(1), sk] + X, [K(5)])
                    yield
                zr, zi = bv(4), bv(5)
                E("pool", CALL("tensor_tensor", bv(2), zr, cs, ALU.mult), [K(4), "cosT"] + X, [K(2)])
                E("pool", CALL("tensor_tensor", bv(3), zi, sn, ALU.mult), [K(5), "sinT"] + X, [K(3)])
                E("pool", CALL("tensor_tensor", bv(6), bv(2), bv(3), ALU.subtract), [K(2), K(3)] + X, [K(6)])
                E("pool", CALL("tensor_tensor", bv(2), zi, cs, ALU.mult), [K(5), "cosT"] + X, [K(2)])
                E("pool", CALL("tensor_tensor", bv(3), zr, sn, ALU.mult), [K(4), "sinT"] + X, [K(3)])
                E("pool", CALL("tensor_tensor", bv(7), bv(2), bv(3), ALU.add), [K(2), K(3)] + X, [K(7)])
                X4 = B[:, 1536:2048].rearrange("p (r q t) -> p r q t", r=2, t=64)
                if L == 64 and un + 1 < nun:
                    E("pool", CALL("tensor_tensor", injN, X4[:, :, :, L - 1], bc_mid(rmag[:, 4 * c:4 * c + 4], 2), ALU.mult), ["rmag", K(6), K(7)] + X, [kinjN])
                yield
                E("act", CALL("copy", xbf[:, :, :, 0:L], X4[:, :, :, 0:L]), [K(6), K(7)] + X, [kxb])
                if L != 64 or un + 1 == nun:
                    E("act", CALL("copy", s5st[:, :, 4 * c:4 * c + 4], X4[:, :, :, L - 1]), [K(6), K(7)] + X, [sk])
                yield
                for qq in range(4):
                    q = 4 * c + qq
                    mm(pY[32 * qq:32 * qq + 32, t0:t0 + L], W3[:, q, 0, :], xbf[:, 0, qq, 0:L], True, False, ["W3", kxb] + X, [pyk], tp=(0, 32 * qq))
                    mm(pY[32 * qq:32 * qq + 32, t0:t0 + L], W3[:, q, 1, :], xbf[:, 1, qq, 0:L], False, True, ["W3", kxb] + X, [pyk], tp=(0, 32 * qq))
                yield

            def s5_post(c):
                pY = pb[2 + c % 2]
                pyk = "p%d" % (2 + c % 2)
                E("dve", CALL("scalar_tensor_tensor", tA[:, 0:Tt], u_bf[:, c, 0:Tt], V("s5_d", c), pY[:, 0:Tt], ALU.mult, ALU.add),
                  ["u_bf", "vec", pyk], ["tA"])
                E("act", CALL("activation", tB[:, 0:Tt], tA[:, 0:Tt], AF.Square), ["tA"], ["tB"])
                E("dve", CALL("tensor_scalar", tB[:, 0:Tt], tB[:, 0:Tt], 0.044715, 1.0, ALU.mult, ALU.add), ["tB"], ["tB"])
                E("pool", CALL("tensor_tensor", tB[:, 0:Tt], tB[:, 0:Tt], tA[:, 0:Tt], ALU.mult), ["tA", "tB"], ["tB"])
                E("act", CALL("activation", tB[:, 0:Tt], tB[:, 0:Tt], AF.Sigmoid, scale=1.5957691216057308), ["tB"], ["tB"])
                E("dve", CALL("tensor_tensor", gT[:, c, 0:Tt], tA[:, 0:Tt], tB[:, 0:Tt], ALU.mult), ["tA", "tB"], ["h%d" % c])

            ybf_ = ybuf[:].rearrange("p a b -> p (a b)")
            rtab = ybf_[:, 2048:4096].rearrange("p (q t) -> p q t", t=64)
            E("dve", CALL("tensor_tensor", rtab, bc_last(rmag[:, :], 64), bc_mid(resetm[:, :], 32), ALU.mult), ["rmag", "resetm", "ybuf"], ["rtab"])
            SA = dict(bufs=s5all[:].rearrange("p a b -> p (a b)"), um=um, xbf=xbfA, inj=inj[0], k="A", extra=[], pS=[pb[0], pb[1]], psk=["p0", "p1"], rtab=rtab)
            SB = dict(bufs=ybf_[:, 0:2048], um=umB, xbf=xbfB, inj=inj[1], k="B", extra=["ybuf"], pS=[pb[4], pb[5]], psk=["p4", "p5"], rtab=rtab)
            for cp in range(KC // 2):
                c0, c1 = 2 * cp, 2 * cp + 1
                for un in range(nun):
                    gens = [s5_unit(c0, un, SA), s5_unit(c1, un, SB)]
                    while gens:
                        for g_ in list(gens):
                            try:
                                next(g_)
                            except StopIteration:
                                gens.remove(g_)
                s5_post(c0)
                s5_post(c1)
            for oc in range(KC):
                ii = wcnt["i"] % 2
                wcnt["i"] += 1
                dma(wir[ii][:], W["s5_glu_w"][:, oc * 128:(oc + 1) * 128].rearrange("(c p) f -> p c f", p=128), writes=["wi%d" % ii])
                po = pb[4 + oc % 2]
                pk = "p%d" % (4 + oc % 2)
                for c in range(KC):
                    mm(po[:, 0:Tt], wir[ii][:, c, :], gT[:, c, 0:Tt], c == 0, c == KC - 1, ["wi%d" % ii, "h%d" % c], [pk])
                E("act", CALL("activation", tA[:, 0:Tt], po[:, 0:Tt], AF.Sigmoid, bias=V("s5_glu_b", oc)), [pk, "vec"], ["tA"])
                E("dve", CALL("tensor_tensor", ybuf[:, oc, 0:Tt], gT[:, oc, 0:Tt], tA[:, 0:Tt], ALU.mult), ["tA", "h%d" % oc], ["ybuf"])
            rmsnorm_stats(lambda c: ybuf[:, c, 0:Tt], KC, Tt, ["ybuf"], onesD)
            for c in range(KC):
                E("dve", CALL("scalar_tensor_tensor", mixed[:, c, 0:Tt], ybuf[:, c, 0:Tt], V("out_norm_s5", c), rstd[:, 0:Tt], ALU.mult, ALU.mult),
                  ["ybuf", "vec", "rstd"], ["mixed"])

        def mlstm_mix(Tt):
            qT = hT
            for c in range(KC):
                eng = "dve"
                E(eng, CALL("tensor_scalar", tA[:, 0:Tt], xmh[:, c, 0:Tt], V("cw0", c), 0.0, ALU.mult, ALU.add), ["xmh", "vec"], ["tA"])
                for j in range(1, 4):
                    E(eng, CALL("scalar_tensor_tensor", tA[:, 0:Tt], xmh[:, c, j:j + Tt], V("cw%d" % j, c), tA[:, 0:Tt], ALU.mult, ALU.add),
                      ["xmh", "vec", "tA"], ["tA"])
                E("act", CALL("activation", xc_bf[:, c, 0:Tt], tA[:, 0:Tt], AF.Silu, bias=V("ml_conv_b", c)), ["tA", "vec"], ["xn"])
            for c in range(KC):
                E("act", CALL("activation", z_bf[:, c, 0:Tt], z_bf[:, c, 0:Tt], AF.Silu), ["z_bf"], ["z_bf"])
            for c in range(KC):
                for wi, (src, dstT, key) in enumerate([(xc_bf[:, c, 0:Tt], qT[:, c, 0:Tt], "h%d" % c),
                                                       (xc_bf[:, c, 0:Tt], qT[:, 8 + c, 0:Tt], "h%d" % (8 + c)),
                                                       (xmh[:, c, 3:3 + Tt], vT[:, c, 0:Tt], "h%d" % (16 + c))]):
                    po = pb[4 + (3 * c + wi) % 2]
                    pk = "p%d" % (4 + (3 * c + wi) % 2)
                    mm(po[:, 0:Tt], BD[:, wi, c, :], src, True, True, ["BD", "xn", "xmh"], [pk])
                    if wi == 1:
                        E("dve", CALL("tensor_copy", dstT, po[:, 0:Tt]), [pk], [key])
                    else:
                        E("act", CALL("copy", dstT, po[:, 0:Tt]), [pk], [key])
            for gi, (gw, pbk) in enumerate([(gwi, 4), (gwf, 5)]):
                for j in range(24):
                    src = qT[:, j, 0:Tt]
                    key = "h%d" % j
                    mm(pb[pbk][0:4, 0:Tt], gw[:, j, :], src, j == 0, j == 23, ["gwi", "gwf", key], ["p%d" % pbk])
            E("act", CALL("activation", igs[:, 0:Tt], pb[4][0:4, 0:Tt], AF.Identity, bias=igb[:, 0:1]), ["p4", "igb"], ["igs"])
            E("act", CALL("activation", MM[:, 0:Tt], pb[5][0:4, 0:Tt], AF.Exp, bias=nfgb[:, 0:1], scale=-1.0), ["p5", "nfgb"], ["MM"])
            E("act", CALL("activation", MM[:, 0:Tt], MM[:, 0:Tt], AF.Ln, bias=onecol[0:4, 0:1]), ["MM", "onecol"], ["MM"])
            E("dve", CALL("tensor_tensor_scan", Fn[:, 0:Tt], bc_row(onecol[0:4, 0:1], Tt), MM[:, 0:Tt], 0.0, ALU.mult, ALU.add), ["MM", "onecol"], ["Fn"])
            E("dve", CALL("tensor_tensor", aa[:, 0:Tt], igs[:, 0:Tt], Fn[:, 0:Tt], ALU.add), ["igs", "Fn"], ["aa"])
            E("dve", CALL("tensor_tensor_scan", MM[:, 0:Tt], bc_row(onecol[0:4, 0:1], Tt), aa[:, 0:Tt], mprev[:, 0:1], ALU.mult, ALU.max),
              ["aa", "onecol", "mprev"], ["MM"])
            E("dve", CALL("tensor_copy", Mpc[:], mprev[:]), ["mprev"], ["Mpc"])
            Lc = min(128, Tt)
            for ch in range(Tt // Lc):
                t0, t1 = ch * Lc, ch * Lc + Lc
                E("dve", CALL("tensor_scalar", negM[:], MM[:, t1 - 1:t1], -1.0, 0.0, ALU.mult, ALU.add), ["MM"], ["negM"])
                E("act", CALL("activation", g4[0][:, 0:Lc], aa[:, t0:t1], AF.Exp, bias=negM[:, 0:1]), ["aa", "negM"], ["g40"])
                E("act", CALL("activation", gcol[:], Mpc[:], AF.Exp, bias=negM[:, 0:1]), ["Mpc", "negM"], ["gcol"])
                E("dve", CALL("tensor_scalar", g4[1][:, 0:Lc], Fn[:, t0:t1], negM[:, 0:1], 0.0, ALU.add, ALU.add), ["Fn", "negM"], ["g41"])
                E("dve", CALL("tensor_copy", Mpc[:], MM[:, t1 - 1:t1]), ["MM", "gcol"], ["Mpc"])
                tr(pb[7][0:Lc, 128:132], g4[0][:, 0:Lc], ident[0:4, 0:4], ["g40"], ["p7"])
                E("dve", CALL("tensor_copy", et[0:Lc, :], pb[7][0:Lc, 128:132]), ["p7"], ["et"])
                E("dve", CALL("tensor_scalar", dg[:], ident[0:4, 0:4], gcol[:, 0:1], 0.0, ALU.mult, ALU.add), ["ident", "gcol"], ["dg"])
                mm(pb[7][:, 132:136], ones4[:], dg[:], True, True, ["ones4", "dg"], ["p7"])
                E("dve", CALL("tensor_copy", gb[:], pb[7][:, 132:136]), ["p7"], ["gb"])
                for c in range(KC):
                    mm(pb[c // 4][0:Lc, (c % 4) * 128:(c % 4) * 128 + 128], xc_bf[:, c, t0:t1], BD[:, 1, c, :], True, True, ["xn", "BD"], ["p%d" % (c // 4)])
                    mm(pb[2 + c // 4][0:Lc, (c % 4) * 128:(c % 4) * 128 + 128], xmh[:, c, 3 + t0:3 + t1], BD[:, 2, c, :], True, True, ["xmh", "BD"], ["p%d" % (2 + c // 4)])
                def SET(pr):
                    if pr == 0:
                        return dict(ktp=ktp, vtp=vtp, vtb=vtb, Sm=Sm, Eb=Eb, Cg=Cg, nrep=nrep, hh=hh, hq=hq, ngc=ngc2[:, 0, :], X=[], k="0")
                    vv = hT[:, 16:20, :]
                    return dict(ktp=vv[:, 0, 0:256], vtp=vv[:, 0, 256:512], vtb=vv[:, 1, 0:256], Sm=vv[:, 1, 256:384], Eb=vv[:, 1, 384:512],
                                Cg=vv[:, 2, :].rearrange("p (a b) -> p a b", a=2), nrep=vv[:, 3, 0:256].rearrange("p (a b) -> p a b", a=2),
                                hh=tAB[:, 0:256].rearrange("p (a b) -> p a b", a=2), hq=tAB[:, 256:512].rearrange("p (a b) -> p a b", a=2),
                                ngc=ngc2[:, 1, :], X=["h16", "h17", "h18", "h19", "tA"], k="1")

                def prep(h):
                    S_ = SET(h % 2)
                    X = S_["X"]
                    N = lambda n: "ml%s%s" % (n, S_["k"])
                    kps = pb[h // 2][0:Lc, (h % 2) * 256:(h % 2) * 256 + 256]
                    vps = pb[2 + h // 2][0:Lc, (h % 2) * 256:(h % 2) * 256 + 256]
                    kk, vk = "p%d" % (h // 2), "p%d" % (2 + h // 2)
                    for kc in range(2):
                        mm(pb[7][0:Lc, 0:Lc], qT[:, 8 + 2 * h + kc, t0:t1], qT[:, 2 * h + kc, t0:t1], kc == 0, kc == 1,
                           ["h%d" % (8 + 2 * h + kc), "h%d" % (2 * h + kc), "p7"], ["p7s"])
                    E("dve", CALL("scalar_tensor_tensor", S_["Sm"][0:Lc, 0:Lc], pb[7][0:Lc, 0:Lc], 1.0 / 16, triu[0:Lc, 0:Lc], ALU.mult, ALU.mult), ["p7s", "p7", "triu"] + X, [N("Sm")])
                    E("dve", CALL("tensor_scalar", S_["vtp"][0:Lc, :], vps, et[0:Lc, h:h + 1], 0.0, ALU.mult, ALU.add), [vk, "et"] + X, [N("vtp")])
                    E("act", CALL("copy", S_["vtb"][0:Lc, :], vps), [vk] + X, [N("vtb")])
                    E("dve", CALL("tensor_scalar", S_["ktp"][0:Lc, :], kps, et[0:Lc, h:h + 1], 1.0 / 16, ALU.mult, ALU.mult), [kk, "et"] + X, [N("ktp")])
                    E("pool", CALL("tensor_scalar", S_["Eb"][0:Lc, :], ones_bf[0:Lc, :], et[0:Lc, h:h + 1], 0.0, ALU.mult, ALU.add), ["ones_bf", "et"] + X, [N("Eb")])
                    E("pool", CALL("tensor_scalar", S_["Cg"].rearrange("p a b -> p (a b)"), CT[:, h, :, :].rearrange("p a b -> p (a b)"), gb[:, h:h + 1], 0.0, ALU.mult, ALU.add),
                      ["CT%d" % h, "gb"] + X, [N("Cg")])
                    E("dve", CALL("tensor_scalar", S_["ngc"], nT[:, 2 * h:2 * h + 2], gb[:, h:h + 1], 0.0, ALU.mult, ALU.add), ["nT%d" % h, "gb"], [N("ngc")])
                    for kc in range(2):
                        E("pool", CALL("tensor_scalar", S_["nrep"][:, kc, :], ones_bf[:, :], S_["ngc"][:, kc:kc + 1], 0.0, ALU.mult, ALU.add), ["ones_bf", N("ngc")] + X, [N("nrep")])

                def mid(h):
                    S_ = SET(h % 2)
                    X = S_["X"]
                    N = lambda n: "ml%s%s" % (n, S_["k"])
                    hh_, hq_ = S_["hh"], S_["hq"]
                    for vc in range(2):
                        o = pb[4][:, vc * 128:vc * 128 + Lc]
                        mm(o, S_["vtp"][0:Lc, vc * 128:vc * 128 + 128], S_["Sm"][0:Lc, 0:Lc], True, False, [N("vtp"), N("Sm")] + X, ["p4"])
                        for kc in range(2):
                            mm(o, S_["Cg"][:, kc, vc * 128:vc * 128 + 128], qT[:, 2 * h + kc, t0:t1], False, kc == 1, [N("Cg"), "h%d" % (2 * h + kc)] + X, ["p4"])
                    o = pb[4][:, 256:256 + Lc]
                    mm(o, S_["Eb"][0:Lc, :], S_["Sm"][0:Lc, 0:Lc], True, False, [N("Eb"), N("Sm")] + X, ["p4"])
                    for kc in range(2):
                        mm(o, S_["nrep"][:, kc, :], qT[:, 2 * h + kc, t0:t1], False, kc == 1, [N("nrep"), "h%d" % (2 * h + kc)] + X, ["p4"])
                    mm(pb[4][:, 384:384 + Lc], sel4[:, h * 128:h * 128 + 128], g4[1][:, 0:Lc], True, True, ["sel4", "g41"], ["p4"])
                    E("act", CALL("activation", mw[0][:, 0:Lc], pb[4][:, 384:384 + Lc], AF.Exp), ["p4"], ["mw0"])
                    E("act", CALL("activation", mwab[:, 0:Lc], pb[4][:, 256:256 + Lc], AF.Abs), ["p4"], ["mwab"])
                    E("dve", CALL("tensor_tensor", mw[0][:, 0:Lc], mwab[:, 0:Lc], mw[0][:, 0:Lc], ALU.max), ["mwab", "mw0"], ["mw0"])
                    E("dve", CALL("reciprocal", mw[0][:, 0:Lc], mw[0][:, 0:Lc]), ["mw0"], ["mw0"])
                    for vc in range(2):
                        E("dve", CALL("tensor_tensor", hh_[:, vc, 0:Lc], pb[4][:, vc * 128:vc * 128 + Lc], mw[0][:, 0:Lc], ALU.mult), ["p4", "mw0"] + X, [N("hh")])
                        E("act", CALL("activation", hq_[:, vc, 0:Lc], hh_[:, vc, 0:Lc], AF.Square), [N("hh")] + X, [N("hq")])
                    for kc in range(2):
                        mm(pb[6][:, kc * 256:kc * 256 + 256], S_["ktp"][0:Lc, kc * 128:kc * 128 + 128], S_["vtb"][0:Lc, :], True, True, [N("ktp"), N("vtb")] + X, ["p6"])
                    E("dve", CALL("scalar_tensor_tensor", CT[:, h, :, :].rearrange("p a b -> p (a b)"), CT[:, h, :, :].rearrange("p a b -> p (a b)"),
                                  gb[:, h:h + 1], pb[6][:, :], ALU.mult, ALU.add), ["CT%d" % h, "gb", "p6", N("Cg")], ["CT%d" % h])
                    for kc in range(2):
                        mm(pb[7][:, 136 + kc:137 + kc], S_["ktp"][0:Lc, kc * 128:kc * 128 + 128], ones_bf[0:Lc, 0:1], True, True, [N("ktp"), "ones_bf", "p7"] + X, ["p7n"])
                    E("dve", CALL("tensor_tensor", nT[:, 2 * h:2 * h + 2], S_["ngc"], pb[7][:, 136:138], ALU.add), [N("ngc"), "p7n", "p7"], ["nT%d" % h])

                def fin(h):
                    S_ = SET(h % 2)
                    X = S_["X"]
                    N = lambda n: "ml%s%s" % (n, S_["k"])
                    hh_, hq_ = S_["hh"], S_["hq"]
                    for vc in range(2):
                        mm(pb[5][:, 0:Lc], ones256[:], hh_[:, vc, 0:Lc], vc == 0, vc == 1, ["ones256", N("hh")] + X, ["p5"])
                    for vc in range(2):
                        mm(pb[5][:, 128:128 + Lc], ones256[:], hq_[:, vc, 0:Lc], vc == 0, vc == 1, ["ones256", N("hq")] + X, ["p5"])
                    E("act", CALL("activation", mw[1][:, 0:Lc], pb[5][:, 0:Lc], AF.Square), ["p5"], ["mw1"])
                    E("dve", CALL("tensor_tensor", mw[1][:, 0:Lc], pb[5][:, 128:128 + Lc], mw[1][:, 0:Lc], ALU.subtract), ["p5", "mw1"], ["mw1"])
                    E("dve", CALL("tensor_scalar", mw[1][:, 0:Lc], mw[1][:, 0:Lc], 0.0, 0.0, ALU.max, ALU.add), ["mw1"], ["mw1"])
                    E("act", CALL("activation", mw[1][:, 0:Lc], mw[1][:, 0:Lc], AF.Sqrt, bias=epscol[:, 0:1]), ["mw1", "epscol"], ["mw1"])
                    E("dve", CALL("reciprocal", mw[1][:, 0:Lc], mw[1][:, 0:Lc]), ["mw1"], ["mw1"])
                    for vc in range(2):
                        c = 2 * h + vc
                        E("dve", CALL("tensor_tensor", mw[2][:, 0:Lc], hh_[:, vc, 0:Lc], pb[5][:, 0:Lc], ALU.subtract), [N("hh"), "p5"] + X, ["mw2"])
                        E("pool", CALL("tensor_tensor", mw[2][:, 0:Lc], mw[2][:, 0:Lc], mw[1][:, 0:Lc], ALU.mult), ["mw2", "mw1"], ["mw2"])
                        E("pool", CALL("tensor_scalar", mw[2][:, 0:Lc], mw[2][:, 0:Lc], V("ml_norm_w", c), 0.0, ALU.mult, ALU.add), ["mw2", "vec"], ["mw2"])
                        E("dve", CALL("scalar_tensor_tensor", mw[3][:, 0:Lc], xc_bf[:, c, t0:t1], V("ml_skip", c), mw[2][:, 0:Lc], ALU.mult, ALU.add),
                          ["xn", "vec", "mw2"], ["mw3"])
                        E("pool", CALL("tensor_tensor", ybuf[:, c, t0:t1], mw[3][:, 0:Lc], z_bf[:, c, t0:t1], ALU.mult), ["mw3", "z_bf"], ["ybuf"])

                for blk in [(prep, 0), (prep, 1), (mid, 0), (prep, 2), (mid, 1), (fin, 0), (prep, 3), (mid, 2), (fin, 1), (mid, 3), (fin, 2), (fin, 3)]:
                    blk[0](blk[1])
            E("dve", CALL("tensor_tensor", mprev[:], MM[:, Tt - 1:Tt], Fn[:, Tt - 1:Tt], ALU.subtract), ["MM", "Fn", "Mpc"], ["mprev"])
            for c in range(KC):
                E("act", CALL("copy", xmh[:, c, 0:3], xmh[:, c, Tt:Tt + 3]), ["xmh"], ["xmh"])
            rmsnorm_stats(lambda c: ybuf[:, c, 0:Tt], KC, Tt, ["ybuf"], onesD)
            for c in range(KC):
                E("dve", CALL("scalar_tensor_tensor", mixed[:, 8 + c, 0:Tt], ybuf[:, c, 0:Tt], V("out_norm_ml", c), rstd[:, 0:Tt], ALU.mult, ALU.mult),
                  ["ybuf", "vec", "rstd"], ["mixed"])

        def out_proj(Tt):
            for c in range(KC):
                po = pb[4 + c % 2]
                pk = "p%d" % (4 + c % 2)
                for hf in range(2):
                    oi = hf
                    dma(wor[oi][:], W["w_out"][hf * 1024:(hf + 1) * 1024, c * 128:(c + 1) * 128].rearrange("(k p) d -> p k d", p=128), writes=["wo%d" % oi])
                    for k2 in range(8):
                        k = hf * 8 + k2
                        mm(po[:, 0:Tt], wor[oi][:, k2, :], mixed[:, k, 0:Tt], k == 0, k == 15, ["wo%d" % oi, "mixed"], [pk])
                E("dve", CALL("tensor_tensor", xT[:, c, 0:Tt], xT[:, c, 0:Tt], po[:, 0:Tt], ALU.add), [pk, "xT"], ["xT"])

        def load_tile(src_rows, Tt):
            nsub = (Tt + 127) // 128
            for n in range(nsub):
                r = min(128, Tt - n * 128)
                dma(xtok[0:r, n, :], src_rows[n * 128:n * 128 + r, :], writes=["ybuf"])
            for c in range(KC):
                for n in range(nsub):
                    r = min(128, Tt - n * 128)
                    tr(pb[7][:, n * 128:n * 128 + r], xtok[0:r, n, c * 128:(c + 1) * 128], ident[0:r, 0:r], ["ybuf"], ["p7"])
                E("dve" if c % 2 == 0 else "act",
                  (CALL("tensor_copy", xT[:, c, 0:Tt], pb[7][:, 0:Tt])) if c % 2 == 0 else (CALL("copy", xT[:, c, 0:Tt], pb[7][:, 0:Tt])),
                  ["p7"], ["xT"])

        def store_tile(dst_rows, Tt):
            rmsnorm_stats(lambda c: xT[:, c, 0:Tt], KC, Tt, ["xT"], onesD)
            nsub = (Tt + 127) // 128
            for c in range(KC):
                E("dve", CALL("scalar_tensor_tensor", ybuf[:, c, 0:Tt], xT[:, c, 0:Tt], V("norm_final", c), rstd[:, 0:Tt], ALU.mult, ALU.mult),
                  ["xT", "vec", "rstd"], ["ybuf"])
            for n in range(nsub):
                r = min(128, Tt - n * 128)
                for c4 in range(2):
                    for c in range(c4 * 4, c4 * 4 + 4):
                        tr(pb[7][0:r, (c % 4) * 128:(c % 4) * 128 + 128], ybuf[:, c, n * 128:n * 128 + r], ident[:], ["ybuf"], ["p7"])
                    E("act", CALL("copy", otok[0:r, c4 * 512:(c4 + 1) * 512], pb[7][0:r, :]), ["p7"], ["tA", "tB"])
                outs.append(dma(dst_rows[n * 128:n * 128 + r, :], otok[0:r, :], reads=["tA", "tB"], q="sp"))

        def init_state(si):
            if si is None:
                for t, k in [(sre, SRE), (sim, SRE), (nT, ["nT0", "nT1", "nT2", "nT3"]), (mprev, ["mprev"])]:
                    E("dve", CALL("memset", t[:], 0.0), [], k)
                E("pool", CALL("memset", CT[:].rearrange("p a b c -> p (a b c)"), 0.0), [], ["CT0", "CT1", "CT2", "CT3"])
                E("pool", CALL("memset", xmh[:, :, 0:3], 0.0), [], ["xmh"])
                return
            for src, dst, k in [(st_s5re, sre, SRE), (st_s5im, sim, SIM)]:
                dma(stg[0:32, 0:128], src[si], writes=["stg"])
                tr(pb[7][:, 0:32], stg[0:32, 0:128], ident[0:32, 0:32], ["stg"], ["p7"])
                E("dve", CALL("tensor_copy", dst[:], pb[7][:, 0:32]), ["p7"], k)
            dma(stg[0:8, 0:128], st_n[si], writes=["stg"])
            tr(pb[7][:, 0:8], stg[0:8, 0:128], ident[0:8, 0:8], ["stg"], ["p7"])
            E("dve", CALL("tensor_copy", nT[:], pb[7][:, 0:8]), ["p7"], ["nT0", "nT1", "nT2", "nT3"])
            dma(mprev[:], st_m[si], writes=["mprev"])
            for c in range(KC):
                dma(stg[0:3, 0:128], st_conv[si][:, c * 128:(c + 1) * 128], writes=["stg"])
                tr(pb[7][:, 0:3], stg[0:3, 0:128], ident[0:3, 0:3], ["stg"], ["p7"])
                E("dve", CALL("tensor_copy", xmh[:, c, 0:3], pb[7][:, 0:3]), ["p7"], ["xmh"])
            for h in range(4):
                for vc in range(2):
                    dma(stg[:, 0:256], st_c[si, h, vc * 128:(vc + 1) * 128, :], writes=["stg"])
                    for kc in range(2):
                        tr(pb[7][:, kc * 128:kc * 128 + 128], stg[:, kc * 128:kc * 128 + 128], ident[:], ["stg"], ["p7"])
                    E("dve", CALL("tensor_copy", CT[:, h, :, vc * 128:vc * 128 + 128], pb[7][:, 0:256].rearrange("p (k v) -> p k v", k=2)), ["p7"], ["CT0", "CT1", "CT2", "CT3"])

        def store_state(oi):
            for src, dst, k in [(sre, o_s5re, SRE), (sim, o_s5im, SIM)]:
                tr(pb[7][0:32, 0:128], src[:], ident[:], k, ["p7"])
                E("dve", CALL("tensor_copy", stg[0:32, 0:128], pb[7][0:32, 0:128]), ["p7"], ["stg"])
                outs.append(dma(dst[oi], stg[0:32, 0:128], reads=["stg"], q="sp"))
            tr(pb[7][0:8, 0:128], nT[:], ident[:], ["nT0", "nT1", "nT2", "nT3"], ["p7"])
            E("dve", CALL("tensor_copy", stg[0:8, 0:128], pb[7][0:8, 0:128]), ["p7"], ["stg"])
            outs.append(dma(o_n[oi], stg[0:8, 0:128], reads=["stg"], q="sp"))
            outs.append(dma(o_m[oi], mprev[:], reads=["mprev"], q="sp"))
            for c in range(KC):
                E("dve", CALL("tensor_copy", mw[0][:, 0:3], xmh[:, c, 0:3]), ["xmh"], ["mw0"])
                tr(pb[7][0:3, 0:128], mw[0][:, 0:3], ident[:], ["mw0"], ["p7"])
                E("dve", CALL("tensor_copy", stg[0:3, 0:128], pb[7][0:3, 0:128]), ["p7"], ["stg"])
                outs.append(dma(o_conv[oi][:, c * 128:(c + 1) * 128], stg[0:3, 0:128], reads=["stg"], q="sp"))
            for h in range(4):
                for vc in range(2):
                    for kc in range(2):
                        tr(pb[7][:, kc * 128:kc * 128 + 128], CT[:, h, kc, vc * 128:vc * 128 + 128], ident[:], ["CT0", "CT1", "CT2", "CT3"], ["p7"])
                    E("dve", CALL("tensor_copy", stg[:, 0:256], pb[7][:, 0:256]), ["p7"], ["stg"])
                    outs.append(dma(o_c[oi, h, vc * 128:(vc + 1) * 128, :], stg[:, 0:256], reads=["stg"], q="sp"))

        def run_tile(src_rows, dst_rows, Tt):
            load_tile(src_rows, Tt)
            if STAGE >= 2:
                ffn("ffn1", "norm_ffn1", Tt)
            if STAGE >= 3:
                in_proj(Tt)
            if STAGE >= 4:
                s5_mix(Tt)
            if STAGE >= 5:
                mlstm_mix(Tt)
            if STAGE >= 6:
                out_proj(Tt)
            if STAGE >= 7:
                ffn("ffn2", "norm_ffn2", Tt)
            if dst_rows is not None:
                store_tile(dst_rows, Tt)

        if STAGE == 0:
            P.emit(final_waits=outs)
            return nc
        init_state(None)
        run_tile(xp[0:NMETA, :], None, NMETA)
        for ti in range((NP - NMETA) // TT):
            r0 = NMETA + ti * TT
            run_tile(xp[r0:r0 + TT, :], yp[r0 - NMETA:r0 - NMETA + TT, :], TT)
        store_state(0)
        for si in range(NSAMP):
            init_state(si)
            run_tile(xs[si], ys[si], SL)
            store_state(1 + si)
        P.emit(final_waits=outs)
    return nc


def host_consts():
    p = np.arange(128)
    c = {}
    c["ident"] = np.eye(128, dtype=np.float32)
    c["maskE"] = np.stack([(p // 64 == 0), (p // 64 == 1)], 1).astype(np.float32)
    p32 = np.arange(32)
    c["mask16"] = np.stack([(p32 // 16 == 0), (p32 // 16 == 1)], 1).astype(np.float32)
    c["triu"] = np.triu(np.ones((128, 128), np.float32))
    c["bdmask"] = (p[:, None] // 4 == np.arange(32)[None, :]).astype(np.float32)
    c["tvec"] = np.broadcast_to(np.arange(1, 65, dtype=np.float32)[None, :], (128, 64)).copy()
    s = np.zeros((4, 4, 128), np.float32)
    for h in range(4):
        s[h, h, :] = 1.0
    c["sel4"] = s.reshape(4, 512)
    c["mask4"] = (p[:, None] // 32 == np.arange(4)[None, :]).astype(np.float32)
    return c


_CACHE = {}


def kernel(**inp):
    f = lambda a: np.ascontiguousarray(np.asarray(a, dtype=np.float32))
    x_prompt = f(inp["x_prompt"])
    x_sample = f(inp["x_sample"])
    NB, SEQ, _ = x_prompt.shape
    NDEC, SL, _ = x_sample.shape
    NP = NMETA + SEQ
    ncores = 8
    NSAMP = NDEC // ncores
    key = (NP, NSAMP, SL)
    if key not in _CACHE:
        _CACHE[key] = build_program(NP, NSAMP, SL)
    nc = _CACHE[key]
    meta = f(inp["meta_tokens"])
    shared = host_consts()
    for n in ["ffn1_gate", "ffn1_up", "ffn1_down", "ffn2_gate", "ffn2_up", "ffn2_down", "w_in", "s5_glu_w", "w_out"]:
        shared[n] = f(inp[n])[0]
    shared["lam_re"] = f(inp["s5_lambda_re"])[0].reshape(32, 128)
    shared["lam_im"] = f(inp["s5_lambda_im"])[0].reshape(32, 128)
    shared["log_dt"] = f(inp["s5_log_dt"])[0]
    shared["b_re"] = f(inp["s5_b_re"])[0]
    shared["b_im"] = f(inp["s5_b_im"])[0]
    shared["c_re"] = f(inp["s5_c_re"])[0]
    shared["c_im"] = f(inp["s5_c_im"])[0]
    shared["wq"] = f(inp["ml_wq"])[0]
    shared["wk"] = f(inp["ml_wk"])[0]
    shared["wv"] = f(inp["ml_wv"])[0]
    shared["igw"] = f(inp["ml_igate_w"])[0]
    shared["fgw"] = f(inp["ml_fgate_w"])[0]
    shared["igb"] = f(inp["ml_igate_b"])[0].reshape(4, 1)
    shared["fgb"] = f(inp["ml_fgate_b"])[0].reshape(4, 1)
    cw = f(inp["ml_conv_w"])[0]
    vd = {"norm_ffn1": f(inp["norm_ffn1"])[0], "norm_mix": f(inp["norm_mix"])[0], "s5_d": f(inp["s5_d"])[0],
          "s5_glu_b": f(inp["s5_glu_b"])[0], "cw0": cw[0], "cw1": cw[1], "cw2": cw[2], "cw3": cw[3],
          "ml_conv_b": f(inp["ml_conv_b"])[0], "ml_norm_w": f(inp["ml_norm_w"])[0], "ml_skip": f(inp["ml_skip"])[0],
          "out_norm_s5": f(inp["out_norm_s5"])[0], "out_norm_ml": f(inp["out_norm_ml"])[0],
          "norm_ffn2": f(inp["norm_ffn2"])[0], "norm_final": f(inp["norm_final"])}
    shared["vecs"] = np.concatenate([vd[n].reshape(8, 128) for n in VEC_NAMES], 0)
    s5re, s5im = f(inp["state_s5_re"])[0], f(inp["state_s5_im"])[0]
    stc, stn, stm, stcv = f(inp["state_mlstm_c"])[0], f(inp["state_mlstm_n"])[0], f(inp["state_mlstm_m"])[0], f(inp["state_mlstm_conv"])[0]
    in_maps = []
    for c in range(ncores):
        b = c % NB
        sl = slice(c * NSAMP, (c + 1) * NSAMP)
        m = dict(shared)
        m["xp"] = np.concatenate([meta, x_prompt[b]], 0)
        m["xs"] = x_sample[sl]
        m["st_s5re"] = s5re[sl].reshape(NSAMP, 32, 128)
        m["st_s5im"] = s5im[sl].reshape(NSAMP, 32, 128)
        m["st_c"] = stc[sl]
        m["st_n"] = stn[sl].reshape(NSAMP, 8, 128)
        m["st_m"] = stm[sl].reshape(NSAMP, 4, 1)
        m["st_conv"] = stcv[sl]
        in_maps.append(m)
    res = run_bass_kernel_spmd(nc, in_maps, core_ids=list(range(ncores))).results
    y_prompt = np.stack([res[b]["yp"] for b in range(NB)], 0)
    y_sample = np.concatenate([res[c]["ys"] for c in range(ncores)], 0)

    def gather(name, shape_tail):
        pr = np.stack([res[b][name][0] for b in range(NB)], 0).reshape((1, NB) + shape_tail)
        sm = np.concatenate([res[c][name][1:] for c in range(ncores)], 0).reshape((1, NDEC) + shape_tail)
        return pr.astype(np.float32), sm.astype(np.float32)

    p_re, s_re = gather("o_s5re", (64, 64))
    p_im, s_im = gather("o_s5im", (64, 64))
    p_c, s_c = gather("o_c", (4, 256, 256))
    p_n, s_n = gather("o_n", (4, 256))
    p_m, s_m = gather("o_m", (4,))
    p_cv, s_cv = gather("o_conv", (3, 1024))
    return (y_prompt.astype(np.float32), y_sample.astype(np.float32), p_re, p_im, p_c, p_n, p_m, p_cv,
            s_re, s_im, s_c, s_n, s_m, s_cv)
```

```python
import math
import numpy as np
import concourse.bass as bass
import concourse.mybir as mybir
from concourse.bass_utils import run_bass_kernel_spmd
from contextlib import ExitStack

F32 = mybir.dt.float32
BF16 = mybir.dt.bfloat16
I32 = mybir.dt.int32
ALU = mybir.AluOpType
AF = mybir.ActivationFunctionType

D = 1024
DFF = 2816
KC = 8
FC = 22
NMETA = 16
EPS = 1e-6
STAGE = 9
ENGS = ("pe", "act", "dve", "pool", "sp")
NDMA_SEM = 12
VEC_NAMES = ["norm_ffn1", "norm_mix", "s5_d", "s5_glu_b", "cw0", "cw1", "cw2", "cw3", "ml_conv_b",
             "ml_norm_w", "ml_skip", "out_norm_s5", "out_norm_ml", "norm_ffn2", "norm_final"]
VI = {n: i for i, n in enumerate(VEC_NAMES)}


class Op:
    __slots__ = ("eng", "fn", "deps", "idx", "signaled", "sigval", "dma", "dsem", "dval", "dprev")

    def __init__(self, eng, fn, dma):
        self.eng = eng
        self.fn = fn
        self.deps = []
        self.signaled = False
        self.sigval = 0
        self.dma = dma
        self.dsem = None
        self.dval = 0
        self.dprev = None


class Prog:
    def __init__(self, nc):
        self.nc = nc
        self.ops = {e: [] for e in ENGS}
        self.last_writer = {}
        self.readers = {}
        self.ndma = {e: 0 for e in ENGS}
        self.dma_ops = {e: [] for e in ENGS}

    def op(self, eng, fn, reads=(), writes=(), dma=False):
        o = Op(eng, fn, dma)
        deps = []
        for r in reads:
            w = self.last_writer.get(r)
            if w is not None:
                deps.append(w)
        for wr in writes:
            w = self.last_writer.get(wr)
            if w is not None:
                deps.append(w)
            deps.extend(self.readers.get(wr, ()))
        seen = set()
        for d in deps:
            if id(d) in seen or d is o:
                continue
            seen.add(id(d))
            if d.eng == "pe" and eng == "pe" and not d.dma and not dma:
                continue
            o.deps.append(d)
        for r in reads:
            self.readers.setdefault(r, []).append(o)
        for wr in writes:
            self.last_writer[wr] = o
            self.readers[wr] = []
        if dma:
            k = self.ndma[eng]
            self.ndma[eng] += 1
            o.dsem = k % NDMA_SEM
            o.dval = 16 * (k // NDMA_SEM + 1)
            if k >= NDMA_SEM:
                o.dprev = self.dma_ops[eng][k - NDMA_SEM]
            self.dma_ops[eng].append(o)
        o.idx = len(self.ops[eng])
        self.ops[eng].append(o)
        return o

    def emit(self, final_waits=()):
        nc = self.nc
        for e in ENGS:
            for o in self.ops[e]:
                for d in o.deps:
                    if not d.dma:
                        d.signaled = True
        for o in final_waits:
            if not o.dma:
                o.signaled = True
        for e in ENGS:
            c = 0
            for o in self.ops[e]:
                if o.signaled and not o.dma:
                    c += 1
                    o.sigval = c
        with ExitStack() as st:
            esem = {e: st.enter_context(nc.semaphore("s_" + e)) for e in ENGS}
            dsem = {e: [st.enter_context(nc.semaphore("d_%s_%d" % (e, i))) for i in range(NDMA_SEM)]
                    for e in ENGS if self.ndma[e] > 0}
            block = st.enter_context(nc.Block())

            def body(e, engine):
                observed = {}

                def wait(key, sem, val):
                    if observed.get(key, 0) >= val:
                        return
                    observed[key] = val
                    engine.wait_ge(sem, val)

                for o in self.ops[e]:
                    for d in o.deps:
                        if d.dma:
                            wait(("d", d.eng, d.dsem), dsem[d.eng][d.dsem], d.dval)
                        else:
                            wait(("e", d.eng), esem[d.eng], d.sigval)
                    if o.dma and o.dprev is not None:
                        wait(("d", e, o.dprev.dsem), dsem[e][o.dprev.dsem], o.dprev.dval)
                    ins = o.fn(engine)
                    if o.dma:
                        ins.then_inc(dsem[e][o.dsem], 16)
                    elif o.signaled:
                        ins.then_inc(esem[e], 1)
                if e == "sp":
                    for o in final_waits:
                        if o.dma:
                            wait(("d", o.eng, o.dsem), dsem[o.eng][o.dsem], o.dval)
                        else:
                            wait(("e", o.eng), esem[o.eng], o.sigval)

            block.sync(lambda eng: body("sp", eng))
            block.scalar(lambda eng: body("act", eng))
            block.vector(lambda eng: body("dve", eng))
            block.gpsimd(lambda eng: body("pool", eng))
            block.tensor(lambda eng: body("pe", eng))


def CALL(name, *args, **kw):
    return lambda e: getattr(e, name)(*args, **kw)


def bc_last(ap, n):
    return bass.AP(ap.tensor, ap.offset, [list(a) for a in ap.ap] + [[0, n]])


def bc_mid(ap, n):
    a = [list(x) for x in ap.ap]
    return bass.AP(ap.tensor, ap.offset, [a[0], [0, n]] + a[1:])


def bc_row(ap, n):
    a = [list(x) for x in ap.ap]
    return bass.AP(ap.tensor, ap.offset, [a[0], [0, n]])


def build_program(NP, NSAMP=2, SL=32):
    nc = bass.Bass("TRN2", target_bir_lowering=False)
    dr = {}

    def din(name, shape, dt=F32):
        dr[name] = nc.dram_tensor(name, list(shape), dt, kind="ExternalInput").ap()
        return dr[name]

    def dout(name, shape):
        dr[name] = nc.dram_tensor(name, list(shape), F32, kind="ExternalOutput").ap()
        return dr[name]

    xp = din("xp", [NP, D])
    xs = din("xs", [NSAMP, SL, D])
    st_s5re = din("st_s5re", [NSAMP, 32, 128])
    st_s5im = din("st_s5im", [NSAMP, 32, 128])
    st_c = din("st_c", [NSAMP, 4, 256, 256])
    st_n = din("st_n", [NSAMP, 8, 128])
    st_m = din("st_m", [NSAMP, 4, 1])
    st_conv = din("st_conv", [NSAMP, 3, D])
    vecs = din("vecs", [len(VEC_NAMES) * 8, 128])
    W = {}
    for n, shp in [("ffn1_gate", [D, DFF]), ("ffn1_up", [D, DFF]), ("ffn1_down", [DFF, D]),
                   ("ffn2_gate", [D, DFF]), ("ffn2_up", [D, DFF]), ("ffn2_down", [DFF, D]),
                   ("w_in", [D, 3 * D]), ("s5_glu_w", [D, D]), ("w_out", [2 * D, D]),
                   ("lam_re", [32, 128]), ("lam_im", [32, 128]), ("log_dt", [64]),
                   ("b_re", [64, 64, 16]), ("b_im", [64, 64, 16]), ("c_re", [64, 16, 64]), ("c_im", [64, 16, 64]),
                   ("wq", [256, 4, 4]), ("wk", [256, 4, 4]), ("wv", [256, 4, 4]),
                   ("igw", [3 * D, 4]), ("fgw", [3 * D, 4]), ("igb", [4, 1]), ("fgb", [4, 1]),
                   ("ident", [128, 128]), ("maskE", [128, 2]), ("mask16", [32, 2]), ("triu", [128, 128]),
                   ("bdmask", [128, 32]), ("tvec", [128, 64]), ("sel4", [4, 512]), ("mask4", [128, 4])]:
        W[n] = din(n, shp)
    NS = 1 + NSAMP
    yp = dout("yp", [NP - NMETA, D])
    ys = dout("ys", [NSAMP, SL, D])
    o_s5re = dout("o_s5re", [NS, 32, 128])
    o_s5im = dout("o_s5im", [NS, 32, 128])
    o_c = dout("o_c", [NS, 4, 256, 256])
    o_n = dout("o_n", [NS, 8, 128])
    o_m = dout("o_m", [NS, 4, 1])
    o_conv = dout("o_conv", [NS, 3, D])

    SG = {n: nc.dram_tensor("sg_" + n, [FC, 128, KC, 128], BF16).ap() for n in ["ffn1_gate", "ffn1_up", "ffn2_gate", "ffn2_up"]}
    SD = {n: nc.dram_tensor("sd_" + n, [KC, 2, 128, FC // 2, 128], BF16).ap() for n in ["ffn1_down", "ffn2_down"]}
    SI = nc.dram_tensor("s_win", [24, 128, KC, 128], BF16).ap()
    SGL = nc.dram_tensor("s_glu", [KC, 128, KC, 128], BF16).ap()
    SO = nc.dram_tensor("s_wout", [KC, 2, 128, 8, 128], BF16).ap()
    P = Prog(nc)
    outs = []
    TT = 512
    with ExitStack() as st:
        def sb(name, shape, dt=F32):
            return st.enter_context(nc.sbuf_tensor("sb_" + name, list(shape), dt))

        st.enter_context(nc.allow_low_precision("bf16 matmul operands with fp32 PSUM accumulation"))
        pb = [st.enter_context(nc.psum_tensor("pb%d" % i, [128, 512], F32)) for i in range(8)]

        xT = sb("xT", [128, KC, TT])
        xn = sb("xn", [128, KC, TT], BF16)
        xc_bf = xn
        hT = sb("hT", [128, 24, TT], BF16)
        u_bf = sb("u_bf", [128, KC, TT], BF16)
        z_bf = sb("z_bf", [128, KC, TT], BF16)
        xmh = sb("xmh", [128, KC, TT + 3], BF16)
        vT = hT[:, 16:24, :]
        mixed = sb("mixed", [128, 16, TT], BF16)
        ybuf = sb("ybuf", [128, KC, TT])
        xtok = ybuf[:].rearrange("p a b -> p (a b)").rearrange("p (n d) -> p n d", d=D)
        rstd = sb("rstd", [128, TT])
        tAB = sb("tAB", [128, 2 * TT])
        tA = tAB[:, 0:TT]
        tB = tAB[:, TT:2 * TT]
        otok = tAB
        wgr = [sb("wgr%d" % i, [128, KC, 128], BF16) for i in range(2)]
        wur = [sb("wur%d" % i, [128, KC, 128], BF16) for i in range(2)]
        wdr = [sb("wdr%d" % i, [128, FC // 2, 128], BF16) for i in range(2)]
        wir = [sb("wir%d" % i, [128, KC, 128], BF16) for i in range(2)]
        wor = [sb("wor%d" % i, [128, 8, 128], BF16) for i in range(2)]
        ident = sb("ident", [128, 128])
        ones_bf = sb("ones_bf", [128, 128], BF16)
        onesD = sb("onesD", [128, 128], BF16)
        ones256 = sb("ones256", [128, 128])
        ones4 = sb("ones4", [4, 128])
        onecol = sb("onecol", [128, 1])
        epscol = sb("epscol", [128, 1])
        vec = sb("vec", [128, len(VEC_NAMES) * 8])
        maskE = sb("maskE", [128, 2])
        mask16 = sb("mask16", [32, 2])
        triu = sb("triu", [128, 128])
        bdmask = sb("bdmask", [128, 32])
        tvec = sb("tvec", [128, 64])
        sel4 = sb("sel4", [4, 512])
        cosT = sb("cosT", [128, 32, 64])
        sinT = sb("sinT", [128, 32, 64])
        rmag = sb("rmag", [128, 32])
        W1 = sb("W1", [128, KC, 2, 128], BF16)
        W3 = sb("W3", [128, 32, 2, 32], BF16)
        s5st = sb("s5st", [128, 2, 32])
        sre = s5st[:, 0, :]
        sim = s5st[:, 1, :]
        s5all = sb("s5all", [128, 8, 256])
        s5w = [s5all[:, i, :].rearrange("p (a b) -> p a b", b=64) for i in range(8)]
        resetm = sb("resetm", [128, 64])
        inj = [sb("inj%d" % i, [128, 2, 2, 4]) for i in range(2)]
        xbfA = sb("xbfA", [128, 2, 4, 64], BF16)
        xbfB = sb("xbfB", [128, 2, 4, 64], BF16)
        um = sb("um", [128, 2, 4, 64], BF16)
        umB = sb("umB", [128, 2, 4, 64], BF16)
        mask4 = sb("mask4", [128, 4])
        CT = sb("CT", [128, 4, 2, 256])
        nT = sb("nT", [128, 8])
        mprev = sb("mprev", [4, 1])
        BD = sb("BD", [128, 3, KC, 128], BF16)
        gwi = sb("gwi", [128, 24, 4], BF16)
        gwf = sb("gwf", [128, 24, 4], BF16)
        igb = sb("igb", [4, 1])
        nfgb = sb("nfgb", [4, 1])
        igs = sb("igs", [4, TT])
        Fn = sb("Fn", [4, TT])
        aa = sb("aa", [4, TT])
        MM = sb("MM", [4, TT])
        g4 = [sb("g4_%d" % i, [4, 128]) for i in range(2)]
        negM = sb("negM", [4, 1])
        gcol = sb("gcol", [4, 1])
        Mpc = sb("Mpc", [4, 1])
        dg = sb("dg", [4, 4])
        et = sb("et", [128, 4])
        gb = sb("gb", [128, 4])
        ngc2 = sb("ngc2", [128, 2, 2])
        mwab = sb("mwab", [128, 128])
        ktp = sb("ktp", [128, 256], BF16)
        vtp = sb("vtp", [128, 256], BF16)
        vtb = sb("vtb", [128, 256], BF16)
        Sm = sb("Sm", [128, 128], BF16)
        Eb = sb("Eb", [128, 128], BF16)
        Cg = sb("Cg", [128, 2, 256], BF16)
        nrep = sb("nrep", [128, 2, 128], BF16)
        hh = sb("hh", [128, 2, 128])
        hq = sb("hq", [128, 2, 128])
        mw = [sb("mw%d" % i, [128, 128]) for i in range(4)]
        stg = sb("stg", [128, 256])

        SRE = ["sre%d" % c for c in range(KC)]
        SIM = ["sim%d" % c for c in range(KC)]
        dq = ["sp", "pool"]
        dqi = [0]

        def dma(out, in_, reads=(), writes=(), q=None, slow=False):
            if out.dtype != in_.dtype:
                q = "pool"
            if q is None:
                q = dq[dqi[0] % 2]
                dqi[0] += 1
            if slow:
                f = CALL("dma_start", out=out, in_=in_, allow_slow_non_contiguous=True)
            else:
                f = CALL("dma_start", out=out, in_=in_)
            return P.op(q, f, reads=reads, writes=writes, dma=True)

        def mm(out, lhsT, rhs, start, stop, reads, writes, tp=None):
            if tp is None:
                f = CALL("matmul", out, lhsT, rhs, start=start, stop=stop)
            else:
                f = CALL("matmul", out, lhsT, rhs, start=start, stop=stop, tile_position=tp)
            return P.op("pe", f, reads=reads, writes=writes)

        def tr(out, in_, idn, reads, writes):
            return P.op("pe", CALL("transpose", out, in_, idn), reads=list(reads) + ["ident"], writes=writes)

        def E(eng, fn, reads, writes):
            return P.op(eng, fn, reads=reads, writes=writes)

        def V(name, c):
            i = VI[name] * 8 + c
            return vec[:, i:i + 1]

        for name, t in [("ident", ident), ("maskE", maskE), ("mask16", mask16), ("triu", triu), ("bdmask", bdmask),
                        ("tvec", tvec), ("sel4", sel4), ("igb", igb), ("mask4", mask4)]:
            dma(t[:], W[name], writes=[name])
        E("dve", CALL("memset", ones_bf[:], 1.0), [], ["ones_bf"])
        E("dve", CALL("memset", onesD[:], 1.0 / D), [], ["onesD"])
        E("dve", CALL("memset", ones256[:], 1.0 / 256), [], ["ones256"])
        E("dve", CALL("memset", ones4[:], 1.0), [], ["ones4"])
        E("dve", CALL("memset", onecol[:], 1.0), [], ["onecol"])
        E("dve", CALL("memset", resetm[:], 1.0), [], ["resetm"])
        E("dve", CALL("memset", resetm[:, 0:1], 0.0), ["resetm"], ["resetm"])
        E("dve", CALL("memset", epscol[:], EPS), [], ["epscol"])
        dma(nfgb[:], W["fgb"], writes=["nfgb"])
        E("dve", CALL("tensor_scalar", nfgb[:], nfgb[:], -1.0, 0.0, ALU.mult, ALU.add), ["nfgb"], ["nfgb"])
        dma(stg[0:len(VEC_NAMES) * 8, 0:128], vecs, writes=["stg"])
        nv = len(VEC_NAMES) * 8
        tr(pb[7][:, 0:nv], stg[0:nv, 0:128], ident[0:nv, 0:nv], ["stg"], ["p7"])
        E("dve", CALL("tensor_copy", vec[:], pb[7][:, 0:nv]), ["p7"], ["vec"])
        dma(gwi[:], W["igw"].rearrange("(c p) h -> p c h", p=128), writes=["gwi"], slow=True)
        dma(gwf[:], W["fgw"].rearrange("(c p) h -> p c h", p=128), writes=["gwf"], slow=True)
        for wi, wn in enumerate(["wq", "wk", "wv"]):
            dma(stg[:, 0:32].rearrange("p (c o) -> p c o", o=4), W[wn].rearrange("(c b) i o -> (b i) c o", b=32),
                reads=[], writes=["stg"], slow=True)
            for c in range(KC):
                E("dve", CALL("tensor_tensor",
                    BD[:, wi, c, :].rearrange("p (b o) -> p b o", o=4),
                    bc_mid(stg[:, c * 4:c * 4 + 4], 32), bc_last(bdmask[:, :], 4), ALU.mult),
                  ["stg", "bdmask"], ["BD"])
        lamr = s5w[0][:, 0, 0:32]
        lami = s5w[0][:, 1, 0:32]
        dtb = s5w[0][:, 2, 0:32]
        th = s5w[1][:, 0, 0:32]
        cth = s5w[1][:, 1, 0:32]
        sth = s5w[1][:, 2, 0:32]
        lbr = s5w[2][:, 0, 0:32]
        lbi = s5w[2][:, 1, 0:32]
        kr = s5w[2][:, 2, 0:32]
        ki_ = s5w[2][:, 3, 0:32]
        t1 = s5w[3][:, 0, 0:32]
        t2 = s5w[3][:, 1, 0:32]
        t3 = s5w[3][:, 2, 0:32]
        for nm, dst in [("lam_re", lamr), ("lam_im", lami)]:
            dma(stg[0:32, 0:128], W[nm], writes=["stg"])
            tr(pb[7][:, 0:32], stg[0:32, 0:128], ident[0:32, 0:32], ["stg"], ["p7"])
            E("dve", CALL("tensor_copy", dst, pb[7][:, 0:32]), ["p7"], ["s5p"])
        ldt = W["log_dt"]
        for g2 in range(2):
            src = bass.AP(ldt.tensor, ldt.offset + g2, [[0, 64], [2, 32]])
            dma(s5w[0][64 * g2:64 * g2 + 64, 2, 0:32], src, writes=["s5p"], slow=True)
        E("act", CALL("activation", dtb, dtb, AF.Exp), ["s5p"], ["s5p"])
        E("dve", CALL("tensor_tensor", t1, lamr, dtb, ALU.mult), ["s5p"], ["s5p"])
        E("act", CALL("activation", rmag[:], t1, AF.Exp), ["s5p"], ["rmag"])
        E("dve", CALL("tensor_tensor", th, lami, dtb, ALU.mult), ["s5p"], ["s5p"])

        ki32 = sb("ki32", [128, 1, 64], I32)

        def sincos(dst, ang, n, shift, key_r, key_w):
            wk = s5w[6][:].rearrange("p a b -> p (a b)")[:, 0:n]
            wf = s5w[7][:].rearrange("p a b -> p (a b)")[:, 0:n]
            wi_ = ki32[:].rearrange("p a b -> p (a b)")[:, 0:n]
            E("dve", CALL("tensor_scalar", wk, ang, shift, 1.0 / (2 * math.pi), ALU.add, ALU.mult), key_r, ["s5t"])
            E("dve", CALL("tensor_copy", wi_, wk), ["s5t"], ["s5t"])
            E("dve", CALL("tensor_copy", wf, wi_), ["s5t"], ["s5t"])
            E("dve", CALL("tensor_scalar", wk, ang, shift, 0.0, ALU.add, ALU.add), key_r + ["s5t"], ["s5t"])
            E("dve", CALL("scalar_tensor_tensor", wk, wf, -2 * math.pi, wk, ALU.mult, ALU.add), ["s5t"], ["s5t"])
            E("dve", CALL("tensor_scalar", wf, wk, math.pi, -2 * math.pi, ALU.is_gt, ALU.mult), ["s5t"], ["s5t"])
            E("dve", CALL("tensor_tensor", wk, wk, wf, ALU.add), ["s5t"], ["s5t"])
            E("dve", CALL("tensor_scalar", wf, wk, -math.pi, 2 * math.pi, ALU.is_lt, ALU.mult), ["s5t"], ["s5t"])
            E("dve", CALL("tensor_tensor", wk, wk, wf, ALU.add), ["s5t"], ["s5t"])
            E("act", CALL("activation", dst, wk, AF.Sin), ["s5t"], key_w)

        sincos(sth, th, 32, 0.0, ["s5p"], ["s5p"])
        sincos(cth, th, 32, math.pi / 2, ["s5p"], ["s5p"])
        E("dve", CALL("tensor_tensor", lbr, rmag[:], cth, ALU.mult), ["s5p", "rmag"], ["s5p"])
        E("dve", CALL("tensor_tensor", lbi, rmag[:], sth, ALU.mult), ["s5p", "rmag"], ["s5p"])
        E("dve", CALL("tensor_scalar", t1, lbr, -1.0, 0.0, ALU.add, ALU.add), ["s5p"], ["s5p"])
        E("dve", CALL("tensor_tensor", t2, lamr, lamr, ALU.mult), ["s5p"], ["s5p"])
        E("dve", CALL("tensor_tensor", t3, lami, lami, ALU.mult), ["s5p"], ["s5p"])
        E("dve", CALL("tensor_tensor", t2, t2, t3, ALU.add), ["s5p"], ["s5p"])
        E("dve", CALL("reciprocal", t2, t2), ["s5p"], ["s5p"])
        E("dve", CALL("tensor_tensor", kr, t1, lamr, ALU.mult), ["s5p"], ["s5p"])
        E("dve", CALL("tensor_tensor", t3, lbi, lami, ALU.mult), ["s5p"], ["s5p"])
        E("dve", CALL("tensor_tensor", kr, kr, t3, ALU.add), ["s5p"], ["s5p"])
        E("dve", CALL("tensor_tensor", kr, kr, t2, ALU.mult), ["s5p"], ["s5p"])
        E("dve", CALL("tensor_tensor", ki_, lbi, lamr, ALU.mult), ["s5p"], ["s5p"])
        E("dve", CALL("tensor_tensor", t3, t1, lami, ALU.mult), ["s5p"], ["s5p"])
        E("dve", CALL("tensor_tensor", ki_, ki_, t3, ALU.subtract), ["s5p"], ["s5p"])
        E("dve", CALL("tensor_tensor", ki_, ki_, t2, ALU.mult), ["s5p"], ["s5p"])
        for q in range(32):
            ang = s5w[5][:, 0, :]
            E("dve", CALL("tensor_scalar", ang, tvec[:, :], th[:, q:q + 1], 0.0, ALU.mult, ALU.add),
              ["s5p", "tvec"], ["s5ang"])
            sincos(sinT[:, q, :], ang, 64, 0.0, ["s5ang"], ["sinT"])
            sincos(cosT[:, q, :], ang, 64, math.pi / 2, ["s5ang"], ["cosT"])
        Bre = CT[:].rearrange("p a b c -> p (a b c)")[:, 0:512].rearrange("p (q j) -> p q j", j=16)
        Bim = CT[:].rearrange("p a b c -> p (a b c)")[:, 512:1024].rearrange("p (q j) -> p q j", j=16)
        Ere = CT[:].rearrange("p a b c -> p (a b c)")[:, 1024:1536].rearrange("p (q j) -> p q j", j=16)
        Eim = CT[:].rearrange("p a b c -> p (a b c)")[:, 1536:2048].rearrange("p (q j) -> p q j", j=16)
        for g2 in range(2):
            dma(Bre[64 * g2:64 * g2 + 64], W["b_re"].rearrange("(q g) p j -> g p q j", g=2)[g2], writes=["CT0", "CT1", "CT2", "CT3"], slow=True)
            dma(Bim[64 * g2:64 * g2 + 64], W["b_im"].rearrange("(q g) p j -> g p q j", g=2)[g2], writes=["CT0", "CT1", "CT2", "CT3"], slow=True)
        kmr = s5w[4][:, 0, 0:32]
        kmi = s5w[4][:, 1, 0:32]
        Eexp_re = ybuf[:].rearrange("p a b -> p (a b)")[:, 0:1024].rearrange("p (q g j) -> p q g j", g=2, j=16)
        Eexp_im = ybuf[:].rearrange("p a b -> p (a b)")[:, 1024:2048].rearrange("p (q g j) -> p q g j", g=2, j=16)
        for g2 in range(2):
            E("dve", CALL("tensor_scalar", kmr, kr, maskE[:, g2:g2 + 1], 0.0, ALU.mult, ALU.add), ["s5p", "maskE"], ["s5k"])
            E("dve", CALL("tensor_scalar", kmi, ki_, maskE[:, g2:g2 + 1], 0.0, ALU.mult, ALU.add), ["s5p", "maskE"], ["s5k"])
            E("dve", CALL("tensor_tensor", Ere, Bre, bc_last(kmr, 16), ALU.mult), ["CT0", "CT1", "CT2", "CT3"] + ["s5k"], ["CT0", "CT1", "CT2", "CT3"])
            E("dve", CALL("tensor_tensor", Eim, Bim, bc_last(kmi, 16), ALU.mult), ["CT0", "CT1", "CT2", "CT3"] + ["s5k"], ["CT0", "CT1", "CT2", "CT3"])
            E("dve", CALL("tensor_tensor", Eexp_re[:, :, g2, :], Ere, Eim, ALU.subtract), ["CT0", "CT1", "CT2", "CT3"], ["ybuf"])
            E("dve", CALL("tensor_tensor", Ere, Bim, bc_last(kmr, 16), ALU.mult), ["CT0", "CT1", "CT2", "CT3"] + ["s5k"], ["CT0", "CT1", "CT2", "CT3"])
            E("dve", CALL("tensor_tensor", Eim, Bre, bc_last(kmi, 16), ALU.mult), ["CT0", "CT1", "CT2", "CT3"] + ["s5k"], ["CT0", "CT1", "CT2", "CT3"])
            E("dve", CALL("tensor_tensor", Eexp_im[:, :, g2, :], Ere, Eim, ALU.add), ["CT0", "CT1", "CT2", "CT3"], ["ybuf"])
        for c in range(KC):
            for ri, Ex in enumerate([Eexp_re, Eexp_im]):
                src = Ex[:, 4 * c:4 * c + 4, :, :].rearrange("p q g j -> p (q g j)")
                tr(pb[7][:, 0:128], src, ident[:], ["ybuf"], ["p7"])
                E("act", CALL("copy", W1[:, c, ri, :], pb[7][:, 0:128]), ["p7"], ["W1"])
        Cst = CT[:].rearrange("p a b c -> p (a b c)")[0:32, 0:2048].rearrange("p (q k) -> p q k", k=64)
        Cx = xT[:].rearrange("p a b -> p (a b)")[0:32, 0:4096].rearrange("p (q g k) -> p q g k", g=2, k=64)
        for ri, (cn, sgn) in enumerate([("c_re", 1.0), ("c_im", -1.0)]):
            for g2 in range(2):
                dma(Cst[16 * g2:16 * g2 + 16], W[cn].rearrange("(q g) h p -> g h q p", g=2)[g2], writes=["CT0", "CT1", "CT2", "CT3"], slow=True)
            for g2 in range(2):
                E("dve", CALL("tensor_scalar", Cx[:, :, g2, :], Cst, mask16[:, g2:g2 + 1], sgn, ALU.mult, ALU.mult),
                  ["CT0", "CT1", "CT2", "CT3"] + ["mask16"], ["xT"])
            for q in range(32):
                tr(pb[6][:, (q % 16) * 32:(q % 16) * 32 + 32], Cx[:, q, :, :].rearrange("p g k -> p (g k)"), ident[0:32, 0:32], ["xT"], ["p6"])
                if q % 16 == 15:
                    q0 = q - 15
                    E("act", CALL("copy", W3[:, q0:q0 + 16, ri, :], pb[6][:].rearrange("p (q h) -> p q h", h=32)), ["p6"], ["W3"])

        hTf = hT[:].rearrange("p a b -> p (a b)")
        pcs = [(hTf[:, 0:4096], ["h%d" % i for i in range(8)]), (hTf[:, 4096:8192], ["h%d" % i for i in range(8, 16)])]
        pci = [0]
        WSCR = ["wscr%d" % i for i in range(12)]

        def precast(src_rows, ncols, dst_ap, pattern, **kw):
            stg_, names = pcs[pci[0] % 2]
            pci[0] += 1
            dma(stg_[:, 0:ncols], src_rows, writes=names, q="pool")
            dma(dst_ap.rearrange(pattern), stg_[:, 0:ncols].rearrange("p (a k) -> p a k", k=128), reads=names, writes=["wscr%d" % ((pci[0] - 1) % 12)], q="sp", slow=True)

        for n in ["ffn1_gate", "ffn1_up", "ffn2_gate", "ffn2_up"]:
            for c in range(KC):
                precast(W[n][c * 128:(c + 1) * 128, :], DFF, SG[n][:, :, c, :], "f p k -> p f k")
        for n in ["ffn1_down", "ffn2_down"]:
            for f in range(FC):
                precast(W[n][f * 128:(f + 1) * 128, :], D, SD[n][:, f // 11, :, f % 11, :], "c p k -> p c k")
        for c in range(KC):
            precast(W["w_in"][c * 128:(c + 1) * 128, :], 3 * D, SI[:, :, c, :], "f p k -> p f k")
        for c in range(KC):
            precast(W["s5_glu_w"][c * 128:(c + 1) * 128, :], D, SGL[:, :, c, :], "f p k -> p f k")
        for k in range(16):
            precast(W["w_out"][k * 128:(k + 1) * 128, :], D, SO[:, k // 8, :, k % 8, :], "c p k -> p c k")

        wcnt = {"g": 0, "u": 0, "d": 0, "i": 0, "o": 0}

        def rmsnorm_stats(src_chunks, nch, Tt, key_r, scale_mat):
            for c in range(nch):
                if c % 2 == 0:
                    E("act", CALL("activation", hT[:, 14 + c, 0:Tt], src_chunks(c), AF.Square), key_r, ["h%d" % (14 + c)])
                else:
                    E("dve", CALL("tensor_tensor", hT[:, 14 + c, 0:Tt], src_chunks(c), src_chunks(c), ALU.mult), key_r, ["h%d" % (14 + c)])
            for c in range(nch):
                mm(pb[6][:, 0:Tt], scale_mat[:], hT[:, 14 + c, 0:Tt], c == 0, c == nch - 1, ["onesD", "h%d" % (14 + c)], ["p6"])
            E("act", CALL("activation", rstd[:, 0:Tt], pb[6][:, 0:Tt], AF.Ln, bias=epscol[:, 0:1]), ["p6", "epscol"], ["rstd"])
            E("act", CALL("activation", rstd[:, 0:Tt], rstd[:, 0:Tt], AF.Exp, scale=-0.5), ["rstd"], ["rstd"])

        def norm_x(gname, Tt):
            rmsnorm_stats(lambda c: xT[:, c, 0:Tt], KC, Tt, ["xT"], onesD)
            for c in range(KC):
                E("dve", CALL("scalar_tensor_tensor", xn[:, c, 0:Tt], xT[:, c, 0:Tt], V(gname, c), rstd[:, 0:Tt], ALU.mult, ALU.mult),
                  ["xT", "vec", "rstd"], ["xn"])

        def ffn(pref, gname, Tt):
            norm_x(gname, Tt)
            wg, wu, wd = W[pref + "_gate"], W[pref + "_up"], W[pref + "_down"]
            for f in range(FC):
                gi = wcnt["g"] % 2
                wcnt["g"] += 1
                dma(wgr[gi][:], SG[pref + "_gate"][f], reads=WSCR, writes=["wg%d" % gi], q="sp")
                dma(wur[gi][:], SG[pref + "_up"][f], reads=WSCR, writes=["wu%d" % gi], q="sp")
                pg, pu = pb[f % 2], pb[2 + f % 2]
                for c in range(KC):
                    mm(pg[:, 0:Tt], wgr[gi][:, c, :], xn[:, c, 0:Tt], c == 0, c == KC - 1, ["wg%d" % gi, "xn"], ["p%d" % (f % 2)])
                for c in range(KC):
                    mm(pu[:, 0:Tt], wur[gi][:, c, :], xn[:, c, 0:Tt], c == 0, c == KC - 1, ["wu%d" % gi, "xn"], ["p%d" % (2 + f % 2)])
                tt = tA if f % 2 == 0 else tB
                tn = "tA" if f % 2 == 0 else "tB"
                E("act", CALL("activation", tt[:, 0:Tt], pg[:, 0:Tt], AF.Silu), ["p%d" % (f % 2)], [tn])
                E("dve", CALL("tensor_tensor", hT[:, f, 0:Tt], tt[:, 0:Tt], pu[:, 0:Tt], ALU.mult),
                  [tn, "p%d" % (2 + f % 2)], ["h%d" % f])
            for c in range(KC):
                pd = pb[4 + c % 2]
                for hf in range(2):
                    di = hf
                    dma(wdr[di][:], SD[pref + "_down"][c, hf], reads=WSCR, writes=["wd%d" % di], q="sp")
                    for f2 in range(FC // 2):
                        f = hf * (FC // 2) + f2
                        mm(pd[:, 0:Tt], wdr[di][:, f2, :], hT[:, f, 0:Tt], f == 0, f == FC - 1, ["wd%d" % di, "h%d" % f], ["p%d" % (4 + c % 2)])
                E("dve", CALL("scalar_tensor_tensor", xT[:, c, 0:Tt], pd[:, 0:Tt], 0.5, xT[:, c, 0:Tt], ALU.mult, ALU.add),
                  ["p%d" % (4 + c % 2), "xT"], ["xT"])

        def in_proj(Tt):
            norm_x("norm_mix", Tt)
            for oc in range(24):
                ii = wcnt["i"] % 2
                wcnt["i"] += 1
                dma(wir[ii][:], SI[oc], reads=WSCR, writes=["wi%d" % ii], q="sp")
                po = pb[4 + oc % 2]
                for c in range(KC):
                    mm(po[:, 0:Tt], wir[ii][:, c, :], xn[:, c, 0:Tt], c == 0, c == KC - 1, ["wi%d" % ii, "xn"], ["p%d" % (4 + oc % 2)])
                if oc < 8:
                    dst, key = u_bf[:, oc, 0:Tt], "u_bf"
                elif oc < 16:
                    dst, key = xmh[:, oc - 8, 3:3 + Tt], "xmh"
                else:
                    dst, key = z_bf[:, oc - 16, 0:Tt], "z_bf"
                if oc % 2 == 0:
                    E("act", CALL("copy", dst, po[:, 0:Tt]), ["p%d" % (4 + oc % 2)], [key])
                else:
                    E("dve", CALL("tensor_copy", dst, po[:, 0:Tt]), ["p%d" % (4 + oc % 2)], [key])

        def s5_mix(Tt):
            gT = hT
            L = min(64, Tt)
            nun = Tt // L
            def s5_pre(c, un, S):
                t0 = un * L
                X = S["extra"]
                pS, psk = S["pS"][un % 2], S["psk"][un % 2]
                umS = S["um"][:, un % 2]
                kum = "s5%sum%d" % (S["k"], un % 2)
                pSv = pS[:].rearrange("p (q r t) -> p q r t", q=4, r=2)
                for qq in range(4):
                    E("act", CALL("activation", umS[:, qq, 0:L], u_bf[:, c, t0:t0 + L], AF.Copy, scale=mask4[:, qq:qq + 1]), ["u_bf", "mask4"] + X, [kum])
                for qq in range(4):
                    for ri in range(2):
                        mm(pSv[:, qq, ri, 0:L], W1[:, c, ri, :], umS[:, qq, 0:L], True, True, ["W1", kum] + X, [psk])

            def s5_unit(c, un, S):
                pY = pb[2 + c % 2]
                pyk = "p%d" % (2 + c % 2)
                t0 = un * L
                X = S["extra"]
                K = lambda i: "s5%s%d" % (S["k"], i)
                pS, psk = S["pS"][un % 2], S["psk"][un % 2]
                xbf = S["xbf"]
                kxb = K(11)
                injC, kinjC = S["inj"][:, un % 2], "s5%sinj%d" % (S["k"], un % 2)
                injN, kinjN = S["inj"][:, (un + 1) % 2], "s5%sinj%d" % (S["k"], (un + 1) % 2)
                sk = "sre%d" % c
                pSv = pS[:].rearrange("p (q r t) -> p q r t", q=4, r=2)
                if un == 0:
                    s5_pre(c, 0, S)
                    E("pool", CALL("tensor_tensor", injC, s5st[:, :, 4 * c:4 * c + 4], bc_mid(rmag[:, 4 * c:4 * c + 4], 2), ALU.mult), ["rmag", sk] + X, [kinjC])
                    yield
                if un + 1 < nun and L == 64:
                    s5_pre(c, un + 1, S)
                    yield
                bre = pSv[:, :, 0, 0:L]
                bim = pSv[:, :, 1, 0:L]
                cs = cosT[:, 4 * c:4 * c + 4, 0:L]
                sn = sinT[:, 4 * c:4 * c + 4, 0:L]
                B = S["bufs"]
                bv = lambda i: B[:, i * 256:(i + 1) * 256].rearrange("p (q t) -> p q t", t=64)[:, :, 0:L]
                E("dve", CALL("tensor_tensor", bv(2), bre, cs, ALU.mult), [psk, "cosT"] + X, [K(2)])
                E("dve", CALL("tensor_tensor", bv(3), bim, sn, ALU.mult), [psk, "sinT"] + X, [K(3)])
                E("dve", CALL("tensor_tensor", bv(6), bim, cs, ALU.mult), [psk, "cosT"] + X, [K(6)])
                E("dve", CALL("tensor_tensor", bv(7), bre, sn, ALU.mult), [psk, "sinT"] + X, [K(7)])
                E("dve", CALL("tensor_tensor", bv(0), bv(2), bv(3), ALU.add), [K(2), K(3)] + X, [K(0)])
                E("dve", CALL("tensor_tensor", bv(1), bv(6), bv(7), ALU.subtract), [K(6), K(7)] + X, [K(1)])
                if L == 64:
                    W2 = B[:, 0:512].rearrange("p (r q t) -> p r q t", r=2, t=64)
                    E("dve", CALL("tensor_tensor", W2[:, :, :, 0], W2[:, :, :, 0], injC, ALU.add), [K(0), K(1), kinjC] + X, [K(0), K(1)])
                    rt = S["rtab"][:, 4 * c:4 * c + 4, :].rearrange("p q t -> p (q t)")
                    E("dve", CALL("tensor_tensor_scan", B[:, 1024:1280], rt, B[:, 0:256], 0.0, ALU.mult, ALU.add), [K(0), "rtab"] + X, [K(4)])
                    E("dve", CALL("tensor_tensor_scan", B[:, 1280:1536], rt, B[:, 256:512], 0.0, ALU.mult, ALU.add), [K(1), "rtab"] + X, [K(5)])
                    yield
                else:
                    for qq in range(4):
                        q = 4 * c + qq
                        E("dve", CALL("tensor_tensor_scan", bv(4)[:, qq, :], bc_row(rmag[:, q:q + 1], L), bv(0)[:, qq, :],
                                      sre[:, q:q + 1], ALU.mult, ALU.add), ["rmag", K(0), sk] + X, [K(4)])
                        E("dve", CALL("tensor_tensor_scan", bv(5)[:, qq, :], bc_row(rmag[:, q:q + 1], L), bv(1)[:, qq, :],
                                      sim[:, q:q + 1], ALU.mult, ALU.add), ["rmag", K(1), sk] + X, [K(5)])
                    yield
                zr, zi = bv(4), bv(5)
                E("pool", CALL("tensor_tensor", bv(2), zr, cs, ALU.mult), [K(4), "cosT"] + X, [K(2)])
                E("pool", CALL("tensor_tensor", bv(3), zi, sn, ALU.mult), [K(5), "sinT"] + X, [K(3)])
                E("pool", CALL("tensor_tensor", bv(6), bv(2), bv(3), ALU.subtract), [K(2), K(3)] + X, [K(6)])
                E("pool", CALL("tensor_tensor", bv(2), zi, cs, ALU.mult), [K(5), "cosT"] + X, [K(2)])
                E("pool", CALL("tensor_tensor", bv(3), zr, sn, ALU.mult), [K(4), "sinT"] + X, [K(3)])
                E("pool", CALL("tensor_tensor", bv(7), bv(2), bv(3), ALU.add), [K(2), K(3)] + X, [K(7)])
                X4 = B[:, 1536:2048].rearrange("p (r q t) -> p r q t", r=2, t=64)
                if L == 64 and un + 1 < nun:
                    E("pool", CALL("tensor_tensor", injN, X4[:, :, :, L - 1], bc_mid(rmag[:, 4 * c:4 * c + 4], 2), ALU.mult), ["rmag", K(6), K(7)] + X, [kinjN])
                yield
                E("act", CALL("copy", xbf[:, :, :, 0:L], X4[:, :, :, 0:L]), [K(6), K(7)] + X, [kxb])
                if L != 64 or un + 1 == nun:
                    E("act", CALL("copy", s5st[:, :, 4 * c:4 * c + 4], X4[:, :, :, L - 1]), [K(6), K(7)] + X, [sk])
                yield
                for qq in range(4):
                    q = 4 * c + qq
                    mm(pY[32 * qq:32 * qq + 32, t0:t0 + L], W3[:, q, 0, :], xbf[:, 0, qq, 0:L], True, False, ["W3", kxb] + X, [pyk], tp=(0, 32 * qq))
                    mm(pY[32 * qq:32 * qq + 32, t0:t0 + L], W3[:, q, 1, :], xbf[:, 1, qq, 0:L], False, True, ["W3", kxb] + X, [pyk], tp=(0, 32 * qq))
                yield

            def s5_post(c):
                pY = pb[2 + c % 2]
                pyk = "p%d" % (2 + c % 2)
                E("dve", CALL("scalar_tensor_tensor", tA[:, 0:Tt], u_bf[:, c, 0:Tt], V("s5_d", c), pY[:, 0:Tt], ALU.mult, ALU.add),
                  ["u_bf", "vec", pyk], ["tA"])
                E("act", CALL("activation", tB[:, 0:Tt], tA[:, 0:Tt], AF.Square), ["tA"], ["tB"])
                E("dve", CALL("tensor_scalar", tB[:, 0:Tt], tB[:, 0:Tt], 0.044715, 1.0, ALU.mult, ALU.add), ["tB"], ["tB"])
                E("pool", CALL("tensor_tensor", tB[:, 0:Tt], tB[:, 0:Tt], tA[:, 0:Tt], ALU.mult), ["tA", "tB"], ["tB"])
                E("act", CALL("activation", tB[:, 0:Tt], tB[:, 0:Tt], AF.Sigmoid, scale=1.5957691216057308), ["tB"], ["tB"])
                E("dve", CALL("tensor_tensor", gT[:, c, 0:Tt], tA[:, 0:Tt], tB[:, 0:Tt], ALU.mult), ["tA", "tB"], ["h%d" % c])

            ybf_ = ybuf[:].rearrange("p a b -> p (a b)")
            rtab = ybf_[:, 2048:4096].rearrange("p (q t) -> p q t", t=64)
            E("dve", CALL("tensor_tensor", rtab, bc_last(rmag[:, :], 64), bc_mid(resetm[:, :], 32), ALU.mult), ["rmag", "resetm", "ybuf"], ["rtab"])
            SA = dict(bufs=s5all[:].rearrange("p a b -> p (a b)"), um=um, xbf=xbfA, inj=inj[0], k="A", extra=[], pS=[pb[0], pb[1]], psk=["p0", "p1"], rtab=rtab)
            SB = dict(bufs=ybf_[:, 0:2048], um=umB, xbf=xbfB, inj=inj[1], k="B", extra=["ybuf"], pS=[pb[4], pb[5]], psk=["p4", "p5"], rtab=rtab)
            for cp in range(KC // 2):
                c0, c1 = 2 * cp, 2 * cp + 1
                for un in range(nun):
                    gens = [s5_unit(c0, un, SA), s5_unit(c1, un, SB)]
                    while gens:
                        for g_ in list(gens):
                            try:
                                next(g_)
                            except StopIteration:
                                gens.remove(g_)
                s5_post(c0)
                s5_post(c1)
            for oc in range(KC):
                ii = wcnt["i"] % 2
                wcnt["i"] += 1
                dma(wir[ii][:], SGL[oc], reads=WSCR, writes=["wi%d" % ii], q="sp")
                po = pb[4 + oc % 2]
                pk = "p%d" % (4 + oc % 2)
                for c in range(KC):
                    mm(po[:, 0:Tt], wir[ii][:, c, :], gT[:, c, 0:Tt], c == 0, c == KC - 1, ["wi%d" % ii, "h%d" % c], [pk])
                E("act", CALL("activation", tA[:, 0:Tt], po[:, 0:Tt], AF.Sigmoid, bias=V("s5_glu_b", oc)), [pk, "vec"], ["tA"])
                E("dve", CALL("tensor_tensor", ybuf[:, oc, 0:Tt], gT[:, oc, 0:Tt], tA[:, 0:Tt], ALU.mult), ["tA", "h%d" % oc], ["ybuf"])
            rmsnorm_stats(lambda c: ybuf[:, c, 0:Tt], KC, Tt, ["ybuf"], onesD)
            for c in range(KC):
                E("dve", CALL("scalar_tensor_tensor", mixed[:, c, 0:Tt], ybuf[:, c, 0:Tt], V("out_norm_s5", c), rstd[:, 0:Tt], ALU.mult, ALU.mult),
                  ["ybuf", "vec", "rstd"], ["mixed"])

        def mlstm_mix(Tt):
            qT = hT
            for c in range(KC):
                eng = "dve"
                E(eng, CALL("tensor_scalar", tA[:, 0:Tt], xmh[:, c, 0:Tt], V("cw0", c), 0.0, ALU.mult, ALU.add), ["xmh", "vec"], ["tA"])
                for j in range(1, 4):
                    E(eng, CALL("scalar_tensor_tensor", tA[:, 0:Tt], xmh[:, c, j:j + Tt], V("cw%d" % j, c), tA[:, 0:Tt], ALU.mult, ALU.add),
                      ["xmh", "vec", "tA"], ["tA"])
                E("act", CALL("activation", xc_bf[:, c, 0:Tt], tA[:, 0:Tt], AF.Silu, bias=V("ml_conv_b", c)), ["tA", "vec"], ["xn"])
            for c in range(KC):
                E("act", CALL("activation", z_bf[:, c, 0:Tt], z_bf[:, c, 0:Tt], AF.Silu), ["z_bf"], ["z_bf"])
            for c in range(KC):
                for wi, (src, dstT, key) in enumerate([(xc_bf[:, c, 0:Tt], qT[:, c, 0:Tt], "h%d" % c),
                                                       (xc_bf[:, c, 0:Tt], qT[:, 8 + c, 0:Tt], "h%d" % (8 + c)),
                                                       (xmh[:, c, 3:3 + Tt], vT[:, c, 0:Tt], "h%d" % (16 + c))]):
                    po = pb[4 + (3 * c + wi) % 2]
                    pk = "p%d" % (4 + (3 * c + wi) % 2)
                    mm(po[:, 0:Tt], BD[:, wi, c, :], src, True, True, ["BD", "xn", "xmh"], [pk])
                    if wi == 1:
                        E("dve", CALL("tensor_copy", dstT, po[:, 0:Tt]), [pk], [key])
                    else:
                        E("act", CALL("copy", dstT, po[:, 0:Tt]), [pk], [key])
            for gi, (gw, pbk) in enumerate([(gwi, 4), (gwf, 5)]):
                for j in range(24):
                    src = qT[:, j, 0:Tt]
                    key = "h%d" % j
                    mm(pb[pbk][0:4, 0:Tt], gw[:, j, :], src, j == 0, j == 23, ["gwi", "gwf", key], ["p%d" % pbk])
            E("act", CALL("activation", igs[:, 0:Tt], pb[4][0:4, 0:Tt], AF.Identity, bias=igb[:, 0:1]), ["p4", "igb"], ["igs"])
            E("act", CALL("activation", MM[:, 0:Tt], pb[5][0:4, 0:Tt], AF.Exp, bias=nfgb[:, 0:1], scale=-1.0), ["p5", "nfgb"], ["MM"])
            E("act", CALL("activation", MM[:, 0:Tt], MM[:, 0:Tt], AF.Ln, bias=onecol[0:4, 0:1]), ["MM", "onecol"], ["MM"])
            E("dve", CALL("tensor_tensor_scan", Fn[:, 0:Tt], bc_row(onecol[0:4, 0:1], Tt), MM[:, 0:Tt], 0.0, ALU.mult, ALU.add), ["MM", "onecol"], ["Fn"])
            E("dve", CALL("tensor_tensor", aa[:, 0:Tt], igs[:, 0:Tt], Fn[:, 0:Tt], ALU.add), ["igs", "Fn"], ["aa"])
            E("dve", CALL("tensor_tensor_scan", MM[:, 0:Tt], bc_row(onecol[0:4, 0:1], Tt), aa[:, 0:Tt], mprev[:, 0:1], ALU.mult, ALU.max),
              ["aa", "onecol", "mprev"], ["MM"])
            E("dve", CALL("tensor_copy", Mpc[:], mprev[:]), ["mprev"], ["Mpc"])
            Lc = min(128, Tt)
            for ch in range(Tt // Lc):
                t0, t1 = ch * Lc, ch * Lc + Lc
                E("dve", CALL("tensor_scalar", negM[:], MM[:, t1 - 1:t1], -1.0, 0.0, ALU.mult, ALU.add), ["MM"], ["negM"])
                E("act", CALL("activation", g4[0][:, 0:Lc], aa[:, t0:t1], AF.Exp, bias=negM[:, 0:1]), ["aa", "negM"], ["g40"])
                E("act", CALL("activation", gcol[:], Mpc[:], AF.Exp, bias=negM[:, 0:1]), ["Mpc", "negM"], ["gcol"])
                E("dve", CALL("tensor_scalar", g4[1][:, 0:Lc], Fn[:, t0:t1], negM[:, 0:1], 0.0, ALU.add, ALU.add), ["Fn", "negM"], ["g41"])
                E("dve", CALL("tensor_copy", Mpc[:], MM[:, t1 - 1:t1]), ["MM", "gcol"], ["Mpc"])
                tr(pb[7][0:Lc, 128:132], g4[0][:, 0:Lc], ident[0:4, 0:4], ["g40"], ["p7"])
                E("dve", CALL("tensor_copy", et[0:Lc, :], pb[7][0:Lc, 128:132]), ["p7"], ["et"])
                E("dve", CALL("tensor_scalar", dg[:], ident[0:4, 0:4], gcol[:, 0:1], 0.0, ALU.mult, ALU.add), ["ident", "gcol"], ["dg"])
                mm(pb[7][:, 132:136], ones4[:], dg[:], True, True, ["ones4", "dg"], ["p7"])
                E("dve", CALL("tensor_copy", gb[:], pb[7][:, 132:136]), ["p7"], ["gb"])
                for c in range(KC):
                    mm(pb[c // 4][0:Lc, (c % 4) * 128:(c % 4) * 128 + 128], xc_bf[:, c, t0:t1], BD[:, 1, c, :], True, True, ["xn", "BD"], ["p%d" % (c // 4)])
                    mm(pb[2 + c // 4][0:Lc, (c % 4) * 128:(c % 4) * 128 + 128], xmh[:, c, 3 + t0:3 + t1], BD[:, 2, c, :], True, True, ["xmh", "BD"], ["p%d" % (2 + c // 4)])
                def SET(pr):
                    if pr == 0:
                        return dict(ktp=ktp, vtp=vtp, vtb=vtb, Sm=Sm, Eb=Eb, Cg=Cg, nrep=nrep, hh=hh, hq=hq, ngc=ngc2[:, 0, :], X=[], k="0")
                    vv = hT[:, 16:20, :]
                    return dict(ktp=vv[:, 0, 0:256], vtp=vv[:, 0, 256:512], vtb=vv[:, 1, 0:256], Sm=vv[:, 1, 256:384], Eb=vv[:, 1, 384:512],
                                Cg=vv[:, 2, :].rearrange("p (a b) -> p a b", a=2), nrep=vv[:, 3, 0:256].rearrange("p (a b) -> p a b", a=2),
                                hh=tAB[:, 0:256].rearrange("p (a b) -> p a b", a=2), hq=tAB[:, 256:512].rearrange("p (a b) -> p a b", a=2),
                                ngc=ngc2[:, 1, :], X=["h16", "h17", "h18", "h19", "tA"], k="1")

                def prep(h):
                    S_ = SET(h % 2)
                    X = S_["X"]
                    N = lambda n: "ml%s%s" % (n, S_["k"])
                    kps = pb[h // 2][0:Lc, (h % 2) * 256:(h % 2) * 256 + 256]
                    vps = pb[2 + h // 2][0:Lc, (h % 2) * 256:(h % 2) * 256 + 256]
                    kk, vk = "p%d" % (h // 2), "p%d" % (2 + h // 2)
                    for kc in range(2):
                        mm(pb[7][0:Lc, 0:Lc], qT[:, 8 + 2 * h + kc, t0:t1], qT[:, 2 * h + kc, t0:t1], kc == 0, kc == 1,
                           ["h%d" % (8 + 2 * h + kc), "h%d" % (2 * h + kc), "p7"], ["p7s"])
                    E("dve", CALL("scalar_tensor_tensor", S_["Sm"][0:Lc, 0:Lc], pb[7][0:Lc, 0:Lc], 1.0 / 16, triu[0:Lc, 0:Lc], ALU.mult, ALU.mult), ["p7s", "p7", "triu"] + X, [N("Sm")])
                    E("dve", CALL("tensor_scalar", S_["vtp"][0:Lc, :], vps, et[0:Lc, h:h + 1], 0.0, ALU.mult, ALU.add), [vk, "et"] + X, [N("vtp")])
                    E("act", CALL("copy", S_["vtb"][0:Lc, :], vps), [vk] + X, [N("vtb")])
                    E("dve", CALL("tensor_scalar", S_["ktp"][0:Lc, :], kps, et[0:Lc, h:h + 1], 1.0 / 16, ALU.mult, ALU.mult), [kk, "et"] + X, [N("ktp")])
                    E("pool", CALL("tensor_scalar", S_["Eb"][0:Lc, :], ones_bf[0:Lc, :], et[0:Lc, h:h + 1], 0.0, ALU.mult, ALU.add), ["ones_bf", "et"] + X, [N("Eb")])
                    E("pool", CALL("tensor_scalar", S_["Cg"].rearrange("p a b -> p (a b)"), CT[:, h, :, :].rearrange("p a b -> p (a b)"), gb[:, h:h + 1], 0.0, ALU.mult, ALU.add),
                      ["CT%d" % h, "gb"] + X, [N("Cg")])
                    E("dve", CALL("tensor_scalar", S_["ngc"], nT[:, 2 * h:2 * h + 2], gb[:, h:h + 1], 0.0, ALU.mult, ALU.add), ["nT%d" % h, "gb"], [N("ngc")])
                    for kc in range(2):
                        E("pool", CALL("tensor_scalar", S_["nrep"][:, kc, :], ones_bf[:, :], S_["ngc"][:, kc:kc + 1], 0.0, ALU.mult, ALU.add), ["ones_bf", N("ngc")] + X, [N("nrep")])

                def mid(h):
                    S_ = SET(h % 2)
                    X = S_["X"]
                    N = lambda n: "ml%s%s" % (n, S_["k"])
                    hh_, hq_ = S_["hh"], S_["hq"]
                    for vc in range(2):
                        o = pb[4][:, vc * 128:vc * 128 + Lc]
                        mm(o, S_["vtp"][0:Lc, vc * 128:vc * 128 + 128], S_["Sm"][0:Lc, 0:Lc], True, False, [N("vtp"), N("Sm")] + X, ["p4"])
                        for kc in range(2):
                            mm(o, S_["Cg"][:, kc, vc * 128:vc * 128 + 128], qT[:, 2 * h + kc, t0:t1], False, kc == 1, [N("Cg"), "h%d" % (2 * h + kc)] + X, ["p4"])
                    o = pb[4][:, 256:256 + Lc]
                    mm(o, S_["Eb"][0:Lc, :], S_["Sm"][0:Lc, 0:Lc], True, False, [N("Eb"), N("Sm")] + X, ["p4"])
                    for kc in range(2):
                        mm(o, S_["nrep"][:, kc, :], qT[:, 2 * h + kc, t0:t1], False, kc == 1, [N("nrep"), "h%d" % (2 * h + kc)] + X, ["p4"])
                    mm(pb[4][:, 384:384 + Lc], sel4[:, h * 128:h * 128 + 128], g4[1][:, 0:Lc], True, True, ["sel4", "g41"], ["p4"])
                    E("act", CALL("activation", mw[0][:, 0:Lc], pb[4][:, 384:384 + Lc], AF.Exp), ["p4"], ["mw0"])
                    E("act", CALL("activation", mwab[:, 0:Lc], pb[4][:, 256:256 + Lc], AF.Abs), ["p4"], ["mwab"])
                    E("dve", CALL("tensor_tensor", mw[0][:, 0:Lc], mwab[:, 0:Lc], mw[0][:, 0:Lc], ALU.max), ["mwab", "mw0"], ["mw0"])
                    E("dve", CALL("reciprocal", mw[0][:, 0:Lc], mw[0][:, 0:Lc]), ["mw0"], ["mw0"])
                    for vc in range(2):
                        E("dve", CALL("tensor_tensor", hh_[:, vc, 0:Lc], pb[4][:, vc * 128:vc * 128 + Lc], mw[0][:, 0:Lc], ALU.mult), ["p4", "mw0"] + X, [N("hh")])
                        E("act", CALL("activation", hq_[:, vc, 0:Lc], hh_[:, vc, 0:Lc], AF.Square), [N("hh")] + X, [N("hq")])
                    for kc in range(2):
                        mm(pb[6][:, kc * 256:kc * 256 + 256], S_["ktp"][0:Lc, kc * 128:kc * 128 + 128], S_["vtb"][0:Lc, :], True, True, [N("ktp"), N("vtb")] + X, ["p6"])
                    E("dve", CALL("scalar_tensor_tensor", CT[:, h, :, :].rearrange("p a b -> p (a b)"), CT[:, h, :, :].rearrange("p a b -> p (a b)"),
                                  gb[:, h:h + 1], pb[6][:, :], ALU.mult, ALU.add), ["CT%d" % h, "gb", "p6", N("Cg")], ["CT%d" % h])
                    for kc in range(2):
                        mm(pb[7][:, 136 + kc:137 + kc], S_["ktp"][0:Lc, kc * 128:kc * 128 + 128], ones_bf[0:Lc, 0:1], True, True, [N("ktp"), "ones_bf", "p7"] + X, ["p7n"])
                    E("dve", CALL("tensor_tensor", nT[:, 2 * h:2 * h + 2], S_["ngc"], pb[7][:, 136:138], ALU.add), [N("ngc"), "p7n", "p7"], ["nT%d" % h])

                def fin(h):
                    S_ = SET(h % 2)
                    X = S_["X"]
                    N = lambda n: "ml%s%s" % (n, S_["k"])
                    hh_, hq_ = S_["hh"], S_["hq"]
                    for vc in range(2):
                        mm(pb[5][:, 0:Lc], ones256[:], hh_[:, vc, 0:Lc], vc == 0, vc == 1, ["ones256", N("hh")] + X, ["p5"])
                    for vc in range(2):
                        mm(pb[5][:, 128:128 + Lc], ones256[:], hq_[:, vc, 0:Lc], vc == 0, vc == 1, ["ones256", N("hq")] + X, ["p5"])
                    E("act", CALL("activation", mw[1][:, 0:Lc], pb[5][:, 0:Lc], AF.Square), ["p5"], ["mw1"])
                    E("dve", CALL("tensor_tensor", mw[1][:, 0:Lc], pb[5][:, 128:128 + Lc], mw[1][:, 0:Lc], ALU.subtract), ["p5", "mw1"], ["mw1"])
                    E("dve", CALL("tensor_scalar", mw[1][:, 0:Lc], mw[1][:, 0:Lc], 0.0, 0.0, ALU.max, ALU.add), ["mw1"], ["mw1"])
                    E("act", CALL("activation", mw[1][:, 0:Lc], mw[1][:, 0:Lc], AF.Sqrt, bias=epscol[:, 0:1]), ["mw1", "epscol"], ["mw1"])
                    E("dve", CALL("reciprocal", mw[1][:, 0:Lc], mw[1][:, 0:Lc]), ["mw1"], ["mw1"])
                    for vc in range(2):
                        c = 2 * h + vc
                        E("dve", CALL("tensor_tensor", mw[2][:, 0:Lc], hh_[:, vc, 0:Lc], pb[5][:, 0:Lc], ALU.subtract), [N("hh"), "p5"] + X, ["mw2"])
                        E("pool", CALL("tensor_tensor", mw[2][:, 0:Lc], mw[2][:, 0:Lc], mw[1][:, 0:Lc], ALU.mult), ["mw2", "mw1"], ["mw2"])
                        E("pool", CALL("tensor_scalar", mw[2][:, 0:Lc], mw[2][:, 0:Lc], V("ml_norm_w", c), 0.0, ALU.mult, ALU.add), ["mw2", "vec"], ["mw2"])
                        E("dve", CALL("scalar_tensor_tensor", mw[3][:, 0:Lc], xc_bf[:, c, t0:t1], V("ml_skip", c), mw[2][:, 0:Lc], ALU.mult, ALU.add),
                          ["xn", "vec", "mw2"], ["mw3"])
                        E("pool", CALL("tensor_tensor", ybuf[:, c, t0:t1], mw[3][:, 0:Lc], z_bf[:, c, t0:t1], ALU.mult), ["mw3", "z_bf"], ["ybuf"])

                for blk in [(prep, 0), (prep, 1), (mid, 0), (prep, 2), (mid, 1), (fin, 0), (prep, 3), (mid, 2), (fin, 1), (mid, 3), (fin, 2), (fin, 3)]:
                    blk[0](blk[1])
            E("dve", CALL("tensor_tensor", mprev[:], MM[:, Tt - 1:Tt], Fn[:, Tt - 1:Tt], ALU.subtract), ["MM", "Fn", "Mpc"], ["mprev"])
            for c in range(KC):
                E("act", CALL("copy", xmh[:, c, 0:3], xmh[:, c, Tt:Tt + 3]), ["xmh"], ["xmh"])
            rmsnorm_stats(lambda c: ybuf[:, c, 0:Tt], KC, Tt, ["ybuf"], onesD)
            for c in range(KC):
                E("dve", CALL("scalar_tensor_tensor", mixed[:, 8 + c, 0:Tt], ybuf[:, c, 0:Tt], V("out_norm_ml", c), rstd[:, 0:Tt], ALU.mult, ALU.mult),
                  ["ybuf", "vec", "rstd"], ["mixed"])

        def out_proj(Tt):
            for c in range(KC):
                po = pb[4 + c % 2]
                pk = "p%d" % (4 + c % 2)
                for hf in range(2):
                    oi = hf
                    dma(wor[oi][:], SO[c, hf], reads=WSCR, writes=["wo%d" % oi], q="sp")
                    for k2 in range(8):
                        k = hf * 8 + k2
                        mm(po[:, 0:Tt], wor[oi][:, k2, :], mixed[:, k, 0:Tt], k == 0, k == 15, ["wo%d" % oi, "mixed"], [pk])
                E("dve", CALL("tensor_tensor", xT[:, c, 0:Tt], xT[:, c, 0:Tt], po[:, 0:Tt], ALU.add), [pk, "xT"], ["xT"])

        def load_tile(src_rows, Tt):
            nsub = (Tt + 127) // 128
            for n in range(nsub):
                r = min(128, Tt - n * 128)
                dma(xtok[0:r, n, :], src_rows[n * 128:n * 128 + r, :], writes=["ybuf"])
            for c in range(KC):
                for n in range(nsub):
                    r = min(128, Tt - n * 128)
                    tr(pb[7][:, n * 128:n * 128 + r], xtok[0:r, n, c * 128:(c + 1) * 128], ident[0:r, 0:r], ["ybuf"], ["p7"])
                E("dve" if c % 2 == 0 else "act",
                  (CALL("tensor_copy", xT[:, c, 0:Tt], pb[7][:, 0:Tt])) if c % 2 == 0 else (CALL("copy", xT[:, c, 0:Tt], pb[7][:, 0:Tt])),
                  ["p7"], ["xT"])

        def store_tile(dst_rows, Tt):
            rmsnorm_stats(lambda c: xT[:, c, 0:Tt], KC, Tt, ["xT"], onesD)
            nsub = (Tt + 127) // 128
            for c in range(KC):
                E("dve", CALL("scalar_tensor_tensor", ybuf[:, c, 0:Tt], xT[:, c, 0:Tt], V("norm_final", c), rstd[:, 0:Tt], ALU.mult, ALU.mult),
                  ["xT", "vec", "rstd"], ["ybuf"])
            for n in range(nsub):
                r = min(128, Tt - n * 128)
                for c4 in range(2):
                    for c in range(c4 * 4, c4 * 4 + 4):
                        tr(pb[7][0:r, (c % 4) * 128:(c % 4) * 128 + 128], ybuf[:, c, n * 128:n * 128 + r], ident[:], ["ybuf"], ["p7"])
                    E("act", CALL("copy", otok[0:r, c4 * 512:(c4 + 1) * 512], pb[7][0:r, :]), ["p7"], ["tA", "tB"])
                outs.append(dma(dst_rows[n * 128:n * 128 + r, :], otok[0:r, :], reads=["tA", "tB"], q="sp"))

        def init_state(si):
            if si is None:
                for t, k in [(sre, SRE), (sim, SRE), (nT, ["nT0", "nT1", "nT2", "nT3"]), (mprev, ["mprev"])]:
                    E("dve", CALL("memset", t[:], 0.0), [], k)
                E("pool", CALL("memset", CT[:].rearrange("p a b c -> p (a b c)"), 0.0), [], ["CT0", "CT1", "CT2", "CT3"])
                E("pool", CALL("memset", xmh[:, :, 0:3], 0.0), [], ["xmh"])
                return
            for src, dst, k in [(st_s5re, sre, SRE), (st_s5im, sim, SIM)]:
                dma(stg[0:32, 0:128], src[si], writes=["stg"])
                tr(pb[7][:, 0:32], stg[0:32, 0:128], ident[0:32, 0:32], ["stg"], ["p7"])
                E("dve", CALL("tensor_copy", dst[:], pb[7][:, 0:32]), ["p7"], k)
            dma(stg[0:8, 0:128], st_n[si], writes=["stg"])
            tr(pb[7][:, 0:8], stg[0:8, 0:128], ident[0:8, 0:8], ["stg"], ["p7"])
            E("dve", CALL("tensor_copy", nT[:], pb[7][:, 0:8]), ["p7"], ["nT0", "nT1", "nT2", "nT3"])
            dma(mprev[:], st_m[si], writes=["mprev"])
            for c in range(KC):
                dma(stg[0:3, 0:128], st_conv[si][:, c * 128:(c + 1) * 128], writes=["stg"])
                tr(pb[7][:, 0:3], stg[0:3, 0:128], ident[0:3, 0:3], ["stg"], ["p7"])
                E("dve", CALL("tensor_copy", xmh[:, c, 0:3], pb[7][:, 0:3]), ["p7"], ["xmh"])
            for h in range(4):
                for vc in range(2):
                    dma(stg[:, 0:256], st_c[si, h, vc * 128:(vc + 1) * 128, :], writes=["stg"])
                    for kc in range(2):
                        tr(pb[7][:, kc * 128:kc * 128 + 128], stg[:, kc * 128:kc * 128 + 128], ident[:], ["stg"], ["p7"])
                    E("dve", CALL("tensor_copy", CT[:, h, :, vc * 128:vc * 128 + 128], pb[7][:, 0:256].rearrange("p (k v) -> p k v", k=2)), ["p7"], ["CT0", "CT1", "CT2", "CT3"])

        def store_state(oi):
            for src, dst, k in [(sre, o_s5re, SRE), (sim, o_s5im, SIM)]:
                tr(pb[7][0:32, 0:128], src[:], ident[:], k, ["p7"])
                E("dve", CALL("tensor_copy", stg[0:32, 0:128], pb[7][0:32, 0:128]), ["p7"], ["stg"])
                outs.append(dma(dst[oi], stg[0:32, 0:128], reads=["stg"], q="sp"))
            tr(pb[7][0:8, 0:128], nT[:], ident[:], ["nT0", "nT1", "nT2", "nT3"], ["p7"])
            E("dve", CALL("tensor_copy", stg[0:8, 0:128], pb[7][0:8, 0:128]), ["p7"], ["stg"])
            outs.append(dma(o_n[oi], stg[0:8, 0:128], reads=["stg"], q="sp"))
            outs.append(dma(o_m[oi], mprev[:], reads=["mprev"], q="sp"))
            for c in range(KC):
                E("dve", CALL("tensor_copy", mw[0][:, 0:3], xmh[:, c, 0:3]), ["xmh"], ["mw0"])
                tr(pb[7][0:3, 0:128], mw[0][:, 0:3], ident[:], ["mw0"], ["p7"])
                E("dve", CALL("tensor_copy", stg[0:3, 0:128], pb[7][0:3, 0:128]), ["p7"], ["stg"])
                outs.append(dma(o_conv[oi][:, c * 128:(c + 1) * 128], stg[0:3, 0:128], reads=["stg"], q="sp"))
            for h in range(4):
                for vc in range(2):
                    for kc in range(2):
                        tr(pb[7][:, kc * 128:kc * 128 + 128], CT[:, h, kc, vc * 128:vc * 128 + 128], ident[:], ["CT0", "CT1", "CT2", "CT3"], ["p7"])
                    E("dve", CALL("tensor_copy", stg[:, 0:256], pb[7][:, 0:256]), ["p7"], ["stg"])
                    outs.append(dma(o_c[oi, h, vc * 128:(vc + 1) * 128, :], stg[:, 0:256], reads=["stg"], q="sp"))

        def run_tile(src_rows, dst_rows, Tt):
            load_tile(src_rows, Tt)
            if STAGE >= 2:
                ffn("ffn1", "norm_ffn1", Tt)
            if STAGE >= 3:
                in_proj(Tt)
            if STAGE >= 4:
                s5_mix(Tt)
            if STAGE >= 5:
                mlstm_mix(Tt)
            if STAGE >= 6:
                out_proj(Tt)
            if STAGE >= 7:
                ffn("ffn2", "norm_ffn2", Tt)
            if dst_rows is not None:
                store_tile(dst_rows, Tt)

        if STAGE == 0:
            P.emit(final_waits=outs)
            return nc
        init_state(None)
        run_tile(xp[0:NMETA, :], None, NMETA)
        for ti in range((NP - NMETA) // TT):
            r0 = NMETA + ti * TT
            run_tile(xp[r0:r0 + TT, :], yp[r0 - NMETA:r0 - NMETA + TT, :], TT)
        store_state(0)
        for si in range(NSAMP):
            init_state(si)
            run_tile(xs[si], ys[si], SL)
            store_state(1 + si)
        P.emit(final_waits=outs)
    return nc


def host_consts():
    p = np.arange(128)
    c = {}
    c["ident"] = np.eye(128, dtype=np.float32)
    c["maskE"] = np.stack([(p // 64 == 0), (p // 64 == 1)], 1).astype(np.float32)
    p32 = np.arange(32)
    c["mask16"] = np.stack([(p32 // 16 == 0), (p32 // 16 == 1)], 1).astype(np.float32)
    c["triu"] = np.triu(np.ones((128, 128), np.float32))
    c["bdmask"] = (p[:, None] // 4 == np.arange(32)[None, :]).astype(np.float32)
    c["tvec"] = np.broadcast_to(np.arange(1, 65, dtype=np.float32)[None, :], (128, 64)).copy()
    s = np.zeros((4, 4, 128), np.float32)
    for h in range(4):
        s[h, h, :] = 1.0
    c["sel4"] = s.reshape(4, 512)
    c["mask4"] = (p[:, None] // 32 == np.arange(4)[None, :]).astype(np.float32)
    return c


_CACHE = {}


def kernel(**inp):
    f = lambda a: np.ascontiguousarray(np.asarray(a, dtype=np.float32))
    x_prompt = f(inp["x_prompt"])
    x_sample = f(inp["x_sample"])
    NB, SEQ, _ = x_prompt.shape
    NDEC, SL, _ = x_sample.shape
    NP = NMETA + SEQ
    ncores = 8
    NSAMP = NDEC // ncores
    key = (NP, NSAMP, SL)
    if key not in _CACHE:
        _CACHE[key] = build_program(NP, NSAMP, SL)
    nc = _CACHE[key]
    meta = f(inp["meta_tokens"])
    shared = host_consts()
    for n in ["ffn1_gate", "ffn1_up", "ffn1_down", "ffn2_gate", "ffn2_up", "ffn2_down", "w_in", "s5_glu_w", "w_out"]:
        shared[n] = f(inp[n])[0]
    shared["lam_re"] = f(inp["s5_lambda_re"])[0].reshape(32, 128)
    shared["lam_im"] = f(inp["s5_lambda_im"])[0].reshape(32, 128)
    shared["log_dt"] = f(inp["s5_log_dt"])[0]
    shared["b_re"] = f(inp["s5_b_re"])[0]
    shared["b_im"] = f(inp["s5_b_im"])[0]
    shared["c_re"] = f(inp["s5_c_re"])[0]
    shared["c_im"] = f(inp["s5_c_im"])[0]
    shared["wq"] = f(inp["ml_wq"])[0]
    shared["wk"] = f(inp["ml_wk"])[0]
    shared["wv"] = f(inp["ml_wv"])[0]
    shared["igw"] = f(inp["ml_igate_w"])[0]
    shared["fgw"] = f(inp["ml_fgate_w"])[0]
    shared["igb"] = f(inp["ml_igate_b"])[0].reshape(4, 1)
    shared["fgb"] = f(inp["ml_fgate_b"])[0].reshape(4, 1)
    cw = f(inp["ml_conv_w"])[0]
    vd = {"norm_ffn1": f(inp["norm_ffn1"])[0], "norm_mix": f(inp["norm_mix"])[0], "s5_d": f(inp["s5_d"])[0],
          "s5_glu_b": f(inp["s5_glu_b"])[0], "cw0": cw[0], "cw1": cw[1], "cw2": cw[2], "cw3": cw[3],
          "ml_conv_b": f(inp["ml_conv_b"])[0], "ml_norm_w": f(inp["ml_norm_w"])[0], "ml_skip": f(inp["ml_skip"])[0],
          "out_norm_s5": f(inp["out_norm_s5"])[0], "out_norm_ml": f(inp["out_norm_ml"])[0],
          "norm_ffn2": f(inp["norm_ffn2"])[0], "norm_final": f(inp["norm_final"])}
    shared["vecs"] = np.concatenate([vd[n].reshape(8, 128) for n in VEC_NAMES], 0)
    s5re, s5im = f(inp["state_s5_re"])[0], f(inp["state_s5_im"])[0]
    stc, stn, stm, stcv = f(inp["state_mlstm_c"])[0], f(inp["state_mlstm_n"])[0], f(inp["state_mlstm_m"])[0], f(inp["state_mlstm_conv"])[0]
    in_maps = []
    for c in range(ncores):
        b = c % NB
        sl = slice(c * NSAMP, (c + 1) * NSAMP)
        m = dict(shared)
        m["xp"] = np.concatenate([meta, x_prompt[b]], 0)
        m["xs"] = x_sample[sl]
        m["st_s5re"] = s5re[sl].reshape(NSAMP, 32, 128)
        m["st_s5im"] = s5im[sl].reshape(NSAMP, 32, 128)
        m["st_c"] = stc[sl]
        m["st_n"] = stn[sl].reshape(NSAMP, 8, 128)
        m["st_m"] = stm[sl].reshape(NSAMP, 4, 1)
        m["st_conv"] = stcv[sl]
        in_maps.append(m)
    res = run_bass_kernel_spmd(nc, in_maps, core_ids=list(range(ncores))).results
    y_prompt = np.stack([res[b]["yp"] for b in range(NB)], 0)
    y_sample = np.concatenate([res[c]["ys"] for c in range(ncores)], 0)

    def gather(name, shape_tail):
        pr = np.stack([res[b][name][0] for b in range(NB)], 0).reshape((1, NB) + shape_tail)
        sm = np.concatenate([res[c][name][1:] for c in range(ncores)], 0).reshape((1, NDEC) + shape_tail)
        return pr.astype(np.float32), sm.astype(np.float32)

    p_re, s_re = gather("o_s5re", (64, 64))
    p_im, s_im = gather("o_s5im", (64, 64))
    p_c, s_c = gather("o_c", (4, 256, 256))
    p_n, s_n = gather("o_n", (4, 256))
    p_m, s_m = gather("o_m", (4,))
    p_cv, s_cv = gather("o_conv", (3, 1024))
    return (y_prompt.astype(np.float32), y_sample.astype(np.float32), p_re, p_im, p_c, p_n, p_m, p_cv,
            s_re, s_im, s_c, s_n, s_m, s_cv)
```

```python
import math
import numpy as np
import concourse.bass as bass
import concourse.mybir as mybir
from concourse.bass_utils import run_bass_kernel_spmd
from contextlib import ExitStack

F32 = mybir.dt.float32
BF16 = mybir.dt.bfloat16
I32 = mybir.dt.int32
ALU = mybir.AluOpType
AF = mybir.ActivationFunctionType

D = 1024
DFF = 2816
KC = 8
FC = 22
NMETA = 16
EPS = 1e-6
STAGE = 9
ENGS = ("pe", "act", "dve", "pool", "sp")
NDMA_SEM = 12
VEC_NAMES = ["norm_ffn1", "norm_mix", "s5_d", "s5_glu_b", "cw0", "cw1", "cw2", "cw3", "ml_conv_b",
             "ml_norm_w", "ml_skip", "out_norm_s5", "out_norm_ml", "norm_ffn2", "norm_final"]
VI = {n: i for i, n in enumerate(VEC_NAMES)}


class Op:
    __slots__ = ("eng", "fn", "deps", "idx", "signaled", "sigval", "dma", "dsem", "dval", "dprev")

    def __init__(self, eng, fn, dma):
        self.eng = eng
        self.fn = fn
        self.deps = []
        self.signaled = False
        self.sigval = 0
        self.dma = dma
        self.dsem = None
        self.dval = 0
        self.dprev = None


class Prog:
    def __init__(self, nc):
        self.nc = nc
        self.ops = {e: [] for e in ENGS}
        self.last_writer = {}
        self.readers = {}
        self.ndma = {e: 0 for e in ENGS}
        self.dma_ops = {e: [] for e in ENGS}

    def op(self, eng, fn, reads=(), writes=(), dma=False):
        o = Op(eng, fn, dma)
        deps = []
        for r in reads:
            w = self.last_writer.get(r)
            if w is not None:
                deps.append(w)
        for wr in writes:
            w = self.last_writer.get(wr)
            if w is not None:
                deps.append(w)
            deps.extend(self.readers.get(wr, ()))
        seen = set()
        for d in deps:
            if id(d) in seen or d is o:
                continue
            seen.add(id(d))
            if d.eng == "pe" and eng == "pe" and not d.dma and not dma:
                continue
            o.deps.append(d)
        for r in reads:
            self.readers.setdefault(r, []).append(o)
        for wr in writes:
            self.last_writer[wr] = o
            self.readers[wr] = []
        if dma:
            k = self.ndma[eng]
            self.ndma[eng] += 1
            o.dsem = k % NDMA_SEM
            o.dval = 16 * (k // NDMA_SEM + 1)
            if k >= NDMA_SEM:
                o.dprev = self.dma_ops[eng][k - NDMA_SEM]
            self.dma_ops[eng].append(o)
        o.idx = len(self.ops[eng])
        self.ops[eng].append(o)
        return o

    def emit(self, final_waits=()):
        nc = self.nc
        for e in ENGS:
            for o in self.ops[e]:
                for d in o.deps:
                    if not d.dma:
                        d.signaled = True
        for o in final_waits:
            if not o.dma:
                o.signaled = True
        for e in ENGS:
            c = 0
            for o in self.ops[e]:
                if o.signaled and not o.dma:
                    c += 1
                    o.sigval = c
        with ExitStack() as st:
            esem = {e: st.enter_context(nc.semaphore("s_" + e)) for e in ENGS}
            dsem = {e: [st.enter_context(nc.semaphore("d_%s_%d" % (e, i))) for i in range(NDMA_SEM)]
                    for e in ENGS if self.ndma[e] > 0}
            block = st.enter_context(nc.Block())

            def body(e, engine):
                observed = {}

                def wait(key, sem, val):
                    if observed.get(key, 0) >= val:
                        return
                    observed[key] = val
                    engine.wait_ge(sem, val)

                for o in self.ops[e]:
                    for d in o.deps:
                        if d.dma:
                            wait(("d", d.eng, d.dsem), dsem[d.eng][d.dsem], d.dval)
                        else:
                            wait(("e", d.eng), esem[d.eng], d.sigval)
                    if o.dma and o.dprev is not None:
                        wait(("d", e, o.dprev.dsem), dsem[e][o.dprev.dsem], o.dprev.dval)
                    ins = o.fn(engine)
                    if o.dma:
                        ins.then_inc(dsem[e][o.dsem], 16)
                    elif o.signaled:
                        ins.then_inc(esem[e], 1)
                if e == "sp":
                    for o in final_waits:
                        if o.dma:
                            wait(("d", o.eng, o.dsem), dsem[o.eng][o.dsem], o.dval)
                        else:
                            wait(("e", o.eng), esem[o.eng], o.sigval)

            block.sync(lambda eng: body("sp", eng))
            block.scalar(lambda eng: body("act", eng))
            block.vector(lambda eng: body("dve", eng))
            block.gpsimd(lambda eng: body("pool", eng))
            block.tensor(lambda eng: body("pe", eng))


def CALL(name, *args, **kw):
    return lambda e: getattr(e, name)(*args, **kw)


def bc_last(ap, n):
    return bass.AP(ap.tensor, ap.offset, [list(a) for a in ap.ap] + [[0, n]])


def bc_mid(ap, n):
    a = [list(x) for x in ap.ap]
    return bass.AP(ap.tensor, ap.offset, [a[0], [0, n]] + a[1:])


def bc_row(ap, n):
    a = [list(x) for x in ap.ap]
    return bass.AP(ap.tensor, ap.offset, [a[0], [0, n]])


def build_program(NP, NSAMP=2, SL=32):
    nc = bass.Bass("TRN2", target_bir_lowering=False)
    dr = {}

    def din(name, shape, dt=F32):
        dr[name] = nc.dram_tensor(name, list(shape), dt, kind="ExternalInput").ap()
        return dr[name]

    def dout(name, shape):
        dr[name] = nc.dram_tensor(name, list(shape), F32, kind="ExternalOutput").ap()
        return dr[name]

    xp = din("xp", [NP, D])
    xs = din("xs", [NSAMP, SL, D])
    st_s5re = din("st_s5re", [NSAMP, 32, 128])
    st_s5im = din("st_s5im", [NSAMP, 32, 128])
    st_c = din("st_c", [NSAMP, 4, 256, 256])
    st_n = din("st_n", [NSAMP, 8, 128])
    st_m = din("st_m", [NSAMP, 4, 1])
    st_conv = din("st_conv", [NSAMP, 3, D])
    vecs = din("vecs", [len(VEC_NAMES) * 8, 128])
    W = {}
    for n, shp in [("ffn1_gate", [D, DFF]), ("ffn1_up", [D, DFF]), ("ffn1_down", [DFF, D]),
                   ("ffn2_gate", [D, DFF]), ("ffn2_up", [D, DFF]), ("ffn2_down", [DFF, D]),
                   ("w_in", [D, 3 * D]), ("s5_glu_w", [D, D]), ("w_out", [2 * D, D]),
                   ("lam_re", [32, 128]), ("lam_im", [32, 128]), ("log_dt", [64]),
                   ("b_re", [64, 64, 16]), ("b_im", [64, 64, 16]), ("c_re", [64, 16, 64]), ("c_im", [64, 16, 64]),
                   ("wq", [256, 4, 4]), ("wk", [256, 4, 4]), ("wv", [256, 4, 4]),
                   ("igw", [3 * D, 4]), ("fgw", [3 * D, 4]), ("igb", [4, 1]), ("fgb", [4, 1]),
                   ("ident", [128, 128]), ("maskE", [128, 2]), ("mask16", [32, 2]), ("triu", [128, 128]),
                   ("bdmask", [128, 32]), ("tvec", [128, 64]), ("sel4", [4, 512]), ("mask4", [128, 4])]:
        W[n] = din(n, shp)
    NS = 1 + NSAMP
    yp = dout("yp", [NP - NMETA, D])
    ys = dout("ys", [NSAMP, SL, D])
    o_s5re = dout("o_s5re", [NS, 32, 128])
    o_s5im = dout("o_s5im", [NS, 32, 128])
    o_c = dout("o_c", [NS, 4, 256, 256])
    o_n = dout("o_n", [NS, 8, 128])
    o_m = dout("o_m", [NS, 4, 1])
    o_conv = dout("o_conv", [NS, 3, D])

    SG = {n: nc.dram_tensor("sg_" + n, [FC, 128, KC, 128], BF16).ap() for n in ["ffn1_gate", "ffn1_up", "ffn2_gate", "ffn2_up"]}
    SD = {n: nc.dram_tensor("sd_" + n, [KC, 2, 128, FC // 2, 128], BF16).ap() for n in ["ffn1_down", "ffn2_down"]}
    SI = nc.dram_tensor("s_win", [24, 128, KC, 128], BF16).ap()
    SGL = nc.dram_tensor("s_glu", [KC, 128, KC, 128], BF16).ap()
    SO = nc.dram_tensor("s_wout", [KC, 2, 128, 8, 128], BF16).ap()
    P = Prog(nc)
    outs = []
    TT = 512
    with ExitStack() as st:
        def sb(name, shape, dt=F32):
            return st.enter_context(nc.sbuf_tensor("sb_" + name, list(shape), dt))

        st.enter_context(nc.allow_low_precision("bf16 matmul operands with fp32 PSUM accumulation"))
        pb = [st.enter_context(nc.psum_tensor("pb%d" % i, [128, 512], F32)) for i in range(8)]

        xT = sb("xT", [128, KC, TT])
        xn = sb("xn", [128, KC, TT], BF16)
        xc_bf = xn
        hT = sb("hT", [128, 24, TT], BF16)
        u_bf = sb("u_bf", [128, KC, TT], BF16)
        z_bf = sb("z_bf", [128, KC, TT], BF16)
        xmh = sb("xmh", [128, KC, TT + 3], BF16)
        vT = hT[:, 16:24, :]
        mixed = sb("mixed", [128, 16, TT], BF16)
        ybuf = sb("ybuf", [128, KC, TT])
        xtok = ybuf[:].rearrange("p a b -> p (a b)").rearrange("p (n d) -> p n d", d=D)
        rstd = sb("rstd", [128, TT])
        tAB = sb("tAB", [128, 2 * TT])
        tA = tAB[:, 0:TT]
        tB = tAB[:, TT:2 * TT]
        otok = tAB
        wgr = [sb("wgr%d" % i, [128, KC, 128], BF16) for i in range(2)]
        wur = [sb("wur%d" % i, [128, KC, 128], BF16) for i in range(2)]
        wdr = [sb("wdr%d" % i, [128, FC // 2, 128], BF16) for i in range(2)]
        wir = [sb("wir%d" % i, [128, KC, 128], BF16) for i in range(2)]
        wor = [sb("wor%d" % i, [128, 8, 128], BF16) for i in range(2)]
        ident = sb("ident", [128, 128])
        ones_bf = sb("ones_bf", [128, 128], BF16)
        onesD = sb("onesD", [128, 128], BF16)
        ones256 = sb("ones256", [128, 128])
        ones4 = sb("ones4", [4, 128])
        onecol = sb("onecol", [128, 1])
        epscol = sb("epscol", [128, 1])
        vec = sb("vec", [128, len(VEC_NAMES) * 8])
        maskE = sb("maskE", [128, 2])
        mask16 = sb("mask16", [32, 2])
        triu = sb("triu", [128, 128])
        bdmask = sb("bdmask", [128, 32])
        tvec = sb("tvec", [128, 64])
        sel4 = sb("sel4", [4, 512])
        cosT = sb("cosT", [128, 32, 64])
        sinT = sb("sinT", [128, 32, 64])
        rmag = sb("rmag", [128, 32])
        W1 = sb("W1", [128, KC, 2, 128], BF16)
        W3 = sb("W3", [128, 32, 2, 32], BF16)
        s5st = sb("s5st", [128, 2, 32])
        sre = s5st[:, 0, :]
        sim = s5st[:, 1, :]
        s5all = sb("s5all", [128, 8, 256])
        s5w = [s5all[:, i, :].rearrange("p (a b) -> p a b", b=64) for i in range(8)]
        resetm = sb("resetm", [128, 64])
        inj = [sb("inj%d" % i, [128, 2, 2, 4]) for i in range(2)]
        xbfA = sb("xbfA", [128, 2, 4, 64], BF16)
        xbfB = sb("xbfB", [128, 2, 4, 64], BF16)
        um = sb("um", [128, 2, 4, 64], BF16)
        umB = sb("umB", [128, 2, 4, 64], BF16)
        mask4 = sb("mask4", [128, 4])
        CT = sb("CT", [128, 4, 2, 256])
        nT = sb("nT", [128, 8])
        mprev = sb("mprev", [4, 1])
        BD = sb("BD", [128, 3, KC, 128], BF16)
        gwi = sb("gwi", [128, 24, 4], BF16)
        gwf = sb("gwf", [128, 24, 4], BF16)
        igb = sb("igb", [4, 1])
        nfgb = sb("nfgb", [4, 1])
        igs = sb("igs", [4, TT])
        Fn = sb("Fn", [4, TT])
        aa = sb("aa", [4, TT])
        MM = sb("MM", [4, TT])
        g4 = [sb("g4_%d" % i, [4, 128]) for i in range(2)]
        negM = sb("negM", [4, 1])
        gcol = sb("gcol", [4, 1])
        Mpc = sb("Mpc", [4, 1])
        dg = sb("dg", [4, 4])
        et = sb("et", [128, 4])
        gb = sb("gb", [128, 4])
        ngc2 = sb("ngc2", [128, 2, 2])
        mwab = sb("mwab", [128, 128])
        ktp = sb("ktp", [128, 256], BF16)
        vtp = sb("vtp", [128, 256], BF16)
        vtb = sb("vtb", [128, 256], BF16)
        Sm = sb("Sm", [128, 128], BF16)
        Eb = sb("Eb", [128, 128], BF16)
        Cg = sb("Cg", [128, 2, 256], BF16)
        nrep = sb("nrep", [128, 2, 128], BF16)
        hh = sb("hh", [128, 2, 128])
        hq = sb("hq", [128, 2, 128])
        mw = [sb("mw%d" % i, [128, 128]) for i in range(4)]
        stg = sb("stg", [128, 256])

        SRE = ["sre%d" % c for c in range(KC)]
        SIM = ["sim%d" % c for c in range(KC)]
        dq = ["sp", "pool"]
        dqi = [0]

        def dma(out, in_, reads=(), writes=(), q=None, slow=False):
            if out.dtype != in_.dtype:
                q = "pool"
            if q is None:
                q = dq[dqi[0] % 2]
                dqi[0] += 1
            if slow:
                f = CALL("dma_start", out=out, in_=in_, allow_slow_non_contiguous=True)
            else:
                f = CALL("dma_start", out=out, in_=in_)
            return P.op(q, f, reads=reads, writes=writes, dma=True)

        def mm(out, lhsT, rhs, start, stop, reads, writes, tp=None):
            if tp is None:
                f = CALL("matmul", out, lhsT, rhs, start=start, stop=stop)
            else:
                f = CALL("matmul", out, lhsT, rhs, start=start, stop=stop, tile_position=tp)
            return P.op("pe", f, reads=reads, writes=writes)

        def tr(out, in_, idn, reads, writes):
            return P.op("pe", CALL("transpose", out, in_, idn), reads=list(reads) + ["ident"], writes=writes)

        def E(eng, fn, reads, writes):
            return P.op(eng, fn, reads=reads, writes=writes)

        def V(name, c):
            i = VI[name] * 8 + c
            return vec[:, i:i + 1]

        for name, t in [("ident", ident), ("maskE", maskE), ("mask16", mask16), ("triu", triu), ("bdmask", bdmask),
                        ("tvec", tvec), ("sel4", sel4), ("igb", igb), ("mask4", mask4)]:
            dma(t[:], W[name], writes=[name])
        E("dve", CALL("memset", ones_bf[:], 1.0), [], ["ones_bf"])
        E("dve", CALL("memset", onesD[:], 1.0 / D), [], ["onesD"])
        E("dve", CALL("memset", ones256[:], 1.0 / 256), [], ["ones256"])
        E("dve", CALL("memset", ones4[:], 1.0), [], ["ones4"])
        E("dve", CALL("memset", onecol[:], 1.0), [], ["onecol"])
        E("dve", CALL("memset", resetm[:], 1.0), [], ["resetm"])
        E("dve", CALL("memset", resetm[:, 0:1], 0.0), ["resetm"], ["resetm"])
        E("dve", CALL("memset", epscol[:], EPS), [], ["epscol"])
        dma(nfgb[:], W["fgb"], writes=["nfgb"])
        E("dve", CALL("tensor_scalar", nfgb[:], nfgb[:], -1.0, 0.0, ALU.mult, ALU.add), ["nfgb"], ["nfgb"])
        dma(stg[0:len(VEC_NAMES) * 8, 0:128], vecs, writes=["stg"])
        nv = len(VEC_NAMES) * 8
        tr(pb[7][:, 0:nv], stg[0:nv, 0:128], ident[0:nv, 0:nv], ["stg"], ["p7"])
        E("dve", CALL("tensor_copy", vec[:], pb[7][:, 0:nv]), ["p7"], ["vec"])
        dma(gwi[:], W["igw"].rearrange("(c p) h -> p c h", p=128), writes=["gwi"], slow=True)
        dma(gwf[:], W["fgw"].rearrange("(c p) h -> p c h", p=128), writes=["gwf"], slow=True)
        for wi, wn in enumerate(["wq", "wk", "wv"]):
            dma(stg[:, 0:32].rearrange("p (c o) -> p c o", o=4), W[wn].rearrange("(c b) i o -> (b i) c o", b=32),
                reads=[], writes=["stg"], slow=True)
            for c in range(KC):
                E("dve", CALL("tensor_tensor",
                    BD[:, wi, c, :].rearrange("p (b o) -> p b o", o=4),
                    bc_mid(stg[:, c * 4:c * 4 + 4], 32), bc_last(bdmask[:, :], 4), ALU.mult),
                  ["stg", "bdmask"], ["BD"])
        lamr = s5w[0][:, 0, 0:32]
        lami = s5w[0][:, 1, 0:32]
        dtb = s5w[0][:, 2, 0:32]
        th = s5w[1][:, 0, 0:32]
        cth = s5w[1][:, 1, 0:32]
        sth = s5w[1][:, 2, 0:32]
        lbr = s5w[2][:, 0, 0:32]
        lbi = s5w[2][:, 1, 0:32]
        kr = s5w[2][:, 2, 0:32]
        ki_ = s5w[2][:, 3, 0:32]
        t1 = s5w[3][:, 0, 0:32]
        t2 = s5w[3][:, 1, 0:32]
        t3 = s5w[3][:, 2, 0:32]
        for nm, dst in [("lam_re", lamr), ("lam_im", lami)]:
            dma(stg[0:32, 0:128], W[nm], writes=["stg"])
            tr(pb[7][:, 0:32], stg[0:32, 0:128], ident[0:32, 0:32], ["stg"], ["p7"])
            E("dve", CALL("tensor_copy", dst, pb[7][:, 0:32]), ["p7"], ["s5p"])
        ldt = W["log_dt"]
        for g2 in range(2):
            src = bass.AP(ldt.tensor, ldt.offset + g2, [[0, 64], [2, 32]])
            dma(s5w[0][64 * g2:64 * g2 + 64, 2, 0:32], src, writes=["s5p"], slow=True)
        E("act", CALL("activation", dtb, dtb, AF.Exp), ["s5p"], ["s5p"])
        E("dve", CALL("tensor_tensor", t1, lamr, dtb, ALU.mult), ["s5p"], ["s5p"])
        E("act", CALL("activation", rmag[:], t1, AF.Exp), ["s5p"], ["rmag"])
        E("dve", CALL("tensor_tensor", th, lami, dtb, ALU.mult), ["s5p"], ["s5p"])

        ki32 = sb("ki32", [128, 1, 64], I32)

        def sincos(dst, ang, n, shift, key_r, key_w):
            wk = s5w[6][:].rearrange("p a b -> p (a b)")[:, 0:n]
            wf = s5w[7][:].rearrange("p a b -> p (a b)")[:, 0:n]
            wi_ = ki32[:].rearrange("p a b -> p (a b)")[:, 0:n]
            E("dve", CALL("tensor_scalar", wk, ang, shift, 1.0 / (2 * math.pi), ALU.add, ALU.mult), key_r, ["s5t"])
            E("dve", CALL("tensor_copy", wi_, wk), ["s5t"], ["s5t"])
            E("dve", CALL("tensor_copy", wf, wi_), ["s5t"], ["s5t"])
            E("dve", CALL("tensor_scalar", wk, ang, shift, 0.0, ALU.add, ALU.add), key_r + ["s5t"], ["s5t"])
            E("dve", CALL("scalar_tensor_tensor", wk, wf, -2 * math.pi, wk, ALU.mult, ALU.add), ["s5t"], ["s5t"])
            E("dve", CALL("tensor_scalar", wf, wk, math.pi, -2 * math.pi, ALU.is_gt, ALU.mult), ["s5t"], ["s5t"])
            E("dve", CALL("tensor_tensor", wk, wk, wf, ALU.add), ["s5t"], ["s5t"])
            E("dve", CALL("tensor_scalar", wf, wk, -math.pi, 2 * math.pi, ALU.is_lt, ALU.mult), ["s5t"], ["s5t"])
            E("dve", CALL("tensor_tensor", wk, wk, wf, ALU.add), ["s5t"], ["s5t"])
            E("act", CALL("activation", dst, wk, AF.Sin), ["s5t"], key_w)

        sincos(sth, th, 32, 0.0, ["s5p"], ["s5p"])
        sincos(cth, th, 32, math.pi / 2, ["s5p"], ["s5p"])
        E("dve", CALL("tensor_tensor", lbr, rmag[:], cth, ALU.mult), ["s5p", "rmag"], ["s5p"])
        E("dve", CALL("tensor_tensor", lbi, rmag[:], sth, ALU.mult), ["s5p", "rmag"], ["s5p"])
        E("dve", CALL("tensor_scalar", t1, lbr, -1.0, 0.0, ALU.add, ALU.add), ["s5p"], ["s5p"])
        E("dve", CALL("tensor_tensor", t2, lamr, lamr, ALU.mult), ["s5p"], ["s5p"])
        E("dve", CALL("tensor_tensor", t3, lami, lami, ALU.mult), ["s5p"], ["s5p"])
        E("dve", CALL("tensor_tensor", t2, t2, t3, ALU.add), ["s5p"], ["s5p"])
        E("dve", CALL("reciprocal", t2, t2), ["s5p"], ["s5p"])
        E("dve", CALL("tensor_tensor", kr, t1, lamr, ALU.mult), ["s5p"], ["s5p"])
        E("dve", CALL("tensor_tensor", t3, lbi, lami, ALU.mult), ["s5p"], ["s5p"])
        E("dve", CALL("tensor_tensor", kr, kr, t3, ALU.add), ["s5p"], ["s5p"])
        E("dve", CALL("tensor_tensor", kr, kr, t2, ALU.mult), ["s5p"], ["s5p"])
        E("dve", CALL("tensor_tensor", ki_, lbi, lamr, ALU.mult), ["s5p"], ["s5p"])
        E("dve", CALL("tensor_tensor", t3, t1, lami, ALU.mult), ["s5p"], ["s5p"])
        E("dve", CALL("tensor_tensor", ki_, ki_, t3, ALU.subtract), ["s5p"], ["s5p"])
        E("dve", CALL("tensor_tensor", ki_, ki_, t2, ALU.mult), ["s5p"], ["s5p"])
        for q in range(32):
            ang = s5w[5][:, 0, :]
            E("dve", CALL("tensor_scalar", ang, tvec[:, :], th[:, q:q + 1], 0.0, ALU.mult, ALU.add),
              ["s5p", "tvec"], ["s5ang"])
            sincos(sinT[:, q, :], ang, 64, 0.0, ["s5ang"], ["sinT"])
            sincos(cosT[:, q, :], ang, 64, math.pi / 2, ["s5ang"], ["cosT"])
        Bre = CT[:].rearrange("p a b c -> p (a b c)")[:, 0:512].rearrange("p (q j) -> p q j", j=16)
        Bim = CT[:].rearrange("p a b c -> p (a b c)")[:, 512:1024].rearrange("p (q j) -> p q j", j=16)
        Ere = CT[:].rearrange("p a b c -> p (a b c)")[:, 1024:1536].rearrange("p (q j) -> p q j", j=16)
        Eim = CT[:].rearrange("p a b c -> p (a b c)")[:, 1536:2048].rearrange("p (q j) -> p q j", j=16)
        for g2 in range(2):
            dma(Bre[64 * g2:64 * g2 + 64], W["b_re"].rearrange("(q g) p j -> g p q j", g=2)[g2], writes=["CT0", "CT1", "CT2", "CT3"], slow=True)
            dma(Bim[64 * g2:64 * g2 + 64], W["b_im"].rearrange("(q g) p j -> g p q j", g=2)[g2], writes=["CT0", "CT1", "CT2", "CT3"], slow=True)
        kmr = s5w[4][:, 0, 0:32]
        kmi = s5w[4][:, 1, 0:32]
        Eexp_re = ybuf[:].rearrange("p a b -> p (a b)")[:, 0:1024].rearrange("p (q g j) -> p q g j", g=2, j=16)
        Eexp_im = ybuf[:].rearrange("p a b -> p (a b)")[:, 1024:2048].rearrange("p (q g j) -> p q g j", g=2, j=16)
        for g2 in range(2):
            E("dve", CALL("tensor_scalar", kmr, kr, maskE[:, g2:g2 + 1], 0.0, ALU.mult, ALU.add), ["s5p", "maskE"], ["s5k"])
            E("dve", CALL("tensor_scalar", kmi, ki_, maskE[:, g2:g2 + 1], 0.0, ALU.mult, ALU.add), ["s5p", "maskE"], ["s5k"])
            E("dve", CALL("tensor_tensor", Ere, Bre, bc_last(kmr, 16), ALU.mult), ["CT0", "CT1", "CT2", "CT3"] + ["s5k"], ["CT0", "CT1", "CT2", "CT3"])
            E("dve", CALL("tensor_tensor", Eim, Bim, bc_last(kmi, 16), ALU.mult), ["CT0", "CT1", "CT2", "CT3"] + ["s5k"], ["CT0", "CT1", "CT2", "CT3"])
            E("dve", CALL("tensor_tensor", Eexp_re[:, :, g2, :], Ere, Eim, ALU.subtract), ["CT0", "CT1", "CT2", "CT3"], ["ybuf"])
            E("dve", CALL("tensor_tensor", Ere, Bim, bc_last(kmr, 16), ALU.mult), ["CT0", "CT1", "CT2", "CT3"] + ["s5k"], ["CT0", "CT1", "CT2", "CT3"])
            E("dve", CALL("tensor_tensor", Eim, Bre, bc_last(kmi, 16), ALU.mult), ["CT0", "CT1", "CT2", "CT3"] + ["s5k"], ["CT0", "CT1", "CT2", "CT3"])
            E("dve", CALL("tensor_tensor", Eexp_im[:, :, g2, :], Ere, Eim, ALU.add), ["CT0", "CT1", "CT2", "CT3"], ["ybuf"])
        for c in range(KC):
            for ri, Ex in enumerate([Eexp_re, Eexp_im]):
                src = Ex[:, 4 * c:4 * c + 4, :, :].rearrange("p q g j -> p (q g j)")
                tr(pb[7][:, 0:128], src, ident[:], ["ybuf"], ["p7"])
                E("act", CALL("copy", W1[:, c, ri, :], pb[7][:, 0:128]), ["p7"], ["W1"])
        Cst = CT[:].rearrange("p a b c -> p (a b c)")[0:32, 0:2048].rearrange("p (q k) -> p q k", k=64)
        Cx = xT[:].rearrange("p a b -> p (a b)")[0:32, 0:4096].rearrange("p (q g k) -> p q g k", g=2, k=64)
        for ri, (cn, sgn) in enumerate([("c_re", 1.0), ("c_im", -1.0)]):
            for g2 in range(2):
                dma(Cst[16 * g2:16 * g2 + 16], W[cn].rearrange("(q g) h p -> g h q p", g=2)[g2], writes=["CT0", "CT1", "CT2", "CT3"], slow=True)
            for g2 in range(2):
                E("dve", CALL("tensor_scalar", Cx[:, :, g2, :], Cst, mask16[:, g2:g2 + 1], sgn, ALU.mult, ALU.mult),
                  ["CT0", "CT1", "CT2", "CT3"] + ["mask16"], ["xT"])
            for q in range(32):
                tr(pb[6][:, (q % 16) * 32:(q % 16) * 32 + 32], Cx[:, q, :, :].rearrange("p g k -> p (g k)"), ident[0:32, 0:32], ["xT"], ["p6"])
                if q % 16 == 15:
                    q0 = q - 15
                    E("act", CALL("copy", W3[:, q0:q0 + 16, ri, :], pb[6][:].rearrange("p (q h) -> p q h", h=32)), ["p6"], ["W3"])

        hTf = hT[:].rearrange("p a b -> p (a b)")
        pcs = [(hTf[:, 0:4096], ["h%d" % i for i in range(8)]), (hTf[:, 4096:8192], ["h%d" % i for i in range(8, 16)])]
        pci = [0]
        WSCR = ["wscr%d" % i for i in range(12)]

        def precast(src_rows, ncols, dst_ap, pattern, **kw):
            stg_, names = pcs[pci[0] % 2]
            pci[0] += 1
            dma(stg_[:, 0:ncols], src_rows, writes=names, q="pool")
            dma(dst_ap.rearrange(pattern), stg_[:, 0:ncols].rearrange("p (a k) -> p a k", k=128), reads=names, writes=["wscr%d" % ((pci[0] - 1) % 12)], q="sp", slow=True)

        for n in ["ffn1_gate", "ffn1_up", "ffn2_gate", "ffn2_up"]:
            for c in range(KC):
                precast(W[n][c * 128:(c + 1) * 128, :], DFF, SG[n][:, :, c, :], "f p k -> p f k")
        for n in ["ffn1_down", "ffn2_down"]:
            for f in range(FC):
                precast(W[n][f * 128:(f + 1) * 128, :], D, SD[n][:, f // 11, :, f % 11, :], "c p k -> p c k")
        for c in range(KC):
            precast(W["w_in"][c * 128:(c + 1) * 128, :], 3 * D, SI[:, :, c, :], "f p k -> p f k")
        for c in range(KC):
            precast(W["s5_glu_w"][c * 128:(c + 1) * 128, :], D, SGL[:, :, c, :], "f p k -> p f k")
        for k in range(16):
            precast(W["w_out"][k * 128:(k + 1) * 128, :], D, SO[:, k // 8, :, k % 8, :], "c p k -> p c k")

        wcnt = {"g": 0, "u": 0, "d": 0, "i": 0, "o": 0}

        def rmsnorm_stats(src_chunks, nch, Tt, key_r, scale_mat):
            for c in range(nch):
                if c % 2 == 0:
                    E("act", CALL("activation", hT[:, 14 + c, 0:Tt], src_chunks(c), AF.Square), key_r, ["h%d" % (14 + c)])
                else:
                    E("dve", CALL("tensor_tensor", hT[:, 14 + c, 0:Tt], src_chunks(c), src_chunks(c), ALU.mult), key_r, ["h%d" % (14 + c)])
            for c in range(nch):
                mm(pb[6][:, 0:Tt], scale_mat[:], hT[:, 14 + c, 0:Tt], c == 0, c == nch - 1, ["onesD", "h%d" % (14 + c)], ["p6"])
            E("act", CALL("activation", rstd[:, 0:Tt], pb[6][:, 0:Tt], AF.Ln, bias=epscol[:, 0:1]), ["p6", "epscol"], ["rstd"])
            E("act", CALL("activation", rstd[:, 0:Tt], rstd[:, 0:Tt], AF.Exp, scale=-0.5), ["rstd"], ["rstd"])

        def norm_x(gname, Tt):
            rmsnorm_stats(lambda c: xT[:, c, 0:Tt], KC, Tt, ["xT"], onesD)
            for c in range(KC):
                E("dve", CALL("scalar_tensor_tensor", xn[:, c, 0:Tt], xT[:, c, 0:Tt], V(gname, c), rstd[:, 0:Tt], ALU.mult, ALU.mult),
                  ["xT", "vec", "rstd"], ["xn%d" % c])

        def ffn(pref, gname, Tt):
            norm_x(gname, Tt)
            wg, wu, wd = W[pref + "_gate"], W[pref + "_up"], W[pref + "_down"]
            for f in range(FC):
                gi = wcnt["g"] % 2
                wcnt["g"] += 1
                dma(wgr[gi][:], SG[pref + "_gate"][f], reads=WSCR, writes=["wg%d" % gi], q="sp")
                dma(wur[gi][:], SG[pref + "_up"][f], reads=WSCR, writes=["wu%d" % gi], q="sp")
                pg, pu = pb[f % 2], pb[2 + f % 2]
                for c in range(KC):
                    mm(pg[:, 0:Tt], wgr[gi][:, c, :], xn[:, c, 0:Tt], c == 0, c == KC - 1, ["wg%d" % gi, "xn%d" % c], ["p%d" % (f % 2)])
                for c in range(KC):
                    mm(pu[:, 0:Tt], wur[gi][:, c, :], xn[:, c, 0:Tt], c == 0, c == KC - 1, ["wu%d" % gi, "xn%d" % c], ["p%d" % (2 + f % 2)])
                tt = tA if f % 2 == 0 else tB
                tn = "tA" if f % 2 == 0 else "tB"
                E("act", CALL("activation", tt[:, 0:Tt], pg[:, 0:Tt], AF.Silu), ["p%d" % (f % 2)], [tn])
                E("dve", CALL("tensor_tensor", hT[:, f, 0:Tt], tt[:, 0:Tt], pu[:, 0:Tt], ALU.mult),
                  [tn, "p%d" % (2 + f % 2)], ["h%d" % f])
            for c in range(KC):
                pd = pb[4 + c % 2]
                for hf in range(2):
                    di = hf
                    dma(wdr[di][:], SD[pref + "_down"][c, hf], reads=WSCR, writes=["wd%d" % di], q="sp")
                    for f2 in range(FC // 2):
                        f = hf * (FC // 2) + f2
                        mm(pd[:, 0:Tt], wdr[di][:, f2, :], hT[:, f, 0:Tt], f == 0, f == FC - 1, ["wd%d" % di, "h%d" % f], ["p%d" % (4 + c % 2)])
                E("dve", CALL("scalar_tensor_tensor", xT[:, c, 0:Tt], pd[:, 0:Tt], 0.5, xT[:, c, 0:Tt], ALU.mult, ALU.add),
                  ["p%d" % (4 + c % 2), "xT"], ["xT"])

        def in_proj(Tt):
            norm_x("norm_mix", Tt)
            for oc in range(24):
                ii = wcnt["i"] % 2
                wcnt["i"] += 1
                dma(wir[ii][:], SI[oc], reads=WSCR, writes=["wi%d" % ii], q="sp")
                po = pb[4 + oc % 2]
                for c in range(KC):
                    mm(po[:, 0:Tt], wir[ii][:, c, :], xn[:, c, 0:Tt], c == 0, c == KC - 1, ["wi%d" % ii, "xn%d" % c], ["p%d" % (4 + oc % 2)])
                if oc < 8:
                    dst, key = u_bf[:, oc, 0:Tt], "u_bf"
                elif oc < 16:
                    dst, key = xmh[:, oc - 8, 3:3 + Tt], "xmh"
                else:
                    dst, key = z_bf[:, oc - 16, 0:Tt], "z_bf"
                if oc % 2 == 0:
                    E("act", CALL("copy", dst, po[:, 0:Tt]), ["p%d" % (4 + oc % 2)], [key])
                else:
                    E("dve", CALL("tensor_copy", dst, po[:, 0:Tt]), ["p%d" % (4 + oc % 2)], [key])

        def s5_mix(Tt):
            gT = hT
            L = min(64, Tt)
            nun = Tt // L
            def s5_pre(c, un, S):
                t0 = un * L
                X = S["extra"]
                pS, psk = S["pS"][un % 2], S["psk"][un % 2]
                umS = S["um"][:, un % 2]
                kum = "s5%sum%d" % (S["k"], un % 2)
                pSv = pS[:].rearrange("p (q r t) -> p q r t", q=4, r=2)
                for qq in range(4):
                    E("act", CALL("activation", umS[:, qq, 0:L], u_bf[:, c, t0:t0 + L], AF.Copy, scale=mask4[:, qq:qq + 1]), ["u_bf", "mask4"] + X, [kum])
                for qq in range(4):
                    for ri in range(2):
                        mm(pSv[:, qq, ri, 0:L], W1[:, c, ri, :], umS[:, qq, 0:L], True, True, ["W1", kum] + X, [psk])

            def s5_unit(c, un, S):
                pY = pb[2 + c % 2]
                pyk = "p%d" % (2 + c % 2)
                t0 = un * L
                X = S["extra"]
                K = lambda i: "s5%s%d" % (S["k"], i)
                pS, psk = S["pS"][un % 2], S["psk"][un % 2]
                xbf = S["xbf"]
                kxb = K(11)
                injC, kinjC = S["inj"][:, un % 2], "s5%sinj%d" % (S["k"], un % 2)
                injN, kinjN = S["inj"][:, (un + 1) % 2], "s5%sinj%d" % (S["k"], (un + 1) % 2)
                sk = "sre%d" % c
                pSv = pS[:].rearrange("p (q r t) -> p q r t", q=4, r=2)
                if un == 0:
                    s5_pre(c, 0, S)
                    E("pool", CALL("tensor_tensor", injC, s5st[:, :, 4 * c:4 * c + 4], bc_mid(rmag[:, 4 * c:4 * c + 4], 2), ALU.mult), ["rmag", sk] + X, [kinjC])
                    yield
                if un + 1 < nun and L == 64:
                    s5_pre(c, un + 1, S)
                    yield
                bre = pSv[:, :, 0, 0:L]
                bim = pSv[:, :, 1, 0:L]
                cs = cosT[:, 4 * c:4 * c + 4, 0:L]
                sn = sinT[:, 4 * c:4 * c + 4, 0:L]
                B = S["bufs"]
                bv = lambda i: B[:, i * 256:(i + 1) * 256].rearrange("p (q t) -> p q t", t=64)[:, :, 0:L]
                E("dve", CALL("tensor_tensor", bv(2), bre, cs, ALU.mult), [psk, "cosT"] + X, [K(2)])
                E("dve", CALL("tensor_tensor", bv(3), bim, sn, ALU.mult), [psk, "sinT"] + X, [K(3)])
                E("dve", CALL("tensor_tensor", bv(6), bim, cs, ALU.mult), [psk, "cosT"] + X, [K(6)])
                E("dve", CALL("tensor_tensor", bv(7), bre, sn, ALU.mult), [psk, "sinT"] + X, [K(7)])
                E("dve", CALL("tensor_tensor", bv(0), bv(2), bv(3), ALU.add), [K(2), K(3)] + X, [K(0)])
                E("dve", CALL("tensor_tensor", bv(1), bv(6), bv(7), ALU.subtract), [K(6), K(7)] + X, [K(1)])
                if L == 64:
                    W2 = B[:, 0:512].rearrange("p (r q t) -> p r q t", r=2, t=64)
                    E("dve", CALL("tensor_tensor", W2[:, :, :, 0], W2[:, :, :, 0], injC, ALU.add), [K(0), K(1), kinjC] + X, [K(0), K(1)])
                    rt = S["rtab"][:, 4 * c:4 * c + 4, :].rearrange("p q t -> p (q t)")
                    E("dve", CALL("tensor_tensor_scan", B[:, 1024:1280], rt, B[:, 0:256], 0.0, ALU.mult, ALU.add), [K(0), "rtab"] + X, [K(4)])
                    E("dve", CALL("tensor_tensor_scan", B[:, 1280:1536], rt, B[:, 256:512], 0.0, ALU.mult, ALU.add), [K(1), "rtab"] + X, [K(5)])
                    yield
                else:
                    for qq in range(4):
                        q = 4 * c + qq
                        E("dve", CALL("tensor_tensor_scan", bv(4)[:, qq, :], bc_row(rmag[:, q:q + 1], L), bv(0)[:, qq, :],
                                      sre[:, q:q + 1], ALU.mult, ALU.add), ["rmag", K(0), sk] + X, [K(4)])
                        E("dve", CALL("tensor_tensor_scan", bv(5)[:, qq, :], bc_row(rmag[:, q:q + 1], L), bv(1)[:, qq, :],
                                      sim[:, q:q + 1], ALU.mult, ALU.add), ["rmag", K(1), sk] + X, [K(5)])
                    yield
                zr, zi = bv(4), bv(5)
                E("pool", CALL("tensor_tensor", bv(2), zr, cs, ALU.mult), [K(4), "cosT"] + X, [K(2)])
                E("pool", CALL("tensor_tensor", bv(3), zi, sn, ALU.mult), [K(5), "sinT"] + X, [K(3)])
                E("pool", CALL("tensor_tensor", bv(6), bv(2), bv(3), ALU.subtract), [K(2), K(3)] + X, [K(6)])
                E("pool", CALL("tensor_tensor", bv(2), zi, cs, ALU.mult), [K(5), "cosT"] + X, [K(2)])
                E("pool", CALL("tensor_tensor", bv(3), zr, sn, ALU.mult), [K(4), "sinT"] + X, [K(3)])
                E("pool", CALL("tensor_tensor", bv(7), bv(2), bv(3), ALU.add), [K(2), K(3)] + X, [K(7)])
                X4 = B[:, 1536:2048].rearrange("p (r q t) -> p r q t", r=2, t=64)
                if L == 64 and un + 1 < nun:
                    E("pool", CALL("tensor_tensor", injN, X4[:, :, :, L - 1], bc_mid(rmag[:, 4 * c:4 * c + 4], 2), ALU.mult), ["rmag", K(6), K(7)] + X, [kinjN])
                yield
                E("act", CALL("copy", xbf[:, :, :, 0:L], X4[:, :, :, 0:L]), [K(6), K(7)] + X, [kxb])
                if L != 64 or un + 1 == nun:
                    E("act", CALL("copy", s5st[:, :, 4 * c:4 * c + 4], X4[:, :, :, L - 1]), [K(6), K(7)] + X, [sk])
                yield
                for qq in range(4):
                    q = 4 * c + qq
                    mm(pY[32 * qq:32 * qq + 32, t0:t0 + L], W3[:, q, 0, :], xbf[:, 0, qq, 0:L], True, False, ["W3", kxb] + X, [pyk], tp=(0, 32 * qq))
                    mm(pY[32 * qq:32 * qq + 32, t0:t0 + L], W3[:, q, 1, :], xbf[:, 1, qq, 0:L], False, True, ["W3", kxb] + X, [pyk], tp=(0, 32 * qq))
                yield

            def s5_post(c):
                pY = pb[2 + c % 2]
                pyk = "p%d" % (2 + c % 2)
                E("dve", CALL("scalar_tensor_tensor", tA[:, 0:Tt], u_bf[:, c, 0:Tt], V("s5_d", c), pY[:, 0:Tt], ALU.mult, ALU.add),
                  ["u_bf", "vec", pyk], ["tA"])
                E("act", CALL("activation", tB[:, 0:Tt], tA[:, 0:Tt], AF.Square), ["tA"], ["tB"])
                E("dve", CALL("tensor_scalar", tB[:, 0:Tt], tB[:, 0:Tt], 0.044715, 1.0, ALU.mult, ALU.add), ["tB"], ["tB"])
                E("pool", CALL("tensor_tensor", tB[:, 0:Tt], tB[:, 0:Tt], tA[:, 0:Tt], ALU.mult), ["tA", "tB"], ["tB"])
                E("act", CALL("activation", tB[:, 0:Tt], tB[:, 0:Tt], AF.Sigmoid, scale=1.5957691216057308), ["tB"], ["tB"])
                E("dve", CALL("tensor_tensor", gT[:, c, 0:Tt], tA[:, 0:Tt], tB[:, 0:Tt], ALU.mult), ["tA", "tB"], ["h%d" % c])

            ybf_ = ybuf[:].rearrange("p a b -> p (a b)")
            rtab = ybf_[:, 2048:4096].rearrange("p (q t) -> p q t", t=64)
            E("dve", CALL("tensor_tensor", rtab, bc_last(rmag[:, :], 64), bc_mid(resetm[:, :], 32), ALU.mult), ["rmag", "resetm", "ybuf"], ["rtab"])
            SA = dict(bufs=s5all[:].rearrange("p a b -> p (a b)"), um=um, xbf=xbfA, inj=inj[0], k="A", extra=[], pS=[pb[0], pb[1]], psk=["p0", "p1"], rtab=rtab)
            SB = dict(bufs=ybf_[:, 0:2048], um=umB, xbf=xbfB, inj=inj[1], k="B", extra=["ybuf"], pS=[pb[4], pb[5]], psk=["p4", "p5"], rtab=rtab)
            for cp in range(KC // 2):
                c0, c1 = 2 * cp, 2 * cp + 1
                for un in range(nun):
                    gens = [s5_unit(c0, un, SA), s5_unit(c1, un, SB)]
                    while gens:
                        for g_ in list(gens):
                            try:
                                next(g_)
                            except StopIteration:
                                gens.remove(g_)
                s5_post(c0)
                s5_post(c1)
            for oc in range(KC):
                ii = wcnt["i"] % 2
                wcnt["i"] += 1
                dma(wir[ii][:], SGL[oc], reads=WSCR, writes=["wi%d" % ii], q="sp")
                po = pb[4 + oc % 2]
                pk = "p%d" % (4 + oc % 2)
                for c in range(KC):
                    mm(po[:, 0:Tt], wir[ii][:, c, :], gT[:, c, 0:Tt], c == 0, c == KC - 1, ["wi%d" % ii, "h%d" % c], [pk])
                E("act", CALL("activation", tA[:, 0:Tt], po[:, 0:Tt], AF.Sigmoid, bias=V("s5_glu_b", oc)), [pk, "vec"], ["tA"])
                E("dve", CALL("tensor_tensor", ybuf[:, oc, 0:Tt], gT[:, oc, 0:Tt], tA[:, 0:Tt], ALU.mult), ["tA", "h%d" % oc], ["ybuf"])
            rmsnorm_stats(lambda c: ybuf[:, c, 0:Tt], KC, Tt, ["ybuf"], onesD)
            for c in range(KC):
                E("dve", CALL("scalar_tensor_tensor", mixed[:, c, 0:Tt], ybuf[:, c, 0:Tt], V("out_norm_s5", c), rstd[:, 0:Tt], ALU.mult, ALU.mult),
                  ["ybuf", "vec", "rstd"], ["mixed"])

        def mlstm_mix(Tt):
            qT = hT
            for c in range(KC):
                eng = "dve"
                E(eng, CALL("tensor_scalar", tA[:, 0:Tt], xmh[:, c, 0:Tt], V("cw0", c), 0.0, ALU.mult, ALU.add), ["xmh", "vec"], ["tA"])
                for j in range(1, 4):
                    E(eng, CALL("scalar_tensor_tensor", tA[:, 0:Tt], xmh[:, c, j:j + Tt], V("cw%d" % j, c), tA[:, 0:Tt], ALU.mult, ALU.add),
                      ["xmh", "vec", "tA"], ["tA"])
                E("act", CALL("activation", xc_bf[:, c, 0:Tt], tA[:, 0:Tt], AF.Silu, bias=V("ml_conv_b", c)), ["tA", "vec"], ["xn%d" % c])
            for c in range(KC):
                E("act", CALL("activation", z_bf[:, c, 0:Tt], z_bf[:, c, 0:Tt], AF.Silu), ["z_bf"], ["z_bf"])
            for c in range(KC):
                for wi, (src, dstT, key) in enumerate([(xc_bf[:, c, 0:Tt], qT[:, c, 0:Tt], "h%d" % c),
                                                       (xc_bf[:, c, 0:Tt], qT[:, 8 + c, 0:Tt], "h%d" % (8 + c)),
                                                       (xmh[:, c, 3:3 + Tt], vT[:, c, 0:Tt], "h%d" % (16 + c))]):
                    po = pb[4 + (3 * c + wi) % 2]
                    pk = "p%d" % (4 + (3 * c + wi) % 2)
                    mm(po[:, 0:Tt], BD[:, wi, c, :], src, True, True, ["BD", "xn%d" % c, "xmh"], [pk])
                    if wi == 1:
                        E("dve", CALL("tensor_copy", dstT, po[:, 0:Tt]), [pk], [key])
                    else:
                        E("act", CALL("copy", dstT, po[:, 0:Tt]), [pk], [key])
            for gi, (gw, pbk) in enumerate([(gwi, 4), (gwf, 5)]):
                for j in range(24):
                    src = qT[:, j, 0:Tt]
                    key = "h%d" % j
                    mm(pb[pbk][0:4, 0:Tt], gw[:, j, :], src, j == 0, j == 23, ["gwi", "gwf", key], ["p%d" % pbk])
            E("act", CALL("activation", igs[:, 0:Tt], pb[4][0:4, 0:Tt], AF.Identity, bias=igb[:, 0:1]), ["p4", "igb"], ["igs"])
            E("act", CALL("activation", MM[:, 0:Tt], pb[5][0:4, 0:Tt], AF.Exp, bias=nfgb[:, 0:1], scale=-1.0), ["p5", "nfgb"], ["MM"])
            E("act", CALL("activation", MM[:, 0:Tt], MM[:, 0:Tt], AF.Ln, bias=onecol[0:4, 0:1]), ["MM", "onecol"], ["MM"])
            E("dve", CALL("tensor_tensor_scan", Fn[:, 0:Tt], bc_row(onecol[0:4, 0:1], Tt), MM[:, 0:Tt], 0.0, ALU.mult, ALU.add), ["MM", "onecol"], ["Fn"])
            E("dve", CALL("tensor_tensor", aa[:, 0:Tt], igs[:, 0:Tt], Fn[:, 0:Tt], ALU.add), ["igs", "Fn"], ["aa"])
            E("dve", CALL("tensor_tensor_scan", MM[:, 0:Tt], bc_row(onecol[0:4, 0:1], Tt), aa[:, 0:Tt], mprev[:, 0:1], ALU.mult, ALU.max),
              ["aa", "onecol", "mprev"], ["MM"])
            E("dve", CALL("tensor_copy", Mpc[:], mprev[:]), ["mprev"], ["Mpc"])
            Lc = min(128, Tt)
            for ch in range(Tt // Lc):
                t0, t1 = ch * Lc, ch * Lc + Lc
                E("dve", CALL("tensor_scalar", negM[:], MM[:, t1 - 1:t1], -1.0, 0.0, ALU.mult, ALU.add), ["MM"], ["negM"])
                E("act", CALL("activation", g4[0][:, 0:Lc], aa[:, t0:t1], AF.Exp, bias=negM[:, 0:1]), ["aa", "negM"], ["g40"])
                E("act", CALL("activation", gcol[:], Mpc[:], AF.Exp, bias=negM[:, 0:1]), ["Mpc", "negM"], ["gcol"])
                E("dve", CALL("tensor_scalar", g4[1][:, 0:Lc], Fn[:, t0:t1], negM[:, 0:1], 0.0, ALU.add, ALU.add), ["Fn", "negM"], ["g41"])
                E("dve", CALL("tensor_copy", Mpc[:], MM[:, t1 - 1:t1]), ["MM", "gcol"], ["Mpc"])
                tr(pb[7][0:Lc, 128:132], g4[0][:, 0:Lc], ident[0:4, 0:4], ["g40"], ["p7"])
                E("dve", CALL("tensor_copy", et[0:Lc, :], pb[7][0:Lc, 128:132]), ["p7"], ["et"])
                E("dve", CALL("tensor_scalar", dg[:], ident[0:4, 0:4], gcol[:, 0:1], 0.0, ALU.mult, ALU.add), ["ident", "gcol"], ["dg"])
                mm(pb[7][:, 132:136], ones4[:], dg[:], True, True, ["ones4", "dg"], ["p7"])
                E("dve", CALL("tensor_copy", gb[:], pb[7][:, 132:136]), ["p7"], ["gb"])
                for c in range(KC):
                    mm(pb[c // 4][0:Lc, (c % 4) * 128:(c % 4) * 128 + 128], xc_bf[:, c, t0:t1], BD[:, 1, c, :], True, True, ["xn%d" % c, "BD"], ["p%d" % (c // 4)])
                    mm(pb[2 + c // 4][0:Lc, (c % 4) * 128:(c % 4) * 128 + 128], xmh[:, c, 3 + t0:3 + t1], BD[:, 2, c, :], True, True, ["xmh", "BD"], ["p%d" % (2 + c // 4)])
                def SET(pr):
                    if pr == 0:
                        return dict(ktp=ktp, vtp=vtp, vtb=vtb, Sm=Sm, Eb=Eb, Cg=Cg, nrep=nrep, hh=hh, hq=hq, ngc=ngc2[:, 0, :], X=[], k="0")
                    vv = hT[:, 16:20, :]
                    return dict(ktp=vv[:, 0, 0:256], vtp=vv[:, 0, 256:512], vtb=vv[:, 1, 0:256], Sm=vv[:, 1, 256:384], Eb=vv[:, 1, 384:512],
                                Cg=vv[:, 2, :].rearrange("p (a b) -> p a b", a=2), nrep=vv[:, 3, 0:256].rearrange("p (a b) -> p a b", a=2),
                                hh=tAB[:, 0:256].rearrange("p (a b) -> p a b", a=2), hq=tAB[:, 256:512].rearrange("p (a b) -> p a b", a=2),
                                ngc=ngc2[:, 1, :], X=["h16", "h17", "h18", "h19", "tA"], k="1")

                def prep(h):
                    S_ = SET(h % 2)
                    X = S_["X"]
                    N = lambda n: "ml%s%s" % (n, S_["k"])
                    kps = pb[h // 2][0:Lc, (h % 2) * 256:(h % 2) * 256 + 256]
                    vps = pb[2 + h // 2][0:Lc, (h % 2) * 256:(h % 2) * 256 + 256]
                    kk, vk = "p%d" % (h // 2), "p%d" % (2 + h // 2)
                    for kc in range(2):
                        mm(pb[7][0:Lc, 0:Lc], qT[:, 8 + 2 * h + kc, t0:t1], qT[:, 2 * h + kc, t0:t1], kc == 0, kc == 1,
                           ["h%d" % (8 + 2 * h + kc), "h%d" % (2 * h + kc), "p7"], ["p7s"])
                    E("dve", CALL("scalar_tensor_tensor", S_["Sm"][0:Lc, 0:Lc], pb[7][0:Lc, 0:Lc], 1.0 / 16, triu[0:Lc, 0:Lc], ALU.mult, ALU.mult), ["p7s", "p7", "triu"] + X, [N("Sm")])
                    E("dve", CALL("tensor_scalar", S_["vtp"][0:Lc, :], vps, et[0:Lc, h:h + 1], 0.0, ALU.mult, ALU.add), [vk, "et"] + X, [N("vtp")])
                    E("act", CALL("copy", S_["vtb"][0:Lc, :], vps), [vk] + X, [N("vtb")])
                    E("dve", CALL("tensor_scalar", S_["ktp"][0:Lc, :], kps, et[0:Lc, h:h + 1], 1.0 / 16, ALU.mult, ALU.mult), [kk, "et"] + X, [N("ktp")])
                    E("pool", CALL("tensor_scalar", S_["Eb"][0:Lc, :], ones_bf[0:Lc, :], et[0:Lc, h:h + 1], 0.0, ALU.mult, ALU.add), ["ones_bf", "et"] + X, [N("Eb")])
                    E("pool", CALL("tensor_scalar", S_["Cg"].rearrange("p a b -> p (a b)"), CT[:, h, :, :].rearrange("p a b -> p (a b)"), gb[:, h:h + 1], 0.0, ALU.mult, ALU.add),
                      ["CT%d" % h, "gb"] + X, [N("Cg")])
                    E("dve", CALL("tensor_scalar", S_["ngc"], nT[:, 2 * h:2 * h + 2], gb[:, h:h + 1], 0.0, ALU.mult, ALU.add), ["nT%d" % h, "gb"], [N("ngc")])
                    for kc in range(2):
                        E("pool", CALL("tensor_scalar", S_["nrep"][:, kc, :], ones_bf[:, :], S_["ngc"][:, kc:kc + 1], 0.0, ALU.mult, ALU.add), ["ones_bf", N("ngc")] + X, [N("nrep")])

                def mid(h):
                    S_ = SET(h % 2)
                    X = S_["X"]
                    N = lambda n: "ml%s%s" % (n, S_["k"])
                    hh_, hq_ = S_["hh"], S_["hq"]
                    for vc in range(2):
                        o = pb[4][:, vc * 128:vc * 128 + Lc]
                        mm(o, S_["vtp"][0:Lc, vc * 128:vc * 128 + 128], S_["Sm"][0:Lc, 0:Lc], True, False, [N("vtp"), N("Sm")] + X, ["p4"])
                        for kc in range(2):
                            mm(o, S_["Cg"][:, kc, vc * 128:vc * 128 + 128], qT[:, 2 * h + kc, t0:t1], False, kc == 1, [N("Cg"), "h%d" % (2 * h + kc)] + X, ["p4"])
                    o = pb[4][:, 256:256 + Lc]
                    mm(o, S_["Eb"][0:Lc, :], S_["Sm"][0:Lc, 0:Lc], True, False, [N("Eb"), N("Sm")] + X, ["p4"])
                    for kc in range(2):
                        mm(o, S_["nrep"][:, kc, :], qT[:, 2 * h + kc, t0:t1], False, kc == 1, [N("nrep"), "h%d" % (2 * h + kc)] + X, ["p4"])
                    mm(pb[4][:, 384:384 + Lc], sel4[:, h * 128:h * 128 + 128], g4[1][:, 0:Lc], True, True, ["sel4", "g41"], ["p4"])
                    E("act", CALL("activation", mw[0][:, 0:Lc], pb[4][:, 384:384 + Lc], AF.Exp), ["p4"], ["mw0"])
                    E("act", CALL("activation", mwab[:, 0:Lc], pb[4][:, 256:256 + Lc], AF.Abs), ["p4"], ["mwab"])
                    E("dve", CALL("tensor_tensor", mw[0][:, 0:Lc], mwab[:, 0:Lc], mw[0][:, 0:Lc], ALU.max), ["mwab", "mw0"], ["mw0"])
                    E("dve", CALL("reciprocal", mw[0][:, 0:Lc], mw[0][:, 0:Lc]), ["mw0"], ["mw0"])
                    for vc in range(2):
                        E("dve", CALL("tensor_tensor", hh_[:, vc, 0:Lc], pb[4][:, vc * 128:vc * 128 + Lc], mw[0][:, 0:Lc], ALU.mult), ["p4", "mw0"] + X, [N("hh")])
                        E("act", CALL("activation", hq_[:, vc, 0:Lc], hh_[:, vc, 0:Lc], AF.Square), [N("hh")] + X, [N("hq")])
                    for kc in range(2):
                        mm(pb[6][:, kc * 256:kc * 256 + 256], S_["ktp"][0:Lc, kc * 128:kc * 128 + 128], S_["vtb"][0:Lc, :], True, True, [N("ktp"), N("vtb")] + X, ["p6"])
                    E("dve", CALL("scalar_tensor_tensor", CT[:, h, :, :].rearrange("p a b -> p (a b)"), CT[:, h, :, :].rearrange("p a b -> p (a b)"),
                                  gb[:, h:h + 1], pb[6][:, :], ALU.mult, ALU.add), ["CT%d" % h, "gb", "p6", N("Cg")], ["CT%d" % h])
                    for kc in range(2):
                        mm(pb[7][:, 136 + kc:137 + kc], S_["ktp"][0:Lc, kc * 128:kc * 128 + 128], ones_bf[0:Lc, 0:1], True, True, [N("ktp"), "ones_bf", "p7"] + X, ["p7n"])
                    E("dve", CALL("tensor_tensor", nT[:, 2 * h:2 * h + 2], S_["ngc"], pb[7][:, 136:138], ALU.add), [N("ngc"), "p7n", "p7"], ["nT%d" % h])

                def fin(h):
                    S_ = SET(h % 2)
                    X = S_["X"]
                    N = lambda n: "ml%s%s" % (n, S_["k"])
                    hh_, hq_ = S_["hh"], S_["hq"]
                    for vc in range(2):
                        mm(pb[5][:, 0:Lc], ones256[:], hh_[:, vc, 0:Lc], vc == 0, vc == 1, ["ones256", N("hh")] + X, ["p5"])
                    for vc in range(2):
                        mm(pb[5][:, 128:128 + Lc], ones256[:], hq_[:, vc, 0:Lc], vc == 0, vc == 1, ["ones256", N("hq")] + X, ["p5"])
                    E("act", CALL("activation", mw[1][:, 0:Lc], pb[5][:, 0:Lc], AF.Square), ["p5"], ["mw1"])
                    E("dve", CALL("tensor_tensor", mw[1][:, 0:Lc], pb[5][:, 128:128 + Lc], mw[1][:, 0:Lc], ALU.subtract), ["p5", "mw1"], ["mw1"])
                    E("dve", CALL("tensor_scalar", mw[1][:, 0:Lc], mw[1][:, 0:Lc], 0.0, 0.0, ALU.max, ALU.add), ["mw1"], ["mw1"])
                    E("act", CALL("activation", mw[1][:, 0:Lc], mw[1][:, 0:Lc], AF.Sqrt, bias=epscol[:, 0:1]), ["mw1", "epscol"], ["mw1"])
                    E("dve", CALL("reciprocal", mw[1][:, 0:Lc], mw[1][:, 0:Lc]), ["mw1"], ["mw1"])
                    for vc in range(2):
                        c = 2 * h + vc
                        E("dve", CALL("tensor_tensor", mw[2][:, 0:Lc], hh_[:, vc, 0:Lc], pb[5][:, 0:Lc], ALU.subtract), [N("hh"), "p5"] + X, ["mw2"])
                        E("pool", CALL("tensor_tensor", mw[2][:, 0:Lc], mw[2][:, 0:Lc], mw[1][:, 0:Lc], ALU.mult), ["mw2", "mw1"], ["mw2"])
                        E("pool", CALL("tensor_scalar", mw[2][:, 0:Lc], mw[2][:, 0:Lc], V("ml_norm_w", c), 0.0, ALU.mult, ALU.add), ["mw2", "vec"], ["mw2"])
                        E("dve", CALL("scalar_tensor_tensor", mw[3][:, 0:Lc], xc_bf[:, c, t0:t1], V("ml_skip", c), mw[2][:, 0:Lc], ALU.mult, ALU.add),
                          ["xn%d" % c, "vec", "mw2"], ["mw3"])
                        E("pool", CALL("tensor_tensor", ybuf[:, c, t0:t1], mw[3][:, 0:Lc], z_bf[:, c, t0:t1], ALU.mult), ["mw3", "z_bf"], ["ybuf"])

                for blk in [(prep, 0), (prep, 1), (mid, 0), (prep, 2), (mid, 1), (fin, 0), (prep, 3), (mid, 2), (fin, 1), (mid, 3), (fin, 2), (fin, 3)]:
                    blk[0](blk[1])
            E("dve", CALL("tensor_tensor", mprev[:], MM[:, Tt - 1:Tt], Fn[:, Tt - 1:Tt], ALU.subtract), ["MM", "Fn", "Mpc"], ["mprev"])
            for c in range(KC):
                E("act", CALL("copy", xmh[:, c, 0:3], xmh[:, c, Tt:Tt + 3]), ["xmh"], ["xmh"])
            rmsnorm_stats(lambda c: ybuf[:, c, 0:Tt], KC, Tt, ["ybuf"], onesD)
            for c in range(KC):
                E("dve", CALL("scalar_tensor_tensor", mixed[:, 8 + c, 0:Tt], ybuf[:, c, 0:Tt], V("out_norm_ml", c), rstd[:, 0:Tt], ALU.mult, ALU.mult),
                  ["ybuf", "vec", "rstd"], ["mixed"])

        def out_proj(Tt):
            for c in range(KC):
                po = pb[4 + c % 2]
                pk = "p%d" % (4 + c % 2)
                for hf in range(2):
                    oi = hf
                    dma(wor[oi][:], SO[c, hf], reads=WSCR, writes=["wo%d" % oi], q="sp")
                    for k2 in range(8):
                        k = hf * 8 + k2
                        mm(po[:, 0:Tt], wor[oi][:, k2, :], mixed[:, k, 0:Tt], k == 0, k == 15, ["wo%d" % oi, "mixed"], [pk])
                E("dve", CALL("tensor_tensor", xT[:, c, 0:Tt], xT[:, c, 0:Tt], po[:, 0:Tt], ALU.add), [pk, "xT"], ["xT"])

        def load_tile(src_rows, Tt):
            nsub = (Tt + 127) // 128
            for n in range(nsub):
                r = min(128, Tt - n * 128)
                dma(xtok[0:r, n, :], src_rows[n * 128:n * 128 + r, :], writes=["ybuf"])
            for c in range(KC):
                for n in range(nsub):
                    r = min(128, Tt - n * 128)
                    tr(pb[7][:, n * 128:n * 128 + r], xtok[0:r, n, c * 128:(c + 1) * 128], ident[0:r, 0:r], ["ybuf"], ["p7"])
                E("dve" if c % 2 == 0 else "act",
                  (CALL("tensor_copy", xT[:, c, 0:Tt], pb[7][:, 0:Tt])) if c % 2 == 0 else (CALL("copy", xT[:, c, 0:Tt], pb[7][:, 0:Tt])),
                  ["p7"], ["xT"])

        def store_tile(dst_rows, Tt):
            rmsnorm_stats(lambda c: xT[:, c, 0:Tt], KC, Tt, ["xT"], onesD)
            nsub = (Tt + 127) // 128
            for c in range(KC):
                E("dve", CALL("scalar_tensor_tensor", ybuf[:, c, 0:Tt], xT[:, c, 0:Tt], V("norm_final", c), rstd[:, 0:Tt], ALU.mult, ALU.mult),
                  ["xT", "vec", "rstd"], ["ybuf"])
            for n in range(nsub):
                r = min(128, Tt - n * 128)
                for c4 in range(2):
                    for c in range(c4 * 4, c4 * 4 + 4):
                        tr(pb[7][0:r, (c % 4) * 128:(c % 4) * 128 + 128], ybuf[:, c, n * 128:n * 128 + r], ident[:], ["ybuf"], ["p7"])
                    E("act", CALL("copy", otok[0:r, c4 * 512:(c4 + 1) * 512], pb[7][0:r, :]), ["p7"], ["tA", "tB"])
                outs.append(dma(dst_rows[n * 128:n * 128 + r, :], otok[0:r, :], reads=["tA", "tB"], q="sp"))

        def init_state(si):
            if si is None:
                for t, k in [(sre, SRE), (sim, SRE), (nT, ["nT0", "nT1", "nT2", "nT3"]), (mprev, ["mprev"])]:
                    E("dve", CALL("memset", t[:], 0.0), [], k)
                E("pool", CALL("memset", CT[:].rearrange("p a b c -> p (a b c)"), 0.0), [], ["CT0", "CT1", "CT2", "CT3"])
                E("pool", CALL("memset", xmh[:, :, 0:3], 0.0), [], ["xmh"])
                return
            for src, dst, k in [(st_s5re, sre, SRE), (st_s5im, sim, SIM)]:
                dma(stg[0:32, 0:128], src[si], writes=["stg"])
                tr(pb[7][:, 0:32], stg[0:32, 0:128], ident[0:32, 0:32], ["stg"], ["p7"])
                E("dve", CALL("tensor_copy", dst[:], pb[7][:, 0:32]), ["p7"], k)
            dma(stg[0:8, 0:128], st_n[si], writes=["stg"])
            tr(pb[7][:, 0:8], stg[0:8, 0:128], ident[0:8, 0:8], ["stg"], ["p7"])
            E("dve", CALL("tensor_copy", nT[:], pb[7][:, 0:8]), ["p7"], ["nT0", "nT1", "nT2", "nT3"])
            dma(mprev[:], st_m[si], writes=["mprev"])
            for c in range(KC):
                dma(stg[0:3, 0:128], st_conv[si][:, c * 128:(c + 1) * 128], writes=["stg"])
                tr(pb[7][:, 0:3], stg[0:3, 0:128], ident[0:3, 0:3], ["stg"], ["p7"])
                E("dve", CALL("tensor_copy", xmh[:, c, 0:3], pb[7][:, 0:3]), ["p7"], ["xmh"])
            for h in range(4):
                for vc in range(2):
                    dma(stg[:, 0:256], st_c[si, h, vc * 128:(vc + 1) * 128, :], writes=["stg"])
                    for kc in range(2):
                        tr(pb[7][:, kc * 128:kc * 128 + 128], stg[:, kc * 128:kc * 128 + 128], ident[:], ["stg"], ["p7"])
                    E("dve", CALL("tensor_copy", CT[:, h, :, vc * 128:vc * 128 + 128], pb[7][:, 0:256].rearrange("p (k v) -> p k v", k=2)), ["p7"], ["CT0", "CT1", "CT2", "CT3"])

        def store_state(oi):
            for src, dst, k in [(sre, o_s5re, SRE), (sim, o_s5im, SIM)]:
                tr(pb[7][0:32, 0:128], src[:], ident[:], k, ["p7"])
                E("dve", CALL("tensor_copy", stg[0:32, 0:128], pb[7][0:32, 0:128]), ["p7"], ["stg"])
                outs.append(dma(dst[oi], stg[0:32, 0:128], reads=["stg"], q="sp"))
            tr(pb[7][0:8, 0:128], nT[:], ident[:], ["nT0", "nT1", "nT2", "nT3"], ["p7"])
            E("dve", CALL("tensor_copy", stg[0:8, 0:128], pb[7][0:8, 0:128]), ["p7"], ["stg"])
            outs.append(dma(o_n[oi], stg[0:8, 0:128], reads=["stg"], q="sp"))
            outs.append(dma(o_m[oi], mprev[:], reads=["mprev"], q="sp"))
            for c in range(KC):
                E("dve", CALL("tensor_copy", mw[0][:, 0:3], xmh[:, c, 0:3]), ["xmh"], ["mw0"])
                tr(pb[7][0:3, 0:128], mw[0][:, 0:3], ident[:], ["mw0"], ["p7"])
                E("dve", CALL("tensor_copy", stg[0:3, 0:128], pb[7][0:3, 0:128]), ["p7"], ["stg"])
                outs.append(dma(o_conv[oi][:, c * 128:(c + 1) * 128], stg[0:3, 0:128], reads=["stg"], q="sp"))
            for h in range(4):
                for vc in range(2):
                    for kc in range(2):
                        tr(pb[7][:, kc * 128:kc * 128 + 128], CT[:, h, kc, vc * 128:vc * 128 + 128], ident[:], ["CT0", "CT1", "CT2", "CT3"], ["p7"])
                    E("dve", CALL("tensor_copy", stg[:, 0:256], pb[7][:, 0:256]), ["p7"], ["stg"])
                    outs.append(dma(o_c[oi, h, vc * 128:(vc + 1) * 128, :], stg[:, 0:256], reads=["stg"], q="sp"))

        def run_tile(src_rows, dst_rows, Tt):
            load_tile(src_rows, Tt)
            if STAGE >= 2:
                ffn("ffn1", "norm_ffn1", Tt)
            if STAGE >= 3:
                in_proj(Tt)
            if STAGE >= 4:
                s5_mix(Tt)
            if STAGE >= 5:
                mlstm_mix(Tt)
            if STAGE >= 6:
                out_proj(Tt)
            if STAGE >= 7:
                ffn("ffn2", "norm_ffn2", Tt)
            if dst_rows is not None:
                store_tile(dst_rows, Tt)

        if STAGE == 0:
            P.emit(final_waits=outs)
            return nc
        init_state(None)
        run_tile(xp[0:NMETA, :], None, NMETA)
        for ti in range((NP - NMETA) // TT):
            r0 = NMETA + ti * TT
            run_tile(xp[r0:r0 + TT, :], yp[r0 - NMETA:r0 - NMETA + TT, :], TT)
        store_state(0)
        for si in range(NSAMP):
            init_state(si)
            run_tile(xs[si], ys[si], SL)
            store_state(1 + si)
        P.emit(final_waits=outs)
    return nc


def host_consts():
    p = np.arange(128)
    c = {}
    c["ident"] = np.eye(128, dtype=np.float32)
    c["maskE"] = np.stack([(p // 64 == 0), (p // 64 == 1)], 1).astype(np.float32)
    p32 = np.arange(32)
    c["mask16"] = np.stack([(p32 // 16 == 0), (p32 // 16 == 1)], 1).astype(np.float32)
    c["triu"] = np.triu(np.ones((128, 128), np.float32))
    c["bdmask"] = (p[:, None] // 4 == np.arange(32)[None, :]).astype(np.float32)
    c["tvec"] = np.broadcast_to(np.arange(1, 65, dtype=np.float32)[None, :], (128, 64)).copy()
    s = np.zeros((4, 4, 128), np.float32)
    for h in range(4):
        s[h, h, :] = 1.0
    c["sel4"] = s.reshape(4, 512)
    c["mask4"] = (p[:, None] // 32 == np.arange(4)[None, :]).astype(np.float32)
    return c


_CACHE = {}


def kernel(**inp):
    f = lambda a: np.ascontiguousarray(np.asarray(a, dtype=np.float32))
    x_prompt = f(inp["x_prompt"])
    x_sample = f(inp["x_sample"])
    NB, SEQ, _ = x_prompt.shape
    NDEC, SL, _ = x_sample.shape
    NP = NMETA + SEQ
    ncores = 8
    NSAMP = NDEC // ncores
    key = (NP, NSAMP, SL)
    if key not in _CACHE:
        _CACHE[key] = build_program(NP, NSAMP, SL)
    nc = _CACHE[key]
    meta = f(inp["meta_tokens"])
    shared = host_consts()
    for n in ["ffn1_gate", "ffn1_up", "ffn1_down", "ffn2_gate", "ffn2_up", "ffn2_down", "w_in", "s5_glu_w", "w_out"]:
        shared[n] = f(inp[n])[0]
    shared["lam_re"] = f(inp["s5_lambda_re"])[0].reshape(32, 128)
    shared["lam_im"] = f(inp["s5_lambda_im"])[0].reshape(32, 128)
    shared["log_dt"] = f(inp["s5_log_dt"])[0]
    shared["b_re"] = f(inp["s5_b_re"])[0]
    shared["b_im"] = f(inp["s5_b_im"])[0]
    shared["c_re"] = f(inp["s5_c_re"])[0]
    shared["c_im"] = f(inp["s5_c_im"])[0]
    shared["wq"] = f(inp["ml_wq"])[0]
    shared["wk"] = f(inp["ml_wk"])[0]
    shared["wv"] = f(inp["ml_wv"])[0]
    shared["igw"] = f(inp["ml_igate_w"])[0]
    shared["fgw"] = f(inp["ml_fgate_w"])[0]
    shared["igb"] = f(inp["ml_igate_b"])[0].reshape(4, 1)
    shared["fgb"] = f(inp["ml_fgate_b"])[0].reshape(4, 1)
    cw = f(inp["ml_conv_w"])[0]
    vd = {"norm_ffn1": f(inp["norm_ffn1"])[0], "norm_mix": f(inp["norm_mix"])[0], "s5_d": f(inp["s5_d"])[0],
          "s5_glu_b": f(inp["s5_glu_b"])[0], "cw0": cw[0], "cw1": cw[1], "cw2": cw[2], "cw3": cw[3],
          "ml_conv_b": f(inp["ml_conv_b"])[0], "ml_norm_w": f(inp["ml_norm_w"])[0], "ml_skip": f(inp["ml_skip"])[0],
          "out_norm_s5": f(inp["out_norm_s5"])[0], "out_norm_ml": f(inp["out_norm_ml"])[0],
          "norm_ffn2": f(inp["norm_ffn2"])[0], "norm_final": f(inp["norm_final"])}
    shared["vecs"] = np.concatenate([vd[n].reshape(8, 128) for n in VEC_NAMES], 0)
    s5re, s5im = f(inp["state_s5_re"])[0], f(inp["state_s5_im"])[0]
    stc, stn, stm, stcv = f(inp["state_mlstm_c"])[0], f(inp["state_mlstm_n"])[0], f(inp["state_mlstm_m"])[0], f(inp["state_mlstm_conv"])[0]
    in_maps = []
    for c in range(ncores):
        b = c % NB
        sl = slice(c * NSAMP, (c + 1) * NSAMP)
        m = dict(shared)
        m["xp"] = np.concatenate([meta, x_prompt[b]], 0)
        m["xs"] = x_sample[sl]
        m["st_s5re"] = s5re[sl].reshape(NSAMP, 32, 128)
        m["st_s5im"] = s5im[sl].reshape(NSAMP, 32, 128)
        m["st_c"] = stc[sl]
        m["st_n"] = stn[sl].reshape(NSAMP, 8, 128)
        m["st_m"] = stm[sl].reshape(NSAMP, 4, 1)
        m["st_conv"] = stcv[sl]
        in_maps.append(m)
    res = run_bass_kernel_spmd(nc, in_maps, core_ids=list(range(ncores))).results
    y_prompt = np.stack([res[b]["yp"] for b in range(NB)], 0)
    y_sample = np.concatenate([res[c]["ys"] for c in range(ncores)], 0)

    def gather(name, shape_tail):
        pr = np.stack([res[b][name][0] for b in range(NB)], 0).reshape((1, NB) + shape_tail)
        sm = np.concatenate([res[c][name][1:] for c in range(ncores)], 0).reshape((1, NDEC) + shape_tail)
        return pr.astype(np.float32), sm.astype(np.float32)

    p_re, s_re = gather("o_s5re", (64, 64))
    p_im, s_im = gather("o_s5im", (64, 64))
    p_c, s_c = gather("o_c", (4, 256, 256))
    p_n, s_n = gather("o_n", (4, 256))
    p_m, s_m = gather("o_m", (4,))
    p_cv, s_cv = gather("o_conv", (3, 1024))
    return (y_prompt.astype(np.float32), y_sample.astype(np.float32), p_re, p_im, p_c, p_n, p_m, p_cv,
            s_re, s_im, s_c, s_n, s_m, s_cv)
```

```python
import math
import numpy as np
import concourse.bass as bass
import concourse.mybir as mybir
from concourse.bass_utils import run_bass_kernel_spmd
from contextlib import ExitStack

F32 = mybir.dt.float32
BF16 = mybir.dt.bfloat16
I32 = mybir.dt.int32
ALU = mybir.AluOpType
AF = mybir.ActivationFunctionType

D = 1024
DFF = 2816
KC = 8
FC = 22
NMETA = 16
EPS = 1e-6
STAGE = 9
ENGS = ("pe", "act", "dve", "pool", "sp")
NDMA_SEM = 12
VEC_NAMES = ["norm_ffn1", "norm_mix", "s5_d", "s5_glu_b", "cw0", "cw1", "cw2", "cw3", "ml_conv_b",
             "ml_norm_w", "ml_skip", "out_norm_s5", "out_norm_ml", "norm_ffn2", "norm_final"]
VI = {n: i for i, n in enumerate(VEC_NAMES)}


class Op:
    __slots__ = ("eng", "fn", "deps", "idx", "signaled", "sigval", "dma", "dsem", "dval", "dprev")

    def __init__(self, eng, fn, dma):
        self.eng = eng
        self.fn = fn
        self.deps = []
        self.signaled = False
        self.sigval = 0
        self.dma = dma
        self.dsem = None
        self.dval = 0
        self.dprev = None


class Prog:
    def __init__(self, nc):
        self.nc = nc
        self.ops = {e: [] for e in ENGS}
        self.last_writer = {}
        self.readers = {}
        self.ndma = {e: 0 for e in ENGS}
        self.dma_ops = {e: [] for e in ENGS}

    def op(self, eng, fn, reads=(), writes=(), dma=False):
        o = Op(eng, fn, dma)
        deps = []
        for r in reads:
            w = self.last_writer.get(r)
            if w is not None:
                deps.append(w)
        for wr in writes:
            w = self.last_writer.get(wr)
            if w is not None:
                deps.append(w)
            deps.extend(self.readers.get(wr, ()))
        seen = set()
        for d in deps:
            if id(d) in seen or d is o:
                continue
            seen.add(id(d))
            if d.eng == "pe" and eng == "pe" and not d.dma and not dma:
                continue
            o.deps.append(d)
        for r in reads:
            self.readers.setdefault(r, []).append(o)
        for wr in writes:
            self.last_writer[wr] = o
            self.readers[wr] = []
        if dma:
            k = self.ndma[eng]
            self.ndma[eng] += 1
            o.dsem = k % NDMA_SEM
            o.dval = 16 * (k // NDMA_SEM + 1)
            if k >= NDMA_SEM:
                o.dprev = self.dma_ops[eng][k - NDMA_SEM]
            self.dma_ops[eng].append(o)
        o.idx = len(self.ops[eng])
        self.ops[eng].append(o)
        return o

    def emit(self, final_waits=()):
        nc = self.nc
        for e in ENGS:
            for o in self.ops[e]:
                for d in o.deps:
                    if not d.dma:
                        d.signaled = True
        for o in final_waits:
            if not o.dma:
                o.signaled = True
        for e in ENGS:
            c = 0
            for o in self.ops[e]:
                if o.signaled and not o.dma:
                    c += 1
                    o.sigval = c
        with ExitStack() as st:
            esem = {e: st.enter_context(nc.semaphore("s_" + e)) for e in ENGS}
            dsem = {e: [st.enter_context(nc.semaphore("d_%s_%d" % (e, i))) for i in range(NDMA_SEM)]
                    for e in ENGS if self.ndma[e] > 0}
            block = st.enter_context(nc.Block())

            def body(e, engine):
                observed = {}

                def wait(key, sem, val):
                    if observed.get(key, 0) >= val:
                        return
                    observed[key] = val
                    engine.wait_ge(sem, val)

                for o in self.ops[e]:
                    for d in o.deps:
                        if d.dma:
                            wait(("d", d.eng, d.dsem), dsem[d.eng][d.dsem], d.dval)
                        else:
                            wait(("e", d.eng), esem[d.eng], d.sigval)
                    if o.dma and o.dprev is not None:
                        wait(("d", e, o.dprev.dsem), dsem[e][o.dprev.dsem], o.dprev.dval)
                    ins = o.fn(engine)
                    if o.dma:
                        ins.then_inc(dsem[e][o.dsem], 16)
                    elif o.signaled:
                        ins.then_inc(esem[e], 1)
                if e == "sp":
                    for o in final_waits:
                        if o.dma:
                            wait(("d", o.eng, o.dsem), dsem[o.eng][o.dsem], o.dval)
                        else:
                            wait(("e", o.eng), esem[o.eng], o.sigval)

            block.sync(lambda eng: body("sp", eng))
            block.scalar(lambda eng: body("act", eng))
            block.vector(lambda eng: body("dve", eng))
            block.gpsimd(lambda eng: body("pool", eng))
            block.tensor(lambda eng: body("pe", eng))


def CALL(name, *args, **kw):
    return lambda e: getattr(e, name)(*args, **kw)


def bc_last(ap, n):
    return bass.AP(ap.tensor, ap.offset, [list(a) for a in ap.ap] + [[0, n]])


def bc_mid(ap, n):
    a = [list(x) for x in ap.ap]
    return bass.AP(ap.tensor, ap.offset, [a[0], [0, n]] + a[1:])


def bc_row(ap, n):
    a = [list(x) for x in ap.ap]
    return bass.AP(ap.tensor, ap.offset, [a[0], [0, n]])


def build_program(NP, NSAMP=2, SL=32):
    nc = bass.Bass("TRN2", target_bir_lowering=False)
    dr = {}

    def din(name, shape, dt=F32):
        dr[name] = nc.dram_tensor(name, list(shape), dt, kind="ExternalInput").ap()
        return dr[name]

    def dout(name, shape):
        dr[name] = nc.dram_tensor(name, list(shape), F32, kind="ExternalOutput").ap()
        return dr[name]

    xp = din("xp", [NP, D])
    xs = din("xs", [NSAMP, SL, D])
    st_s5re = din("st_s5re", [NSAMP, 32, 128])
    st_s5im = din("st_s5im", [NSAMP, 32, 128])
    st_c = din("st_c", [NSAMP, 4, 256, 256])
    st_n = din("st_n", [NSAMP, 8, 128])
    st_m = din("st_m", [NSAMP, 4, 1])
    st_conv = din("st_conv", [NSAMP, 3, D])
    vecs = din("vecs", [len(VEC_NAMES) * 8, 128])
    W = {}
    for n, shp in [("ffn1_gate", [D, DFF]), ("ffn1_up", [D, DFF]), ("ffn1_down", [DFF, D]),
                   ("ffn2_gate", [D, DFF]), ("ffn2_up", [D, DFF]), ("ffn2_down", [DFF, D]),
                   ("w_in", [D, 3 * D]), ("s5_glu_w", [D, D]), ("w_out", [2 * D, D]),
                   ("lam_re", [32, 128]), ("lam_im", [32, 128]), ("log_dt", [64]),
                   ("b_re", [64, 64, 16]), ("b_im", [64, 64, 16]), ("c_re", [64, 16, 64]), ("c_im", [64, 16, 64]),
                   ("wq", [256, 4, 4]), ("wk", [256, 4, 4]), ("wv", [256, 4, 4]),
                   ("igw", [3 * D, 4]), ("fgw", [3 * D, 4]), ("igb", [4, 1]), ("fgb", [4, 1]),
                   ("ident", [128, 128]), ("maskE", [128, 2]), ("mask16", [32, 2]), ("triu", [128, 128]),
                   ("bdmask", [128, 32]), ("tvec", [128, 64]), ("sel4", [4, 512]), ("mask4", [128, 4])]:
        W[n] = din(n, shp)
    NS = 1 + NSAMP
    yp = dout("yp", [NP - NMETA, D])
    ys = dout("ys", [NSAMP, SL, D])
    o_s5re = dout("o_s5re", [NS, 32, 128])
    o_s5im = dout("o_s5im", [NS, 32, 128])
    o_c = dout("o_c", [NS, 4, 256, 256])
    o_n = dout("o_n", [NS, 8, 128])
    o_m = dout("o_m", [NS, 4, 1])
    o_conv = dout("o_conv", [NS, 3, D])

    SG = {n: nc.dram_tensor("sg_" + n, [FC, 128, KC, 128], BF16).ap() for n in ["ffn1_gate", "ffn1_up", "ffn2_gate", "ffn2_up"]}
    SD = {n: nc.dram_tensor("sd_" + n, [KC, 2, 128, FC // 2, 128], BF16).ap() for n in ["ffn1_down", "ffn2_down"]}
    SI = nc.dram_tensor("s_win", [24, 128, KC, 128], BF16).ap()
    SGL = nc.dram_tensor("s_glu", [KC, 128, KC, 128], BF16).ap()
    SO = nc.dram_tensor("s_wout", [KC, 2, 128, 8, 128], BF16).ap()
    P = Prog(nc)
    outs = []
    TT = 512
    with ExitStack() as st:
        def sb(name, shape, dt=F32):
            return st.enter_context(nc.sbuf_tensor("sb_" + name, list(shape), dt))

        st.enter_context(nc.allow_low_precision("bf16 matmul operands with fp32 PSUM accumulation"))
        pb = [st.enter_context(nc.psum_tensor("pb%d" % i, [128, 512], F32)) for i in range(8)]

        xT = sb("xT", [128, KC, TT])
        xn = sb("xn", [128, KC, TT], BF16)
        xc_bf = xn
        hT = sb("hT", [128, 24, TT], BF16)
        u_bf = sb("u_bf", [128, KC, TT], BF16)
        z_bf = sb("z_bf", [128, KC, TT], BF16)
        xmh = sb("xmh", [128, KC, TT + 3], BF16)
        vT = hT[:, 16:24, :]
        mixed = sb("mixed", [128, 16, TT], BF16)
        ybuf = sb("ybuf", [128, KC, TT])
        xtok = ybuf[:].rearrange("p a b -> p (a b)").rearrange("p (n d) -> p n d", d=D)
        rstd = sb("rstd", [128, TT])
        tAB = sb("tAB", [128, 2 * TT])
        tA = tAB[:, 0:TT]
        tB = tAB[:, TT:2 * TT]
        otok = tAB
        wgr = [sb("wgr%d" % i, [128, KC, 128], BF16) for i in range(2)]
        wur = [sb("wur%d" % i, [128, KC, 128], BF16) for i in range(2)]
        wdr = [sb("wdr%d" % i, [128, FC // 2, 128], BF16) for i in range(2)]
        wir = [sb("wir%d" % i, [128, KC, 128], BF16) for i in range(2)]
        wor = [sb("wor%d" % i, [128, 8, 128], BF16) for i in range(2)]
        ident = sb("ident", [128, 128])
        ones_bf = sb("ones_bf", [128, 128], BF16)
        onesD = sb("onesD", [128, 128], BF16)
        ones256 = sb("ones256", [128, 128])
        ones4 = sb("ones4", [4, 128])
        onecol = sb("onecol", [128, 1])
        epscol = sb("epscol", [128, 1])
        vec = sb("vec", [128, len(VEC_NAMES) * 8])
        maskE = sb("maskE", [128, 2])
        mask16 = sb("mask16", [32, 2])
        triu = sb("triu", [128, 128])
        bdmask = sb("bdmask", [128, 32])
        tvec = sb("tvec", [128, 64])
        sel4 = sb("sel4", [4, 512])
        cosT = sb("cosT", [128, 32, 64])
        sinT = sb("sinT", [128, 32, 64])
        rmag = sb("rmag", [128, 32])
        W1 = sb("W1", [128, KC, 2, 128], BF16)
        W3 = sb("W3", [128, 32, 2, 32], BF16)
        s5st = sb("s5st", [128, 2, 32])
        sre = s5st[:, 0, :]
        sim = s5st[:, 1, :]
        s5all = sb("s5all", [128, 8, 256])
        s5w = [s5all[:, i, :].rearrange("p (a b) -> p a b", b=64) for i in range(8)]
        resetm = sb("resetm", [128, 64])
        inj = [sb("inj%d" % i, [128, 2, 2, 4]) for i in range(2)]
        xbfA = sb("xbfA", [128, 2, 4, 64], BF16)
        xbfB = sb("xbfB", [128, 2, 4, 64], BF16)
        um = sb("um", [128, 2, 4, 64], BF16)
        umB = sb("umB", [128, 2, 4, 64], BF16)
        mask4 = sb("mask4", [128, 4])
        CT = sb("CT", [128, 4, 2, 256])
        nT = sb("nT", [128, 8])
        mprev = sb("mprev", [4, 1])
        BD = sb("BD", [128, 3, KC, 128], BF16)
        gwi = sb("gwi", [128, 24, 4], BF16)
        gwf = sb("gwf", [128, 24, 4], BF16)
        igb = sb("igb", [4, 1])
        nfgb = sb("nfgb", [4, 1])
        igs = sb("igs", [4, TT])
        Fn = sb("Fn", [4, TT])
        aa = sb("aa", [4, TT])
        MM = sb("MM", [4, TT])
        g4 = [sb("g4_%d" % i, [4, 128]) for i in range(2)]
        negM = sb("negM", [4, 1])
        gcol = sb("gcol", [4, 1])
        Mpc = sb("Mpc", [4, 1])
        dg = sb("dg", [4, 4])
        et = sb("et", [128, 4])
        gb = sb("gb", [128, 4])
        ngc2 = sb("ngc2", [128, 2, 2])
        mwab = sb("mwab", [128, 128])
        ktp = sb("ktp", [128, 256], BF16)
        vtp = sb("vtp", [128, 256], BF16)
        vtb = sb("vtb", [128, 256], BF16)
        Sm = sb("Sm", [128, 128], BF16)
        Eb = sb("Eb", [128, 128], BF16)
        Cg = sb("Cg", [128, 2, 256], BF16)
        nrep = sb("nrep", [128, 2, 128], BF16)
        hh = sb("hh", [128, 2, 128])
        hq = sb("hq", [128, 2, 128])
        mw = [sb("mw%d" % i, [128, 128]) for i in range(4)]
        stg = sb("stg", [128, 256])

        SRE = ["sre%d" % c for c in range(KC)]
        SIM = ["sim%d" % c for c in range(KC)]
        dq = ["sp", "pool"]
        dqi = [0]

        def dma(out, in_, reads=(), writes=(), q=None, slow=False):
            if out.dtype != in_.dtype:
                q = "pool"
            if q is None:
                q = dq[dqi[0] % 2]
                dqi[0] += 1
            if slow:
                f = CALL("dma_start", out=out, in_=in_, allow_slow_non_contiguous=True)
            else:
                f = CALL("dma_start", out=out, in_=in_)
            return P.op(q, f, reads=reads, writes=writes, dma=True)

        def mm(out, lhsT, rhs, start, stop, reads, writes, tp=None):
            if tp is None:
                f = CALL("matmul", out, lhsT, rhs, start=start, stop=stop)
            else:
                f = CALL("matmul", out, lhsT, rhs, start=start, stop=stop, tile_position=tp)
            return P.op("pe", f, reads=reads, writes=writes)

        def tr(out, in_, idn, reads, writes):
            return P.op("pe", CALL("transpose", out, in_, idn), reads=list(reads) + ["ident"], writes=writes)

        def E(eng, fn, reads, writes):
            return P.op(eng, fn, reads=reads, writes=writes)

        def V(name, c):
            i = VI[name] * 8 + c
            return vec[:, i:i + 1]

        for name, t in [("ident", ident), ("maskE", maskE), ("mask16", mask16), ("triu", triu), ("bdmask", bdmask),
                        ("tvec", tvec), ("sel4", sel4), ("igb", igb), ("mask4", mask4)]:
            dma(t[:], W[name], writes=[name])
        E("dve", CALL("memset", ones_bf[:], 1.0), [], ["ones_bf"])
        E("dve", CALL("memset", onesD[:], 1.0 / D), [], ["onesD"])
        E("dve", CALL("memset", ones256[:], 1.0 / 256), [], ["ones256"])
        E("dve", CALL("memset", ones4[:], 1.0), [], ["ones4"])
        E("dve", CALL("memset", onecol[:], 1.0), [], ["onecol"])
        E("dve", CALL("memset", resetm[:], 1.0), [], ["resetm"])
        E("dve", CALL("memset", resetm[:, 0:1], 0.0), ["resetm"], ["resetm"])
        E("dve", CALL("memset", epscol[:], EPS), [], ["epscol"])
        dma(nfgb[:], W["fgb"], writes=["nfgb"])
        E("dve", CALL("tensor_scalar", nfgb[:], nfgb[:], -1.0, 0.0, ALU.mult, ALU.add), ["nfgb"], ["nfgb"])
        dma(stg[0:len(VEC_NAMES) * 8, 0:128], vecs, writes=["stg"])
        nv = len(VEC_NAMES) * 8
        tr(pb[7][:, 0:nv], stg[0:nv, 0:128], ident[0:nv, 0:nv], ["stg"], ["p7"])
        E("dve", CALL("tensor_copy", vec[:], pb[7][:, 0:nv]), ["p7"], ["vec"])
        dma(gwi[:], W["igw"].rearrange("(c p) h -> p c h", p=128), writes=["gwi"], slow=True)
        dma(gwf[:], W["fgw"].rearrange("(c p) h -> p c h", p=128), writes=["gwf"], slow=True)
        for wi, wn in enumerate(["wq", "wk", "wv"]):
            dma(stg[:, 0:32].rearrange("p (c o) -> p c o", o=4), W[wn].rearrange("(c b) i o -> (b i) c o", b=32),
                reads=[], writes=["stg"], slow=True)
            for c in range(KC):
                E("dve", CALL("tensor_tensor",
                    BD[:, wi, c, :].rearrange("p (b o) -> p b o", o=4),
                    bc_mid(stg[:, c * 4:c * 4 + 4], 32), bc_last(bdmask[:, :], 4), ALU.mult),
                  ["stg", "bdmask"], ["BD"])
        lamr = s5w[0][:, 0, 0:32]
        lami = s5w[0][:, 1, 0:32]
        dtb = s5w[0][:, 2, 0:32]
        th = s5w[1][:, 0, 0:32]
        cth = s5w[1][:, 1, 0:32]
        sth = s5w[1][:, 2, 0:32]
        lbr = s5w[2][:, 0, 0:32]
        lbi = s5w[2][:, 1, 0:32]
        kr = s5w[2][:, 2, 0:32]
        ki_ = s5w[2][:, 3, 0:32]
        t1 = s5w[3][:, 0, 0:32]
        t2 = s5w[3][:, 1, 0:32]
        t3 = s5w[3][:, 2, 0:32]
        for nm, dst in [("lam_re", lamr), ("lam_im", lami)]:
            dma(stg[0:32, 0:128], W[nm], writes=["stg"])
            tr(pb[7][:, 0:32], stg[0:32, 0:128], ident[0:32, 0:32], ["stg"], ["p7"])
            E("dve", CALL("tensor_copy", dst, pb[7][:, 0:32]), ["p7"], ["s5p"])
        ldt = W["log_dt"]
        for g2 in range(2):
            src = bass.AP(ldt.tensor, ldt.offset + g2, [[0, 64], [2, 32]])
            dma(s5w[0][64 * g2:64 * g2 + 64, 2, 0:32], src, writes=["s5p"], slow=True)
        E("act", CALL("activation", dtb, dtb, AF.Exp), ["s5p"], ["s5p"])
        E("dve", CALL("tensor_tensor", t1, lamr, dtb, ALU.mult), ["s5p"], ["s5p"])
        E("act", CALL("activation", rmag[:], t1, AF.Exp), ["s5p"], ["rmag"])
        E("dve", CALL("tensor_tensor", th, lami, dtb, ALU.mult), ["s5p"], ["s5p"])

        ki32 = sb("ki32", [128, 1, 64], I32)

        def sincos(dst, ang, n, shift, key_r, key_w):
            wk = s5w[6][:].rearrange("p a b -> p (a b)")[:, 0:n]
            wf = s5w[7][:].rearrange("p a b -> p (a b)")[:, 0:n]
            wi_ = ki32[:].rearrange("p a b -> p (a b)")[:, 0:n]
            E("dve", CALL("tensor_scalar", wk, ang, shift, 1.0 / (2 * math.pi), ALU.add, ALU.mult), key_r, ["s5t"])
            E("dve", CALL("tensor_copy", wi_, wk), ["s5t"], ["s5t"])
            E("dve", CALL("tensor_copy", wf, wi_), ["s5t"], ["s5t"])
            E("dve", CALL("tensor_scalar", wk, ang, shift, 0.0, ALU.add, ALU.add), key_r + ["s5t"], ["s5t"])
            E("dve", CALL("scalar_tensor_tensor", wk, wf, -2 * math.pi, wk, ALU.mult, ALU.add), ["s5t"], ["s5t"])
            E("dve", CALL("tensor_scalar", wf, wk, math.pi, -2 * math.pi, ALU.is_gt, ALU.mult), ["s5t"], ["s5t"])
            E("dve", CALL("tensor_tensor", wk, wk, wf, ALU.add), ["s5t"], ["s5t"])
            E("dve", CALL("tensor_scalar", wf, wk, -math.pi, 2 * math.pi, ALU.is_lt, ALU.mult), ["s5t"], ["s5t"])
            E("dve", CALL("tensor_tensor", wk, wk, wf, ALU.add), ["s5t"], ["s5t"])
            E("act", CALL("activation", dst, wk, AF.Sin), ["s5t"], key_w)

        sincos(sth, th, 32, 0.0, ["s5p"], ["s5p"])
        sincos(cth, th, 32, math.pi / 2, ["s5p"], ["s5p"])
        E("dve", CALL("tensor_tensor", lbr, rmag[:], cth, ALU.mult), ["s5p", "rmag"], ["s5p"])
        E("dve", CALL("tensor_tensor", lbi, rmag[:], sth, ALU.mult), ["s5p", "rmag"], ["s5p"])
        E("dve", CALL("tensor_scalar", t1, lbr, -1.0, 0.0, ALU.add, ALU.add), ["s5p"], ["s5p"])
        E("dve", CALL("tensor_tensor", t2, lamr, lamr, ALU.mult), ["s5p"], ["s5p"])
        E("dve", CALL("tensor_tensor", t3, lami, lami, ALU.mult), ["s5p"], ["s5p"])
        E("dve", CALL("tensor_tensor", t2, t2, t3, ALU.add), ["s5p"], ["s5p"])
        E("dve", CALL("reciprocal", t2, t2), ["s5p"], ["s5p"])
        E("dve", CALL("tensor_tensor", kr, t1, lamr, ALU.mult), ["s5p"], ["s5p"])
        E("dve", CALL("tensor_tensor", t3, lbi, lami, ALU.mult), ["s5p"], ["s5p"])
        E("dve", CALL("tensor_tensor", kr, kr, t3, ALU.add), ["s5p"], ["s5p"])
        E("dve", CALL("tensor_tensor", kr, kr, t2, ALU.mult), ["s5p"], ["s5p"])
        E("dve", CALL("tensor_tensor", ki_, lbi, lamr, ALU.mult), ["s5p"], ["s5p"])
        E("dve", CALL("tensor_tensor", t3, t1, lami, ALU.mult), ["s5p"], ["s5p"])
        E("dve", CALL("tensor_tensor", ki_, ki_, t3, ALU.subtract), ["s5p"], ["s5p"])
        E("dve", CALL("tensor_tensor", ki_, ki_, t2, ALU.mult), ["s5p"], ["s5p"])
        for q in range(32):
            ang = s5w[5][:, 0, :]
            E("dve", CALL("tensor_scalar", ang, tvec[:, :], th[:, q:q + 1], 0.0, ALU.mult, ALU.add),
              ["s5p", "tvec"], ["s5ang"])
            sincos(sinT[:, q, :], ang, 64, 0.0, ["s5ang"], ["sinT"])
            sincos(cosT[:, q, :], ang, 64, math.pi / 2, ["s5ang"], ["cosT"])
        Bre = CT[:].rearrange("p a b c -> p (a b c)")[:, 0:512].rearrange("p (q j) -> p q j", j=16)
        Bim = CT[:].rearrange("p a b c -> p (a b c)")[:, 512:1024].rearrange("p (q j) -> p q j", j=16)
        Ere = CT[:].rearrange("p a b c -> p (a b c)")[:, 1024:1536].rearrange("p (q j) -> p q j", j=16)
        Eim = CT[:].rearrange("p a b c -> p (a b c)")[:, 1536:2048].rearrange("p (q j) -> p q j", j=16)
        for g2 in range(2):
            dma(Bre[64 * g2:64 * g2 + 64], W["b_re"].rearrange("(q g) p j -> g p q j", g=2)[g2], writes=["CT0", "CT1", "CT2", "CT3"], slow=True)
            dma(Bim[64 * g2:64 * g2 + 64], W["b_im"].rearrange("(q g) p j -> g p q j", g=2)[g2], writes=["CT0", "CT1", "CT2", "CT3"], slow=True)
        kmr = s5w[4][:, 0, 0:32]
        kmi = s5w[4][:, 1, 0:32]
        Eexp_re = ybuf[:].rearrange("p a b -> p (a b)")[:, 0:1024].rearrange("p (q g j) -> p q g j", g=2, j=16)
        Eexp_im = ybuf[:].rearrange("p a b -> p (a b)")[:, 1024:2048].rearrange("p (q g j) -> p q g j", g=2, j=16)
        for g2 in range(2):
            E("dve", CALL("tensor_scalar", kmr, kr, maskE[:, g2:g2 + 1], 0.0, ALU.mult, ALU.add), ["s5p", "maskE"], ["s5k"])
            E("dve", CALL("tensor_scalar", kmi, ki_, maskE[:, g2:g2 + 1], 0.0, ALU.mult, ALU.add), ["s5p", "maskE"], ["s5k"])
            E("dve", CALL("tensor_tensor", Ere, Bre, bc_last(kmr, 16), ALU.mult), ["CT0", "CT1", "CT2", "CT3"] + ["s5k"], ["CT0", "CT1", "CT2", "CT3"])
            E("dve", CALL("tensor_tensor", Eim, Bim, bc_last(kmi, 16), ALU.mult), ["CT0", "CT1", "CT2", "CT3"] + ["s5k"], ["CT0", "CT1", "CT2", "CT3"])
            E("dve", CALL("tensor_tensor", Eexp_re[:, :, g2, :], Ere, Eim, ALU.subtract), ["CT0", "CT1", "CT2", "CT3"], ["ybuf"])
            E("dve", CALL("tensor_tensor", Ere, Bim, bc_last(kmr, 16), ALU.mult), ["CT0", "CT1", "CT2", "CT3"] + ["s5k"], ["CT0", "CT1", "CT2", "CT3"])
            E("dve", CALL("tensor_tensor", Eim, Bre, bc_last(kmi, 16), ALU.mult), ["CT0", "CT1", "CT2", "CT3"] + ["s5k"], ["CT0", "CT1", "CT2", "CT3"])
            E("dve", CALL("tensor_tensor", Eexp_im[:, :, g2, :], Ere, Eim, ALU.add), ["CT0", "CT1", "CT2", "CT3"], ["ybuf"])
        for c in range(KC):
            for ri, Ex in enumerate([Eexp_re, Eexp_im]):
                src = Ex[:, 4 * c:4 * c + 4, :, :].rearrange("p q g j -> p (q g j)")
                tr(pb[7][:, 0:128], src, ident[:], ["ybuf"], ["p7"])
                E("act", CALL("copy", W1[:, c, ri, :], pb[7][:, 0:128]), ["p7"], ["W1"])
        Cst = CT[:].rearrange("p a b c -> p (a b c)")[0:32, 0:2048].rearrange("p (q k) -> p q k", k=64)
        Cx = xT[:].rearrange("p a b -> p (a b)")[0:32, 0:4096].rearrange("p (q g k) -> p q g k", g=2, k=64)
        for ri, (cn, sgn) in enumerate([("c_re", 1.0), ("c_im", -1.0)]):
            for g2 in range(2):
                dma(Cst[16 * g2:16 * g2 + 16], W[cn].rearrange("(q g) h p -> g h q p", g=2)[g2], writes=["CT0", "CT1", "CT2", "CT3"], slow=True)
            for g2 in range(2):
                E("dve", CALL("tensor_scalar", Cx[:, :, g2, :], Cst, mask16[:, g2:g2 + 1], sgn, ALU.mult, ALU.mult),
                  ["CT0", "CT1", "CT2", "CT3"] + ["mask16"], ["xT"])
            for q in range(32):
                tr(pb[6][:, (q % 16) * 32:(q % 16) * 32 + 32], Cx[:, q, :, :].rearrange("p g k -> p (g k)"), ident[0:32, 0:32], ["xT"], ["p6"])
                if q % 16 == 15:
                    q0 = q - 15
                    E("act", CALL("copy", W3[:, q0:q0 + 16, ri, :], pb[6][:].rearrange("p (q h) -> p q h", h=32)), ["p6"], ["W3"])

        hTf = hT[:].rearrange("p a b -> p (a b)")
        pcs = [(hTf[:, 0:4096], ["h%d" % i for i in range(8)]), (hTf[:, 4096:8192], ["h%d" % i for i in range(8, 16)])]
        pci = [0]
        WSCR = ["wscr%d" % i for i in range(12)]

        def precast(src_rows, ncols, dst_ap, pattern, **kw):
            stg_, names = pcs[pci[0] % 2]
            pci[0] += 1
            dma(stg_[:, 0:ncols], src_rows, writes=names, q="pool")
            dma(dst_ap.rearrange(pattern), stg_[:, 0:ncols].rearrange("p (a k) -> p a k", k=128), reads=names, writes=["wscr%d" % ((pci[0] - 1) % 12)], q="sp", slow=True)

        for n in ["ffn1_gate", "ffn1_up", "ffn2_gate", "ffn2_up"]:
            for c in range(KC):
                precast(W[n][c * 128:(c + 1) * 128, :], DFF, SG[n][:, :, c, :], "f p k -> p f k")
        for n in ["ffn1_down", "ffn2_down"]:
            for f in range(FC):
                precast(W[n][f * 128:(f + 1) * 128, :], D, SD[n][:, f // 11, :, f % 11, :], "c p k -> p c k")
        for c in range(KC):
            precast(W["w_in"][c * 128:(c + 1) * 128, :], 3 * D, SI[:, :, c, :], "f p k -> p f k")
        for c in range(KC):
            precast(W["s5_glu_w"][c * 128:(c + 1) * 128, :], D, SGL[:, :, c, :], "f p k -> p f k")
        for k in range(16):
            precast(W["w_out"][k * 128:(k + 1) * 128, :], D, SO[:, k // 8, :, k % 8, :], "c p k -> p c k")

        wcnt = {"g": 0, "u": 0, "d": 0, "i": 0, "o": 0}

        def rmsnorm_stats(src_chunks, nch, Tt, key_r, scale_mat):
            for c in range(nch):
                if c % 2 == 0:
                    E("act", CALL("activation", hT[:, 14 + c, 0:Tt], src_chunks(c), AF.Square), key_r, ["h%d" % (14 + c)])
                else:
                    E("dve", CALL("tensor_tensor", hT[:, 14 + c, 0:Tt], src_chunks(c), src_chunks(c), ALU.mult), key_r, ["h%d" % (14 + c)])
            for c in range(nch):
                mm(pb[6][:, 0:Tt], scale_mat[:], hT[:, 14 + c, 0:Tt], c == 0, c == nch - 1, ["onesD", "h%d" % (14 + c)], ["p6"])
            E("act", CALL("activation", rstd[:, 0:Tt], pb[6][:, 0:Tt], AF.Ln, bias=epscol[:, 0:1]), ["p6", "epscol"], ["rstd"])
            E("act", CALL("activation", rstd[:, 0:Tt], rstd[:, 0:Tt], AF.Exp, scale=-0.5), ["rstd"], ["rstd"])

        def norm_x(gname, Tt):
            rmsnorm_stats(lambda c: xT[:, c, 0:Tt], KC, Tt, ["xT"], onesD)
            for c in range(KC):
                E("dve", CALL("scalar_tensor_tensor", xn[:, c, 0:Tt], xT[:, c, 0:Tt], V(gname, c), rstd[:, 0:Tt], ALU.mult, ALU.mult),
                  ["xT", "vec", "rstd"], ["xn%d" % c])

        def ffn(pref, gname, Tt):
            norm_x(gname, Tt)
            wg, wu, wd = W[pref + "_gate"], W[pref + "_up"], W[pref + "_down"]
            for f in range(FC):
                gi = wcnt["g"] % 2
                wcnt["g"] += 1
                dma(wgr[gi][:], SG[pref + "_gate"][f], reads=WSCR, writes=["wg%d" % gi], q="sp")
                dma(wur[gi][:], SG[pref + "_up"][f], reads=WSCR, writes=["wu%d" % gi], q="sp")
                pg, pu = pb[f % 2], pb[2 + f % 2]
                for c in range(KC):
                    mm(pg[:, 0:Tt], wgr[gi][:, c, :], xn[:, c, 0:Tt], c == 0, c == KC - 1, ["wg%d" % gi, "xn%d" % c], ["p%d" % (f % 2)])
                for c in range(KC):
                    mm(pu[:, 0:Tt], wur[gi][:, c, :], xn[:, c, 0:Tt], c == 0, c == KC - 1, ["wu%d" % gi, "xn%d" % c], ["p%d" % (2 + f % 2)])
                tt = tA if f % 2 == 0 else tB
                tn = "tA" if f % 2 == 0 else "tB"
                E("act", CALL("activation", tt[:, 0:Tt], pg[:, 0:Tt], AF.Silu), ["p%d" % (f % 2)], [tn])
                E("dve", CALL("tensor_tensor", hT[:, f, 0:Tt], tt[:, 0:Tt], pu[:, 0:Tt], ALU.mult),
                  [tn, "p%d" % (2 + f % 2)], ["h%d" % f])
            for c in range(KC):
                pd = pb[4 + c % 2]
                for hf in range(2):
                    di = hf
                    dma(wdr[di][:], SD[pref + "_down"][c, hf], reads=WSCR, writes=["wd%d" % di], q="sp")
                    for f2 in range(FC // 2):
                        f = hf * (FC // 2) + f2
                        mm(pd[:, 0:Tt], wdr[di][:, f2, :], hT[:, f, 0:Tt], f == 0, f == FC - 1, ["wd%d" % di, "h%d" % f], ["p%d" % (4 + c % 2)])
                E("dve", CALL("scalar_tensor_tensor", xT[:, c, 0:Tt], pd[:, 0:Tt], 0.5, xT[:, c, 0:Tt], ALU.mult, ALU.add),
                  ["p%d" % (4 + c % 2), "xT"], ["xT"])

        def in_proj(Tt):
            norm_x("norm_mix", Tt)
            for oc in range(24):
                ii = wcnt["i"] % 2
                wcnt["i"] += 1
                dma(wir[ii][:], SI[oc], reads=WSCR, writes=["wi%d" % ii], q="sp")
                po = pb[4 + oc % 2]
                for c in range(KC):
                    mm(po[:, 0:Tt], wir[ii][:, c, :], xn[:, c, 0:Tt], c == 0, c == KC - 1, ["wi%d" % ii, "xn%d" % c], ["p%d" % (4 + oc % 2)])
                if oc < 8:
                    dst, key = u_bf[:, oc, 0:Tt], "u_bf"
                elif oc < 16:
                    dst, key = xmh[:, oc - 8, 3:3 + Tt], "xmh"
                else:
                    dst, key = z_bf[:, oc - 16, 0:Tt], "z_bf"
                if oc % 2 == 0:
                    E("act", CALL("copy", dst, po[:, 0:Tt]), ["p%d" % (4 + oc % 2)], [key])
                else:
                    E("dve", CALL("tensor_copy", dst, po[:, 0:Tt]), ["p%d" % (4 + oc % 2)], [key])

        def s5_mix(Tt):
            gT = hT
            L = min(64, Tt)
            nun = Tt // L
            def s5_pre(c, un, S):
                t0 = un * L
                X = S["extra"]
                pS, psk = S["pS"][un % 2], S["psk"][un % 2]
                umS = S["um"][:, un % 2]
                kum = "s5%sum%d" % (S["k"], un % 2)
                pSv = pS[:].rearrange("p (q r t) -> p q r t", q=4, r=2)
                for qq in range(4):
                    E("act", CALL("activation", umS[:, qq, 0:L], u_bf[:, c, t0:t0 + L], AF.Copy, scale=mask4[:, qq:qq + 1]), ["u_bf", "mask4"] + X, [kum])
                for qq in range(4):
                    for ri in range(2):
                        mm(pSv[:, qq, ri, 0:L], W1[:, c, ri, :], umS[:, qq, 0:L], True, True, ["W1", kum] + X, [psk])

            def s5_unit(c, un, S):
                pY = pb[2 + c % 2]
                pyk = "p%d" % (2 + c % 2)
                t0 = un * L
                X = S["extra"]
                K = lambda i: "s5%s%d" % (S["k"], i)
                pS, psk = S["pS"][un % 2], S["psk"][un % 2]
                xbf = S["xbf"]
                kxb = K(11)
                injC, kinjC = S["inj"][:, un % 2], "s5%sinj%d" % (S["k"], un % 2)
                injN, kinjN = S["inj"][:, (un + 1) % 2], "s5%sinj%d" % (S["k"], (un + 1) % 2)
                sk = "sre%d" % c
                pSv = pS[:].rearrange("p (q r t) -> p q r t", q=4, r=2)
                if un == 0:
                    s5_pre(c, 0, S)
                    E("pool", CALL("tensor_tensor", injC, s5st[:, :, 4 * c:4 * c + 4], bc_mid(rmag[:, 4 * c:4 * c + 4], 2), ALU.mult), ["rmag", sk] + X, [kinjC])
                    yield
                if un + 1 < nun and L == 64:
                    s5_pre(c, un + 1, S)
                    yield
                bre = pSv[:, :, 0, 0:L]
                bim = pSv[:, :, 1, 0:L]
                cs = cosT[:, 4 * c:4 * c + 4, 0:L]
                sn = sinT[:, 4 * c:4 * c + 4, 0:L]
                B = S["bufs"]
                bv = lambda i: B[:, i * 256:(i + 1) * 256].rearrange("p (q t) -> p q t", t=64)[:, :, 0:L]
                E("dve", CALL("tensor_tensor", bv(2), bre, cs, ALU.mult), [psk, "cosT"] + X, [K(2)])
                E("dve", CALL("tensor_tensor", bv(3), bim, sn, ALU.mult), [psk, "sinT"] + X, [K(3)])
                E("dve", CALL("tensor_tensor", bv(6), bim, cs, ALU.mult), [psk, "cosT"] + X, [K(6)])
                E("dve", CALL("tensor_tensor", bv(7), bre, sn, ALU.mult), [psk, "sinT"] + X, [K(7)])
                E("dve", CALL("tensor_tensor", bv(0), bv(2), bv(3), ALU.add), [K(2), K(3)] + X, [K(0)])
                E("dve", CALL("tensor_tensor", bv(1), bv(6), bv(7), ALU.subtract), [K(6), K(7)] + X, [K(1)])
                if L == 64:
                    W2 = B[:, 0:512].rearrange("p (r q t) -> p r q t", r=2, t=64)
                    E("dve", CALL("tensor_tensor", W2[:, :, :, 0], W2[:, :, :, 0], injC, ALU.add), [K(0), K(1), kinjC] + X, [K(0), K(1)])
                    rt = S["rtab"][:, 4 * c:4 * c + 4, :].rearrange("p q t -> p (q t)")
                    E("dve", CALL("tensor_tensor_scan", B[:, 1024:1280], rt, B[:, 0:256], 0.0, ALU.mult, ALU.add), [K(0), "rtab"] + X, [K(4)])
                    E("dve", CALL("tensor_tensor_scan", B[:, 1280:1536], rt, B[:, 256:512], 0.0, ALU.mult, ALU.add), [K(1), "rtab"] + X, [K(5)])
                    yield
                else:
                    for qq in range(4):
                        q = 4 * c + qq
                        E("dve", CALL("tensor_tensor_scan", bv(4)[:, qq, :], bc_row(rmag[:, q:q + 1], L), bv(0)[:, qq, :],
                                      sre[:, q:q + 1], ALU.mult, ALU.add), ["rmag", K(0), sk] + X, [K(4)])
                        E("dve", CALL("tensor_tensor_scan", bv(5)[:, qq, :], bc_row(rmag[:, q:q + 1], L), bv(1)[:, qq, :],
                                      sim[:, q:q + 1], ALU.mult, ALU.add), ["rmag", K(1), sk] + X, [K(5)])
                    yield
                zr, zi = bv(4), bv(5)
                E("pool", CALL("tensor_tensor", bv(2), zr, cs, ALU.mult), [K(4), "cosT"] + X, [K(2)])
                E("pool", CALL("tensor_tensor", bv(3), zi, sn, ALU.mult), [K(5), "sinT"] + X, [K(3)])
                E("pool", CALL("tensor_tensor", bv(6), bv(2), bv(3), ALU.subtract), [K(2), K(3)] + X, [K(6)])
                E("pool", CALL("tensor_tensor", bv(2), zi, cs, ALU.mult), [K(5), "cosT"] + X, [K(2)])
                E("pool", CALL("tensor_tensor", bv(3), zr, sn, ALU.mult), [K(4), "sinT"] + X, [K(3)])
                E("pool", CALL("tensor_tensor", bv(7), bv(2), bv(3), ALU.add), [K(2), K(3)] + X, [K(7)])
                X4 = B[:, 1536:2048].rearrange("p (r q t) -> p r q t", r=2, t=64)
                if L == 64 and un + 1 < nun:
                    E("pool", CALL("tensor_tensor", injN, X4[:, :, :, L - 1], bc_mid(rmag[:, 4 * c:4 * c + 4], 2), ALU.mult), ["rmag", K(6), K(7)] + X, [kinjN])
                yield
                E("act", CALL("copy", xbf[:, :, :, 0:L], X4[:, :, :, 0:L]), [K(6), K(7)] + X, [kxb])
                if L != 64 or un + 1 == nun:
                    E("act", CALL("copy", s5st[:, :, 4 * c:4 * c + 4], X4[:, :, :, L - 1]), [K(6), K(7)] + X, [sk])
                yield
                for qq in range(4):
                    q = 4 * c + qq
                    mm(pY[32 * qq:32 * qq + 32, t0:t0 + L], W3[:, q, 0, :], xbf[:, 0, qq, 0:L], True, False, ["W3", kxb] + X, [pyk], tp=(0, 32 * qq))
                    mm(pY[32 * qq:32 * qq + 32, t0:t0 + L], W3[:, q, 1, :], xbf[:, 1, qq, 0:L], False, True, ["W3", kxb] + X, [pyk], tp=(0, 32 * qq))
                yield

            def s5_post(c):
                pY = pb[2 + c % 2]
                pyk = "p%d" % (2 + c % 2)
                E("dve", CALL("scalar_tensor_tensor", tA[:, 0:Tt], u_bf[:, c, 0:Tt], V("s5_d", c), pY[:, 0:Tt], ALU.mult, ALU.add),
                  ["u_bf", "vec", pyk], ["tA"])
                E("act", CALL("activation", tB[:, 0:Tt], tA[:, 0:Tt], AF.Square), ["tA"], ["tB"])
                E("dve", CALL("tensor_scalar", tB[:, 0:Tt], tB[:, 0:Tt], 0.044715, 1.0, ALU.mult, ALU.add), ["tB"], ["tB"])
                E("pool", CALL("tensor_tensor", tB[:, 0:Tt], tB[:, 0:Tt], tA[:, 0:Tt], ALU.mult), ["tA", "tB"], ["tB"])
                E("act", CALL("activation", tB[:, 0:Tt], tB[:, 0:Tt], AF.Sigmoid, scale=1.5957691216057308), ["tB"], ["tB"])
                E("dve", CALL("tensor_tensor", gT[:, c, 0:Tt], tA[:, 0:Tt], tB[:, 0:Tt], ALU.mult), ["tA", "tB"], ["h%d" % c])

            ybf_ = ybuf[:].rearrange("p a b -> p (a b)")
            rtab = ybf_[:, 2048:4096].rearrange("p (q t) -> p q t", t=64)
            E("dve", CALL("tensor_tensor", rtab, bc_last(rmag[:, :], 64), bc_mid(resetm[:, :], 32), ALU.mult), ["rmag", "resetm", "ybuf"], ["rtab"])
            SA = dict(bufs=s5all[:].rearrange("p a b -> p (a b)"), um=um, xbf=xbfA, inj=inj[0], k="A", extra=[], pS=[pb[0], pb[1]], psk=["p0", "p1"], rtab=rtab)
            SB = dict(bufs=ybf_[:, 0:2048], um=umB, xbf=xbfB, inj=inj[1], k="B", extra=["ybuf"], pS=[pb[4], pb[5]], psk=["p4", "p5"], rtab=rtab)
            for cp in range(KC // 2):
                c0, c1 = 2 * cp, 2 * cp + 1
                for un in range(nun):
                    gens = [s5_unit(c0, un, SA), s5_unit(c1, un, SB)]
                    while gens:
                        for g_ in list(gens):
                            try:
                                next(g_)
                            except StopIteration:
                                gens.remove(g_)
                s5_post(c0)
                s5_post(c1)
            for oc in range(KC):
                ii = wcnt["i"] % 2
                wcnt["i"] += 1
                dma(wir[ii][:], SGL[oc], reads=WSCR, writes=["wi%d" % ii], q="sp")
                po = pb[4 + oc % 2]
                pk = "p%d" % (4 + oc % 2)
                for c in range(KC):
                    mm(po[:, 0:Tt], wir[ii][:, c, :], gT[:, c, 0:Tt], c == 0, c == KC - 1, ["wi%d" % ii, "h%d" % c], [pk])
                E("act", CALL("activation", tA[:, 0:Tt], po[:, 0:Tt], AF.Sigmoid, bias=V("s5_glu_b", oc)), [pk, "vec"], ["tA"])
                E("dve", CALL("tensor_tensor", ybuf[:, oc, 0:Tt], gT[:, oc, 0:Tt], tA[:, 0:Tt], ALU.mult), ["tA", "h%d" % oc], ["ybuf"])
            rmsnorm_stats(lambda c: ybuf[:, c, 0:Tt], KC, Tt, ["ybuf"], onesD)
            for c in range(KC):
                E("dve", CALL("scalar_tensor_tensor", mixed[:, c, 0:Tt], ybuf[:, c, 0:Tt], V("out_norm_s5", c), rstd[:, 0:Tt], ALU.mult, ALU.mult),
                  ["ybuf", "vec", "rstd"], ["mixed"])

        def mlstm_mix(Tt):
            qT = hT
            for c in range(KC):
                eng = "dve"
                E(eng, CALL("tensor_scalar", tA[:, 0:Tt], xmh[:, c, 0:Tt], V("cw0", c), 0.0, ALU.mult, ALU.add), ["xmh", "vec"], ["tA"])
                for j in range(1, 4):
                    E(eng, CALL("scalar_tensor_tensor", tA[:, 0:Tt], xmh[:, c, j:j + Tt], V("cw%d" % j, c), tA[:, 0:Tt], ALU.mult, ALU.add),
                      ["xmh", "vec", "tA"], ["tA"])
                E("act", CALL("activation", xc_bf[:, c, 0:Tt], tA[:, 0:Tt], AF.Silu, bias=V("ml_conv_b", c)), ["tA", "vec"], ["xn%d" % c])
            for c in range(KC):
                E("act", CALL("activation", z_bf[:, c, 0:Tt], z_bf[:, c, 0:Tt], AF.Silu), ["z_bf"], ["z_bf"])
            for c in range(KC):
                for wi, (src, dstT, key) in enumerate([(xc_bf[:, c, 0:Tt], qT[:, c, 0:Tt], "h%d" % c),
                                                       (xc_bf[:, c, 0:Tt], qT[:, 8 + c, 0:Tt], "h%d" % (8 + c)),
                                                       (xmh[:, c, 3:3 + Tt], vT[:, c, 0:Tt], "h%d" % (16 + c))]):
                    po = pb[4 + (3 * c + wi) % 2]
                    pk = "p%d" % (4 + (3 * c + wi) % 2)
                    mm(po[:, 0:Tt], BD[:, wi, c, :], src, True, True, ["BD", "xn%d" % c, "xmh"], [pk])
                    if wi == 1:
                        E("dve", CALL("tensor_copy", dstT, po[:, 0:Tt]), [pk], [key])
                    else:
                        E("act", CALL("copy", dstT, po[:, 0:Tt]), [pk], [key])
            for gi, (gw, pbk) in enumerate([(gwi, 4), (gwf, 5)]):
                for j in range(24):
                    src = qT[:, j, 0:Tt]
                    key = "h%d" % j
                    mm(pb[pbk][0:4, 0:Tt], gw[:, j, :], src, j == 0, j == 23, ["gwi", "gwf", key], ["p%d" % pbk])
            E("act", CALL("activation", igs[:, 0:Tt], pb[4][0:4, 0:Tt], AF.Identity, bias=igb[:, 0:1]), ["p4", "igb"], ["igs"])
            E("act", CALL("activation", MM[:, 0:Tt], pb[5][0:4, 0:Tt], AF.Exp, bias=nfgb[:, 0:1], scale=-1.0), ["p5", "nfgb"], ["MM"])
            E("act", CALL("activation", MM[:, 0:Tt], MM[:, 0:Tt], AF.Ln, bias=onecol[0:4, 0:1]), ["MM", "onecol"], ["MM"])
            E("dve", CALL("tensor_tensor_scan", Fn[:, 0:Tt], bc_row(onecol[0:4, 0:1], Tt), MM[:, 0:Tt], 0.0, ALU.mult, ALU.add), ["MM", "onecol"], ["Fn"])
            E("dve", CALL("tensor_tensor", aa[:, 0:Tt], igs[:, 0:Tt], Fn[:, 0:Tt], ALU.add), ["igs", "Fn"], ["aa"])
            E("dve", CALL("tensor_tensor_scan", MM[:, 0:Tt], bc_row(onecol[0:4, 0:1], Tt), aa[:, 0:Tt], mprev[:, 0:1], ALU.mult, ALU.max),
              ["aa", "onecol", "mprev"], ["MM"])
            E("dve", CALL("tensor_copy", Mpc[:], mprev[:]), ["mprev"], ["Mpc"])
            Lc = min(128, Tt)
            for ch in range(Tt // Lc):
                t0, t1 = ch * Lc, ch * Lc + Lc
                E("dve", CALL("tensor_scalar", negM[:], MM[:, t1 - 1:t1], -1.0, 0.0, ALU.mult, ALU.add), ["MM"], ["negM"])
                E("act", CALL("activation", g4[0][:, 0:Lc], aa[:, t0:t1], AF.Exp, bias=negM[:, 0:1]), ["aa", "negM"], ["g40"])
                E("act", CALL("activation", gcol[:], Mpc[:], AF.Exp, bias=negM[:, 0:1]), ["Mpc", "negM"], ["gcol"])
                E("dve", CALL("tensor_scalar", g4[1][:, 0:Lc], Fn[:, t0:t1], negM[:, 0:1], 0.0, ALU.add, ALU.add), ["Fn", "negM"], ["g41"])
                E("dve", CALL("tensor_copy", Mpc[:], MM[:, t1 - 1:t1]), ["MM", "gcol"], ["Mpc"])
                tr(pb[7][0:Lc, 128:132], g4[0][:, 0:Lc], ident[0:4, 0:4], ["g40"], ["p7"])
                E("dve", CALL("tensor_copy", et[0:Lc, :], pb[7][0:Lc, 128:132]), ["p7"], ["et"])
                E("dve", CALL("tensor_scalar", dg[:], ident[0:4, 0:4], gcol[:, 0:1], 0.0, ALU.mult, ALU.add), ["ident", "gcol"], ["dg"])
                mm(pb[7][:, 132:136], ones4[:], dg[:], True, True, ["ones4", "dg"], ["p7"])
                E("dve", CALL("tensor_copy", gb[:], pb[7][:, 132:136]), ["p7"], ["gb"])
                for c in range(KC):
                    mm(pb[c // 4][0:Lc, (c % 4) * 128:(c % 4) * 128 + 128], xc_bf[:, c, t0:t1], BD[:, 1, c, :], True, True, ["xn%d" % c, "BD"], ["p%d" % (c // 4)])
                    mm(pb[2 + c // 4][0:Lc, (c % 4) * 128:(c % 4) * 128 + 128], xmh[:, c, 3 + t0:3 + t1], BD[:, 2, c, :], True, True, ["xmh", "BD"], ["p%d" % (2 + c // 4)])
                def SET(pr):
                    if pr == 0:
                        return dict(ktp=ktp, vtp=vtp, vtb=vtb, Sm=Sm, Eb=Eb, Cg=Cg, nrep=nrep, hh=hh, hq=hq, ngc=ngc2[:, 0, :], X=[], k="0")
                    vv = hT[:, 16:20, :]
                    return dict(ktp=vv[:, 0, 0:256], vtp=vv[:, 0, 256:512], vtb=vv[:, 1, 0:256], Sm=vv[:, 1, 256:384], Eb=vv[:, 1, 384:512],
                                Cg=vv[:, 2, :].rearrange("p (a b) -> p a b", a=2), nrep=vv[:, 3, 0:256].rearrange("p (a b) -> p a b", a=2),
                                hh=tAB[:, 0:256].rearrange("p (a b) -> p a b", a=2), hq=tAB[:, 256:512].rearrange("p (a b) -> p a b", a=2),
                                ngc=ngc2[:, 1, :], X=["h16", "h17", "h18", "h19", "tA"], k="1")

                def prep(h):
                    S_ = SET(h % 2)
                    X = S_["X"]
                    N = lambda n: "ml%s%s" % (n, S_["k"])
                    kps = pb[h // 2][0:Lc, (h % 2) * 256:(h % 2) * 256 + 256]
                    vps = pb[2 + h // 2][0:Lc, (h % 2) * 256:(h % 2) * 256 + 256]
                    kk, vk = "p%d" % (h // 2), "p%d" % (2 + h // 2)
                    for kc in range(2):
                        mm(pb[7][0:Lc, 0:Lc], qT[:, 8 + 2 * h + kc, t0:t1], qT[:, 2 * h + kc, t0:t1], kc == 0, kc == 1,
                           ["h%d" % (8 + 2 * h + kc), "h%d" % (2 * h + kc), "p7"], ["p7s"])
                    E("dve", CALL("scalar_tensor_tensor", S_["Sm"][0:Lc, 0:Lc], pb[7][0:Lc, 0:Lc], 1.0 / 16, triu[0:Lc, 0:Lc], ALU.mult, ALU.mult), ["p7s", "p7", "triu"] + X, [N("Sm")])
                    E("dve", CALL("tensor_scalar", S_["vtp"][0:Lc, :], vps, et[0:Lc, h:h + 1], 0.0, ALU.mult, ALU.add), [vk, "et"] + X, [N("vtp")])
                    E("act", CALL("copy", S_["vtb"][0:Lc, :], vps), [vk] + X, [N("vtb")])
                    E("dve", CALL("tensor_scalar", S_["ktp"][0:Lc, :], kps, et[0:Lc, h:h + 1], 1.0 / 16, ALU.mult, ALU.mult), [kk, "et"] + X, [N("ktp")])
                    E("pool", CALL("tensor_scalar", S_["Eb"][0:Lc, :], ones_bf[0:Lc, :], et[0:Lc, h:h + 1], 0.0, ALU.mult, ALU.add), ["ones_bf", "et"] + X, [N("Eb")])
                    E("pool", CALL("tensor_scalar", S_["Cg"].rearrange("p a b -> p (a b)"), CT[:, h, :, :].rearrange("p a b -> p (a b)"), gb[:, h:h + 1], 0.0, ALU.mult, ALU.add),
                      ["CT%d" % h, "gb"] + X, [N("Cg")])
                    E("dve", CALL("tensor_scalar", S_["ngc"], nT[:, 2 * h:2 * h + 2], gb[:, h:h + 1], 0.0, ALU.mult, ALU.add), ["nT%d" % h, "gb"], [N("ngc")])
                    for kc in range(2):
                        E("pool", CALL("tensor_scalar", S_["nrep"][:, kc, :], ones_bf[:, :], S_["ngc"][:, kc:kc + 1], 0.0, ALU.mult, ALU.add), ["ones_bf", N("ngc")] + X, [N("nrep")])

                def mid(h):
                    S_ = SET(h % 2)
                    X = S_["X"]
                    N = lambda n: "ml%s%s" % (n, S_["k"])
                    hh_, hq_ = S_["hh"], S_["hq"]
                    for vc in range(2):
                        o = pb[4][:, vc * 128:vc * 128 + Lc]
                        mm(o, S_["vtp"][0:Lc, vc * 128:vc * 128 + 128], S_["Sm"][0:Lc, 0:Lc], True, False, [N("vtp"), N("Sm")] + X, ["p4"])
                        for kc in range(2):
                            mm(o, S_["Cg"][:, kc, vc * 128:vc * 128 + 128], qT[:, 2 * h + kc, t0:t1], False, kc == 1, [N("Cg"), "h%d" % (2 * h + kc)] + X, ["p4"])
                    o = pb[4][:, 256:256 + Lc]
                    mm(o, S_["Eb"][0:Lc, :], S_["Sm"][0:Lc, 0:Lc], True, False, [N("Eb"), N("Sm")] + X, ["p4"])
                    for kc in range(2):
                        mm(o, S_["nrep"][:, kc, :], qT[:, 2 * h + kc, t0:t1], False, kc == 1, [N("nrep"), "h%d" % (2 * h + kc)] + X, ["p4"])
                    mm(pb[4][:, 384:384 + Lc], sel4[:, h * 128:h * 128 + 128], g4[1][:, 0:Lc], True, True, ["sel4", "g41"], ["p4"])
                    E("act", CALL("activation", mw[0][:, 0:Lc], pb[4][:, 384:384 + Lc], AF.Exp), ["p4"], ["mw0"])
                    E("act", CALL("activation", mwab[:, 0:Lc], pb[4][:, 256:256 + Lc], AF.Abs), ["p4"], ["mwab"])
                    E("dve", CALL("tensor_tensor", mw[0][:, 0:Lc], mwab[:, 0:Lc], mw[0][:, 0:Lc], ALU.max), ["mwab", "mw0"], ["mw0"])
                    E("act", CALL("activation", mw[0][:, 0:Lc], mw[0][:, 0:Lc], AF.Ln), ["mw0"], ["mw0"])
                    E("act", CALL("activation", mw[0][:, 0:Lc], mw[0][:, 0:Lc], AF.Exp, scale=-1.0), ["mw0"], ["mw0"])
                    for vc in range(2):
                        E("dve", CALL("tensor_tensor", hh_[:, vc, 0:Lc], pb[4][:, vc * 128:vc * 128 + Lc], mw[0][:, 0:Lc], ALU.mult), ["p4", "mw0"] + X, [N("hh")])
                        E("act", CALL("activation", hq_[:, vc, 0:Lc], hh_[:, vc, 0:Lc], AF.Square), [N("hh")] + X, [N("hq")])
                    for kc in range(2):
                        mm(pb[6][:, kc * 256:kc * 256 + 256], S_["ktp"][0:Lc, kc * 128:kc * 128 + 128], S_["vtb"][0:Lc, :], True, True, [N("ktp"), N("vtb")] + X, ["p6"])
                    E("dve", CALL("scalar_tensor_tensor", CT[:, h, :, :].rearrange("p a b -> p (a b)"), CT[:, h, :, :].rearrange("p a b -> p (a b)"),
                                  gb[:, h:h + 1], pb[6][:, :], ALU.mult, ALU.add), ["CT%d" % h, "gb", "p6", N("Cg")], ["CT%d" % h])
                    for kc in range(2):
                        mm(pb[7][:, 136 + kc:137 + kc], S_["ktp"][0:Lc, kc * 128:kc * 128 + 128], ones_bf[0:Lc, 0:1], True, True, [N("ktp"), "ones_bf", "p7"] + X, ["p7n"])
                    E("dve", CALL("tensor_tensor", nT[:, 2 * h:2 * h + 2], S_["ngc"], pb[7][:, 136:138], ALU.add), [N("ngc"), "p7n", "p7"], ["nT%d" % h])

                def fin(h):
                    S_ = SET(h % 2)
                    X = S_["X"]
                    N = lambda n: "ml%s%s" % (n, S_["k"])
                    hh_, hq_ = S_["hh"], S_["hq"]
                    for vc in range(2):
                        mm(pb[5][:, 0:Lc], ones256[:], hh_[:, vc, 0:Lc], vc == 0, vc == 1, ["ones256", N("hh")] + X, ["p5"])
                    for vc in range(2):
                        mm(pb[5][:, 128:128 + Lc], ones256[:], hq_[:, vc, 0:Lc], vc == 0, vc == 1, ["ones256", N("hq")] + X, ["p5"])
                    E("act", CALL("activation", mw[1][:, 0:Lc], pb[5][:, 0:Lc], AF.Square), ["p5"], ["mw1"])
                    E("dve", CALL("tensor_tensor", mw[1][:, 0:Lc], pb[5][:, 128:128 + Lc], mw[1][:, 0:Lc], ALU.subtract), ["p5", "mw1"], ["mw1"])
                    E("dve", CALL("tensor_scalar", mw[1][:, 0:Lc], mw[1][:, 0:Lc], 0.0, 0.0, ALU.max, ALU.add), ["mw1"], ["mw1"])
                    E("act", CALL("activation", mw[1][:, 0:Lc], mw[1][:, 0:Lc], AF.Ln, bias=epscol[:, 0:1]), ["mw1", "epscol"], ["mw1"])
                    E("act", CALL("activation", mw[1][:, 0:Lc], mw[1][:, 0:Lc], AF.Exp, scale=-0.5), ["mw1"], ["mw1"])
                    for vc in range(2):
                        c = 2 * h + vc
                        E("dve", CALL("tensor_tensor", mw[2][:, 0:Lc], hh_[:, vc, 0:Lc], pb[5][:, 0:Lc], ALU.subtract), [N("hh"), "p5"] + X, ["mw2"])
                        E("pool", CALL("tensor_tensor", mw[2][:, 0:Lc], mw[2][:, 0:Lc], mw[1][:, 0:Lc], ALU.mult), ["mw2", "mw1"], ["mw2"])
                        E("pool", CALL("tensor_scalar", mw[2][:, 0:Lc], mw[2][:, 0:Lc], V("ml_norm_w", c), 0.0, ALU.mult, ALU.add), ["mw2", "vec"], ["mw2"])
                        E("dve", CALL("scalar_tensor_tensor", mw[3][:, 0:Lc], xc_bf[:, c, t0:t1], V("ml_skip", c), mw[2][:, 0:Lc], ALU.mult, ALU.add),
                          ["xn%d" % c, "vec", "mw2"], ["mw3"])
                        E("pool", CALL("tensor_tensor", ybuf[:, c, t0:t1], mw[3][:, 0:Lc], z_bf[:, c, t0:t1], ALU.mult), ["mw3", "z_bf"], ["ybuf"])

                for blk in [(prep, 0), (prep, 1), (mid, 0), (prep, 2), (mid, 1), (fin, 0), (prep, 3), (mid, 2), (fin, 1), (mid, 3), (fin, 2), (fin, 3)]:
                    blk[0](blk[1])
            E("dve", CALL("tensor_tensor", mprev[:], MM[:, Tt - 1:Tt], Fn[:, Tt - 1:Tt], ALU.subtract), ["MM", "Fn", "Mpc"], ["mprev"])
            for c in range(KC):
                E("act", CALL("copy", xmh[:, c, 0:3], xmh[:, c, Tt:Tt + 3]), ["xmh"], ["xmh"])
            rmsnorm_stats(lambda c: ybuf[:, c, 0:Tt], KC, Tt, ["ybuf"], onesD)
            for c in range(KC):
                E("dve", CALL("scalar_tensor_tensor", mixed[:, 8 + c, 0:Tt], ybuf[:, c, 0:Tt], V("out_norm_ml", c), rstd[:, 0:Tt], ALU.mult, ALU.mult),
                  ["ybuf", "vec", "rstd"], ["mixed"])

        def out_proj(Tt):
            for c in range(KC):
                po = pb[4 + c % 2]
                pk = "p%d" % (4 + c % 2)
                for hf in range(2):
                    oi = hf
                    dma(wor[oi][:], SO[c, hf], reads=WSCR, writes=["wo%d" % oi], q="sp")
                    for k2 in range(8):
                        k = hf * 8 + k2
                        mm(po[:, 0:Tt], wor[oi][:, k2, :], mixed[:, k, 0:Tt], k == 0, k == 15, ["wo%d" % oi, "mixed"], [pk])
                E("dve", CALL("tensor_tensor", xT[:, c, 0:Tt], xT[:, c, 0:Tt], po[:, 0:Tt], ALU.add), [pk, "xT"], ["xT"])

        def load_tile(src_rows, Tt):
            nsub = (Tt + 127) // 128
            for n in range(nsub):
                r = min(128, Tt - n * 128)
                dma(xtok[0:r, n, :], src_rows[n * 128:n * 128 + r, :], writes=["ybuf"])
            for c in range(KC):
                for n in range(nsub):
                    r = min(128, Tt - n * 128)
                    tr(pb[7][:, n * 128:n * 128 + r], xtok[0:r, n, c * 128:(c + 1) * 128], ident[0:r, 0:r], ["ybuf"], ["p7"])
                E("dve" if c % 2 == 0 else "act",
                  (CALL("tensor_copy", xT[:, c, 0:Tt], pb[7][:, 0:Tt])) if c % 2 == 0 else (CALL("copy", xT[:, c, 0:Tt], pb[7][:, 0:Tt])),
                  ["p7"], ["xT"])

        def store_tile(dst_rows, Tt):
            rmsnorm_stats(lambda c: xT[:, c, 0:Tt], KC, Tt, ["xT"], onesD)
            nsub = (Tt + 127) // 128
            for c in range(KC):
                E("dve", CALL("scalar_tensor_tensor", ybuf[:, c, 0:Tt], xT[:, c, 0:Tt], V("norm_final", c), rstd[:, 0:Tt], ALU.mult, ALU.mult),
                  ["xT", "vec", "rstd"], ["ybuf"])
            for n in range(nsub):
                r = min(128, Tt - n * 128)
                for c4 in range(2):
                    for c in range(c4 * 4, c4 * 4 + 4):
                        tr(pb[7][0:r, (c % 4) * 128:(c % 4) * 128 + 128], ybuf[:, c, n * 128:n * 128 + r], ident[:], ["ybuf"], ["p7"])
                    E("act", CALL("copy", otok[0:r, c4 * 512:(c4 + 1) * 512], pb[7][0:r, :]), ["p7"], ["tA", "tB"])
                outs.append(dma(dst_rows[n * 128:n * 128 + r, :], otok[0:r, :], reads=["tA", "tB"], q="sp"))

        def init_state(si):
            if si is None:
                for t, k in [(sre, SRE), (sim, SRE), (nT, ["nT0", "nT1", "nT2", "nT3"]), (mprev, ["mprev"])]:
                    E("dve", CALL("memset", t[:], 0.0), [], k)
                E("pool", CALL("memset", CT[:].rearrange("p a b c -> p (a b c)"), 0.0), [], ["CT0", "CT1", "CT2", "CT3"])
                E("pool", CALL("memset", xmh[:, :, 0:3], 0.0), [], ["xmh"])
                return
            for src, dst, k in [(st_s5re, sre, SRE), (st_s5im, sim, SIM)]:
                dma(stg[0:32, 0:128], src[si], writes=["stg"])
                tr(pb[7][:, 0:32], stg[0:32, 0:128], ident[0:32, 0:32], ["stg"], ["p7"])
                E("dve", CALL("tensor_copy", dst[:], pb[7][:, 0:32]), ["p7"], k)
            dma(stg[0:8, 0:128], st_n[si], writes=["stg"])
            tr(pb[7][:, 0:8], stg[0:8, 0:128], ident[0:8, 0:8], ["stg"], ["p7"])
            E("dve", CALL("tensor_copy", nT[:], pb[7][:, 0:8]), ["p7"], ["nT0", "nT1", "nT2", "nT3"])
            dma(mprev[:], st_m[si], writes=["mprev"])
            for c in range(KC):
                dma(stg[0:3, 0:128], st_conv[si][:, c * 128:(c + 1) * 128], writes=["stg"])
                tr(pb[7][:, 0:3], stg[0:3, 0:128], ident[0:3, 0:3], ["stg"], ["p7"])
                E("dve", CALL("tensor_copy", xmh[:, c, 0:3], pb[7][:, 0:3]), ["p7"], ["xmh"])
            for h in range(4):
                for vc in range(2):
                    dma(stg[:, 0:256], st_c[si, h, vc * 128:(vc + 1) * 128, :], writes=["stg"])
                    for kc in range(2):
                        tr(pb[7][:, kc * 128:kc * 128 + 128], stg[:, kc * 128:kc * 128 + 128], ident[:], ["stg"], ["p7"])
                    E("dve", CALL("tensor_copy", CT[:, h, :, vc * 128:vc * 128 + 128], pb[7][:, 0:256].rearrange("p (k v) -> p k v", k=2)), ["p7"], ["CT0", "CT1", "CT2", "CT3"])

        def store_state(oi):
            for src, dst, k in [(sre, o_s5re, SRE), (sim, o_s5im, SIM)]:
                tr(pb[7][0:32, 0:128], src[:], ident[:], k, ["p7"])
                E("dve", CALL("tensor_copy", stg[0:32, 0:128], pb[7][0:32, 0:128]), ["p7"], ["stg"])
                outs.append(dma(dst[oi], stg[0:32, 0:128], reads=["stg"], q="sp"))
            tr(pb[7][0:8, 0:128], nT[:], ident[:], ["nT0", "nT1", "nT2", "nT3"], ["p7"])
            E("dve", CALL("tensor_copy", stg[0:8, 0:128], pb[7][0:8, 0:128]), ["p7"], ["stg"])
            outs.append(dma(o_n[oi], stg[0:8, 0:128], reads=["stg"], q="sp"))
            outs.append(dma(o_m[oi], mprev[:], reads=["mprev"], q="sp"))
            for c in range(KC):
                E("dve", CALL("tensor_copy", mw[0][:, 0:3], xmh[:, c, 0:3]), ["xmh"], ["mw0"])
                tr(pb[7][0:3, 0:128], mw[0][:, 0:3], ident[:], ["mw0"], ["p7"])
                E("dve", CALL("tensor_copy", stg[0:3, 0:128], pb[7][0:3, 0:128]), ["p7"], ["stg"])
                outs.append(dma(o_conv[oi][:, c * 128:(c + 1) * 128], stg[0:3, 0:128], reads=["stg"], q="sp"))
            for h in range(4):
                for vc in range(2):
                    for kc in range(2):
                        tr(pb[7][:, kc * 128:kc * 128 + 128], CT[:, h, kc, vc * 128:vc * 128 + 128], ident[:], ["CT0", "CT1", "CT2", "CT3"], ["p7"])
                    E("dve", CALL("tensor_copy", stg[:, 0:256], pb[7][:, 0:256]), ["p7"], ["stg"])
                    outs.append(dma(o_c[oi, h, vc * 128:(vc + 1) * 128, :], stg[:, 0:256], reads=["stg"], q="sp"))

        def run_tile(src_rows, dst_rows, Tt):
            load_tile(src_rows, Tt)
            if STAGE >= 2:
                ffn("ffn1", "norm_ffn1", Tt)
            if STAGE >= 3:
                in_proj(Tt)
            if STAGE >= 4:
                s5_mix(Tt)
            if STAGE >= 5:
                mlstm_mix(Tt)
            if STAGE >= 6:
                out_proj(Tt)
            if STAGE >= 7:
                ffn("ffn2", "norm_ffn2", Tt)
            if dst_rows is not None:
                store_tile(dst_rows, Tt)

        if STAGE == 0:
            P.emit(final_waits=outs)
            return nc
        init_state(None)
        run_tile(xp[0:NMETA, :], None, NMETA)
        for ti in range((NP - NMETA) // TT):
            r0 = NMETA + ti * TT
            run_tile(xp[r0:r0 + TT, :], yp[r0 - NMETA:r0 - NMETA + TT, :], TT)
        store_state(0)
        for si in range(NSAMP):
            init_state(si)
            run_tile(xs[si], ys[si], SL)
            store_state(1 + si)
        P.emit(final_waits=outs)
    return nc


def host_consts():
    p = np.arange(128)
    c = {}
    c["ident"] = np.eye(128, dtype=np.float32)
    c["maskE"] = np.stack([(p // 64 == 0), (p // 64 == 1)], 1).astype(np.float32)
    p32 = np.arange(32)
    c["mask16"] = np.stack([(p32 // 16 == 0), (p32 // 16 == 1)], 1).astype(np.float32)
    c["triu"] = np.triu(np.ones((128, 128), np.float32))
    c["bdmask"] = (p[:, None] // 4 == np.arange(32)[None, :]).astype(np.float32)
    c["tvec"] = np.broadcast_to(np.arange(1, 65, dtype=np.float32)[None, :], (128, 64)).copy()
    s = np.zeros((4, 4, 128), np.float32)
    for h in range(4):
        s[h, h, :] = 1.0
    c["sel4"] = s.reshape(4, 512)
    c["mask4"] = (p[:, None] // 32 == np.arange(4)[None, :]).astype(np.float32)
    return c


_CACHE = {}


def kernel(**inp):
    f = lambda a: np.ascontiguousarray(np.asarray(a, dtype=np.float32))
    x_prompt = f(inp["x_prompt"])
    x_sample = f(inp["x_sample"])
    NB, SEQ, _ = x_prompt.shape
    NDEC, SL, _ = x_sample.shape
    NP = NMETA + SEQ
    ncores = 8
    NSAMP = NDEC // ncores
    key = (NP, NSAMP, SL)
    if key not in _CACHE:
        _CACHE[key] = build_program(NP, NSAMP, SL)
    nc = _CACHE[key]
    meta = f(inp["meta_tokens"])
    shared = host_consts()
    for n in ["ffn1_gate", "ffn1_up", "ffn1_down", "ffn2_gate", "ffn2_up", "ffn2_down", "w_in", "s5_glu_w", "w_out"]:
        shared[n] = f(inp[n])[0]
    shared["lam_re"] = f(inp["s5_lambda_re"])[0].reshape(32, 128)
    shared["lam_im"] = f(inp["s5_lambda_im"])[0].reshape(32, 128)
    shared["log_dt"] = f(inp["s5_log_dt"])[0]
    shared["b_re"] = f(inp["s5_b_re"])[0]
    shared["b_im"] = f(inp["s5_b_im"])[0]
    shared["c_re"] = f(inp["s5_c_re"])[0]
    shared["c_im"] = f(inp["s5_c_im"])[0]
    shared["wq"] = f(inp["ml_wq"])[0]
    shared["wk"] = f(inp["ml_wk"])[0]
    shared["wv"] = f(inp["ml_wv"])[0]
    shared["igw"] = f(inp["ml_igate_w"])[0]
    shared["fgw"] = f(inp["ml_fgate_w"])[0]
    shared["igb"] = f(inp["ml_igate_b"])[0].reshape(4, 1)
    shared["fgb"] = f(inp["ml_fgate_b"])[0].reshape(4, 1)
    cw = f(inp["ml_conv_w"])[0]
    vd = {"norm_ffn1": f(inp["norm_ffn1"])[0], "norm_mix": f(inp["norm_mix"])[0], "s5_d": f(inp["s5_d"])[0],
          "s5_glu_b": f(inp["s5_glu_b"])[0], "cw0": cw[0], "cw1": cw[1], "cw2": cw[2], "cw3": cw[3],
          "ml_conv_b": f(inp["ml_conv_b"])[0], "ml_norm_w": f(inp["ml_norm_w"])[0], "ml_skip": f(inp["ml_skip"])[0],
          "out_norm_s5": f(inp["out_norm_s5"])[0], "out_norm_ml": f(inp["out_norm_ml"])[0],
          "norm_ffn2": f(inp["norm_ffn2"])[0], "norm_final": f(inp["norm_final"])}
    shared["vecs"] = np.concatenate([vd[n].reshape(8, 128) for n in VEC_NAMES], 0)
    s5re, s5im = f(inp["state_s5_re"])[0], f(inp["state_s5_im"])[0]
    stc, stn, stm, stcv = f(inp["state_mlstm_c"])[0], f(inp["state_mlstm_n"])[0], f(inp["state_mlstm_m"])[0], f(inp["state_mlstm_conv"])[0]
    in_maps = []
    for c in range(ncores):
        b = c % NB
        sl = slice(c * NSAMP, (c + 1) * NSAMP)
        m = dict(shared)
        m["xp"] = np.concatenate([meta, x_prompt[b]], 0)
        m["xs"] = x_sample[sl]
        m["st_s5re"] = s5re[sl].reshape(NSAMP, 32, 128)
        m["st_s5im"] = s5im[sl].reshape(NSAMP, 32, 128)
        m["st_c"] = stc[sl]
        m["st_n"] = stn[sl].reshape(NSAMP, 8, 128)
        m["st_m"] = stm[sl].reshape(NSAMP, 4, 1)
        m["st_conv"] = stcv[sl]
        in_maps.append(m)
    res = run_bass_kernel_spmd(nc, in_maps, core_ids=list(range(ncores))).results
    y_prompt = np.stack([res[b]["yp"] for b in range(NB)], 0)
    y_sample = np.concatenate([res[c]["ys"] for c in range(ncores)], 0)

    def gather(name, shape_tail):
        pr = np.stack([res[b][name][0] for b in range(NB)], 0).reshape((1, NB) + shape_tail)
        sm = np.concatenate([res[c][name][1:] for c in range(ncores)], 0).reshape((1, NDEC) + shape_tail)
        return pr.astype(np.float32), sm.astype(np.float32)

    p_re, s_re = gather("o_s5re", (64, 64))
    p_im, s_im = gather("o_s5im", (64, 64))
    p_c, s_c = gather("o_c", (4, 256, 256))
    p_n, s_n = gather("o_n", (4, 256))
    p_m, s_m = gather("o_m", (4,))
    p_cv, s_cv = gather("o_conv", (3, 1024))
    return (y_prompt.astype(np.float32), y_sample.astype(np.float32), p_re, p_im, p_c, p_n, p_m, p_cv,
            s_re, s_im, s_c, s_n, s_m, s_cv)
```

```python
import math
import numpy as np
import concourse.bass as bass
import concourse.mybir as mybir
from concourse.bass_utils import run_bass_kernel_spmd
from contextlib import ExitStack

F32 = mybir.dt.float32
BF16 = mybir.dt.bfloat16
I32 = mybir.dt.int32
ALU = mybir.AluOpType
AF = mybir.ActivationFunctionType

D = 1024
DFF = 2816
KC = 8
FC = 22
NMETA = 16
EPS = 1e-6
STAGE = 9
ENGS = ("pe", "act", "dve", "pool", "sp")
NDMA_SEM = 12
VEC_NAMES = ["norm_ffn1", "norm_mix", "s5_d", "s5_glu_b", "cw0", "cw1", "cw2", "cw3", "ml_conv_b",
             "ml_norm_w", "ml_skip", "out_norm_s5", "out_norm_ml", "norm_ffn2", "norm_final"]
VI = {n: i for i, n in enumerate(VEC_NAMES)}


class Op:
    __slots__ = ("eng", "fn", "deps", "idx", "signaled", "sigval", "dma", "dsem", "dval", "dprev")

    def __init__(self, eng, fn, dma):
        self.eng = eng
        self.fn = fn
        self.deps = []
        self.signaled = False
        self.sigval = 0
        self.dma = dma
        self.dsem = None
        self.dval = 0
        self.dprev = None


class Prog:
    def __init__(self, nc):
        self.nc = nc
        self.ops = {e: [] for e in ENGS}
        self.last_writer = {}
        self.readers = {}
        self.ndma = {e: 0 for e in ENGS}
        self.dma_ops = {e: [] for e in ENGS}

    def op(self, eng, fn, reads=(), writes=(), dma=False):
        o = Op(eng, fn, dma)
        deps = []
        for r in reads:
            w = self.last_writer.get(r)
            if w is not None:
                deps.append(w)
        for wr in writes:
            w = self.last_writer.get(wr)
            if w is not None:
                deps.append(w)
            deps.extend(self.readers.get(wr, ()))
        seen = set()
        for d in deps:
            if id(d) in seen or d is o:
                continue
            seen.add(id(d))
            if d.eng == "pe" and eng == "pe" and not d.dma and not dma:
                continue
            o.deps.append(d)
        for r in reads:
            self.readers.setdefault(r, []).append(o)
        for wr in writes:
            self.last_writer[wr] = o
            self.readers[wr] = []
        if dma:
            k = self.ndma[eng]
            self.ndma[eng] += 1
            o.dsem = k % NDMA_SEM
            o.dval = 16 * (k // NDMA_SEM + 1)
            if k >= NDMA_SEM:
                o.dprev = self.dma_ops[eng][k - NDMA_SEM]
            self.dma_ops[eng].append(o)
        o.idx = len(self.ops[eng])
        self.ops[eng].append(o)
        return o

    def emit(self, final_waits=()):
        nc = self.nc
        for e in ENGS:
            for o in self.ops[e]:
                for d in o.deps:
                    if not d.dma:
                        d.signaled = True
        for o in final_waits:
            if not o.dma:
                o.signaled = True
        for e in ENGS:
            c = 0
            for o in self.ops[e]:
                if o.signaled and not o.dma:
                    c += 1
                    o.sigval = c
        with ExitStack() as st:
            esem = {e: st.enter_context(nc.semaphore("s_" + e)) for e in ENGS}
            dsem = {e: [st.enter_context(nc.semaphore("d_%s_%d" % (e, i))) for i in range(NDMA_SEM)]
                    for e in ENGS if self.ndma[e] > 0}
            block = st.enter_context(nc.Block())

            def body(e, engine):
                observed = {}

                def wait(key, sem, val):
                    if observed.get(key, 0) >= val:
                        return
                    observed[key] = val
                    engine.wait_ge(sem, val)

                for o in self.ops[e]:
                    for d in o.deps:
                        if d.dma:
                            wait(("d", d.eng, d.dsem), dsem[d.eng][d.dsem], d.dval)
                        else:
                            wait(("e", d.eng), esem[d.eng], d.sigval)
                    if o.dma and o.dprev is not None:
                        wait(("d", e, o.dprev.dsem), dsem[e][o.dprev.dsem], o.dprev.dval)
                    ins = o.fn(engine)
                    if o.dma:
                        ins.then_inc(dsem[e][o.dsem], 16)
                    elif o.signaled:
                        ins.then_inc(esem[e], 1)
                if e == "sp":
                    for o in final_waits:
                        if o.dma:
                            wait(("d", o.eng, o.dsem), dsem[o.eng][o.dsem], o.dval)
                        else:
                            wait(("e", o.eng), esem[o.eng], o.sigval)

            block.sync(lambda eng: body("sp", eng))
            block.scalar(lambda eng: body("act", eng))
            block.vector(lambda eng: body("dve", eng))
            block.gpsimd(lambda eng: body("pool", eng))
            block.tensor(lambda eng: body("pe", eng))


def CALL(name, *args, **kw):
    return lambda e: getattr(e, name)(*args, **kw)


def bc_last(ap, n):
    return bass.AP(ap.tensor, ap.offset, [list(a) for a in ap.ap] + [[0, n]])


def bc_mid(ap, n):
    a = [list(x) for x in ap.ap]
    return bass.AP(ap.tensor, ap.offset, [a[0], [0, n]] + a[1:])


def bc_row(ap, n):
    a = [list(x) for x in ap.ap]
    return bass.AP(ap.tensor, ap.offset, [a[0], [0, n]])


def build_program(NP, NSAMP=2, SL=32):
    nc = bass.Bass("TRN2", target_bir_lowering=False)
    dr = {}

    def din(name, shape, dt=F32):
        dr[name] = nc.dram_tensor(name, list(shape), dt, kind="ExternalInput").ap()
        return dr[name]

    def dout(name, shape):
        dr[name] = nc.dram_tensor(name, list(shape), F32, kind="ExternalOutput").ap()
        return dr[name]

    xp = din("xp", [NP, D])
    xs = din("xs", [NSAMP, SL, D])
    st_s5re = din("st_s5re", [NSAMP, 32, 128])
    st_s5im = din("st_s5im", [NSAMP, 32, 128])
    st_c = din("st_c", [NSAMP, 4, 256, 256])
    st_n = din("st_n", [NSAMP, 8, 128])
    st_m = din("st_m", [NSAMP, 4, 1])
    st_conv = din("st_conv", [NSAMP, 3, D])
    vecs = din("vecs", [len(VEC_NAMES) * 8, 128])
    W = {}
    for n, shp in [("ffn1_gate", [D, DFF]), ("ffn1_up", [D, DFF]), ("ffn1_down", [DFF, D]),
                   ("ffn2_gate", [D, DFF]), ("ffn2_up", [D, DFF]), ("ffn2_down", [DFF, D]),
                   ("w_in", [D, 3 * D]), ("s5_glu_w", [D, D]), ("w_out", [2 * D, D]),
                   ("lam_re", [32, 128]), ("lam_im", [32, 128]), ("log_dt", [64]),
                   ("b_re", [64, 64, 16]), ("b_im", [64, 64, 16]), ("c_re", [64, 16, 64]), ("c_im", [64, 16, 64]),
                   ("wq", [256, 4, 4]), ("wk", [256, 4, 4]), ("wv", [256, 4, 4]),
                   ("igw", [3 * D, 4]), ("fgw", [3 * D, 4]), ("igb", [4, 1]), ("fgb", [4, 1]),
                   ("ident", [128, 128]), ("maskE", [128, 2]), ("mask16", [32, 2]), ("triu", [128, 128]),
                   ("bdmask", [128, 32]), ("tvec", [128, 64]), ("sel4", [4, 512]), ("mask4", [128, 4])]:
        W[n] = din(n, shp)
    NS = 1 + NSAMP
    yp = dout("yp", [NP - NMETA, D])
    ys = dout("ys", [NSAMP, SL, D])
    o_s5re = dout("o_s5re", [NS, 32, 128])
    o_s5im = dout("o_s5im", [NS, 32, 128])
    o_c = dout("o_c", [NS, 4, 256, 256])
    o_n = dout("o_n", [NS, 8, 128])
    o_m = dout("o_m", [NS, 4, 1])
    o_conv = dout("o_conv", [NS, 3, D])

    SG = {n: nc.dram_tensor("sg_" + n, [FC, 128, KC, 128], BF16).ap() for n in ["ffn1_gate", "ffn1_up", "ffn2_gate", "ffn2_up"]}
    SD = {n: nc.dram_tensor("sd_" + n, [KC, 2, 128, FC // 2, 128], BF16).ap() for n in ["ffn1_down", "ffn2_down"]}
    SI = nc.dram_tensor("s_win", [24, 128, KC, 128], BF16).ap()
    SGL = nc.dram_tensor("s_glu", [KC, 128, KC, 128], BF16).ap()
    SO = nc.dram_tensor("s_wout", [KC, 2, 128, 8, 128], BF16).ap()
    P = Prog(nc)
    outs = []
    TT = 512
    with ExitStack() as st:
        def sb(name, shape, dt=F32):
            return st.enter_context(nc.sbuf_tensor("sb_" + name, list(shape), dt))

        st.enter_context(nc.allow_low_precision("bf16 matmul operands with fp32 PSUM accumulation"))
        pb = [st.enter_context(nc.psum_tensor("pb%d" % i, [128, 512], F32)) for i in range(8)]

        xT = sb("xT", [128, KC, TT])
        xn = sb("xn", [128, KC, TT], BF16)
        xc_bf = xn
        hT = sb("hT", [128, 24, TT], BF16)
        u_bf = sb("u_bf", [128, KC, TT], BF16)
        z_bf = sb("z_bf", [128, KC, TT], BF16)
        xmh = sb("xmh", [128, KC, TT + 3], BF16)
        vT = hT[:, 16:24, :]
        mixed = sb("mixed", [128, 16, TT], BF16)
        ybuf = sb("ybuf", [128, KC, TT])
        xtok = ybuf[:].rearrange("p a b -> p (a b)").rearrange("p (n d) -> p n d", d=D)
        rstd = sb("rstd", [128, TT])
        tAB = sb("tAB", [128, 2 * TT])
        tA = tAB[:, 0:TT]
        tB = tAB[:, TT:2 * TT]
        otok = tAB
        wgr = [sb("wgr%d" % i, [128, KC, 128], BF16) for i in range(2)]
        wur = [sb("wur%d" % i, [128, KC, 128], BF16) for i in range(2)]
        wdr = [sb("wdr%d" % i, [128, FC // 2, 128], BF16) for i in range(2)]
        wir = [sb("wir%d" % i, [128, KC, 128], BF16) for i in range(2)]
        wor = [sb("wor%d" % i, [128, 8, 128], BF16) for i in range(2)]
        ident = sb("ident", [128, 128])
        ones_bf = sb("ones_bf", [128, 128], BF16)
        onesD = sb("onesD", [128, 128], BF16)
        ones256 = sb("ones256", [128, 128])
        ones4 = sb("ones4", [4, 128])
        onecol = sb("onecol", [128, 1])
        epscol = sb("epscol", [128, 1])
        vec = sb("vec", [128, len(VEC_NAMES) * 8])
        maskE = sb("maskE", [128, 2])
        mask16 = sb("mask16", [32, 2])
        triu = sb("triu", [128, 128])
        bdmask = sb("bdmask", [128, 32])
        tvec = sb("tvec", [128, 64])
        sel4 = sb("sel4", [4, 512])
        cosT = sb("cosT", [128, 32, 64])
        sinT = sb("sinT", [128, 32, 64])
        rmag = sb("rmag", [128, 32])
        W1 = sb("W1", [128, KC, 2, 128], BF16)
        W3 = sb("W3", [128, 32, 2, 32], BF16)
        s5st = sb("s5st", [128, 2, 32])
        sre = s5st[:, 0, :]
        sim = s5st[:, 1, :]
        s5all = sb("s5all", [128, 8, 256])
        s5w = [s5all[:, i, :].rearrange("p (a b) -> p a b", b=64) for i in range(8)]
        resetm = sb("resetm", [128, 64])
        inj = [sb("inj%d" % i, [128, 2, 2, 4]) for i in range(2)]
        xbfA = sb("xbfA", [128, 2, 4, 64], BF16)
        xbfB = sb("xbfB", [128, 2, 4, 64], BF16)
        um = sb("um", [128, 2, 4, 64], BF16)
        umB = sb("umB", [128, 2, 4, 64], BF16)
        mask4 = sb("mask4", [128, 4])
        CT = sb("CT", [128, 4, 2, 256])
        nT = sb("nT", [128, 8])
        mprev = sb("mprev", [4, 1])
        BD = sb("BD", [128, 3, KC, 128], BF16)
        gwi = sb("gwi", [128, 24, 4], BF16)
        gwf = sb("gwf", [128, 24, 4], BF16)
        igb = sb("igb", [4, 1])
        nfgb = sb("nfgb", [4, 1])
        igs = sb("igs", [4, TT])
        Fn = sb("Fn", [4, TT])
        aa = sb("aa", [4, TT])
        MM = sb("MM", [4, TT])
        g4 = [sb("g4_%d" % i, [4, 128]) for i in range(2)]
        negM = sb("negM", [4, 1])
        gcol = sb("gcol", [4, 1])
        Mpc = sb("Mpc", [4, 1])
        dg = sb("dg", [4, 4])
        et = sb("et", [128, 4])
        gb = sb("gb", [128, 4])
        ngc2 = sb("ngc2", [128, 2, 2])
        mwab = sb("mwab", [128, 128])
        ktp = sb("ktp", [128, 256], BF16)
        vtp = sb("vtp", [128, 256], BF16)
        vtb = sb("vtb", [128, 256], BF16)
        Sm = sb("Sm", [128, 128], BF16)
        Eb = sb("Eb", [128, 128], BF16)
        Cg = sb("Cg", [128, 2, 256], BF16)
        nrep = sb("nrep", [128, 2, 128], BF16)
        hh = sb("hh", [128, 2, 128])
        hq = sb("hq", [128, 2, 128])
        mw = [sb("mw%d" % i, [128, 128]) for i in range(4)]
        stg = sb("stg", [128, 256])

        SRE = ["sre%d" % c for c in range(KC)]
        SIM = ["sim%d" % c for c in range(KC)]
        dq = ["sp", "pool"]
        dqi = [0]

        def dma(out, in_, reads=(), writes=(), q=None, slow=False):
            if out.dtype != in_.dtype:
                q = "pool"
            if q is None:
                q = dq[dqi[0] % 2]
                dqi[0] += 1
            if slow:
                f = CALL("dma_start", out=out, in_=in_, allow_slow_non_contiguous=True)
            else:
                f = CALL("dma_start", out=out, in_=in_)
            return P.op(q, f, reads=reads, writes=writes, dma=True)

        def mm(out, lhsT, rhs, start, stop, reads, writes, tp=None):
            if tp is None:
                f = CALL("matmul", out, lhsT, rhs, start=start, stop=stop)
            else:
                f = CALL("matmul", out, lhsT, rhs, start=start, stop=stop, tile_position=tp)
            return P.op("pe", f, reads=reads, writes=writes)

        def tr(out, in_, idn, reads, writes):
            return P.op("pe", CALL("transpose", out, in_, idn), reads=list(reads) + ["ident"], writes=writes)

        def E(eng, fn, reads, writes):
            return P.op(eng, fn, reads=reads, writes=writes)

        def V(name, c):
            i = VI[name] * 8 + c
            return vec[:, i:i + 1]

        for name, t in [("ident", ident), ("maskE", maskE), ("mask16", mask16), ("triu", triu), ("bdmask", bdmask),
                        ("tvec", tvec), ("sel4", sel4), ("igb", igb), ("mask4", mask4)]:
            dma(t[:], W[name], writes=[name])
        E("dve", CALL("memset", ones_bf[:], 1.0), [], ["ones_bf"])
        E("dve", CALL("memset", onesD[:], 1.0 / D), [], ["onesD"])
        E("dve", CALL("memset", ones256[:], 1.0 / 256), [], ["ones256"])
        E("dve", CALL("memset", ones4[:], 1.0), [], ["ones4"])
        E("dve", CALL("memset", onecol[:], 1.0), [], ["onecol"])
        E("dve", CALL("memset", resetm[:], 1.0), [], ["resetm"])
        E("dve", CALL("memset", resetm[:, 0:1], 0.0), ["resetm"], ["resetm"])
        E("dve", CALL("memset", epscol[:], EPS), [], ["epscol"])
        dma(nfgb[:], W["fgb"], writes=["nfgb"])
        E("dve", CALL("tensor_scalar", nfgb[:], nfgb[:], -1.0, 0.0, ALU.mult, ALU.add), ["nfgb"], ["nfgb"])
        dma(stg[0:len(VEC_NAMES) * 8, 0:128], vecs, writes=["stg"])
        nv = len(VEC_NAMES) * 8
        tr(pb[7][:, 0:nv], stg[0:nv, 0:128], ident[0:nv, 0:nv], ["stg"], ["p7"])
        E("dve", CALL("tensor_copy", vec[:], pb[7][:, 0:nv]), ["p7"], ["vec"])
        dma(gwi[:], W["igw"].rearrange("(c p) h -> p c h", p=128), writes=["gwi"], slow=True)
        dma(gwf[:], W["fgw"].rearrange("(c p) h -> p c h", p=128), writes=["gwf"], slow=True)
        for wi, wn in enumerate(["wq", "wk", "wv"]):
            dma(stg[:, 0:32].rearrange("p (c o) -> p c o", o=4), W[wn].rearrange("(c b) i o -> (b i) c o", b=32),
                reads=[], writes=["stg"], slow=True)
            for c in range(KC):
                E("dve", CALL("tensor_tensor",
                    BD[:, wi, c, :].rearrange("p (b o) -> p b o", o=4),
                    bc_mid(stg[:, c * 4:c * 4 + 4], 32), bc_last(bdmask[:, :], 4), ALU.mult),
                  ["stg", "bdmask"], ["BD"])
        lamr = s5w[0][:, 0, 0:32]
        lami = s5w[0][:, 1, 0:32]
        dtb = s5w[0][:, 2, 0:32]
        th = s5w[1][:, 0, 0:32]
        cth = s5w[1][:, 1, 0:32]
        sth = s5w[1][:, 2, 0:32]
        lbr = s5w[2][:, 0, 0:32]
        lbi = s5w[2][:, 1, 0:32]
        kr = s5w[2][:, 2, 0:32]
        ki_ = s5w[2][:, 3, 0:32]
        t1 = s5w[3][:, 0, 0:32]
        t2 = s5w[3][:, 1, 0:32]
        t3 = s5w[3][:, 2, 0:32]
        for nm, dst in [("lam_re", lamr), ("lam_im", lami)]:
            dma(stg[0:32, 0:128], W[nm], writes=["stg"])
            tr(pb[7][:, 0:32], stg[0:32, 0:128], ident[0:32, 0:32], ["stg"], ["p7"])
            E("dve", CALL("tensor_copy", dst, pb[7][:, 0:32]), ["p7"], ["s5p"])
        ldt = W["log_dt"]
        for g2 in range(2):
            src = bass.AP(ldt.tensor, ldt.offset + g2, [[0, 64], [2, 32]])
            dma(s5w[0][64 * g2:64 * g2 + 64, 2, 0:32], src, writes=["s5p"], slow=True)
        E("act", CALL("activation", dtb, dtb, AF.Exp), ["s5p"], ["s5p"])
        E("dve", CALL("tensor_tensor", t1, lamr, dtb, ALU.mult), ["s5p"], ["s5p"])
        E("act", CALL("activation", rmag[:], t1, AF.Exp), ["s5p"], ["rmag"])
        E("dve", CALL("tensor_tensor", th, lami, dtb, ALU.mult), ["s5p"], ["s5p"])

        ki32 = sb("ki32", [128, 1, 64], I32)

        def sincos(dst, ang, n, shift, key_r, key_w):
            wk = s5w[6][:].rearrange("p a b -> p (a b)")[:, 0:n]
            wf = s5w[7][:].rearrange("p a b -> p (a b)")[:, 0:n]
            wi_ = ki32[:].rearrange("p a b -> p (a b)")[:, 0:n]
            E("dve", CALL("tensor_scalar", wk, ang, shift, 1.0 / (2 * math.pi), ALU.add, ALU.mult), key_r, ["s5t"])
            E("dve", CALL("tensor_copy", wi_, wk), ["s5t"], ["s5t"])
            E("dve", CALL("tensor_copy", wf, wi_), ["s5t"], ["s5t"])
            E("dve", CALL("tensor_scalar", wk, ang, shift, 0.0, ALU.add, ALU.add), key_r + ["s5t"], ["s5t"])
            E("dve", CALL("scalar_tensor_tensor", wk, wf, -2 * math.pi, wk, ALU.mult, ALU.add), ["s5t"], ["s5t"])
            E("dve", CALL("tensor_scalar", wf, wk, math.pi, -2 * math.pi, ALU.is_gt, ALU.mult), ["s5t"], ["s5t"])
            E("dve", CALL("tensor_tensor", wk, wk, wf, ALU.add), ["s5t"], ["s5t"])
            E("dve", CALL("tensor_scalar", wf, wk, -math.pi, 2 * math.pi, ALU.is_lt, ALU.mult), ["s5t"], ["s5t"])
            E("dve", CALL("tensor_tensor", wk, wk, wf, ALU.add), ["s5t"], ["s5t"])
            E("act", CALL("activation", dst, wk, AF.Sin), ["s5t"], key_w)

        sincos(sth, th, 32, 0.0, ["s5p"], ["s5p"])
        sincos(cth, th, 32, math.pi / 2, ["s5p"], ["s5p"])
        E("dve", CALL("tensor_tensor", lbr, rmag[:], cth, ALU.mult), ["s5p", "rmag"], ["s5p"])
        E("dve", CALL("tensor_tensor", lbi, rmag[:], sth, ALU.mult), ["s5p", "rmag"], ["s5p"])
        E("dve", CALL("tensor_scalar", t1, lbr, -1.0, 0.0, ALU.add, ALU.add), ["s5p"], ["s5p"])
        E("dve", CALL("tensor_tensor", t2, lamr, lamr, ALU.mult), ["s5p"], ["s5p"])
        E("dve", CALL("tensor_tensor", t3, lami, lami, ALU.mult), ["s5p"], ["s5p"])
        E("dve", CALL("tensor_tensor", t2, t2, t3, ALU.add), ["s5p"], ["s5p"])
        E("dve", CALL("reciprocal", t2, t2), ["s5p"], ["s5p"])
        E("dve", CALL("tensor_tensor", kr, t1, lamr, ALU.mult), ["s5p"], ["s5p"])
        E("dve", CALL("tensor_tensor", t3, lbi, lami, ALU.mult), ["s5p"], ["s5p"])
        E("dve", CALL("tensor_tensor", kr, kr, t3, ALU.add), ["s5p"], ["s5p"])
        E("dve", CALL("tensor_tensor", kr, kr, t2, ALU.mult), ["s5p"], ["s5p"])
        E("dve", CALL("tensor_tensor", ki_, lbi, lamr, ALU.mult), ["s5p"], ["s5p"])
        E("dve", CALL("tensor_tensor", t3, t1, lami, ALU.mult), ["s5p"], ["s5p"])
        E("dve", CALL("tensor_tensor", ki_, ki_, t3, ALU.subtract), ["s5p"], ["s5p"])
        E("dve", CALL("tensor_tensor", ki_, ki_, t2, ALU.mult), ["s5p"], ["s5p"])
        for q in range(32):
            ang = s5w[5][:, 0, :]
            E("dve", CALL("tensor_scalar", ang, tvec[:, :], th[:, q:q + 1], 0.0, ALU.mult, ALU.add),
              ["s5p", "tvec"], ["s5ang"])
            sincos(sinT[:, q, :], ang, 64, 0.0, ["s5ang"], ["sinT"])
            sincos(cosT[:, q, :], ang, 64, math.pi / 2, ["s5ang"], ["cosT"])
        Bre = CT[:].rearrange("p a b c -> p (a b c)")[:, 0:512].rearrange("p (q j) -> p q j", j=16)
        Bim = CT[:].rearrange("p a b c -> p (a b c)")[:, 512:1024].rearrange("p (q j) -> p q j", j=16)
        Ere = CT[:].rearrange("p a b c -> p (a b c)")[:, 1024:1536].rearrange("p (q j) -> p q j", j=16)
        Eim = CT[:].rearrange("p a b c -> p (a b c)")[:, 1536:2048].rearrange("p (q j) -> p q j", j=16)
        for g2 in range(2):
            dma(Bre[64 * g2:64 * g2 + 64], W["b_re"].rearrange("(q g) p j -> g p q j", g=2)[g2], writes=["CT0", "CT1", "CT2", "CT3"], slow=True)
            dma(Bim[64 * g2:64 * g2 + 64], W["b_im"].rearrange("(q g) p j -> g p q j", g=2)[g2], writes=["CT0", "CT1", "CT2", "CT3"], slow=True)
        kmr = s5w[4][:, 0, 0:32]
        kmi = s5w[4][:, 1, 0:32]
        Eexp_re = ybuf[:].rearrange("p a b -> p (a b)")[:, 0:1024].rearrange("p (q g j) -> p q g j", g=2, j=16)
        Eexp_im = ybuf[:].rearrange("p a b -> p (a b)")[:, 1024:2048].rearrange("p (q g j) -> p q g j", g=2, j=16)
        for g2 in range(2):
            E("dve", CALL("tensor_scalar", kmr, kr, maskE[:, g2:g2 + 1], 0.0, ALU.mult, ALU.add), ["s5p", "maskE"], ["s5k"])
            E("dve", CALL("tensor_scalar", kmi, ki_, maskE[:, g2:g2 + 1], 0.0, ALU.mult, ALU.add), ["s5p", "maskE"], ["s5k"])
            E("dve", CALL("tensor_tensor", Ere, Bre, bc_last(kmr, 16), ALU.mult), ["CT0", "CT1", "CT2", "CT3"] + ["s5k"], ["CT0", "CT1", "CT2", "CT3"])
            E("dve", CALL("tensor_tensor", Eim, Bim, bc_last(kmi, 16), ALU.mult), ["CT0", "CT1", "CT2", "CT3"] + ["s5k"], ["CT0", "CT1", "CT2", "CT3"])
            E("dve", CALL("tensor_tensor", Eexp_re[:, :, g2, :], Ere, Eim, ALU.subtract), ["CT0", "CT1", "CT2", "CT3"], ["ybuf"])
            E("dve", CALL("tensor_tensor", Ere, Bim, bc_last(kmr, 16), ALU.mult), ["CT0", "CT1", "CT2", "CT3"] + ["s5k"], ["CT0", "CT1", "CT2", "CT3"])
            E("dve", CALL("tensor_tensor", Eim, Bre, bc_last(kmi, 16), ALU.mult), ["CT0", "CT1", "CT2", "CT3"] + ["s5k"], ["CT0", "CT1", "CT2", "CT3"])
            E("dve", CALL("tensor_tensor", Eexp_im[:, :, g2, :], Ere, Eim, ALU.add), ["CT0", "CT1", "CT2", "CT3"], ["ybuf"])
        for c in range(KC):
            for ri, Ex in enumerate([Eexp_re, Eexp_im]):
                src = Ex[:, 4 * c:4 * c + 4, :, :].rearrange("p q g j -> p (q g j)")
                tr(pb[7][:, 0:128], src, ident[:], ["ybuf"], ["p7"])
                E("act", CALL("copy", W1[:, c, ri, :], pb[7][:, 0:128]), ["p7"], ["W1"])
        Cst = CT[:].rearrange("p a b c -> p (a b c)")[0:32, 0:2048].rearrange("p (q k) -> p q k", k=64)
        Cx = xT[:].rearrange("p a b -> p (a b)")[0:32, 0:4096].rearrange("p (q g k) -> p q g k", g=2, k=64)
        for ri, (cn, sgn) in enumerate([("c_re", 1.0), ("c_im", -1.0)]):
            for g2 in range(2):
                dma(Cst[16 * g2:16 * g2 + 16], W[cn].rearrange("(q g) h p -> g h q p", g=2)[g2], writes=["CT0", "CT1", "CT2", "CT3"], slow=True)
            for g2 in range(2):
                E("dve", CALL("tensor_scalar", Cx[:, :, g2, :], Cst, mask16[:, g2:g2 + 1], sgn, ALU.mult, ALU.mult),
                  ["CT0", "CT1", "CT2", "CT3"] + ["mask16"], ["xT"])
            for q in range(32):
                tr(pb[6][:, (q % 16) * 32:(q % 16) * 32 + 32], Cx[:, q, :, :].rearrange("p g k -> p (g k)"), ident[0:32, 0:32], ["xT"], ["p6"])
                if q % 16 == 15:
                    q0 = q - 15
                    E("act", CALL("copy", W3[:, q0:q0 + 16, ri, :], pb[6][:].rearrange("p (q h) -> p q h", h=32)), ["p6"], ["W3"])

        hTf = hT[:].rearrange("p a b -> p (a b)")
        pcs = [(hTf[:, 0:4096], ["h%d" % i for i in range(8)]), (hTf[:, 4096:8192], ["h%d" % i for i in range(8, 16)])]
        pci = [0]
        WSCR = ["wscr%d" % i for i in range(12)]

        def precast(src_rows, ncols, dst_ap, pattern, **kw):
            stg_, names = pcs[pci[0] % 2]
            pci[0] += 1
            dma(stg_[:, 0:ncols], src_rows, writes=names, q="pool")
            dma(dst_ap.rearrange(pattern), stg_[:, 0:ncols].rearrange("p (a k) -> p a k", k=128), reads=names, writes=["wscr%d" % ((pci[0] - 1) % 12)], q="sp", slow=True)

        for n in ["ffn1_gate", "ffn1_up", "ffn2_gate", "ffn2_up"]:
            for c in range(KC):
                precast(W[n][c * 128:(c + 1) * 128, :], DFF, SG[n][:, :, c, :], "f p k -> p f k")
        for n in ["ffn1_down", "ffn2_down"]:
            for f in range(FC):
                precast(W[n][f * 128:(f + 1) * 128, :], D, SD[n][:, f // 11, :, f % 11, :], "c p k -> p c k")
        for c in range(KC):
            precast(W["w_in"][c * 128:(c + 1) * 128, :], 3 * D, SI[:, :, c, :], "f p k -> p f k")
        for c in range(KC):
            precast(W["s5_glu_w"][c * 128:(c + 1) * 128, :], D, SGL[:, :, c, :], "f p k -> p f k")
        for k in range(16):
            precast(W["w_out"][k * 128:(k + 1) * 128, :], D, SO[:, k // 8, :, k % 8, :], "c p k -> p c k")

        wcnt = {"g": 0, "u": 0, "d": 0, "i": 0, "o": 0}

        def rmsnorm_stats(src_chunks, nch, Tt, key_r, scale_mat):
            for c in range(nch):
                if c % 2 == 0:
                    E("act", CALL("activation", hT[:, 14 + c, 0:Tt], src_chunks(c), AF.Square), key_r, ["h%d" % (14 + c)])
                else:
                    E("dve", CALL("tensor_tensor", hT[:, 14 + c, 0:Tt], src_chunks(c), src_chunks(c), ALU.mult), key_r, ["h%d" % (14 + c)])
            for c in range(nch):
                mm(pb[6][:, 0:Tt], scale_mat[:], hT[:, 14 + c, 0:Tt], c == 0, c == nch - 1, ["onesD", "h%d" % (14 + c)], ["p6"])
            E("act", CALL("activation", rstd[:, 0:Tt], pb[6][:, 0:Tt], AF.Ln, bias=epscol[:, 0:1]), ["p6", "epscol"], ["rstd"])
            E("act", CALL("activation", rstd[:, 0:Tt], rstd[:, 0:Tt], AF.Exp, scale=-0.5), ["rstd"], ["rstd"])

        def norm_x(gname, Tt):
            rmsnorm_stats(lambda c: xT[:, c, 0:Tt], KC, Tt, ["xT"], onesD)
            for c in range(KC):
                E("dve", CALL("scalar_tensor_tensor", xn[:, c, 0:Tt], xT[:, c, 0:Tt], V(gname, c), rstd[:, 0:Tt], ALU.mult, ALU.mult),
                  ["xT", "vec", "rstd"], ["xn%d" % c])

        def ffn(pref, gname, Tt):
            norm_x(gname, Tt)
            wg, wu, wd = W[pref + "_gate"], W[pref + "_up"], W[pref + "_down"]
            for f in range(FC):
                gi = wcnt["g"] % 2
                wcnt["g"] += 1
                dma(wgr[gi][:], SG[pref + "_gate"][f], reads=WSCR, writes=["wg%d" % gi], q="sp")
                dma(wur[gi][:], SG[pref + "_up"][f], reads=WSCR, writes=["wu%d" % gi], q="sp")
                pg, pu = pb[f % 2], pb[2 + f % 2]
                for c in range(KC):
                    mm(pg[:, 0:Tt], wgr[gi][:, c, :], xn[:, c, 0:Tt], c == 0, c == KC - 1, ["wg%d" % gi, "xn%d" % c], ["p%d" % (f % 2)])
                for c in range(KC):
                    mm(pu[:, 0:Tt], wur[gi][:, c, :], xn[:, c, 0:Tt], c == 0, c == KC - 1, ["wu%d" % gi, "xn%d" % c], ["p%d" % (2 + f % 2)])
                tt = tA if f % 2 == 0 else tB
                tn = "tA" if f % 2 == 0 else "tB"
                E("act", CALL("activation", tt[:, 0:Tt], pg[:, 0:Tt], AF.Silu), ["p%d" % (f % 2)], [tn])
                E("dve", CALL("tensor_tensor", hT[:, f, 0:Tt], tt[:, 0:Tt], pu[:, 0:Tt], ALU.mult),
                  [tn, "p%d" % (2 + f % 2)], ["h%d" % f])
            for c in range(KC):
                pd = pb[4 + c % 2]
                for hf in range(2):
                    di = hf
                    dma(wdr[di][:], SD[pref + "_down"][c, hf], reads=WSCR, writes=["wd%d" % di], q="sp")
                    for f2 in range(FC // 2):
                        f = hf * (FC // 2) + f2
                        mm(pd[:, 0:Tt], wdr[di][:, f2, :], hT[:, f, 0:Tt], f == 0, f == FC - 1, ["wd%d" % di, "h%d" % f], ["p%d" % (4 + c % 2)])
                E("dve", CALL("scalar_tensor_tensor", xT[:, c, 0:Tt], pd[:, 0:Tt], 0.5, xT[:, c, 0:Tt], ALU.mult, ALU.add),
                  ["p%d" % (4 + c % 2), "xT"], ["xT"])

        def in_proj(Tt):
            norm_x("norm_mix", Tt)
            for oc in range(24):
                ii = wcnt["i"] % 2
                wcnt["i"] += 1
                dma(wir[ii][:], SI[oc], reads=WSCR, writes=["wi%d" % ii], q="sp")
                po = pb[4 + oc % 2]
                for c in range(KC):
                    mm(po[:, 0:Tt], wir[ii][:, c, :], xn[:, c, 0:Tt], c == 0, c == KC - 1, ["wi%d" % ii, "xn%d" % c], ["p%d" % (4 + oc % 2)])
                if oc < 8:
                    dst, key = u_bf[:, oc, 0:Tt], "u_bf"
                elif oc < 16:
                    dst, key = xmh[:, oc - 8, 3:3 + Tt], "xmh"
                else:
                    dst, key = z_bf[:, oc - 16, 0:Tt], "z_bf"
                if oc % 2 == 0:
                    E("act", CALL("copy", dst, po[:, 0:Tt]), ["p%d" % (4 + oc % 2)], [key])
                else:
                    E("dve", CALL("tensor_copy", dst, po[:, 0:Tt]), ["p%d" % (4 + oc % 2)], [key])

        def s5_mix(Tt):
            gT = hT
            L = min(64, Tt)
            nun = Tt // L
            def s5_pre(c, un, S):
                t0 = un * L
                X = S["extra"]
                pS, psk = S["pS"][un % 2], S["psk"][un % 2]
                umS = S["um"][:, un % 2]
                kum = "s5%sum%d" % (S["k"], un % 2)
                pSv = pS[:].rearrange("p (q r t) -> p q r t", q=4, r=2)
                for qq in range(4):
                    E("act", CALL("activation", umS[:, qq, 0:L], u_bf[:, c, t0:t0 + L], AF.Copy, scale=mask4[:, qq:qq + 1]), ["u_bf", "mask4"] + X, [kum])
                for qq in range(4):
                    for ri in range(2):
                        mm(pSv[:, qq, ri, 0:L], W1[:, c, ri, :], umS[:, qq, 0:L], True, True, ["W1", kum] + X, [psk])

            def s5_unit(c, un, S):
                pY = pb[2 + c % 2]
                pyk = "p%d" % (2 + c % 2)
                t0 = un * L
                X = S["extra"]
                K = lambda i: "s5%s%d" % (S["k"], i)
                pS, psk = S["pS"][un % 2], S["psk"][un % 2]
                xbf = S["xbf"]
                kxb = K(11)
                injC, kinjC = S["inj"][:, un % 2], "s5%sinj%d" % (S["k"], un % 2)
                injN, kinjN = S["inj"][:, (un + 1) % 2], "s5%sinj%d" % (S["k"], (un + 1) % 2)
                sk = "sre%d" % c
                pSv = pS[:].rearrange("p (q r t) -> p q r t", q=4, r=2)
                if un == 0:
                    s5_pre(c, 0, S)
                    E("pool", CALL("tensor_tensor", injC, s5st[:, :, 4 * c:4 * c + 4], bc_mid(rmag[:, 4 * c:4 * c + 4], 2), ALU.mult), ["rmag", sk] + X, [kinjC])
                    yield
                if un + 1 < nun and L == 64:
                    s5_pre(c, un + 1, S)
                    yield
                bre = pSv[:, :, 0, 0:L]
                bim = pSv[:, :, 1, 0:L]
                cs = cosT[:, 4 * c:4 * c + 4, 0:L]
                sn = sinT[:, 4 * c:4 * c + 4, 0:L]
                B = S["bufs"]
                bv = lambda i: B[:, i * 256:(i + 1) * 256].rearrange("p (q t) -> p q t", t=64)[:, :, 0:L]
                E("dve", CALL("tensor_tensor", bv(2), bre, cs, ALU.mult), [psk, "cosT"] + X, [K(2)])
                E("dve", CALL("tensor_tensor", bv(3), bim, sn, ALU.mult), [psk, "sinT"] + X, [K(3)])
                E("dve", CALL("tensor_tensor", bv(6), bim, cs, ALU.mult), [psk, "cosT"] + X, [K(6)])
                E("dve", CALL("tensor_tensor", bv(7), bre, sn, ALU.mult), [psk, "sinT"] + X, [K(7)])
                E("dve", CALL("tensor_tensor", bv(0), bv(2), bv(3), ALU.add), [K(2), K(3)] + X, [K(0)])
                E("dve", CALL("tensor_tensor", bv(1), bv(6), bv(7), ALU.subtract), [K(6), K(7)] + X, [K(1)])
                if L == 64:
                    W2 = B[:, 0:512].rearrange("p (r q t) -> p r q t", r=2, t=64)
                    E("dve", CALL("tensor_tensor", W2[:, :, :, 0], W2[:, :, :, 0], injC, ALU.add), [K(0), K(1), kinjC] + X, [K(0), K(1)])
                    rt = S["rtab"][:, 4 * c:4 * c + 4, :].rearrange("p q t -> p (q t)")
                    E("dve", CALL("tensor_tensor_scan", B[:, 1024:1280], rt, B[:, 0:256], 0.0, ALU.mult, ALU.add), [K(0), "rtab"] + X, [K(4)])
                    E("dve", CALL("tensor_tensor_scan", B[:, 1280:1536], rt, B[:, 256:512], 0.0, ALU.mult, ALU.add), [K(1), "rtab"] + X, [K(5)])
                    yield
                else:
                    for qq in range(4):
                        q = 4 * c + qq
                        E("dve", CALL("tensor_tensor_scan", bv(4)[:, qq, :], bc_row(rmag[:, q:q + 1], L), bv(0)[:, qq, :],
                                      sre[:, q:q + 1], ALU.mult, ALU.add), ["rmag", K(0), sk] + X, [K(4)])
                        E("dve", CALL("tensor_tensor_scan", bv(5)[:, qq, :], bc_row(rmag[:, q:q + 1], L), bv(1)[:, qq, :],
                                      sim[:, q:q + 1], ALU.mult, ALU.add), ["rmag", K(1), sk] + X, [K(5)])
                    yield
                zr, zi = bv(4), bv(5)
                E("pool", CALL("tensor_tensor", bv(2), zr, cs, ALU.mult), [K(4), "cosT"] + X, [K(2)])
                E("pool", CALL("tensor_tensor", bv(3), zi, sn, ALU.mult), [K(5), "sinT"] + X, [K(3)])
                E("pool", CALL("tensor_tensor", bv(6), bv(2), bv(3), ALU.subtract), [K(2), K(3)] + X, [K(6)])
                E("pool", CALL("tensor_tensor", bv(2), zi, cs, ALU.mult), [K(5), "cosT"] + X, [K(2)])
                E("pool", CALL("tensor_tensor", bv(3), zr, sn, ALU.mult), [K(4), "sinT"] + X, [K(3)])
                E("pool", CALL("tensor_tensor", bv(7), bv(2), bv(3), ALU.add), [K(2), K(3)] + X, [K(7)])
                X4 = B[:, 1536:2048].rearrange("p (r q t) -> p r q t", r=2, t=64)
                if L == 64 and un + 1 < nun:
                    E("pool", CALL("tensor_tensor", injN, X4[:, :, :, L - 1], bc_mid(rmag[:, 4 * c:4 * c + 4], 2), ALU.mult), ["rmag", K(6), K(7)] + X, [kinjN])
                yield
                E("act", CALL("copy", xbf[:, :, :, 0:L], X4[:, :, :, 0:L]), [K(6), K(7)] + X, [kxb])
                if L != 64 or un + 1 == nun:
                    E("act", CALL("copy", s5st[:, :, 4 * c:4 * c + 4], X4[:, :, :, L - 1]), [K(6), K(7)] + X, [sk])
                yield
                for qq in range(4):
                    q = 4 * c + qq
                    mm(pY[32 * qq:32 * qq + 32, t0:t0 + L], W3[:, q, 0, :], xbf[:, 0, qq, 0:L], True, False, ["W3", kxb] + X, [pyk], tp=(0, 32 * qq))
                    mm(pY[32 * qq:32 * qq + 32, t0:t0 + L], W3[:, q, 1, :], xbf[:, 1, qq, 0:L], False, True, ["W3", kxb] + X, [pyk], tp=(0, 32 * qq))
                yield

            def s5_post(c):
                pY = pb[2 + c % 2]
                pyk = "p%d" % (2 + c % 2)
                E("dve", CALL("scalar_tensor_tensor", tA[:, 0:Tt], u_bf[:, c, 0:Tt], V("s5_d", c), pY[:, 0:Tt], ALU.mult, ALU.add),
                  ["u_bf", "vec", pyk], ["tA"])
                E("act", CALL("activation", tB[:, 0:Tt], tA[:, 0:Tt], AF.Square), ["tA"], ["tB"])
                E("dve", CALL("tensor_scalar", tB[:, 0:Tt], tB[:, 0:Tt], 0.044715, 1.0, ALU.mult, ALU.add), ["tB"], ["tB"])
                E("pool", CALL("tensor_tensor", tB[:, 0:Tt], tB[:, 0:Tt], tA[:, 0:Tt], ALU.mult), ["tA", "tB"], ["tB"])
                E("act", CALL("activation", tB[:, 0:Tt], tB[:, 0:Tt], AF.Sigmoid, scale=1.5957691216057308), ["tB"], ["tB"])
                E("dve", CALL("tensor_tensor", gT[:, c, 0:Tt], tA[:, 0:Tt], tB[:, 0:Tt], ALU.mult), ["tA", "tB"], ["h%d" % c])

            ybf_ = ybuf[:].rearrange("p a b -> p (a b)")
            rtab = ybf_[:, 2048:4096].rearrange("p (q t) -> p q t", t=64)
            E("dve", CALL("tensor_tensor", rtab, bc_last(rmag[:, :], 64), bc_mid(resetm[:, :], 32), ALU.mult), ["rmag", "resetm", "ybuf"], ["rtab"])
            SA = dict(bufs=s5all[:].rearrange("p a b -> p (a b)"), um=um, xbf=xbfA, inj=inj[0], k="A", extra=[], pS=[pb[0], pb[1]], psk=["p0", "p1"], rtab=rtab)
            SB = dict(bufs=ybf_[:, 0:2048], um=umB, xbf=xbfB, inj=inj[1], k="B", extra=["ybuf"], pS=[pb[4], pb[5]], psk=["p4", "p5"], rtab=rtab)
            for cp in range(KC // 2):
                c0, c1 = 2 * cp, 2 * cp + 1
                for un in range(nun):
                    gens = [s5_unit(c0, un, SA), s5_unit(c1, un, SB)]
                    while gens:
                        for g_ in list(gens):
                            try:
                                next(g_)
                            except StopIteration:
                                gens.remove(g_)
                s5_post(c0)
                s5_post(c1)
            for oc in range(KC):
                ii = wcnt["i"] % 2
                wcnt["i"] += 1
                dma(wir[ii][:], SGL[oc], reads=WSCR, writes=["wi%d" % ii], q="sp")
                po = pb[4 + oc % 2]
                pk = "p%d" % (4 + oc % 2)
                for c in range(KC):
                    mm(po[:, 0:Tt], wir[ii][:, c, :], gT[:, c, 0:Tt], c == 0, c == KC - 1, ["wi%d" % ii, "h%d" % c], [pk])
                E("act", CALL("activation", tA[:, 0:Tt], po[:, 0:Tt], AF.Sigmoid, bias=V("s5_glu_b", oc)), [pk, "vec"], ["tA"])
                E("dve", CALL("tensor_tensor", ybuf[:, oc, 0:Tt], gT[:, oc, 0:Tt], tA[:, 0:Tt], ALU.mult), ["tA", "h%d" % oc], ["ybuf"])
            rmsnorm_stats(lambda c: ybuf[:, c, 0:Tt], KC, Tt, ["ybuf"], onesD)
            for c in range(KC):
                E("dve", CALL("scalar_tensor_tensor", mixed[:, c, 0:Tt], ybuf[:, c, 0:Tt], V("out_norm_s5", c), rstd[:, 0:Tt], ALU.mult, ALU.mult),
                  ["ybuf", "vec", "rstd"], ["mixed"])

        def mlstm_mix(Tt):
            qT = hT
            for c in range(KC):
                eng = "dve"
                E(eng, CALL("tensor_scalar", tA[:, 0:Tt], xmh[:, c, 0:Tt], V("cw0", c), 0.0, ALU.mult, ALU.add), ["xmh", "vec"], ["tA"])
                for j in range(1, 4):
                    E(eng, CALL("scalar_tensor_tensor", tA[:, 0:Tt], xmh[:, c, j:j + Tt], V("cw%d" % j, c), tA[:, 0:Tt], ALU.mult, ALU.add),
                      ["xmh", "vec", "tA"], ["tA"])
                E("act", CALL("activation", xc_bf[:, c, 0:Tt], tA[:, 0:Tt], AF.Silu, bias=V("ml_conv_b", c)), ["tA", "vec"], ["xn%d" % c])
            for c in range(KC):
                E("act", CALL("activation", z_bf[:, c, 0:Tt], z_bf[:, c, 0:Tt], AF.Silu), ["z_bf"], ["z_bf"])
            for c in range(KC):
                for wi, (src, dstT, key) in enumerate([(xc_bf[:, c, 0:Tt], qT[:, c, 0:Tt], "h%d" % c),
                                                       (xc_bf[:, c, 0:Tt], qT[:, 8 + c, 0:Tt], "h%d" % (8 + c)),
                                                       (xmh[:, c, 3:3 + Tt], vT[:, c, 0:Tt], "h%d" % (16 + c))]):
                    po = pb[4 + (3 * c + wi) % 2]
                    pk = "p%d" % (4 + (3 * c + wi) % 2)
                    mm(po[:, 0:Tt], BD[:, wi, c, :], src, True, True, ["BD", "xn%d" % c, "xmh"], [pk])
                    if wi == 1:
                        E("dve", CALL("tensor_copy", dstT, po[:, 0:Tt]), [pk], [key])
                    else:
                        E("act", CALL("copy", dstT, po[:, 0:Tt]), [pk], [key])
            for gi, (gw, pbk) in enumerate([(gwi, 4), (gwf, 5)]):
                for j in range(24):
                    src = qT[:, j, 0:Tt]
                    key = "h%d" % j
                    mm(pb[pbk][0:4, 0:Tt], gw[:, j, :], src, j == 0, j == 23, ["gwi", "gwf", key], ["p%d" % pbk])
            E("act", CALL("activation", igs[:, 0:Tt], pb[4][0:4, 0:Tt], AF.Identity, bias=igb[:, 0:1]), ["p4", "igb"], ["igs"])
            E("act", CALL("activation", MM[:, 0:Tt], pb[5][0:4, 0:Tt], AF.Exp, bias=nfgb[:, 0:1], scale=-1.0), ["p5", "nfgb"], ["MM"])
            E("act", CALL("activation", MM[:, 0:Tt], MM[:, 0:Tt], AF.Ln, bias=onecol[0:4, 0:1]), ["MM", "onecol"], ["MM"])
            E("dve", CALL("tensor_tensor_scan", Fn[:, 0:Tt], bc_row(onecol[0:4, 0:1], Tt), MM[:, 0:Tt], 0.0, ALU.mult, ALU.add), ["MM", "onecol"], ["Fn"])
            E("dve", CALL("tensor_tensor", aa[:, 0:Tt], igs[:, 0:Tt], Fn[:, 0:Tt], ALU.add), ["igs", "Fn"], ["aa"])
            E("dve", CALL("tensor_tensor_scan", MM[:, 0:Tt], bc_row(onecol[0:4, 0:1], Tt), aa[:, 0:Tt], mprev[:, 0:1], ALU.mult, ALU.max),
              ["aa", "onecol", "mprev"], ["MM"])
            E("dve", CALL("tensor_copy", Mpc[:], mprev[:]), ["mprev"], ["Mpc"])
            Lc = min(128, Tt)
            for ch in range(Tt // Lc):
                t0, t1 = ch * Lc, ch * Lc + Lc
                E("dve", CALL("tensor_scalar", negM[:], MM[:, t1 - 1:t1], -1.0, 0.0, ALU.mult, ALU.add), ["MM"], ["negM"])
                E("act", CALL("activation", g4[0][:, 0:Lc], aa[:, t0:t1], AF.Exp, bias=negM[:, 0:1]), ["aa", "negM"], ["g40"])
                E("act", CALL("activation", gcol[:], Mpc[:], AF.Exp, bias=negM[:, 0:1]), ["Mpc", "negM"], ["gcol"])
                E("dve", CALL("tensor_scalar", g4[1][:, 0:Lc], Fn[:, t0:t1], negM[:, 0:1], 0.0, ALU.add, ALU.add), ["Fn", "negM"], ["g41"])
                E("dve", CALL("tensor_copy", Mpc[:], MM[:, t1 - 1:t1]), ["MM", "gcol"], ["Mpc"])
                tr(pb[7][0:Lc, 128:132], g4[0][:, 0:Lc], ident[0:4, 0:4], ["g40"], ["p7"])
                E("dve", CALL("tensor_copy", et[0:Lc, :], pb[7][0:Lc, 128:132]), ["p7"], ["et"])
                E("dve", CALL("tensor_scalar", dg[:], ident[0:4, 0:4], gcol[:, 0:1], 0.0, ALU.mult, ALU.add), ["ident", "gcol"], ["dg"])
                mm(pb[7][:, 132:136], ones4[:], dg[:], True, True, ["ones4", "dg"], ["p7"])
                E("dve", CALL("tensor_copy", gb[:], pb[7][:, 132:136]), ["p7"], ["gb"])
                for c in range(KC):
                    mm(pb[c // 4][0:Lc, (c % 4) * 128:(c % 4) * 128 + 128], xc_bf[:, c, t0:t1], BD[:, 1, c, :], True, True, ["xn%d" % c, "BD"], ["p%d" % (c // 4)])
                    mm(pb[2 + c // 4][0:Lc, (c % 4) * 128:(c % 4) * 128 + 128], xmh[:, c, 3 + t0:3 + t1], BD[:, 2, c, :], True, True, ["xmh", "BD"], ["p%d" % (2 + c // 4)])
                def SET(pr):
                    if pr == 0:
                        return dict(ktp=ktp, vtp=vtp, vtb=vtb, Sm=Sm, Eb=Eb, Cg=Cg, nrep=nrep, hh=hh, hq=hq, ngc=ngc2[:, 0, :], X=[], k="0")
                    vv = hT[:, 16:20, :]
                    return dict(ktp=vv[:, 0, 0:256], vtp=vv[:, 0, 256:512], vtb=vv[:, 1, 0:256], Sm=vv[:, 1, 256:384], Eb=vv[:, 1, 384:512],
                                Cg=vv[:, 2, :].rearrange("p (a b) -> p a b", a=2), nrep=vv[:, 3, 0:256].rearrange("p (a b) -> p a b", a=2),
                                hh=tAB[:, 0:256].rearrange("p (a b) -> p a b", a=2), hq=tAB[:, 256:512].rearrange("p (a b) -> p a b", a=2),
                                ngc=ngc2[:, 1, :], X=["h16", "h17", "h18", "h19", "tA"], k="1")

                def prep(h):
                    S_ = SET(h % 2)
                    X = S_["X"]
                    N = lambda n: "ml%s%s" % (n, S_["k"])
                    kps = pb[h // 2][0:Lc, (h % 2) * 256:(h % 2) * 256 + 256]
                    vps = pb[2 + h // 2][0:Lc, (h % 2) * 256:(h % 2) * 256 + 256]
                    kk, vk = "p%d" % (h // 2), "p%d" % (2 + h // 2)
                    for kc in range(2):
                        mm(pb[7][0:Lc, 0:Lc], qT[:, 8 + 2 * h + kc, t0:t1], qT[:, 2 * h + kc, t0:t1], kc == 0, kc == 1,
                           ["h%d" % (8 + 2 * h + kc), "h%d" % (2 * h + kc)], ["p7"])
                    E("dve", CALL("scalar_tensor_tensor", S_["Sm"][0:Lc, 0:Lc], pb[7][0:Lc, 0:Lc], 1.0 / 16, triu[0:Lc, 0:Lc], ALU.mult, ALU.mult), ["p7", "triu"] + X, [N("Sm")])
                    E("dve", CALL("tensor_scalar", S_["vtp"][0:Lc, :], vps, et[0:Lc, h:h + 1], 0.0, ALU.mult, ALU.add), [vk, "et"] + X, [N("vtp")])
                    E("act", CALL("copy", S_["vtb"][0:Lc, :], vps), [vk] + X, [N("vtb")])
                    E("dve", CALL("tensor_scalar", S_["ktp"][0:Lc, :], kps, et[0:Lc, h:h + 1], 1.0 / 16, ALU.mult, ALU.mult), [kk, "et"] + X, [N("ktp")])
                    E("pool", CALL("tensor_scalar", S_["Eb"][0:Lc, :], ones_bf[0:Lc, :], et[0:Lc, h:h + 1], 0.0, ALU.mult, ALU.add), ["ones_bf", "et"] + X, [N("Eb")])
                    E("pool", CALL("tensor_scalar", S_["Cg"].rearrange("p a b -> p (a b)"), CT[:, h, :, :].rearrange("p a b -> p (a b)"), gb[:, h:h + 1], 0.0, ALU.mult, ALU.add),
                      ["CT%d" % h, "gb"] + X, [N("Cg")])
                    E("dve", CALL("tensor_scalar", S_["ngc"], nT[:, 2 * h:2 * h + 2], gb[:, h:h + 1], 0.0, ALU.mult, ALU.add), ["nT%d" % h, "gb"], [N("ngc")])
                    for kc in range(2):
                        E("pool", CALL("tensor_scalar", S_["nrep"][:, kc, :], ones_bf[:, :], S_["ngc"][:, kc:kc + 1], 0.0, ALU.mult, ALU.add), ["ones_bf", N("ngc")] + X, [N("nrep")])

                def mid(h):
                    S_ = SET(h % 2)
                    X = S_["X"]
                    N = lambda n: "ml%s%s" % (n, S_["k"])
                    hh_, hq_ = S_["hh"], S_["hq"]
                    for vc in range(2):
                        o = pb[4][:, vc * 128:vc * 128 + Lc]
                        mm(o, S_["vtp"][0:Lc, vc * 128:vc * 128 + 128], S_["Sm"][0:Lc, 0:Lc], True, False, [N("vtp"), N("Sm")] + X, ["p4"])
                        for kc in range(2):
                            mm(o, S_["Cg"][:, kc, vc * 128:vc * 128 + 128], qT[:, 2 * h + kc, t0:t1], False, kc == 1, [N("Cg"), "h%d" % (2 * h + kc)] + X, ["p4"])
                    o = pb[4][:, 256:256 + Lc]
                    mm(o, S_["Eb"][0:Lc, :], S_["Sm"][0:Lc, 0:Lc], True, False, [N("Eb"), N("Sm")] + X, ["p4"])
                    for kc in range(2):
                        mm(o, S_["nrep"][:, kc, :], qT[:, 2 * h + kc, t0:t1], False, kc == 1, [N("nrep"), "h%d" % (2 * h + kc)] + X, ["p4"])
                    mm(pb[4][:, 384:384 + Lc], sel4[:, h * 128:h * 128 + 128], g4[1][:, 0:Lc], True, True, ["sel4", "g41"], ["p4"])
                    E("act", CALL("activation", mw[0][:, 0:Lc], pb[4][:, 384:384 + Lc], AF.Exp), ["p4"], ["mw0"])
                    E("act", CALL("activation", mwab[:, 0:Lc], pb[4][:, 256:256 + Lc], AF.Abs), ["p4"], ["mwab"])
                    E("dve", CALL("tensor_tensor", mw[0][:, 0:Lc], mwab[:, 0:Lc], mw[0][:, 0:Lc], ALU.max), ["mwab", "mw0"], ["mw0"])
                    E("act", CALL("activation", mw[0][:, 0:Lc], mw[0][:, 0:Lc], AF.Ln), ["mw0"], ["mw0"])
                    E("act", CALL("activation", mw[0][:, 0:Lc], mw[0][:, 0:Lc], AF.Exp, scale=-1.0), ["mw0"], ["mw0"])
                    for vc in range(2):
                        E("dve", CALL("tensor_tensor", hh_[:, vc, 0:Lc], pb[4][:, vc * 128:vc * 128 + Lc], mw[0][:, 0:Lc], ALU.mult), ["p4", "mw0"] + X, [N("hh")])
                        E("act", CALL("activation", hq_[:, vc, 0:Lc], hh_[:, vc, 0:Lc], AF.Square), [N("hh")] + X, [N("hq")])
                    for kc in range(2):
                        mm(pb[6][:, kc * 256:kc * 256 + 256], S_["ktp"][0:Lc, kc * 128:kc * 128 + 128], S_["vtb"][0:Lc, :], True, True, [N("ktp"), N("vtb")] + X, ["p6"])
                    E("dve", CALL("scalar_tensor_tensor", CT[:, h, :, :].rearrange("p a b -> p (a b)"), CT[:, h, :, :].rearrange("p a b -> p (a b)"),
                                  gb[:, h:h + 1], pb[6][:, :], ALU.mult, ALU.add), ["CT%d" % h, "gb", "p6", N("Cg")], ["CT%d" % h])
                    for kc in range(2):
                        mm(pb[7][:, 136 + kc:137 + kc], S_["ktp"][0:Lc, kc * 128:kc * 128 + 128], ones_bf[0:Lc, 0:1], True, True, [N("ktp"), "ones_bf"] + X, ["p7"])
                    E("dve", CALL("tensor_tensor", nT[:, 2 * h:2 * h + 2], S_["ngc"], pb[7][:, 136:138], ALU.add), [N("ngc"), "p7"], ["nT%d" % h])

                def fin(h):
                    S_ = SET(h % 2)
                    X = S_["X"]
                    N = lambda n: "ml%s%s" % (n, S_["k"])
                    hh_, hq_ = S_["hh"], S_["hq"]
                    for vc in range(2):
                        mm(pb[5][:, 0:Lc], ones256[:], hh_[:, vc, 0:Lc], vc == 0, vc == 1, ["ones256", N("hh")] + X, ["p5"])
                    for vc in range(2):
                        mm(pb[5][:, 128:128 + Lc], ones256[:], hq_[:, vc, 0:Lc], vc == 0, vc == 1, ["ones256", N("hq")] + X, ["p5"])
                    E("act", CALL("activation", mw[1][:, 0:Lc], pb[5][:, 0:Lc], AF.Square), ["p5"], ["mw1"])
                    E("dve", CALL("tensor_tensor", mw[1][:, 0:Lc], pb[5][:, 128:128 + Lc], mw[1][:, 0:Lc], ALU.subtract), ["p5", "mw1"], ["mw1"])
                    E("dve", CALL("tensor_scalar", mw[1][:, 0:Lc], mw[1][:, 0:Lc], 0.0, 0.0, ALU.max, ALU.add), ["mw1"], ["mw1"])
                    E("act", CALL("activation", mw[1][:, 0:Lc], mw[1][:, 0:Lc], AF.Ln, bias=epscol[:, 0:1]), ["mw1", "epscol"], ["mw1"])
                    E("act", CALL("activation", mw[1][:, 0:Lc], mw[1][:, 0:Lc], AF.Exp, scale=-0.5), ["mw1"], ["mw1"])
                    for vc in range(2):
                        c = 2 * h + vc
                        E("dve", CALL("tensor_tensor", mw[2][:, 0:Lc], hh_[:, vc, 0:Lc], pb[5][:, 0:Lc], ALU.subtract), [N("hh"), "p5"] + X, ["mw2"])
                        E("pool", CALL("tensor_tensor", mw[2][:, 0:Lc], mw[2][:, 0:Lc], mw[1][:, 0:Lc], ALU.mult), ["mw2", "mw1"], ["mw2"])
                        E("pool", CALL("tensor_scalar", mw[2][:, 0:Lc], mw[2][:, 0:Lc], V("ml_norm_w", c), 0.0, ALU.mult, ALU.add), ["mw2", "vec"], ["mw2"])
                        E("dve", CALL("scalar_tensor_tensor", mw[3][:, 0:Lc], xc_bf[:, c, t0:t1], V("ml_skip", c), mw[2][:, 0:Lc], ALU.mult, ALU.add),
                          ["xn%d" % c, "vec", "mw2"], ["mw3"])
                        E("pool", CALL("tensor_tensor", ybuf[:, c, t0:t1], mw[3][:, 0:Lc], z_bf[:, c, t0:t1], ALU.mult), ["mw3", "z_bf"], ["ybuf"])

                for blk in [(prep, 0), (prep, 1), (mid, 0), (prep, 2), (mid, 1), (fin, 0), (prep, 3), (mid, 2), (fin, 1), (mid, 3), (fin, 2), (fin, 3)]:
                    blk[0](blk[1])
            E("dve", CALL("tensor_tensor", mprev[:], MM[:, Tt - 1:Tt], Fn[:, Tt - 1:Tt], ALU.subtract), ["MM", "Fn", "Mpc"], ["mprev"])
            for c in range(KC):
                E("act", CALL("copy", xmh[:, c, 0:3], xmh[:, c, Tt:Tt + 3]), ["xmh"], ["xmh"])
            rmsnorm_stats(lambda c: ybuf[:, c, 0:Tt], KC, Tt, ["ybuf"], onesD)
            for c in range(KC):
                E("dve", CALL("scalar_tensor_tensor", mixed[:, 8 + c, 0:Tt], ybuf[:, c, 0:Tt], V("out_norm_ml", c), rstd[:, 0:Tt], ALU.mult, ALU.mult),
                  ["ybuf", "vec", "rstd"], ["mixed"])

        def out_proj(Tt):
            for c in range(KC):
                po = pb[4 + c % 2]
                pk = "p%d" % (4 + c % 2)
                for hf in range(2):
                    oi = hf
                    dma(wor[oi][:], SO[c, hf], reads=WSCR, writes=["wo%d" % oi], q="sp")
                    for k2 in range(8):
                        k = hf * 8 + k2
                        mm(po[:, 0:Tt], wor[oi][:, k2, :], mixed[:, k, 0:Tt], k == 0, k == 15, ["wo%d" % oi, "mixed"], [pk])
                E("dve", CALL("tensor_tensor", xT[:, c, 0:Tt], xT[:, c, 0:Tt], po[:, 0:Tt], ALU.add), [pk, "xT"], ["xT"])

        def load_tile(src_rows, Tt):
            nsub = (Tt + 127) // 128
            for n in range(nsub):
                r = min(128, Tt - n * 128)
                dma(xtok[0:r, n, :], src_rows[n * 128:n * 128 + r, :], writes=["ybuf"])
            for c in range(KC):
                for n in range(nsub):
                    r = min(128, Tt - n * 128)
                    tr(pb[7][:, n * 128:n * 128 + r], xtok[0:r, n, c * 128:(c + 1) * 128], ident[0:r, 0:r], ["ybuf"], ["p7"])
                E("dve" if c % 2 == 0 else "act",
                  (CALL("tensor_copy", xT[:, c, 0:Tt], pb[7][:, 0:Tt])) if c % 2 == 0 else (CALL("copy", xT[:, c, 0:Tt], pb[7][:, 0:Tt])),
                  ["p7"], ["xT"])

        def store_tile(dst_rows, Tt):
            rmsnorm_stats(lambda c: xT[:, c, 0:Tt], KC, Tt, ["xT"], onesD)
            nsub = (Tt + 127) // 128
            for c in range(KC):
                E("dve", CALL("scalar_tensor_tensor", ybuf[:, c, 0:Tt], xT[:, c, 0:Tt], V("norm_final", c), rstd[:, 0:Tt], ALU.mult, ALU.mult),
                  ["xT", "vec", "rstd"], ["ybuf"])
            for n in range(nsub):
                r = min(128, Tt - n * 128)
                for c4 in range(2):
                    for c in range(c4 * 4, c4 * 4 + 4):
                        tr(pb[7][0:r, (c % 4) * 128:(c % 4) * 128 + 128], ybuf[:, c, n * 128:n * 128 + r], ident[:], ["ybuf"], ["p7"])
                    E("act", CALL("copy", otok[0:r, c4 * 512:(c4 + 1) * 512], pb[7][0:r, :]), ["p7"], ["tA", "tB"])
                outs.append(dma(dst_rows[n * 128:n * 128 + r, :], otok[0:r, :], reads=["tA", "tB"], q="sp"))

        def init_state(si):
            if si is None:
                for t, k in [(sre, SRE), (sim, SRE), (nT, ["nT0", "nT1", "nT2", "nT3"]), (mprev, ["mprev"])]:
                    E("dve", CALL("memset", t[:], 0.0), [], k)
                E("pool", CALL("memset", CT[:].rearrange("p a b c -> p (a b c)"), 0.0), [], ["CT0", "CT1", "CT2", "CT3"])
                E("pool", CALL("memset", xmh[:, :, 0:3], 0.0), [], ["xmh"])
                return
            for src, dst, k in [(st_s5re, sre, SRE), (st_s5im, sim, SIM)]:
                dma(stg[0:32, 0:128], src[si], writes=["stg"])
                tr(pb[7][:, 0:32], stg[0:32, 0:128], ident[0:32, 0:32], ["stg"], ["p7"])
                E("dve", CALL("tensor_copy", dst[:], pb[7][:, 0:32]), ["p7"], k)
            dma(stg[0:8, 0:128], st_n[si], writes=["stg"])
            tr(pb[7][:, 0:8], stg[0:8, 0:128], ident[0:8, 0:8], ["stg"], ["p7"])
            E("dve", CALL("tensor_copy", nT[:], pb[7][:, 0:8]), ["p7"], ["nT0", "nT1", "nT2", "nT3"])
            dma(mprev[:], st_m[si], writes=["mprev"])
            for c in range(KC):
                dma(stg[0:3, 0:128], st_conv[si][:, c * 128:(c + 1) * 128], writes=["stg"])
                tr(pb[7][:, 0:3], stg[0:3, 0:128], ident[0:3, 0:3], ["stg"], ["p7"])
                E("dve", CALL("tensor_copy", xmh[:, c, 0:3], pb[7][:, 0:3]), ["p7"], ["xmh"])
            for h in range(4):
                for vc in range(2):
                    dma(stg[:, 0:256], st_c[si, h, vc * 128:(vc + 1) * 128, :], writes=["stg"])
                    for kc in range(2):
                        tr(pb[7][:, kc * 128:kc * 128 + 128], stg[:, kc * 128:kc * 128 + 128], ident[:], ["stg"], ["p7"])
                    E("dve", CALL("tensor_copy", CT[:, h, :, vc * 128:vc * 128 + 128], pb[7][:, 0:256].rearrange("p (k v) -> p k v", k=2)), ["p7"], ["CT0", "CT1", "CT2", "CT3"])

        def store_state(oi):
            for src, dst, k in [(sre, o_s5re, SRE), (sim, o_s5im, SIM)]:
                tr(pb[7][0:32, 0:128], src[:], ident[:], k, ["p7"])
                E("dve", CALL("tensor_copy", stg[0:32, 0:128], pb[7][0:32, 0:128]), ["p7"], ["stg"])
                outs.append(dma(dst[oi], stg[0:32, 0:128], reads=["stg"], q="sp"))
            tr(pb[7][0:8, 0:128], nT[:], ident[:], ["nT0", "nT1", "nT2", "nT3"], ["p7"])
            E("dve", CALL("tensor_copy", stg[0:8, 0:128], pb[7][0:8, 0:128]), ["p7"], ["stg"])
            outs.append(dma(o_n[oi], stg[0:8, 0:128], reads=["stg"], q="sp"))
            outs.append(dma(o_m[oi], mprev[:], reads=["mprev"], q="sp"))
            for c in range(KC):
                E("dve", CALL("tensor_copy", mw[0][:, 0:3], xmh[:, c, 0:3]), ["xmh"], ["mw0"])
                tr(pb[7][0:3, 0:128], mw[0][:, 0:3], ident[:], ["mw0"], ["p7"])
                E("dve", CALL("tensor_copy", stg[0:3, 0:128], pb[7][0:3, 0:128]), ["p7"], ["stg"])
                outs.append(dma(o_conv[oi][:, c * 128:(c + 1) * 128], stg[0:3, 0:128], reads=["stg"], q="sp"))
            for h in range(4):
                for vc in range(2):
                    for kc in range(2):
                        tr(pb[7][:, kc * 128:kc * 128 + 128], CT[:, h, kc, vc * 128:vc * 128 + 128], ident[:], ["CT0", "CT1", "CT2", "CT3"], ["p7"])
                    E("dve", CALL("tensor_copy", stg[:, 0:256], pb[7][:, 0:256]), ["p7"], ["stg"])
                    outs.append(dma(o_c[oi, h, vc * 128:(vc + 1) * 128, :], stg[:, 0:256], reads=["stg"], q="sp"))

        def run_tile(src_rows, dst_rows, Tt):
            load_tile(src_rows, Tt)
            if STAGE >= 2:
                ffn("ffn1", "norm_ffn1", Tt)
            if STAGE >= 3:
                in_proj(Tt)
            if STAGE >= 4:
                s5_mix(Tt)
            if STAGE >= 5:
                mlstm_mix(Tt)
            if STAGE >= 6:
                out_proj(Tt)
            if STAGE >= 7:
                ffn("ffn2", "norm_ffn2", Tt)
            if dst_rows is not None:
                store_tile(dst_rows, Tt)

        if STAGE == 0:
            P.emit(final_waits=outs)
            return nc
        init_state(None)
        run_tile(xp[0:NMETA, :], None, NMETA)
        for ti in range((NP - NMETA) // TT):
            r0 = NMETA + ti * TT
            run_tile(xp[r0:r0 + TT, :], yp[r0 - NMETA:r0 - NMETA + TT, :], TT)
        store_state(0)
        for si in range(NSAMP):
            init_state(si)
            run_tile(xs[si], ys[si], SL)
            store_state(1 + si)
        P.emit(final_waits=outs)
    return nc


def host_consts():
    p = np.arange(128)
    c = {}
    c["ident"] = np.eye(128, dtype=np.float32)
    c["maskE"] = np.stack([(p // 64 == 0), (p // 64 == 1)], 1).astype(np.float32)
    p32 = np.arange(32)
    c["mask16"] = np.stack([(p32 // 16 == 0), (p32 // 16 == 1)], 1).astype(np.float32)
    c["triu"] = np.triu(np.ones((128, 128), np.float32))
    c["bdmask"] = (p[:, None] // 4 == np.arange(32)[None, :]).astype(np.float32)
    c["tvec"] = np.broadcast_to(np.arange(1, 65, dtype=np.float32)[None, :], (128, 64)).copy()
    s = np.zeros((4, 4, 128), np.float32)
    for h in range(4):
        s[h, h, :] = 1.0
    c["sel4"] = s.reshape(4, 512)
    c["mask4"] = (p[:, None] // 32 == np.arange(4)[None, :]).astype(np.float32)
    return c


_CACHE = {}


def kernel(**inp):
    f = lambda a: np.ascontiguousarray(np.asarray(a, dtype=np.float32))
    x_prompt = f(inp["x_prompt"])
    x_sample = f(inp["x_sample"])
    NB, SEQ, _ = x_prompt.shape
    NDEC, SL, _ = x_sample.shape
    NP = NMETA + SEQ
    ncores = 8
    NSAMP = NDEC // ncores
    key = (NP, NSAMP, SL)
    if key not in _CACHE:
        _CACHE[key] = build_program(NP, NSAMP, SL)
    nc = _CACHE[key]
    meta = f(inp["meta_tokens"])
    shared = host_consts()
    for n in ["ffn1_gate", "ffn1_up", "ffn1_down", "ffn2_gate", "ffn2_up", "ffn2_down", "w_in", "s5_glu_w", "w_out"]:
        shared[n] = f(inp[n])[0]
    shared["lam_re"] = f(inp["s5_lambda_re"])[0].reshape(32, 128)
    shared["lam_im"] = f(inp["s5_lambda_im"])[0].reshape(32, 128)
    shared["log_dt"] = f(inp["s5_log_dt"])[0]
    shared["b_re"] = f(inp["s5_b_re"])[0]
    shared["b_im"] = f(inp["s5_b_im"])[0]
    shared["c_re"] = f(inp["s5_c_re"])[0]
    shared["c_im"] = f(inp["s5_c_im"])[0]
    shared["wq"] = f(inp["ml_wq"])[0]
    shared["wk"] = f(inp["ml_wk"])[0]
    shared["wv"] = f(inp["ml_wv"])[0]
    shared["igw"] = f(inp["ml_igate_w"])[0]
    shared["fgw"] = f(inp["ml_fgate_w"])[0]
    shared["igb"] = f(inp["ml_igate_b"])[0].reshape(4, 1)
    shared["fgb"] = f(inp["ml_fgate_b"])[0].reshape(4, 1)
    cw = f(inp["ml_conv_w"])[0]
    vd = {"norm_ffn1": f(inp["norm_ffn1"])[0], "norm_mix": f(inp["norm_mix"])[0], "s5_d": f(inp["s5_d"])[0],
          "s5_glu_b": f(inp["s5_glu_b"])[0], "cw0": cw[0], "cw1": cw[1], "cw2": cw[2], "cw3": cw[3],
          "ml_conv_b": f(inp["ml_conv_b"])[0], "ml_norm_w": f(inp["ml_norm_w"])[0], "ml_skip": f(inp["ml_skip"])[0],
          "out_norm_s5": f(inp["out_norm_s5"])[0], "out_norm_ml": f(inp["out_norm_ml"])[0],
          "norm_ffn2": f(inp["norm_ffn2"])[0], "norm_final": f(inp["norm_final"])}
    shared["vecs"] = np.concatenate([vd[n].reshape(8, 128) for n in VEC_NAMES], 0)
    s5re, s5im = f(inp["state_s5_re"])[0], f(inp["state_s5_im"])[0]
    stc, stn, stm, stcv = f(inp["state_mlstm_c"])[0], f(inp["state_mlstm_n"])[0], f(inp["state_mlstm_m"])[0], f(inp["state_mlstm_conv"])[0]
    in_maps = []
    for c in range(ncores):
        b = c % NB
        sl = slice(c * NSAMP, (c + 1) * NSAMP)
        m = dict(shared)
        m["xp"] = np.concatenate([meta, x_prompt[b]], 0)
        m["xs"] = x_sample[sl]
        m["st_s5re"] = s5re[sl].reshape(NSAMP, 32, 128)
        m["st_s5im"] = s5im[sl].reshape(NSAMP, 32, 128)
        m["st_c"] = stc[sl]
        m["st_n"] = stn[sl].reshape(NSAMP, 8, 128)
        m["st_m"] = stm[sl].reshape(NSAMP, 4, 1)
        m["st_conv"] = stcv[sl]
        in_maps.append(m)
    res = run_bass_kernel_spmd(nc, in_maps, core_ids=list(range(ncores))).results
    y_prompt = np.stack([res[b]["yp"] for b in range(NB)], 0)
    y_sample = np.concatenate([res[c]["ys"] for c in range(ncores)], 0)

    def gather(name, shape_tail):
        pr = np.stack([res[b][name][0] for b in range(NB)], 0).reshape((1, NB) + shape_tail)
        sm = np.concatenate([res[c][name][1:] for c in range(ncores)], 0).reshape((1, NDEC) + shape_tail)
        return pr.astype(np.float32), sm.astype(np.float32)

    p_re, s_re = gather("o_s5re", (64, 64))
    p_im, s_im = gather("o_s5im", (64, 64))
    p_c, s_c = gather("o_c", (4, 256, 256))
    p_n, s_n = gather("o_n", (4, 256))
    p_m, s_m = gather("o_m", (4,))
    p_cv, s_cv = gather("o_conv", (3, 1024))
    return (y_prompt.astype(np.float32), y_sample.astype(np.float32), p_re, p_im, p_c, p_n, p_m, p_cv,
            s_re, s_im, s_c, s_n, s_m, s_cv)
```
